# Optimizing a Trainium2 kernel written in Bass

```python
import math
import jax, jax.numpy as jnp
from jax import lax
import numpy as np

D_MODEL = 1024
BATCH = 8
SEQ = 2048
DEPTH = 2

ATT_HEADS = 8
ATT_HEAD_DIM = 64
ATT_WIDTH = ATT_HEADS * ATT_HEAD_DIM
DILATED_PATTERNS = ((128, 1), (512, 4), (2048, 16))
GLA_HEADS = 4
GLA_KEY_WIDTH = D_MODEL // 4
GLA_VAL_WIDTH = D_MODEL // 2
GLA_GATE_RANK = 16
GLA_GATE_TEMP = 16.0
GLA_CHUNK = 64
D_FF = ((8 * D_MODEL // 3 + 127) // 128) * 128
LN_EPS = 1e-5
DEEPNORM_ALPHA = (2 * DEPTH) ** 0.25
DEEPNORM_BETA = (8 * DEPTH) ** -0.25
IN_SPLITS = (ATT_WIDTH, ATT_WIDTH, ATT_WIDTH,
             GLA_KEY_WIDTH, GLA_KEY_WIDTH, GLA_VAL_WIDTH, GLA_GATE_RANK, GLA_VAL_WIDTH,
             D_MODEL, D_MODEL)
IN_OFFSETS = tuple(int(v) for v in np.cumsum(IN_SPLITS)[:-1])
N_IN = sum(IN_SPLITS)

kernel_name = 'hybrid_dilated_gla_macaron_deepnorm'


def layer_norm(x, g, b):
    xf = x.astype(jnp.float32)
    mu = jnp.mean(xf, axis=-1, keepdims=True)
    var = jnp.mean(jnp.square(xf - mu), axis=-1, keepdims=True)
    return ((xf - mu) * lax.rsqrt(var + LN_EPS) * g + b).astype(x.dtype)


def swiglu(x, w1, w3, w2):
    return (jax.nn.silu(x @ w1) * (x @ w3)) @ w2


def alibi_slopes(n_heads):
    return jnp.exp2(-8.0 * jnp.arange(1, n_heads + 1, dtype=jnp.float32) / n_heads)


def dilated_pattern(q, k, v, window, dilation, slopes):
    B, S, H, Dh = q.shape
    r = dilation
    L = S // r
    n_back = window // r
    Q = min(n_back, L)
    nb = -(-L // Q)
    Lp = nb * Q

    def to_blocks(t):
        t = t.reshape(B, L, r, H, Dh).transpose(0, 2, 3, 1, 4)
        t = jnp.pad(t, ((0, 0), (0, 0), (0, 0), (0, Lp - L), (0, 0)))
        return t.reshape(B, r, H, nb, Q, Dh)

    def with_prev(t):
        prev = jnp.pad(t[:, :, :, :-1], ((0, 0), (0, 0), (0, 0), (1, 0), (0, 0), (0, 0)))
        return jnp.concatenate([prev, t], axis=4)

    qb = to_blocks(q)
    kk = with_prev(to_blocks(k))
    vv = with_prev(to_blocks(v))
    s = jnp.einsum('brhnqd,brhnkd->brhnqk', qb, kk).astype(jnp.float32) * (Dh ** -0.5)
    qpos = jnp.arange(nb)[:, None, None] * Q + jnp.arange(Q)[None, :, None]
    kpos = jnp.arange(nb)[:, None, None] * Q - Q + jnp.arange(2 * Q)[None, None, :]
    dist = qpos - kpos
    valid = (dist >= 0) & (dist <= n_back) & (kpos >= 0)
    bias = -slopes[:, None, None, None] * (dist * r).astype(jnp.float32)
    s = jnp.where(valid, s + bias, -jnp.inf)
    lse = jax.nn.logsumexp(s, axis=-1)
    p = jnp.exp(s - lse[..., None])
    o = jnp.einsum('brhnqk,brhnkd->brhnqd', p.astype(v.dtype), vv)

    def from_blocks(t, tail):
        t = t.reshape((B, r, H, Lp) + tail)[:, :, :, :L]
        t = jnp.moveaxis(t, 3, 1)
        return t.reshape((B, S, H) + tail)

    return from_blocks(o, (Dh,)), from_blocks(lse, ())


def dilated_attention(q, k, v):
    B, S, H, Dh = q.shape
    slopes = alibi_slopes(H)
    outs, lses = [], []
    for window, dilation in DILATED_PATTERNS:
        o_p, lse_p = dilated_pattern(q, k, v, window, dilation, slopes)
        outs.append(o_p.astype(jnp.float32))
        lses.append(lse_p)
    w = jax.nn.softmax(jnp.stack(lses, axis=0), axis=0)
    o = jnp.sum(w[..., None] * jnp.stack(outs, axis=0), axis=0)
    return o.reshape(B, S, H * Dh).astype(q.dtype)


def gla(q, k, v, log_alpha):
    B, S, H, dk = q.shape
    dv = v.shape[-1]
    C = GLA_CHUNK
    n = S // C

    def chunks(t):
        return t.astype(jnp.float32).reshape(B, n, C, H, t.shape[-1]).transpose(1, 0, 3, 2, 4)

    causal = jnp.tril(jnp.ones((C, C), dtype=bool))

    def step(state, inp):
        qc, kc, vc, gc = inp
        b = jnp.cumsum(gc, axis=2)
        o_inter = jnp.einsum('bhtk,bhkv->bhtv', qc * jnp.exp(b), state)
        diff = b[:, :, :, None, :] - b[:, :, None, :, :]
        decay = jnp.exp(jnp.where(causal[:, :, None], diff, -jnp.inf))
        a = jnp.einsum('bhtk,bhsk,bhtsk->bhts', qc, kc, decay)
        o = o_inter + jnp.einsum('bhts,bhsv->bhtv', a, vc)
        b_last = b[:, :, -1:, :]
        state = (jnp.exp(b_last[:, :, 0, :, None]) * state
                 + jnp.einsum('bhsk,bhsv->bhkv', kc * jnp.exp(b_last - b), vc))
        return state, o

    state0 = jnp.zeros((B, H, dk, dv), jnp.float32)
    q = q * (dk ** -0.5)
    _, o = lax.scan(step, state0, (chunks(q), chunks(k), chunks(v), chunks(log_alpha)))
    return o.transpose(1, 0, 3, 2, 4).reshape(B, S, H, dv)


def hybrid_mixer(x, w_in, w_gate_up, b_gate_up, gn_g, gn_b, w_attn_proj, w_gla_proj, w_out):
    B, S, _ = x.shape
    h = x @ w_in
    aq, ak, av, gq, gk, gv, g_lr, g_r, gate_a, gate_b = jnp.split(h, IN_OFFSETS, axis=-1)
    heads = lambda t, nh: t.reshape(B, S, nh, -1)
    o_a = dilated_attention(heads(aq, ATT_HEADS), heads(ak, ATT_HEADS), heads(av, ATT_HEADS))
    log_alpha = jax.nn.log_sigmoid((g_lr @ w_gate_up + b_gate_up).astype(jnp.float32)) / GLA_GATE_TEMP
    o_g = gla(heads(gq, GLA_HEADS), heads(gk, GLA_HEADS), heads(gv, GLA_HEADS),
              heads(log_alpha, GLA_HEADS))
    mu = jnp.mean(o_g, axis=-1, keepdims=True)
    var = jnp.mean(jnp.square(o_g - mu), axis=-1, keepdims=True)
    o_g = ((o_g - mu) * lax.rsqrt(var + LN_EPS)).reshape(B, S, GLA_VAL_WIDTH) * gn_g + gn_b
    o_g = (o_g * jax.nn.silu(g_r.astype(jnp.float32))).astype(x.dtype)
    merged = (jax.nn.sigmoid(gate_a) * (o_a @ w_attn_proj)
              + jax.nn.sigmoid(gate_b) * (o_g @ w_gla_proj))
    return merged @ w_out


def setup_inputs(seed: int = 0) -> dict:
    key = jax.random.key(seed)
    ks = jax.random.split(key, 17)
    nrm = lambda k, shape, scale: jax.random.normal(k, shape, jnp.float32) * scale
    L = DEPTH
    return {
        'x': nrm(ks[0], (BATCH, SEQ, D_MODEL), 1.0),
        'w_in': nrm(ks[1], (L, D_MODEL, N_IN), D_MODEL ** -0.5),
        'w_gate_up': nrm(ks[2], (L, GLA_GATE_RANK, GLA_KEY_WIDTH), GLA_GATE_RANK ** -0.5),
        'b_gate_up': nrm(ks[3], (L, GLA_KEY_WIDTH), 0.1),
        'gla_norm_g': 1.0 + nrm(ks[4], (L, GLA_VAL_WIDTH), 0.02),
        'gla_norm_b': nrm(ks[5], (L, GLA_VAL_WIDTH), 0.02),
        'w_attn_proj': nrm(ks[6], (L, ATT_WIDTH, D_MODEL), ATT_WIDTH ** -0.5),
        'w_gla_proj': nrm(ks[7], (L, GLA_VAL_WIDTH, D_MODEL), GLA_VAL_WIDTH ** -0.5),
        'w_out': nrm(ks[8], (L, D_MODEL, D_MODEL), D_MODEL ** -0.5 * DEEPNORM_BETA),
        'ffn1_w1': nrm(ks[9], (L, D_MODEL, D_FF), D_MODEL ** -0.5),
        'ffn1_w3': nrm(ks[10], (L, D_MODEL, D_FF), D_MODEL ** -0.5),
        'ffn1_w2': nrm(ks[11], (L, D_FF, D_MODEL), D_FF ** -0.5 * DEEPNORM_BETA),
        'ffn2_w1': nrm(ks[12], (L, D_MODEL, D_FF), D_MODEL ** -0.5),
        'ffn2_w3': nrm(ks[13], (L, D_MODEL, D_FF), D_MODEL ** -0.5),
        'ffn2_w2': nrm(ks[14], (L, D_FF, D_MODEL), D_FF ** -0.5 * DEEPNORM_BETA),
        'ln_g': 1.0 + nrm(ks[15], (L, 3, D_MODEL), 0.02),
        'ln_b': nrm(ks[16], (L, 3, D_MODEL), 0.02),
    }


def reference(x, w_in, w_gate_up, b_gate_up, gla_norm_g, gla_norm_b, w_attn_proj, w_gla_proj,
              w_out, ffn1_w1, ffn1_w3, ffn1_w2, ffn2_w1, ffn2_w3, ffn2_w2, ln_g, ln_b):
    for l in range(DEPTH):
        x = layer_norm(DEEPNORM_ALPHA * x + 0.5 * swiglu(x, ffn1_w1[l], ffn1_w3[l], ffn1_w2[l]),
                       ln_g[l, 0], ln_b[l, 0])
        x = layer_norm(DEEPNORM_ALPHA * x + hybrid_mixer(x, w_in[l], w_gate_up[l], b_gate_up[l],
                                                         gla_norm_g[l], gla_norm_b[l], w_attn_proj[l],
                                                         w_gla_proj[l], w_out[l]),
                       ln_g[l, 1], ln_b[l, 1])
        x = layer_norm(DEEPNORM_ALPHA * x + 0.5 * swiglu(x, ffn2_w1[l], ffn2_w3[l], ffn2_w2[l]),
                       ln_g[l, 2], ln_b[l, 2])
    return x
```

```python
import numpy as np
import concourse.bass as bass
import concourse.mybir as mybir
from concourse.bass_utils import run_bass_kernel_spmd

F32 = mybir.dt.float32
BF16 = mybir.dt.bfloat16
AF = mybir.ActivationFunctionType
ALU = mybir.AluOpType

S = 2048
D = 1024
DFF = 2816
NC_ = 8
NTB = 4
TB = 512
NF = 22
DEPTH = 2
ALPHA = float((2 * DEPTH) ** 0.25)
LN_EPS = 1e-5
FFN_GROUPS = [4, 4, 4, 4, 3, 3]
GMAX = 4
NL3 = DEPTH * 3 * NC_
SB_BASE = 16512
SB_TOP = 229344

O_AQ, O_AK, O_AV = 0, 512, 1024
O_GQ, O_GK, O_GV, O_GLR, O_GR = 1536, 1792, 2048, 2560, 2576
O_GA, O_GB = 3088, 4112
N_IN = 5136


class Buf:
    __slots__ = ("name", "w", "r", "excl")

    def __init__(self, name="", excl=False):
        self.name = name
        self.w = None
        self.r = {}
        self.excl = excl


def bufs(n):
    return [Buf() for _ in range(n)]


class Op:
    __slots__ = ("eng", "fn", "deps", "needed", "semval", "dsem", "dval")

    def __init__(self, eng, fn, deps, dsem):
        self.eng = eng
        self.fn = fn
        self.deps = deps
        self.needed = False
        self.semval = None
        self.dsem = dsem
        self.dval = None


class Sched:
    ENGS = ("pe", "act", "dve", "pool", "sp")

    def __init__(self):
        self.q = {e: [] for e in self.ENGS}
        self.dma_count = {}
        self.last_dma = {}
        self.extra = {e: [] for e in self.ENGS}

    def op(self, eng, fn, R=(), W=(), dsem=None, after=()):
        deps = [(3, a) for a in after if a is not None]
        if any(b.excl for b in R):
            W = list(W) + [b for b in R if b.excl]
            R = [b for b in R if not b.excl]
        W = list(dict.fromkeys(W))
        for b in R:
            if b.w is not None:
                deps.append((0, b.w))
        for b in W:
            if b.w is not None:
                deps.append((1, b.w))
            for r in b.r.values():
                deps.append((2, r))
        if self.extra[eng]:
            deps.extend((0, d) for d in self.extra[eng])
            self.extra[eng] = []
        o = Op(eng, fn, deps, dsem)
        if dsem is not None:
            self.dma_count[dsem] = self.dma_count.get(dsem, 0) + 16
            o.dval = self.dma_count[dsem]
            self.last_dma[dsem] = o
        self.q[eng].append(o)
        key = dsem if dsem is not None else eng
        for b in R:
            b.r[key] = o
        for b in W:
            b.w = o
            b.r = {}
        return o

    def fence(self):
        snap = []
        for e in self.ENGS:
            for o in reversed(self.q[e]):
                if o.dsem is None:
                    snap.append(o)
                    break
        snap.extend(self.last_dma.values())
        for e in self.ENGS:
            self.extra[e] = list(snap)

    def finalize(self):
        for eng in self.ENGS:
            for o in self.q[eng]:
                keep = []
                for kind, d in o.deps:
                    if d is o:
                        continue
                    if d.dsem is not None:
                        keep.append(d)
                    elif d.eng == o.eng:
                        if o.eng == "pe" and kind != 3:
                            continue
                        keep.append(d)
                    else:
                        keep.append(d)
                for d in keep:
                    if d.dsem is None:
                        d.needed = True
                o.deps = keep
        for eng in self.ENGS:
            c = 0
            for o in self.q[eng]:
                if o.dsem is None and o.needed:
                    c += 1
                    o.semval = c

    def replay(self, eng, e, esem, dsems):
        seen = {}
        for o in self.q[eng]:
            for d in o.deps:
                if d.dsem is not None:
                    key, val, sem = ("d", d.dsem), d.dval, dsems[d.dsem]
                else:
                    key, val, sem = ("e", d.eng), d.semval, esem[d.eng]
                if seen.get(key, 0) >= val:
                    continue
                seen[key] = val
                e.wait_ge(sem, val)
            ins = o.fn(e)
            if o.dsem is not None:
                ins.then_inc(dsems[o.dsem], 16)
            elif o.needed:
                ins.then_inc(esem[eng], 1)


DT_SIZE = {F32: 4, BF16: 2}


class Arena:
    UID = 0

    def __init__(self, nc, base, top):
        self.nc = nc
        self.base = base
        self.top = top
        self.off = base
        self.uid = 0
        self.peak = base

    def alloc(self, shape, dt, name="t"):
        n = 1
        for d in shape[1:]:
            n *= d
        nbytes = (n * DT_SIZE[dt] + 63) // 64 * 64
        if self.off + nbytes > self.top:
            raise RuntimeError(f"arena overflow allocating {name} {shape}: off={self.off - self.base} need {nbytes} cap {self.top - self.base}")
        Arena.UID += 1
        t = self.nc.alloc_sbuf_tensor_at(f"{name}_{Arena.UID}", list(shape), dt, offset=self.off)
        self.off += nbytes
        self.peak = max(self.peak, self.off)
        return t

    def mark(self):
        return self.off

    def release(self, m):
        self.off = m


def build(phases, debug=None):
    nc = bass.Bass("TRN2", target_bir_lowering=False)
    s = Sched()
    dram = {}
    dsem_names = []

    def new_dsem(name):
        nm = f"{name}_{len(dsem_names)}"
        dsem_names.append(nm)
        return nm

    def din(name, shape, dt=F32):
        if name not in dram:
            dram[name] = nc.dram_tensor(name, list(shape), dt, kind="ExternalInput").ap()
        return dram[name]

    debug = debug or ()

    def tap(name, t, bl):
        if name in debug:
            dd = nc.dram_tensor("dbg_" + name, list(t.shape), t.dtype, kind="ExternalOutput").ap()
            s.op("sp", lambda e: e.dma_start(out=dd, in_=t[:]), R=bl, dsem=new_dsem("dbg"))

    xT_d = din("xT", [D, S])
    yT_d = nc.dram_tensor("yT", [D, S], F32, kind="ExternalOutput").ap()
    lng_d = din("ln_g", [128, NL3])
    lnb_d = din("ln_b", [128, NL3])
    consts_d = din("consts", [128, 5 * 128])
    has_mix = any(p[0] == "mix" for p in phases)

    A = Arena(nc, SB_BASE, SB_TOP)
    xs = A.alloc([128, NC_, S], F32, "xs")
    xb = A.alloc([128, NC_, S], BF16, "xb")
    xs_b = [bufs(NTB) for _ in range(NC_)]
    xb_b = [bufs(NTB) for _ in range(NC_)]
    lng = A.alloc([128, NL3], F32, "lng")
    lnb = A.alloc([128, NL3], F32, "lnb")
    lnga = A.alloc([128, NL3], F32, "lnga")
    lnba = A.alloc([128, NL3], F32, "lnba")
    ln_c = Buf()
    consts = A.alloc([128, 5 * 128], F32, "consts")
    consts_b = Buf()
    Umat = consts[:, 0:128]
    Lmat = consts[:, 128:256]
    Mblk = consts[:, 256:384]
    ones = consts[:, 384:512]
    gones = consts[:, 512:640]
    ones1 = A.alloc([128, 64], F32, "ones1")
    ones1_b = Buf()
    mixc_b = Buf()
    if has_mix:
        wgu = A.alloc([16, DEPTH, 256], F32, "wgu")
        bgu = A.alloc([128, DEPTH, 256], F32, "bgu")
        gng = A.alloc([128, DEPTH * 4], F32, "gng")
        gnb = A.alloc([128, DEPTH * 4], F32, "gnb")
    arena_base = A.mark()

    psum = [nc.alloc_psum_tensor(f"bank{i}", [128, TB], F32) for i in range(8)]
    pq = [[Buf(f"bank{i}", excl=True)] * 4 for i in range(8)]

    def pb_(i, c0=0, c1=TB):
        return pq[i][c0 // 128:(c1 + 127) // 128]

    out_sem = new_dsem("out")

    lng_b0, lnb_b0 = Buf(), Buf()
    s.op("sp", lambda e: e.dma_start(out=lng[:], in_=lng_d), W=[lng_b0], dsem=new_dsem("io"))
    s.op("sp", lambda e: e.dma_start(out=lnb[:], in_=lnb_d), W=[lnb_b0], dsem=new_dsem("io"))
    s.op("sp", lambda e: e.dma_start(out=consts[:], in_=consts_d), W=[consts_b], dsem=new_dsem("io"))
    if has_mix:
        wgu_d = din("wgu", [16, DEPTH, 256])
        bgu_d = din("bgu", [128, DEPTH, 256])
        gng_d = din("gng", [128, DEPTH * 4])
        gnb_d = din("gnb", [128, DEPTH * 4])
        mb = bufs(4)
        s.op("sp", lambda e: e.dma_start(out=wgu[:], in_=wgu_d), W=[mb[0]], dsem=new_dsem("io"))
        s.op("sp", lambda e: e.dma_start(out=bgu[:], in_=bgu_d), W=[mb[1]], dsem=new_dsem("io"))
        s.op("sp", lambda e: e.dma_start(out=gng[:], in_=gng_d), W=[mb[2]], dsem=new_dsem("io"))
        s.op("sp", lambda e: e.dma_start(out=gnb[:], in_=gnb_d), W=[mb[3]], dsem=new_dsem("io"))
        s.op("dve", lambda e: e.memset(ones1[:], 1.0), R=mb, W=[ones1_b, mixc_b])
    for c in range(NC_):
        s.op("sp", lambda e, c=c: e.dma_start(out=xs[:, c, :], in_=xT_d[c * 128:(c + 1) * 128, :]),
             W=xs_b[c], dsem=new_dsem("iox"))
    s.op("act", lambda e: e.mul(lnga[:], lng[:], ALPHA), R=[lng_b0], W=[ln_c])
    s.op("act", lambda e: e.mul(lnba[:], lnb[:], ALPHA), R=[lnb_b0], W=[ln_c])
    for c in range(NC_):
        for t in range(NTB):
            sl = slice(t * TB, (t + 1) * TB)
            s.op("dve", lambda e, c=c, sl=sl: e.tensor_copy(out=xb[:, c, sl], in_=xs[:, c, sl]),
                 R=[xs_b[c][t]], W=[xb_b[c][t]])
            s.op("act", lambda e, c=c, sl=sl: e.mul(xs[:, c, sl], xs[:, c, sl], ALPHA),
                 R=[xs_b[c][t]], W=[xs_b[c][t]])

    def layer_norm(l, i):
        col0 = (l * 3 + i) * NC_
        s.fence()
        A.release(arena_base)
        sq = A.alloc([128, NC_, TB], F32, "sq")
        sq_b = bufs(NC_)
        mean_sb = [A.alloc([128, TB], F32, "mean") for _ in range(2)]
        m2_sb = [A.alloc([128, TB], F32, "m2") for _ in range(2)]
        rstd_sb = [A.alloc([128, TB], F32, "rstd") for _ in range(2)]
        mean_b, m2_b, rstd_b = bufs(2), bufs(2), bufs(2)
        t1 = [A.alloc([128, TB], F32, "t1") for _ in range(2)]
        t2 = [A.alloc([128, TB], F32, "t2") for _ in range(3)]
        t1_b, t2_b = bufs(2), bufs(3)
        cn = {"t1": 0, "t2": 0}

        def stats(t):
            sl = slice(t * TB, (t + 1) * TB)
            p = t % 2
            bm, bq = (6, 7) if p == 0 else (4, 5)
            pm, pq_ = psum[bm], psum[bq]
            for c in range(NC_):
                s.op("act", lambda e, c=c, sl=sl: e.activation(out=sq[:, c, :], in_=xs[:, c, sl], func=AF.Square),
                     R=[xs_b[c][t]], W=[sq_b[c]])
            for c in range(NC_):
                s.op("pe", lambda e, c=c, sl=sl, pm=pm: e.matmul(pm[:], lhsT=ones, rhs=xs[:, c, sl],
                                                                 start=(c == 0), stop=(c == NC_ - 1)),
                     R=[consts_b, xs_b[c][t]], W=pb_(bm))
            for c in range(NC_):
                s.op("pe", lambda e, c=c, pq_=pq_: e.matmul(pq_[:], lhsT=ones, rhs=sq[:, c, :],
                                                            start=(c == 0), stop=(c == NC_ - 1)),
                     R=[consts_b, sq_b[c]], W=pb_(bq))
            s.op("act", lambda e, pm=pm, p=p: e.activation(out=m2_sb[p][:], in_=pm[:], func=AF.Square), R=pb_(bm), W=[m2_b[p]])
            s.op("act", lambda e, pm=pm, p=p: e.copy(out=mean_sb[p][:], in_=pm[:]), R=pb_(bm), W=[mean_b[p]])
            s.op("dve", lambda e, pq_=pq_, p=p: e.tensor_tensor(out=rstd_sb[p][:], in0=pq_[:], in1=m2_sb[p][:], op=ALU.subtract),
                 R=pb_(bq) + [m2_b[p]], W=[rstd_b[p]])
            s.op("act", lambda e, p=p: e.activation(out=m2_sb[p][:], in_=rstd_sb[p][:], func=AF.Ln, bias=LN_EPS, scale=1.0),
                 R=[rstd_b[p]], W=[m2_b[p]])
            s.op("act", lambda e, p=p: e.activation(out=rstd_sb[p][:], in_=m2_sb[p][:], func=AF.Exp, scale=-0.5),
                 R=[m2_b[p]], W=[rstd_b[p]])

        def norm(t):
            sl = slice(t * TB, (t + 1) * TB)
            p = t % 2
            for c in range(NC_):
                j = cn["t1"] % 2
                cn["t1"] += 1
                j2 = cn["t2"] % 3
                cn["t2"] += 1
                s.op("dve", lambda e, c=c, sl=sl, j=j, p=p: e.tensor_tensor(out=t1[j][:], in0=xs[:, c, sl], in1=mean_sb[p][:], op=ALU.subtract),
                     R=[xs_b[c][t], mean_b[p]], W=[t1_b[j]])
                s.op("pool", lambda e, j=j, j2=j2, p=p: e.tensor_tensor(out=t2[j2][:], in0=t1[j][:], in1=rstd_sb[p][:], op=ALU.mult),
                     R=[t1_b[j], rstd_b[p]], W=[t2_b[j2]])
                s.op("act", lambda e, c=c, sl=sl, j2=j2: e.activation(out=xs[:, c, sl], in_=t2[j2][:], func=AF.Identity,
                                                                   scale=lnga[:, col0 + c:col0 + c + 1],
                                                                   bias=lnba[:, col0 + c:col0 + c + 1]),
                     R=[t2_b[j2], ln_c], W=[xs_b[c][t]])
                s.op("dve", lambda e, c=c, sl=sl, j2=j2: e.tensor_scalar(
                    out=xb[:, c, sl], in0=t2[j2][:], scalar1=lng[:, col0 + c:col0 + c + 1], scalar2=lnb[:, col0 + c:col0 + c + 1],
                    op0=ALU.mult, op1=ALU.add),
                    R=[t2_b[j2], ln_c], W=[xb_b[c][t]])

        stats(0)
        for t in range(NTB):
            if t + 1 < NTB:
                stats(t + 1)
            norm(t)

    def make_ln(l, i, arenas, sbanks=((0, 1), (2, 3)), lag=2):
        col0 = (l * 3 + i) * NC_

        def al(shape, dt, name):
            for a in arenas:
                n = 1
                for d in shape[1:]:
                    n *= d
                if a.off + (n * DT_SIZE[dt] + 63) // 64 * 64 <= a.top:
                    return a.alloc(shape, dt, name)
            raise RuntimeError("make_ln: no room for " + name)

        NSQ = 3
        sqr = [al([128, TB], F32, "lsq") for _ in range(NSQ)]
        sqr_b = bufs(NSQ)
        mean_sb = [al([128, TB], F32, "lmean") for _ in range(2)]
        m2_sb = [al([128, TB], F32, "lm2") for _ in range(2)]
        rstd_sb = [al([128, TB], F32, "lrstd") for _ in range(2)]
        mean_b, m2_b, rstd_b = bufs(2), bufs(2), bufs(2)
        NT = 3
        t1 = [al([128, TB], F32, "lt1") for _ in range(NT)]
        t2 = [al([128, TB], F32, "lt2") for _ in range(NT)]
        t1_b, t2_b = bufs(NT), bufs(NT)
        cn = {"sq": 0, "t1": 0, "t2": 0, "seen": {}}
        pending = []
        avail = []
        fl = {"s1": None, "s2": None}

        def tick():
            if fl["s2"] is not None:
                c, t, j2 = fl["s2"]
                sl = slice(t * TB, (t + 1) * TB)
                s.op("act", lambda e, c=c, sl=sl, j2=j2: e.activation(out=xs[:, c, sl], in_=t2[j2][:], func=AF.Identity,
                                                                   scale=lnga[:, col0 + c:col0 + c + 1],
                                                                   bias=lnba[:, col0 + c:col0 + c + 1]),
                     R=[t2_b[j2], ln_c], W=[xs_b[c][t]])
                s.op("dve", lambda e, c=c, sl=sl, j2=j2: e.tensor_scalar(
                    out=xb[:, c, sl], in0=t2[j2][:], scalar1=lng[:, col0 + c:col0 + c + 1], scalar2=lnb[:, col0 + c:col0 + c + 1],
                    op0=ALU.mult, op1=ALU.add),
                    R=[t2_b[j2], ln_c], W=[xb_b[c][t]])
                fl["s2"] = None
            if fl["s1"] is not None:
                c, t, j = fl["s1"]
                p = t % 2
                j2 = cn["t2"] % NT
                cn["t2"] += 1
                s.op("pool", lambda e, j=j, j2=j2, p=p: e.tensor_tensor(out=t2[j2][:], in0=t1[j][:], in1=rstd_sb[p][:], op=ALU.mult),
                     R=[t1_b[j], rstd_b[p]], W=[t2_b[j2]])
                fl["s2"] = (c, t, j2)
                fl["s1"] = None
            if avail:
                c, t = avail.pop(0)
                sl = slice(t * TB, (t + 1) * TB)
                p = t % 2
                j = cn["t1"] % NT
                cn["t1"] += 1
                s.op("dve", lambda e, c=c, sl=sl, j=j, p=p: e.tensor_tensor(out=t1[j][:], in0=xs[:, c, sl], in1=mean_sb[p][:], op=ALU.subtract),
                     R=[xs_b[c][t], mean_b[p]], W=[t1_b[j]])
                fl["s1"] = (c, t, j)

        def emit(entry):
            dc, t, k = entry
            sl = slice(t * TB, (t + 1) * TB)
            p = t % 2
            bm, bq = sbanks[p]
            n = cn["seen"].get(t, 0)
            cn["seen"][t] = n + 1
            s.op("pe", lambda e, dc=dc, sl=sl, bm=bm, n=n: e.matmul(psum[bm][:], lhsT=ones, rhs=xs[:, dc, sl],
                                                                   start=(n == 0), stop=(n == NC_ - 1)),
                 R=[consts_b, xs_b[dc][t]], W=pb_(bm))
            s.op("pe", lambda e, k=k, bq=bq, n=n: e.matmul(psum[bq][:], lhsT=ones, rhs=sqr[k][:],
                                                           start=(n == 0), stop=(n == NC_ - 1)),
                 R=[consts_b, sqr_b[k]], W=pb_(bq))
            if n == NC_ - 1:
                s.op("act", lambda e, bm=bm, p=p: e.activation(out=m2_sb[p][:], in_=psum[bm][:], func=AF.Square), R=pb_(bm), W=[m2_b[p]])
                s.op("act", lambda e, bm=bm, p=p: e.copy(out=mean_sb[p][:], in_=psum[bm][:]), R=pb_(bm), W=[mean_b[p]])
                s.op("dve", lambda e, bq=bq, p=p: e.tensor_tensor(out=rstd_sb[p][:], in0=psum[bq][:], in1=m2_sb[p][:], op=ALU.subtract),
                     R=pb_(bq) + [m2_b[p]], W=[rstd_b[p]])
                s.op("act", lambda e, p=p: e.activation(out=m2_sb[p][:], in_=rstd_sb[p][:], func=AF.Ln, bias=LN_EPS, scale=1.0),
                     R=[rstd_b[p]], W=[m2_b[p]])
                s.op("act", lambda e, p=p: e.activation(out=rstd_sb[p][:], in_=m2_sb[p][:], func=AF.Exp, scale=-0.5),
                     R=[m2_b[p]], W=[rstd_b[p]])
                avail.extend((c, t) for c in range(NC_))

        def chunk_done(dc, t):
            sl = slice(t * TB, (t + 1) * TB)
            k = cn["sq"] % NSQ
            cn["sq"] += 1
            s.op("act", lambda e, dc=dc, sl=sl, k=k: e.activation(out=sqr[k][:], in_=xs[:, dc, sl], func=AF.Square),
                 R=[xs_b[dc][t]], W=[sqr_b[k]])
            pending.append((dc, t, k))
            if len(pending) > lag:
                emit(pending.pop(0))
            tick()

        def flush():
            while pending:
                emit(pending.pop(0))
                tick()
            while avail or fl["s1"] is not None or fl["s2"] is not None:
                tick()

        return chunk_done, flush

    def ffn(l, i):
        w1_d = din(f"f{i}w1_{l}", [NF, 128, D])
        w3_d = din(f"f{i}w3_{l}", [NF, 128, D])
        w2_d = din(f"f{i}w2_{l}", [DFF, D])
        s.fence()
        A.release(arena_base)
        W13_SLOTS = 3
        w13 = [A.alloc([128, 2, D], BF16, "w13") for _ in range(W13_SLOTS)]
        w13_b = [bufs(2) for _ in range(W13_SLOTS)]
        w13_sem = [[new_dsem("w13") for _ in range(2)] for _ in range(W13_SLOTS)]
        w2 = [A.alloc([128, GMAX, D], BF16, "w2") for _ in range(2)]
        w2_b = bufs(2)
        w2_sem = [new_dsem("w2") for _ in range(2)]
        gT = [A.alloc([128, GMAX, S], BF16, "gT") for _ in range(2)]
        gT_b = [[bufs(NTB) for _ in range(GMAX)] for _ in range(2)]
        silu_t = [A.alloc([128, TB], F32, "silu") for _ in range(2)]
        silu_b = bufs(2)
        ln_chunk, ln_flush = make_ln(l, 0 if i == 0 else 2, [A])
        cnt = {"w13": 0, "w2": 0, "psA": 0, "psY": 0, "silu": 0}
        m0 = 0
        for gi, G in enumerate(FFN_GROUPS):
            ms = list(range(m0, m0 + G))
            m0 += G
            gs = gi % 2
            ws = cnt["w2"] % 2
            cnt["w2"] += 1
            s.op("pool", lambda e, ws=ws, ms=ms, G=G: e.dma_start(
                out=w2[ws][:, 0:G, :],
                in_=w2_d[ms[0] * 128:(ms[0] + G) * 128, :].rearrange("(g p) n -> p g n", p=128)),
                W=[w2_b[ws]], dsem=w2_sem[ws])
            for ml, m in enumerate(ms):
                slot = cnt["w13"] % W13_SLOTS
                cnt["w13"] += 1
                s.op("pool", lambda e, slot=slot, m=m: e.dma_start(out=w13[slot][:, 0, :], in_=w1_d[m]),
                     W=[w13_b[slot][0]], dsem=w13_sem[slot][0])
                s.op("pool", lambda e, slot=slot, m=m: e.dma_start(out=w13[slot][:, 1, :], in_=w3_d[m]),
                     W=[w13_b[slot][1]], dsem=w13_sem[slot][1])
                for t in range(NTB):
                    sl = slice(t * TB, (t + 1) * TB)
                    pj = cnt["psA"] % 2
                    cnt["psA"] += 1
                    for which, bi in ((0, pj), (1, 2 + pj)):
                        for k in range(NC_):
                            s.op("pe", lambda e, slot=slot, which=which, k=k, sl=sl, bi=bi: e.matmul(
                                psum[bi][:], lhsT=w13[slot][:, which, k * 128:(k + 1) * 128], rhs=xb[:, k, sl],
                                start=(k == 0), stop=(k == NC_ - 1)),
                                R=[w13_b[slot][which], xb_b[k][t]], W=pb_(bi))
                    sj = cnt["silu"] % 2
                    cnt["silu"] += 1
                    s.op("act", lambda e, pj=pj, sj=sj: e.activation(out=silu_t[sj][:], in_=psum[pj][:], func=AF.Silu),
                         R=pb_(pj), W=[silu_b[sj]])
                    s.op("dve", lambda e, pj=pj, sj=sj, gs=gs, ml=ml, sl=sl: e.tensor_tensor(
                        out=gT[gs][:, ml, sl], in0=psum[2 + pj][:], in1=silu_t[sj][:], op=ALU.mult),
                        R=pb_(2 + pj) + [silu_b[sj]], W=[gT_b[gs][ml][t]])
            last = (gi == len(FFN_GROUPS) - 1)
            order = [(dc, t) for t in range(NTB) for dc in range(NC_)] if last else [(dc, t) for dc in range(NC_) for t in range(NTB)]
            for dc, t in order:
                if True:
                    sl = slice(t * TB, (t + 1) * TB)
                    bi = 4 + cnt["psY"] % 2
                    cnt["psY"] += 1
                    for ml in range(G):
                        s.op("pe", lambda e, ws=ws, ml=ml, dc=dc, gs=gs, sl=sl, bi=bi, G=G: e.matmul(
                            psum[bi][:], lhsT=w2[ws][:, ml, dc * 128:(dc + 1) * 128], rhs=gT[gs][:, ml, sl],
                            start=(ml == 0), stop=(ml == G - 1)),
                            R=[w2_b[ws], gT_b[gs][ml][t]], W=pb_(bi))
                    s.op("dve", lambda e, bi=bi, dc=dc, sl=sl: e.scalar_tensor_tensor(
                        out=xs[:, dc, sl], in0=psum[bi][:], scalar=0.5, in1=xs[:, dc, sl],
                        op0=ALU.mult, op1=ALU.add),
                        R=pb_(bi) + [xs_b[dc][t]], W=[xs_b[dc][t]])
                    if last:
                        ln_chunk(dc, t)
        ln_flush()

    def mixer(l, stop=None):
        amask_d = din("amask", [8, 128, S])
        wglr_d = din(f"wglr_{l}", [128, NC_ * 16])
        wgqk_d = din(f"wgqk_{l}", [128, NC_ * 512])
        wgkv_d = din(f"wgkv_{l}", [128, NC_ * 768])
        wgr_d = din(f"wgr_{l}", [128, NC_ * 512])
        waqkv_d = din(f"waqkv_{l}", [4, 128, NC_ * 384])
        wgab_d = din(f"wgab_{l}", [NC_, 128, NC_ * 256])
        wap_d = din(f"wap_{l}", [NC_, 128, 4 * 128])
        wgp_d = din(f"wgp_{l}", [NC_, 128, 4 * 128])
        wo_d = din(f"wo_{l}", [NC_, 128, NC_ * 128])
        blk = amask_blocks
        s.fence()
        A.release(arena_base)
        o_gnT = A.alloc([128, 4, S], BF16, "o_gnT")
        o_gn_b = [bufs(NTB) for _ in range(4)]
        o_a_b = [bufs(NTB) for _ in range(8)]
        mix_base = A.mark()

        qT = A.alloc([128, 2, S], BF16, "qT")
        kT = A.alloc([128, 2, S], BF16, "kT")
        qT_b = [bufs(16) for _ in range(2)]
        kT_b = [bufs(16) for _ in range(2)]
        khat = A.alloc([128, 16, 256], BF16, "khat")
        khat_b = bufs(16)
        gv = A.alloc([128, 16, 512], BF16, "gv")
        gv_b = bufs(16)
        dec = A.alloc([128, 2, 32], F32, "dec")
        dec_b = bufs(NTB)
        gla_base = A.mark()
        wglr = A.alloc([128, NC_, 16], BF16, "wglr")
        wgqk = A.alloc([128, NC_, 512], BF16, "wgqk")
        wgkv = A.alloc([128, NC_, 768], BF16, "wgkv")
        wB_b = bufs(3)
        s.op("pool", lambda e: e.dma_start(out=wglr[:], in_=wglr_d.rearrange("p (k n) -> p k n", k=NC_)),
             W=[wB_b[0]], dsem=new_dsem("wB"))
        s.op("pool", lambda e: e.dma_start(out=wgqk[:], in_=wgqk_d.rearrange("p (k n) -> p k n", k=NC_)),
             W=[wB_b[1]], dsem=new_dsem("wB"))
        s.op("pool", lambda e: e.dma_start(out=wgkv[:], in_=wgkv_d.rearrange("p (k n) -> p k n", k=NC_)),
             W=[wB_b[2]], dsem=new_dsem("wB"))
        glrT = [A.alloc([16, TB], F32, "glrT") for _ in range(2)]
        glrT_b = bufs(2)
        z_sb = [A.alloc([128, 256], F32, "z") for _ in range(2)]
        z_b = bufs(2)
        la_sb = [A.alloc([128, 256], F32, "la") for _ in range(2)]
        la_b = bufs(2)
        Eb = A.alloc([128, 2, TB], F32, "Eb")
        Einv = A.alloc([128, 2, TB], F32, "Einv")
        E_b, Einv_b = bufs(2), bufs(2)
        Ft = A.alloc([128, 4, 256], F32, "Ft")
        F_b = bufs(4)
        zc = 0
        pc = 0
        for tb in range(NTB):
            sl = slice(tb * TB, (tb + 1) * TB)
            gj = tb % 2
            for k in range(NC_):
                s.op("pe", lambda e, k=k, sl=sl: e.matmul(psum[0][0:16, :], lhsT=wglr[:, k, :], rhs=xb[:, k, sl],
                                                         start=(k == 0), stop=(k == NC_ - 1)),
                     R=[wB_b[0], xb_b[k][tb]], W=pb_(0))
            s.op("act", lambda e, gj=gj: e.copy(out=glrT[gj][:], in_=psum[0][0:16, :]), R=pb_(0), W=[glrT_b[gj]])
            for jj in range(4):
                j = tb * 4 + jj
                zi = zc % 2
                zc += 1
                bz = 1 + zi
                s.op("pe", lambda e, gj=gj, jj=jj, bz=bz: e.matmul(
                    psum[bz][:, 0:256], lhsT=glrT[gj][:, jj * 128:(jj + 1) * 128], rhs=wgu[:, l, :],
                    start=True, stop=True),
                    R=[glrT_b[gj], mixc_b], W=pb_(bz, 0, 256))
                s.op("dve", lambda e, zi=zi, bz=bz: e.tensor_tensor(out=z_sb[zi][:], in0=psum[bz][:, 0:256], in1=bgu[:, l, :], op=ALU.add),
                     R=pb_(bz, 0, 256) + [mixc_b], W=[z_b[zi]])
                tsl = slice(j * 128, (j + 1) * 128)
                bi2 = 6 + pc % 2
                pc += 1
                for k in range(NC_):
                    s.op("pe", lambda e, k=k, tsl=tsl, bi2=bi2: e.matmul(
                        psum[bi2][:], lhsT=xb[:, k, tsl], rhs=wgkv[:, k, 256:768],
                        start=(k == 0), stop=(k == NC_ - 1)),
                        R=[wB_b[2], xb_b[k][tb]], W=pb_(bi2))
                s.op("dve", lambda e, j=j, bi2=bi2: e.tensor_copy(out=gv[:, j, :], in_=psum[bi2][:]),
                     R=pb_(bi2), W=[gv_b[j]])
                s.op("act", lambda e, zi=zi: e.activation(out=z_sb[zi][:], in_=z_sb[zi][:], func=AF.Exp, scale=-1.0),
                     R=[z_b[zi]], W=[z_b[zi]])
                s.op("act", lambda e, zi=zi: e.activation(out=la_sb[zi][:], in_=z_sb[zi][:], func=AF.Ln, bias=1.0, scale=1.0),
                     R=[z_b[zi]], W=[la_b[zi]])
                for ch in range(2):
                    s.op("pe", lambda e, zi=zi, ch=ch, jj=jj: e.matmul(
                        psum[3 + ch][:, jj * 128:(jj + 1) * 128], lhsT=la_sb[zi][:, ch * 128:(ch + 1) * 128], rhs=Umat,
                        start=True, stop=True),
                        R=[la_b[zi], consts_b], W=[pq[3 + ch][jj]])
                s.op("pe", lambda e, zi=zi: e.matmul(psum[5][:, 0:256], lhsT=Lmat, rhs=la_sb[zi][:], start=True, stop=True),
                     R=[la_b[zi], consts_b], W=pb_(5, 0, 256))
                s.op("act", lambda e, jj=jj: e.activation(out=Ft[:, jj, :], in_=psum[5][:, 0:256], func=AF.Exp),
                     R=pb_(5, 0, 256), W=[F_b[jj]])
            for ch in range(2):
                s.op("act", lambda e, ch=ch: e.activation(out=Eb[:, ch, :], in_=psum[3 + ch][:], func=AF.Exp),
                     R=pb_(3 + ch), W=[E_b[ch]])
                s.op("act", lambda e, ch=ch: e.activation(out=Einv[:, ch, :], in_=psum[3 + ch][:], func=AF.Exp, scale=-1.0),
                     R=pb_(3 + ch), W=[Einv_b[ch]])
            s.op("dve", lambda e, tb=tb: e.tensor_copy(out=dec[:, :, tb * 8:(tb + 1) * 8], in_=Eb[:, :, 63::64]),
                 R=E_b, W=[dec_b[tb]])
            for m in range(4):
                ch = m % 2
                bi = 6 + pc % 2
                pc += 1
                for k in range(NC_):
                    s.op("pe", lambda e, m=m, k=k, sl=sl, bi=bi: e.matmul(
                        psum[bi][:], lhsT=wgqk[:, k, m * 128:(m + 1) * 128], rhs=xb[:, k, sl],
                        start=(k == 0), stop=(k == NC_ - 1)),
                        R=[wB_b[1], xb_b[k][tb]], W=pb_(bi))
                if m < 2:
                    s.op("dve", lambda e, ch=ch, sl=sl, bi=bi: e.scalar_tensor_tensor(
                        out=qT[:, ch, sl], in0=psum[bi][:], scalar=0.125, in1=Eb[:, ch, :], op0=ALU.mult, op1=ALU.mult),
                        R=pb_(bi) + [E_b[ch]], W=qT_b[ch][tb * 4:(tb + 1) * 4])
                else:
                    s.op("dve", lambda e, ch=ch, sl=sl, bi=bi: e.tensor_tensor(
                        out=kT[:, ch, sl], in0=psum[bi][:], in1=Einv[:, ch, :], op=ALU.mult),
                        R=pb_(bi) + [Einv_b[ch]], W=kT_b[ch][tb * 4:(tb + 1) * 4])
            for jj in range(4):
                j = tb * 4 + jj
                tsl = slice(j * 128, (j + 1) * 128)
                bi = 1 + jj % 2
                for k in range(NC_):
                    s.op("pe", lambda e, k=k, tsl=tsl, bi=bi: e.matmul(
                        psum[bi][:, 0:256], lhsT=xb[:, k, tsl], rhs=wgkv[:, k, 0:256],
                        start=(k == 0), stop=(k == NC_ - 1)),
                        R=[wB_b[2], xb_b[k][tb]], W=pb_(bi, 0, 256))
                s.op("dve", lambda e, j=j, jj=jj, bi=bi: e.tensor_tensor(
                    out=khat[:, j, :], in0=psum[bi][:, 0:256], in1=Ft[:, jj, :], op=ALU.mult),
                    R=pb_(bi, 0, 256) + [F_b[jj]], W=[khat_b[j]])

        if stop == "B":
            return
        tap("qT", qT, [b for bb in qT_b for b in bb])
        tap("kT", kT, [b for bb in kT_b for b in bb])
        tap("khat", khat, khat_b)
        tap("gv", gv, gv_b)
        tap("dec", dec, dec_b)
        s.fence()
        A.release(gla_base)
        wgr = A.alloc([128, NC_, 512], BF16, "wgr")
        wgr_b = Buf()
        s.op("pool", lambda e: e.dma_start(out=wgr[:], in_=wgr_d.rearrange("p (k n) -> p k n", k=NC_)),
             W=[wgr_b], dsem=new_dsem("wgr"))
        ograw = [A.alloc([128, 4, TB], F32, "ograw") for _ in range(2)]
        ograw_b = [[bufs(4) for _ in range(4)] for _ in range(2)]
        st_f = A.alloc([128, 4, 128], F32, "st_f")
        st_b = A.alloc([128, 4, 128], BF16, "st_b")
        stf_b, stb_b = bufs(4), bufs(4)
        ATt = [A.alloc([128, 4, 128], BF16, "AT") for _ in range(2)]
        AT_b = [bufs(4) for _ in range(2)]
        sg = [A.alloc([128, TB], F32, "sg") for _ in range(4)]
        sg_b = bufs(4)
        gsq = A.alloc([128, TB], F32, "gsq")
        gsq_b = Buf()
        gm2 = A.alloc([128, TB], F32, "gm2")
        grs = A.alloc([128, TB], F32, "grs")
        gm2_b, grs_b = Buf(), Buf()
        s.op("dve", lambda e: e.memset(st_f[:], 0.0), W=stf_b)
        s.op("dve", lambda e: e.memset(st_b[:], 0.0), W=stb_b)
        bS0, bS1 = 4, 5
        HORD = (0, 2, 1, 3)

        def emit_AT_mm(j):
            bA = j % 2
            tok = slice(j * 128, (j + 1) * 128)
            prev = None
            for h in HORD:
                ch, pb = h // 2, (h % 2) * 64
                hs = slice(h * 128, (h + 1) * 128)
                prev = s.op("pe", lambda e, ch=ch, pb=pb, hs=hs, tok=tok, bA=bA: e.matmul(
                    psum[bA][:, hs], lhsT=kT[pb:pb + 64, ch, tok], rhs=qT[pb:pb + 64, ch, tok], start=True, stop=True),
                    R=[kT_b[ch][j], qT_b[ch][j]], W=[pq[bA][h]], after=[prev] if h == 1 else [])

        def emit_AT_mask(j):
            bA = j % 2
            aj = j % 2
            for h in range(4):
                hs = slice(h * 128, (h + 1) * 128)
                s.op("dve", lambda e, h=h, hs=hs, bA=bA, aj=aj: e.tensor_tensor(
                    out=ATt[aj][:, h, :], in0=psum[bA][:, hs], in1=Mblk, op=ALU.mult),
                    R=[pq[bA][h], consts_b], W=[AT_b[aj][h]])

        def emit_dS(j, half):
            bS = bS0 if half == 0 else bS1
            rows = slice(half * 64, half * 64 + 64)
            for h in range(4):
                ch = h // 2
                hs = slice(h * 128, (h + 1) * 128)
                s.op("pe", lambda e, ch=ch, hs=hs, rows=rows, bS=bS, j=j: e.matmul(
                    psum[bS][:, hs], lhsT=khat[rows, j, ch * 128:(ch + 1) * 128], rhs=gv[rows, j, hs],
                    start=True, stop=True),
                    R=[khat_b[j], gv_b[j]], W=[pq[bS][h]])

        def emit_update(j, half):
            bS = bS0 if half == 0 else bS1
            c = 2 * j + half
            for h in range(4):
                ch, pb = h // 2, (h % 2) * 64
                hs = slice(h * 128, (h + 1) * 128)
                ps_ = slice(pb, pb + 64)
                s.op("dve", lambda e, h=h, ch=ch, ps_=ps_, hs=hs, bS=bS, c=c: e.scalar_tensor_tensor(
                    out=st_b[ps_, h, :], in0=st_f[ps_, h, :], scalar=dec[ps_, ch, c:c + 1], in1=psum[bS][ps_, hs],
                    op0=ALU.mult, op1=ALU.add),
                    R=[stf_b[h], dec_b[c // 8], pq[bS][h]], W=[stb_b[h]])
                s.op("dve", lambda e, h=h, ch=ch, ps_=ps_, hs=hs, bS=bS, c=c: e.scalar_tensor_tensor(
                    out=st_f[ps_, h, :], in0=st_f[ps_, h, :], scalar=dec[ps_, ch, c:c + 1], in1=psum[bS][ps_, hs],
                    op0=ALU.mult, op1=ALU.add),
                    R=[stf_b[h], dec_b[c // 8], pq[bS][h]], W=[stf_b[h]])

        gtasks = []
        gstate = {"B": None}

        def gate_batch(tb):
            sl = slice(tb * TB, (tb + 1) * TB)
            for h in range(4):
                bi = 6 + h % 2
                for k in range(NC_):
                    s.op("pe", lambda e, k=k, h=h, sl=sl, bi=bi: e.matmul(
                        psum[bi][:], lhsT=wgr[:, k, h * 128:(h + 1) * 128], rhs=xb[:, k, sl],
                        start=(k == 0), stop=(k == NC_ - 1)),
                        R=[wgr_b, xb_b[k][tb]], W=pb_(bi))
                s.op("act", lambda e, h=h, bi=bi: e.activation(out=sg[h][:], in_=psum[bi][:], func=AF.Silu),
                     R=pb_(bi), W=[sg_b[h]])
            for h in range(4):
                gtasks.append((tb, h))

        def gn_A(tb, h):
            ob = tb % 2
            og = ograw[ob][:, h, :]
            ogb = ograw_b[ob][h]
            s.op("act", lambda e, og=og: e.activation(out=gsq[:], in_=og, func=AF.Square), R=ogb, W=[gsq_b])
            s.op("pe", lambda e, og=og: e.matmul(psum[6][:], lhsT=gones, rhs=og, start=True, stop=True),
                 R=ogb + [consts_b], W=pb_(6))
            s.op("pe", lambda e: e.matmul(psum[7][:], lhsT=gones, rhs=gsq[:], start=True, stop=True),
                 R=[gsq_b, consts_b], W=pb_(7))
            s.op("act", lambda e: e.activation(out=gm2[:], in_=psum[6][:], func=AF.Square), R=pb_(6), W=[gm2_b])
            s.op("dve", lambda e, og=og: e.tensor_tensor(out=og, in0=og, in1=psum[6][:], op=ALU.subtract),
                 R=ogb + pb_(6), W=ogb)
            s.op("dve", lambda e: e.tensor_tensor(out=grs[:], in0=psum[7][:], in1=gm2[:], op=ALU.subtract),
                 R=pb_(7) + [gm2_b], W=[grs_b])
            s.op("act", lambda e: e.activation(out=gm2[:], in_=grs[:], func=AF.Ln, bias=LN_EPS, scale=1.0),
                 R=[grs_b], W=[gm2_b])
            s.op("act", lambda e: e.activation(out=grs[:], in_=gm2[:], func=AF.Exp, scale=-0.5), R=[gm2_b], W=[grs_b])

        def gn_B(tb, h):
            ob = tb % 2
            og = ograw[ob][:, h, :]
            ogb = ograw_b[ob][h]
            col = l * 4 + h
            sl = slice(tb * TB, (tb + 1) * TB)
            s.op("pool", lambda e, og=og: e.tensor_tensor(out=og, in0=og, in1=grs[:], op=ALU.mult),
                 R=ogb + [grs_b], W=ogb)
            s.op("act", lambda e, og=og, col=col: e.activation(out=og, in_=og, func=AF.Identity,
                                                             scale=gng[:, col:col + 1], bias=gnb[:, col:col + 1]),
                 R=ogb + [mixc_b], W=ogb)
            s.op("dve", lambda e, og=og, h=h, sl=sl: e.tensor_tensor(out=o_gnT[:, h, sl], in0=og, in1=sg[h][:], op=ALU.mult),
                 R=ogb + [sg_b[h]], W=[o_gn_b[h][tb]])

        def gn_slot():
            if gstate["B"] is not None:
                gn_B(*gstate["B"])
                gstate["B"] = None
            if gtasks:
                t_ = gtasks.pop(0)
                gn_A(*t_)
                gstate["B"] = t_

        def gn_flush():
            while gtasks or gstate["B"] is not None:
                gn_slot()

        emit_AT_mm(0)
        emit_dS(0, 0)
        emit_dS(0, 1)
        emit_AT_mask(0)
        for j in range(16):
            tb, jj = j // 4, j % 4
            ob = tb % 2
            aj = j % 2
            bO = 2 + aj
            t0 = slice(j * 128, j * 128 + 64)
            t1_ = slice(j * 128 + 64, (j + 1) * 128)
            for h in range(4):
                hs = slice(h * 128, (h + 1) * 128)
                s.op("pe", lambda e, h=h, hs=hs, bO=bO, aj=aj, j=j: e.matmul(
                    psum[bO][:, hs], lhsT=gv[:, j, hs], rhs=ATt[aj][:, h, :], start=(h == 0), stop=False, skip_group_check=True),
                    R=[gv_b[j], AT_b[aj][h]], W=[pq[bO][h]])
            prev = None
            for h in HORD:
                ch, pb = h // 2, (h % 2) * 64
                prev = s.op("pe", lambda e, h=h, ch=ch, pb=pb, bO=bO, t0=t0: e.matmul(
                    psum[bO][:, h * 128:h * 128 + 64], lhsT=st_b[pb:pb + 64, h, :], rhs=qT[pb:pb + 64, ch, t0],
                    start=False, stop=False, skip_group_check=True),
                    R=[stb_b[h], qT_b[ch][j]], W=[pq[bO][h]], after=[prev] if h == 1 else [])
            if j + 1 < 16:
                emit_AT_mm(j + 1)
            emit_update(j, 0)
            prev = None
            for h in HORD:
                ch, pb = h // 2, (h % 2) * 64
                prev = s.op("pe", lambda e, h=h, ch=ch, pb=pb, bO=bO, t1_=t1_: e.matmul(
                    psum[bO][:, h * 128 + 64:(h + 1) * 128], lhsT=st_b[pb:pb + 64, h, :], rhs=qT[pb:pb + 64, ch, t1_],
                    start=False, stop=True, skip_group_check=True),
                    R=[stb_b[h], qT_b[ch][j]], W=[pq[bO][h]], after=[prev] if h == 1 else [])
            if j + 1 < 16:
                emit_AT_mask(j + 1)
                emit_dS(j + 1, 0)
            for h in range(4):
                hs = slice(h * 128, (h + 1) * 128)
                s.op("act", lambda e, h=h, hs=hs, bO=bO, ob=ob, jj=jj: e.copy(
                    out=ograw[ob][:, h, jj * 128:(jj + 1) * 128], in_=psum[bO][:, hs]),
                    R=[pq[bO][h]], W=[ograw_b[ob][h][jj]])
            emit_update(j, 1)
            if j + 1 < 16:
                emit_dS(j + 1, 1)
            if j == 0:
                tap("st1", st_f, stf_b)
            gn_slot()
            if jj == 3:
                gn_flush()
                if tb == 0:
                    tap("ograw0", ograw[0], [b for bb in ograw_b[0] for b in bb])
                gate_batch(tb)
        gn_flush()

        if stop == "D":
            return
        tap("o_gnT", o_gnT, [b for bb in o_gn_b for b in bb])
        s.fence()
        A.release(mix_base)
        o_aT = A.alloc([128, 4, S], BF16, "o_aT")
        mix_base = A.mark()
        waqkv = A.alloc([128, NC_, 384], BF16, "waqkv")
        waqkv_b = Buf()
        waqkv_sem = new_dsem("waqkv")
        aqT = [[A.alloc([128, S], BF16, "aqT") for _ in range(2)] for _ in range(2)]
        akT = [A.alloc([128, S], BF16, "akT") for _ in range(2)]
        aqT_b = [bufs(NTB) for _ in range(2)]
        aqz_b = [bufs(2) for _ in range(2)]
        for wi_ in range(2):
            for hh_ in range(2):
                oth = slice(64, 128) if hh_ == 0 else slice(0, 64)
                s.op("pool", lambda e, wi_=wi_, hh_=hh_, oth=oth: e.memset(aqT[wi_][hh_][oth, :], 0.0), W=[aqz_b[wi_][hh_]])
        akT_b = [bufs(16) for _ in range(2)]
        Vp = [A.alloc([128, 16, 128], BF16, "Vp") for _ in range(2)]
        Vp_b = [bufs(16) for _ in range(2)]
        mask_s = [A.alloc([128, S], F32, "mask") for _ in range(2)]
        mask_b = bufs(2)
        mask_sem = [new_dsem("mask") for _ in range(2)]
        NE = 5
        LOOK = 3
        Et = [A.alloc([128, TB], F32, "Et") for _ in range(NE)]
        Pt = [A.alloc([128, TB], BF16, "Pt") for _ in range(NE)]
        Et_b, Pt_b = bufs(NE), bufs(NE)
        dcp = [A.alloc([128, TB], F32, "dcp") for _ in range(1)] * 2
        dcp_b = bufs(1) * 2
        onesb = A.alloc([128, 128], BF16, "onesb")
        onesb_b = Buf()
        s.op("pool", lambda e: e.memset(onesb[:], 1.0), W=[onesb_b])
        SB = (0, 1, 2, 3)
        stepc = 0
        def load_waqkv(ch):
            s.op("pool", lambda e, ch=ch: e.dma_start(out=waqkv[:], in_=waqkv_d[ch].rearrange("p (k n) -> p k n", k=NC_)),
                 W=[waqkv_b], dsem=waqkv_sem)

        def load_mask(h):
            mi = h % 2
            s.op("sp", lambda e, mi=mi, h=h: e.dma_start(out=mask_s[mi][:], in_=amask_d[h]),
                 W=[mask_b[mi]], dsem=mask_sem[mi])

        load_waqkv(0)
        load_mask(0)
        load_mask(1)
        for ch in range(4):
            wi = ch % 2
            for tb in range(NTB):
                sl = slice(tb * TB, (tb + 1) * TB)
                for which in range(2):
                    bi = SB[(2 * tb + which) % 4]
                    for k in range(NC_):
                        s.op("pe", lambda e, k=k, which=which, sl=sl, bi=bi: e.matmul(
                            psum[bi][:], lhsT=waqkv[:, k, which * 128:(which + 1) * 128], rhs=xb[:, k, sl],
                            start=(k == 0), stop=(k == NC_ - 1)),
                            R=[waqkv_b, xb_b[k][tb]], W=pb_(bi))
                    if which == 0:
                        s.op("act", lambda e, wi=wi, sl=sl, bi=bi: e.mul(aqT[wi][0][0:64, sl], psum[bi][0:64, :], 0.125),
                             R=pb_(bi), W=[aqT_b[wi][tb]])
                        s.op("act", lambda e, wi=wi, sl=sl, bi=bi: e.mul(aqT[wi][1][64:128, sl], psum[bi][64:128, :], 0.125),
                             R=pb_(bi), W=[aqT_b[wi][tb]])
                    else:
                        s.op("dve", lambda e, wi=wi, sl=sl, bi=bi: e.tensor_copy(out=akT[wi][:, sl], in_=psum[bi][:]),
                             R=pb_(bi), W=akT_b[wi][tb * 4:(tb + 1) * 4])
            for j4 in range(4):
                bi = SB[j4 % 4]
                for jj in range(4):
                    j = j4 * 4 + jj
                    tsl = slice(j * 128, (j + 1) * 128)
                    for k in range(NC_):
                        s.op("pe", lambda e, k=k, tsl=tsl, bi=bi, jj=jj: e.matmul(
                            psum[bi][:, jj * 128:(jj + 1) * 128], lhsT=xb[:, k, tsl], rhs=waqkv[:, k, 256:384],
                            start=(k == 0 and jj == 0), stop=(k == NC_ - 1), skip_group_check=True),
                            R=[waqkv_b, xb_b[k][j4]], W=pb_(bi))
                s.op("act", lambda e, wi=wi, j4=j4, bi=bi: e.copy(
                    out=Vp[wi][:, j4 * 4:(j4 + 1) * 4, :], in_=psum[bi][:].rearrange("p (a b) -> p a b", a=4)),
                    R=pb_(bi), W=Vp_b[wi][j4 * 4:(j4 + 1) * 4])
            if ch + 1 < 4:
                load_waqkv(ch + 1)
            steps = []
            for hh in range(2):
                h = 2 * ch + hh
                for qp in range(NTB):
                    kbs = [kb for kb in range(4 * qp + 4) if blk[h][kb][qp]]
                    for ki, kb in enumerate(kbs):
                        steps.append((hh, h, qp, kb, ki, len(kbs)))
            info = {}
            for idx in range(len(steps) + LOOK):
                if idx < len(steps):
                    hh, h, qp, kb, ki, nk = steps[idx]
                    pb = hh * 64
                    mi = h % 2
                    q0 = qp * TB
                    n0 = max(q0, 128 * kb)
                    n = q0 + TB - n0
                    bS = SB[stepc % 4]
                    ei = stepc % NE
                    stepc += 1
                    info[idx] = (ei, n0, n)
                    s.op("pe", lambda e, wi=wi, hh=hh, kb=kb, n0=n0, n=n, bS=bS: e.matmul(
                        psum[bS][:, 0:n], lhsT=akT[wi][:, kb * 128:(kb + 1) * 128],
                        rhs=aqT[wi][hh][:, n0:n0 + n], start=True, stop=True),
                        R=[akT_b[wi][kb], aqT_b[wi][qp], aqz_b[wi][hh]], W=pb_(bS))
                    s.op("act", lambda e, ei=ei, bS=bS, n=n: e.activation(out=Et[ei][:, 0:n], in_=psum[bS][:, 0:n], func=AF.Exp),
                         R=pb_(bS), W=[Et_b[ei]])
                    mo = n0 - 128 * kb
                    s.op("dve", lambda e, ei=ei, mi=mi, mo=mo, n=n: e.tensor_tensor(
                        out=Pt[ei][:, 0:n], in0=Et[ei][:, 0:n], in1=mask_s[mi][:, mo:mo + n], op=ALU.mult),
                        R=[Et_b[ei], mask_b[mi]], W=[Pt_b[ei]])
                    if h + 2 < 8 and (idx + 1 == len(steps) or steps[idx + 1][1] != h):
                        load_mask(h + 2)
                pidx = idx - LOOK
                if pidx >= 0:
                    hh, h, qp, kb, ki, nk = steps[pidx]
                    ei, n0, n = info[pidx]
                    pb = hh * 64
                    q0 = qp * TB
                    par = (h * NTB + qp) % 2
                    bO, bD = 4 + par, 6 + par
                    cs = slice(n0 - q0, n0 - q0 + n)
                    s.op("pe", lambda e, wi=wi, kb=kb, ei=ei, n=n, cs=cs, bO=bO, ki=ki, nk=nk: e.matmul(
                        psum[bO][:, cs], lhsT=Vp[wi][:, kb, :], rhs=Pt[ei][:, 0:n],
                        start=(ki == 0), stop=(ki == nk - 1), skip_group_check=True),
                        R=[Vp_b[wi][kb], Pt_b[ei]], W=pb_(bO))
                    s.op("pe", lambda e, ei=ei, n=n, cs=cs, bD=bD, ki=ki, nk=nk: e.matmul(
                        psum[bD][:, cs], lhsT=onesb[:], rhs=Pt[ei][:, 0:n],
                        start=(ki == 0), stop=(ki == nk - 1), skip_group_check=True),
                        R=[onesb_b, Pt_b[ei]], W=pb_(bD))
                    if ki == nk - 1:
                        ps_ = slice(pb, pb + 64)
                        s.op("act", lambda e, par=par, bD=bD, ps_=ps_: e.activation(out=dcp[par][ps_, :], in_=psum[bD][ps_, :], func=AF.Ln),
                             R=pb_(bD), W=[dcp_b[par]])
                        s.op("act", lambda e, par=par, ps_=ps_: e.activation(out=dcp[par][ps_, :], in_=dcp[par][ps_, :], func=AF.Exp, scale=-1.0),
                             R=[dcp_b[par]], W=[dcp_b[par]])
                        s.op("dve", lambda e, ch=ch, ps_=ps_, q0=q0, bO=bO, par=par: e.tensor_tensor(
                            out=o_aT[ps_, ch, q0:q0 + TB], in0=psum[bO][ps_, :], in1=dcp[par][ps_, :], op=ALU.mult),
                            R=pb_(bO) + [dcp_b[par]], W=[o_a_b[h][qp]])

        if stop == "C":
            return
        tap("o_aT", o_aT, [b for bb in o_a_b for b in bb])
        s.fence()
        A.release(mix_base)
        e_low_top = A.mark()
        mT = A.alloc([128, NC_, S], BF16, "mT")
        mT_b = [bufs(NTB) for _ in range(NC_)]
        e_base = A.mark()
        wE = [A.alloc([128, NC_, 256], BF16, "wgab") for _ in range(2)]
        wap = [A.alloc([128, 4, 128], BF16, "wap") for _ in range(2)]
        wgp = [A.alloc([128, 4, 128], BF16, "wgp") for _ in range(2)]
        wE_b = [bufs(3) for _ in range(2)]
        wE_sem = [[new_dsem("wE") for _ in range(3)] for _ in range(2)]
        sa = [A.alloc([128, TB], F32, "sa") for _ in range(2)]
        sbt = [A.alloc([128, TB], F32, "sbt") for _ in range(2)]
        sa_b, sbt_b = bufs(2), bufs(2)
        ecn = 0

        def load_wE(dc):
            wi = dc % 2
            s.op("pool", lambda e, wi=wi, dc=dc: e.dma_start(out=wE[wi][:], in_=wgab_d[dc].rearrange("p (k n) -> p k n", k=NC_)),
                 W=[wE_b[wi][0]], dsem=wE_sem[wi][0])
            s.op("pool", lambda e, wi=wi, dc=dc: e.dma_start(out=wap[wi][:], in_=wap_d[dc].rearrange("p (k n) -> p k n", k=4)),
                 W=[wE_b[wi][1]], dsem=wE_sem[wi][1])
            s.op("pool", lambda e, wi=wi, dc=dc: e.dma_start(out=wgp[wi][:], in_=wgp_d[dc].rearrange("p (k n) -> p k n", k=4)),
                 W=[wE_b[wi][2]], dsem=wE_sem[wi][2])

        load_wE(0)
        for dc in range(NC_):
            wi = dc % 2
            if dc + 1 < NC_:
                load_wE(dc + 1)
            for tb in range(NTB):
                sl = slice(tb * TB, (tb + 1) * TB)
                pj = ecn % 2
                ecn += 1
                bGA, bGB, bPA, bPG = 0 + pj, 2 + pj, 4 + pj, 6 + pj
                for which, bi in ((0, bGA), (1, bGB)):
                    for k in range(NC_):
                        s.op("pe", lambda e, wi=wi, which=which, k=k, sl=sl, bi=bi: e.matmul(
                            psum[bi][:], lhsT=wE[wi][:, k, which * 128:(which + 1) * 128], rhs=xb[:, k, sl],
                            start=(k == 0), stop=(k == NC_ - 1)),
                            R=[wE_b[wi][0], xb_b[k][tb]], W=pb_(bi))
                for c in range(4):
                    s.op("pe", lambda e, wi=wi, c=c, sl=sl, bPA=bPA: e.matmul(
                        psum[bPA][:], lhsT=wap[wi][:, c, :], rhs=o_aT[:, c, sl], start=(c == 0), stop=(c == 3)),
                        R=[wE_b[wi][1], o_a_b[2 * c][tb], o_a_b[2 * c + 1][tb]], W=pb_(bPA))
                for c in range(4):
                    s.op("pe", lambda e, wi=wi, c=c, sl=sl, bPG=bPG: e.matmul(
                        psum[bPG][:], lhsT=wgp[wi][:, c, :], rhs=o_gnT[:, c, sl], start=(c == 0), stop=(c == 3)),
                        R=[wE_b[wi][2], o_gn_b[c][tb]], W=pb_(bPG))
                s.op("act", lambda e, pj=pj, bGA=bGA: e.activation(out=sa[pj][:], in_=psum[bGA][:], func=AF.Sigmoid),
                     R=pb_(bGA), W=[sa_b[pj]])
                s.op("act", lambda e, pj=pj, bGB=bGB: e.activation(out=sbt[pj][:], in_=psum[bGB][:], func=AF.Sigmoid),
                     R=pb_(bGB), W=[sbt_b[pj]])
                s.op("dve", lambda e, pj=pj, bPA=bPA: e.tensor_tensor(out=sa[pj][:], in0=psum[bPA][:], in1=sa[pj][:], op=ALU.mult),
                     R=pb_(bPA) + [sa_b[pj]], W=[sa_b[pj]])
                s.op("dve", lambda e, pj=pj, bPG=bPG: e.tensor_tensor(out=sbt[pj][:], in0=psum[bPG][:], in1=sbt[pj][:], op=ALU.mult),
                     R=pb_(bPG) + [sbt_b[pj]], W=[sbt_b[pj]])
                s.op("pool", lambda e, pj=pj, dc=dc, sl=sl: e.tensor_tensor(out=mT[:, dc, sl], in0=sa[pj][:], in1=sbt[pj][:], op=ALU.add),
                     R=[sa_b[pj], sbt_b[pj]], W=[mT_b[dc][tb]])
        tap("mT", mT, [b for bb in mT_b for b in bb])
        s.fence()
        A.release(e_base)
        Alow = Arena(nc, arena_base, e_low_top)
        wo = Alow.alloc([128, NC_, NC_, 128], BF16, "wo")
        wo_b = bufs(NC_)
        for dc in range(NC_):
            s.op("pool", lambda e, dc=dc: e.dma_start(out=wo[:, dc, :, :], in_=wo_d[dc].rearrange("p (k n) -> p k n", k=NC_)),
                 W=[wo_b[dc]], dsem=new_dsem("wo"))
        ln_chunk, ln_flush = make_ln(l, 1, [Alow, A])
        yc = 0
        for tb in range(NTB):
            sl = slice(tb * TB, (tb + 1) * TB)
            for dc in range(NC_):
                bi = 4 + yc % 2
                yc += 1
                for k in range(NC_):
                    s.op("pe", lambda e, dc=dc, k=k, sl=sl, bi=bi: e.matmul(
                        psum[bi][:], lhsT=wo[:, dc, k, :], rhs=mT[:, k, sl], start=(k == 0), stop=(k == NC_ - 1)),
                        R=[wo_b[dc], mT_b[k][tb]], W=pb_(bi))
                s.op("dve", lambda e, bi=bi, dc=dc, sl=sl: e.tensor_tensor(
                    out=xs[:, dc, sl], in0=psum[bi][:], in1=xs[:, dc, sl], op=ALU.add),
                    R=pb_(bi) + [xs_b[dc][tb]], W=[xs_b[dc][tb]])
                ln_chunk(dc, tb)
        ln_flush()

    amask_blocks = _mask_blocks()
    for ph in phases:
        if ph[0] == "ffn":
            ffn(ph[1], ph[2])
        elif ph[0] == "mix":
            mixer(ph[1], ph[2] if len(ph) > 2 else None)
        else:
            raise ValueError(ph)

    for c in range(NC_):
        for t in range(NTB):
            sl = slice(t * TB, (t + 1) * TB)
            s.op("dve", lambda e, c=c, sl=sl: e.tensor_scalar_mul(out=xs[:, c, sl], in0=xs[:, c, sl], scalar1=1.0 / ALPHA),
                 R=[xs_b[c][t]], W=[xs_b[c][t]])
    last = None
    for c in range(NC_):
        last = s.op("sp", lambda e, c=c: e.dma_start(out=yT_d[c * 128:(c + 1) * 128, :], in_=xs[:, c, :]),
                    R=xs_b[c], dsem=out_sem)
    fin = Buf()
    fin.w = last
    s.op("sp", lambda e: e.nop(), R=[fin])

    s.finalize()
    from contextlib import ExitStack
    with ExitStack() as ctx:
        esem = {}
        for en in Sched.ENGS:
            esem[en] = ctx.enter_context(nc.semaphore(f"sem_{en}"))
        dsems = {}
        for nm in dsem_names:
            dsems[nm] = ctx.enter_context(nc.semaphore(f"d_{nm}"))
        with nc.Block() as block:
            @block.tensor
            def _(e):
                s.replay("pe", e, esem, dsems)

            @block.scalar
            def _(e):
                s.replay("act", e, esem, dsems)

            @block.vector
            def _(e):
                s.replay("dve", e, esem, dsems)

            @block.gpsimd
            def _(e):
                s.replay("pool", e, esem, dsems)

            @block.sync
            def _(e):
                s.replay("sp", e, esem, dsems)
    return nc


_MASK = None


def _alibi_mask():
    global _MASK
    if _MASK is None:
        d = np.arange(S)[None, :] - np.arange(128)[:, None]
        mult = ((d <= 128).astype(np.float64) + ((d % 4 == 0) & (d <= 512)) + ((d % 16 == 0) & (d <= 2048)))
        mult = np.where(d >= 0, mult, 0.0)
        slopes = np.exp2(-8.0 * np.arange(1, 9) / 8.0)
        m = mult[None] * np.exp(-slopes[:, None, None] * np.maximum(d, 0)[None])
        m = np.where(m < 1e-37, 0.0, m)
        _MASK = np.ascontiguousarray(m.astype(np.float32))
    return _MASK


def _mask_blocks():
    m = _alibi_mask()
    blk = [[[False] * NTB for _ in range(16)] for _ in range(8)]
    for h in range(8):
        for kb in range(16):
            for qp in range(NTB):
                n0 = max(qp * TB, 128 * kb)
                n1 = qp * TB + TB
                if n1 <= n0:
                    continue
                blk[h][kb][qp] = bool(m[h][:, n0 - 128 * kb:n1 - 128 * kb].any())
    return blk


def _consts():
    s_ = np.arange(128)[:, None]
    t_ = np.arange(128)[None, :]
    same = (s_ // 64) == (t_ // 64)
    U = np.where(same & (s_ <= t_), -1.0 / 16.0, 0.0)
    L = np.where(same & (s_ > t_), -1.0 / 16.0, 0.0)
    M = np.where(same & (s_ <= t_), 1.0, 0.0)
    o1 = np.full((128, 128), 1.0 / D)
    o2 = np.full((128, 128), 1.0 / 128.0)
    return np.ascontiguousarray(np.concatenate([U, L, M, o1, o2], axis=1).astype(np.float32))


def _lay_w13(w):
    return np.ascontiguousarray(w.reshape(NC_, 128, NF, 128).transpose(2, 1, 0, 3).reshape(NF, 128, D))


def _lay_ln(v):
    return np.ascontiguousarray(v.reshape(DEPTH, 3, NC_, 128).transpose(3, 0, 1, 2).reshape(128, NL3))


def _lay_cols(w):
    n = w.shape[1]
    return np.ascontiguousarray(w.reshape(NC_, 128, n).transpose(1, 0, 2).reshape(128, NC_ * n))


def make_inputs(phases, inp):
    m = {"ln_g": _lay_ln(inp["ln_g"]), "ln_b": _lay_ln(inp["ln_b"]), "consts": _consts()}
    has_mix = any(p[0] == "mix" for p in phases)
    if has_mix:
        m["amask"] = _alibi_mask()
        m["wgu"] = np.ascontiguousarray(inp["w_gate_up"].transpose(1, 0, 2))
        m["bgu"] = np.ascontiguousarray(np.broadcast_to(inp["b_gate_up"][None], (128, DEPTH, 256)))
        m["gng"] = np.ascontiguousarray(inp["gla_norm_g"].reshape(DEPTH, 4, 128).transpose(2, 0, 1).reshape(128, DEPTH * 4))
        m["gnb"] = np.ascontiguousarray(inp["gla_norm_b"].reshape(DEPTH, 4, 128).transpose(2, 0, 1).reshape(128, DEPTH * 4))
    for ph in phases:
        if ph[0] == "ffn":
            l, i = ph[1], ph[2]
            pre = "ffn1" if i == 0 else "ffn2"
            m[f"f{i}w1_{l}"] = _lay_w13(inp[pre + "_w1"][l])
            m[f"f{i}w3_{l}"] = _lay_w13(inp[pre + "_w3"][l])
            m[f"f{i}w2_{l}"] = np.ascontiguousarray(inp[pre + "_w2"][l])
        else:
            l = ph[1]
            w = inp["w_in"][l]
            m[f"wglr_{l}"] = _lay_cols(w[:, O_GLR:O_GLR + 16])
            m[f"wgqk_{l}"] = _lay_cols(w[:, O_GQ:O_GQ + 512])
            m[f"wgkv_{l}"] = _lay_cols(w[:, O_GK:O_GK + 768])
            m[f"wgr_{l}"] = _lay_cols(w[:, O_GR:O_GR + 512])
            m[f"waqkv_{l}"] = np.stack([_lay_cols(np.concatenate(
                [w[:, O_AQ + c * 128:O_AQ + (c + 1) * 128], w[:, O_AK + c * 128:O_AK + (c + 1) * 128],
                 w[:, O_AV + c * 128:O_AV + (c + 1) * 128]], axis=1)) for c in range(4)], axis=0)
            m[f"wgab_{l}"] = np.stack([_lay_cols(np.concatenate(
                [w[:, O_GA + c * 128:O_GA + (c + 1) * 128], w[:, O_GB + c * 128:O_GB + (c + 1) * 128]], axis=1))
                for c in range(NC_)], axis=0)
            wa = inp["w_attn_proj"][l]
            m[f"wap_{l}"] = np.ascontiguousarray(wa.reshape(4, 128, NC_, 128).transpose(2, 1, 0, 3).reshape(NC_, 128, 4 * 128))
            wg = inp["w_gla_proj"][l]
            m[f"wgp_{l}"] = np.ascontiguousarray(wg.reshape(4, 128, NC_, 128).transpose(2, 1, 0, 3).reshape(NC_, 128, 4 * 128))
            wo_ = inp["w_out"][l]
            m[f"wo_{l}"] = np.ascontiguousarray(wo_.reshape(NC_, 128, NC_, 128).transpose(2, 1, 0, 3).reshape(NC_, 128, NC_ * 128))
    return m


def run_phases(phases, x, inp, n_cores=8, trace=False, debug=None):
    nc = build(phases, debug)
    shared = make_inputs(phases, inp)
    in_maps = []
    for b in range(n_cores):
        d = dict(shared)
        d["xT"] = np.ascontiguousarray(x[b].T)
        in_maps.append(d)
    res = run_bass_kernel_spmd(nc, in_maps, core_ids=list(range(n_cores)), trace=trace)
    out = np.stack([np.ascontiguousarray(r["yT"].T) for r in res.results], axis=0)
    return out, res


LAUNCHES = [[("ffn", 0, 0), ("mix", 0), ("ffn", 0, 1), ("ffn", 1, 0), ("mix", 1), ("ffn", 1, 1)]]


def kernel(**inputs):
    inp = {k: np.asarray(v) for k, v in inputs.items()}
    x = np.ascontiguousarray(inp["x"], dtype=np.float32)
    for phases in LAUNCHES:
        x, _ = run_phases(phases, x, inp)
    return np.ascontiguousarray(x, dtype=np.float32)
```

```python
import numpy as np
import concourse.bass as bass
import concourse.mybir as mybir
from concourse.bass_utils import run_bass_kernel_spmd

F32 = mybir.dt.float32
BF16 = mybir.dt.bfloat16
AF = mybir.ActivationFunctionType
ALU = mybir.AluOpType

S = 2048
D = 1024
DFF = 2816
NC_ = 8
NTB = 4
TB = 512
NF = 22
DEPTH = 2
ALPHA = float((2 * DEPTH) ** 0.25)
LN_EPS = 1e-5
FFN_GROUPS = [4, 4, 4, 4, 3, 3]
GMAX = 4
NL3 = DEPTH * 3 * NC_
SB_BASE = 16512
SB_TOP = 229344

O_AQ, O_AK, O_AV = 0, 512, 1024
O_GQ, O_GK, O_GV, O_GLR, O_GR = 1536, 1792, 2048, 2560, 2576
O_GA, O_GB = 3088, 4112
N_IN = 5136


class Buf:
    __slots__ = ("name", "w", "r", "excl")

    def __init__(self, name="", excl=False):
        self.name = name
        self.w = None
        self.r = {}
        self.excl = excl


def bufs(n):
    return [Buf() for _ in range(n)]


class Op:
    __slots__ = ("eng", "fn", "deps", "needed", "semval", "dsem", "dval")

    def __init__(self, eng, fn, deps, dsem):
        self.eng = eng
        self.fn = fn
        self.deps = deps
        self.needed = False
        self.semval = None
        self.dsem = dsem
        self.dval = None


class Sched:
    ENGS = ("pe", "act", "dve", "pool", "sp")

    def __init__(self):
        self.q = {e: [] for e in self.ENGS}
        self.dma_count = {}
        self.last_dma = {}
        self.extra = {e: [] for e in self.ENGS}

    def op(self, eng, fn, R=(), W=(), dsem=None, after=()):
        deps = [(3, a) for a in after if a is not None]
        if any(b.excl for b in R):
            W = list(W) + [b for b in R if b.excl]
            R = [b for b in R if not b.excl]
        W = list(dict.fromkeys(W))
        for b in R:
            if b.w is not None:
                deps.append((0, b.w))
        for b in W:
            if b.w is not None:
                deps.append((1, b.w))
            for r in b.r.values():
                deps.append((2, r))
        if self.extra[eng]:
            deps.extend((0, d) for d in self.extra[eng])
            self.extra[eng] = []
        o = Op(eng, fn, deps, dsem)
        if dsem is not None:
            self.dma_count[dsem] = self.dma_count.get(dsem, 0) + 16
            o.dval = self.dma_count[dsem]
            self.last_dma[dsem] = o
        self.q[eng].append(o)
        key = dsem if dsem is not None else eng
        for b in R:
            b.r[key] = o
        for b in W:
            b.w = o
            b.r = {}
        return o

    def fence(self):
        snap = []
        for e in self.ENGS:
            for o in reversed(self.q[e]):
                if o.dsem is None:
                    snap.append(o)
                    break
        snap.extend(self.last_dma.values())
        for e in self.ENGS:
            self.extra[e] = list(snap)

    def finalize(self):
        for eng in self.ENGS:
            for o in self.q[eng]:
                keep = []
                for kind, d in o.deps:
                    if d is o:
                        continue
                    if d.dsem is not None:
                        keep.append(d)
                    elif d.eng == o.eng:
                        if o.eng == "pe" and kind != 3:
                            continue
                        keep.append(d)
                    else:
                        keep.append(d)
                for d in keep:
                    if d.dsem is None:
                        d.needed = True
                o.deps = keep
        for eng in self.ENGS:
            c = 0
            for o in self.q[eng]:
                if o.dsem is None and o.needed:
                    c += 1
                    o.semval = c

    def replay(self, eng, e, esem, dsems):
        seen = {}
        for o in self.q[eng]:
            for d in o.deps:
                if d.dsem is not None:
                    key, val, sem = ("d", d.dsem), d.dval, dsems[d.dsem]
                else:
                    key, val, sem = ("e", d.eng), d.semval, esem[d.eng]
                if seen.get(key, 0) >= val:
                    continue
                seen[key] = val
                e.wait_ge(sem, val)
            ins = o.fn(e)
            if o.dsem is not None:
                ins.then_inc(dsems[o.dsem], 16)
            elif o.needed:
                ins.then_inc(esem[eng], 1)


DT_SIZE = {F32: 4, BF16: 2}


class Arena:
    UID = 0

    def __init__(self, nc, base, top):
        self.nc = nc
        self.base = base
        self.top = top
        self.off = base
        self.uid = 0
        self.peak = base

    def alloc(self, shape, dt, name="t"):
        n = 1
        for d in shape[1:]:
            n *= d
        nbytes = (n * DT_SIZE[dt] + 63) // 64 * 64
        if self.off + nbytes > self.top:
            raise RuntimeError(f"arena overflow allocating {name} {shape}: off={self.off - self.base} need {nbytes} cap {self.top - self.base}")
        Arena.UID += 1
        t = self.nc.alloc_sbuf_tensor_at(f"{name}_{Arena.UID}", list(shape), dt, offset=self.off)
        self.off += nbytes
        self.peak = max(self.peak, self.off)
        return t

    def mark(self):
        return self.off

    def release(self, m):
        self.off = m


def build(phases, debug=None):
    nc = bass.Bass("TRN2", target_bir_lowering=False)
    s = Sched()
    dram = {}
    dsem_names = []

    def new_dsem(name):
        nm = f"{name}_{len(dsem_names)}"
        dsem_names.append(nm)
        return nm

    def din(name, shape, dt=F32):
        if name not in dram:
            dram[name] = nc.dram_tensor(name, list(shape), dt, kind="ExternalInput").ap()
        return dram[name]

    debug = debug or ()

    def tap(name, t, bl):
        if name in debug:
            dd = nc.dram_tensor("dbg_" + name, list(t.shape), t.dtype, kind="ExternalOutput").ap()
            s.op("sp", lambda e: e.dma_start(out=dd, in_=t[:]), R=bl, dsem=new_dsem("dbg"))

    xT_d = din("xT", [D, S])
    yT_d = nc.dram_tensor("yT", [D, S], F32, kind="ExternalOutput").ap()
    lng_d = din("ln_g", [128, NL3])
    lnb_d = din("ln_b", [128, NL3])
    consts_d = din("consts", [128, 5 * 128])
    has_mix = any(p[0] == "mix" for p in phases)

    A = Arena(nc, SB_BASE, SB_TOP)
    xs = A.alloc([128, NC_, S], F32, "xs")
    xb = A.alloc([128, NC_, S], BF16, "xb")
    xs_b = [bufs(NTB) for _ in range(NC_)]
    xb_b = [bufs(NTB) for _ in range(NC_)]
    lng = A.alloc([128, NL3], F32, "lng")
    lnb = A.alloc([128, NL3], F32, "lnb")
    lnga = A.alloc([128, NL3], F32, "lnga")
    lnba = A.alloc([128, NL3], F32, "lnba")
    ln_c = Buf()
    consts = A.alloc([128, 5 * 128], F32, "consts")
    consts_b = Buf()
    Umat = consts[:, 0:128]
    Lmat = consts[:, 128:256]
    Mblk = consts[:, 256:384]
    ones = consts[:, 384:512]
    gones = consts[:, 512:640]
    ones1 = A.alloc([128, 64], F32, "ones1")
    ones1_b = Buf()
    mixc_b = Buf()
    if has_mix:
        wgu = A.alloc([16, DEPTH, 256], F32, "wgu")
        bgu = A.alloc([128, DEPTH, 256], F32, "bgu")
        gng = A.alloc([128, DEPTH * 4], F32, "gng")
        gnb = A.alloc([128, DEPTH * 4], F32, "gnb")
    arena_base = A.mark()

    psum = [nc.alloc_psum_tensor(f"bank{i}", [128, TB], F32) for i in range(8)]
    pq = [[Buf(f"bank{i}", excl=True)] * 4 for i in range(8)]

    def pb_(i, c0=0, c1=TB):
        return pq[i][c0 // 128:(c1 + 127) // 128]

    out_sem = new_dsem("out")

    lng_b0, lnb_b0 = Buf(), Buf()
    s.op("sp", lambda e: e.dma_start(out=lng[:], in_=lng_d), W=[lng_b0], dsem=new_dsem("io"))
    s.op("sp", lambda e: e.dma_start(out=lnb[:], in_=lnb_d), W=[lnb_b0], dsem=new_dsem("io"))
    s.op("sp", lambda e: e.dma_start(out=consts[:], in_=consts_d), W=[consts_b], dsem=new_dsem("io"))
    if has_mix:
        wgu_d = din("wgu", [16, DEPTH, 256])
        bgu_d = din("bgu", [128, DEPTH, 256])
        gng_d = din("gng", [128, DEPTH * 4])
        gnb_d = din("gnb", [128, DEPTH * 4])
        mb = bufs(4)
        s.op("sp", lambda e: e.dma_start(out=wgu[:], in_=wgu_d), W=[mb[0]], dsem=new_dsem("io"))
        s.op("sp", lambda e: e.dma_start(out=bgu[:], in_=bgu_d), W=[mb[1]], dsem=new_dsem("io"))
        s.op("sp", lambda e: e.dma_start(out=gng[:], in_=gng_d), W=[mb[2]], dsem=new_dsem("io"))
        s.op("sp", lambda e: e.dma_start(out=gnb[:], in_=gnb_d), W=[mb[3]], dsem=new_dsem("io"))
        s.op("dve", lambda e: e.memset(ones1[:], 1.0), R=mb, W=[ones1_b, mixc_b])
    for c in range(NC_):
        s.op("sp", lambda e, c=c: e.dma_start(out=xs[:, c, :], in_=xT_d[c * 128:(c + 1) * 128, :]),
             W=xs_b[c], dsem=new_dsem("iox"))
    s.op("act", lambda e: e.mul(lnga[:], lng[:], ALPHA), R=[lng_b0], W=[ln_c])
    s.op("act", lambda e: e.mul(lnba[:], lnb[:], ALPHA), R=[lnb_b0], W=[ln_c])
    for c in range(NC_):
        for t in range(NTB):
            sl = slice(t * TB, (t + 1) * TB)
            s.op("dve", lambda e, c=c, sl=sl: e.tensor_copy(out=xb[:, c, sl], in_=xs[:, c, sl]),
                 R=[xs_b[c][t]], W=[xb_b[c][t]])
            s.op("act", lambda e, c=c, sl=sl: e.mul(xs[:, c, sl], xs[:, c, sl], ALPHA),
                 R=[xs_b[c][t]], W=[xs_b[c][t]])

    def layer_norm(l, i):
        col0 = (l * 3 + i) * NC_
        s.fence()
        A.release(arena_base)
        sq = A.alloc([128, NC_, TB], F32, "sq")
        sq_b = bufs(NC_)
        mean_sb = [A.alloc([128, TB], F32, "mean") for _ in range(2)]
        m2_sb = [A.alloc([128, TB], F32, "m2") for _ in range(2)]
        rstd_sb = [A.alloc([128, TB], F32, "rstd") for _ in range(2)]
        mean_b, m2_b, rstd_b = bufs(2), bufs(2), bufs(2)
        t1 = [A.alloc([128, TB], F32, "t1") for _ in range(2)]
        t2 = [A.alloc([128, TB], F32, "t2") for _ in range(3)]
        t1_b, t2_b = bufs(2), bufs(3)
        cn = {"t1": 0, "t2": 0}

        def stats(t):
            sl = slice(t * TB, (t + 1) * TB)
            p = t % 2
            bm, bq = (6, 7) if p == 0 else (4, 5)
            pm, pq_ = psum[bm], psum[bq]
            for c in range(NC_):
                s.op("act", lambda e, c=c, sl=sl: e.activation(out=sq[:, c, :], in_=xs[:, c, sl], func=AF.Square),
                     R=[xs_b[c][t]], W=[sq_b[c]])
            for c in range(NC_):
                s.op("pe", lambda e, c=c, sl=sl, pm=pm: e.matmul(pm[:], lhsT=ones, rhs=xs[:, c, sl],
                                                                 start=(c == 0), stop=(c == NC_ - 1)),
                     R=[consts_b, xs_b[c][t]], W=pb_(bm))
            for c in range(NC_):
                s.op("pe", lambda e, c=c, pq_=pq_: e.matmul(pq_[:], lhsT=ones, rhs=sq[:, c, :],
                                                            start=(c == 0), stop=(c == NC_ - 1)),
                     R=[consts_b, sq_b[c]], W=pb_(bq))
            s.op("act", lambda e, pm=pm, p=p: e.activation(out=m2_sb[p][:], in_=pm[:], func=AF.Square), R=pb_(bm), W=[m2_b[p]])
            s.op("act", lambda e, pm=pm, p=p: e.copy(out=mean_sb[p][:], in_=pm[:]), R=pb_(bm), W=[mean_b[p]])
            s.op("dve", lambda e, pq_=pq_, p=p: e.tensor_tensor(out=rstd_sb[p][:], in0=pq_[:], in1=m2_sb[p][:], op=ALU.subtract),
                 R=pb_(bq) + [m2_b[p]], W=[rstd_b[p]])
            s.op("act", lambda e, p=p: e.activation(out=m2_sb[p][:], in_=rstd_sb[p][:], func=AF.Ln, bias=LN_EPS, scale=1.0),
                 R=[rstd_b[p]], W=[m2_b[p]])
            s.op("act", lambda e, p=p: e.activation(out=rstd_sb[p][:], in_=m2_sb[p][:], func=AF.Exp, scale=-0.5),
                 R=[m2_b[p]], W=[rstd_b[p]])

        def norm(t):
            sl = slice(t * TB, (t + 1) * TB)
            p = t % 2
            for c in range(NC_):
                j = cn["t1"] % 2
                cn["t1"] += 1
                j2 = cn["t2"] % 3
                cn["t2"] += 1
                s.op("dve", lambda e, c=c, sl=sl, j=j, p=p: e.tensor_tensor(out=t1[j][:], in0=xs[:, c, sl], in1=mean_sb[p][:], op=ALU.subtract),
                     R=[xs_b[c][t], mean_b[p]], W=[t1_b[j]])
                s.op("pool", lambda e, j=j, j2=j2, p=p: e.tensor_tensor(out=t2[j2][:], in0=t1[j][:], in1=rstd_sb[p][:], op=ALU.mult),
                     R=[t1_b[j], rstd_b[p]], W=[t2_b[j2]])
                s.op("act", lambda e, c=c, sl=sl, j2=j2: e.activation(out=xs[:, c, sl], in_=t2[j2][:], func=AF.Identity,
                                                                   scale=lnga[:, col0 + c:col0 + c + 1],
                                                                   bias=lnba[:, col0 + c:col0 + c + 1]),
                     R=[t2_b[j2], ln_c], W=[xs_b[c][t]])
                s.op("dve", lambda e, c=c, sl=sl, j2=j2: e.tensor_scalar(
                    out=xb[:, c, sl], in0=t2[j2][:], scalar1=lng[:, col0 + c:col0 + c + 1], scalar2=lnb[:, col0 + c:col0 + c + 1],
                    op0=ALU.mult, op1=ALU.add),
                    R=[t2_b[j2], ln_c], W=[xb_b[c][t]])

        stats(0)
        for t in range(NTB):
            if t + 1 < NTB:
                stats(t + 1)
            norm(t)

    def make_ln(l, i, arenas, sbanks=((0, 1), (2, 3)), lag=2):
        col0 = (l * 3 + i) * NC_

        def al(shape, dt, name):
            for a in arenas:
                n = 1
                for d in shape[1:]:
                    n *= d
                if a.off + (n * DT_SIZE[dt] + 63) // 64 * 64 <= a.top:
                    return a.alloc(shape, dt, name)
            raise RuntimeError("make_ln: no room for " + name)

        NSQ = 3
        sqr = [al([128, TB], F32, "lsq") for _ in range(NSQ)]
        sqr_b = bufs(NSQ)
        mean_sb = [al([128, TB], F32, "lmean") for _ in range(2)]
        m2_sb = [al([128, TB], F32, "lm2") for _ in range(2)]
        rstd_sb = [al([128, TB], F32, "lrstd") for _ in range(2)]
        mean_b, m2_b, rstd_b = bufs(2), bufs(2), bufs(2)
        NT = 3
        t1 = [al([128, TB], F32, "lt1") for _ in range(NT)]
        t2 = [al([128, TB], F32, "lt2") for _ in range(NT)]
        t1_b, t2_b = bufs(NT), bufs(NT)
        cn = {"sq": 0, "t1": 0, "t2": 0, "seen": {}}
        pending = []
        avail = []
        fl = {"s1": None, "s2": None}

        def tick():
            if fl["s2"] is not None:
                c, t, j2 = fl["s2"]
                sl = slice(t * TB, (t + 1) * TB)
                s.op("act", lambda e, c=c, sl=sl, j2=j2: e.activation(out=xs[:, c, sl], in_=t2[j2][:], func=AF.Identity,
                                                                   scale=lnga[:, col0 + c:col0 + c + 1],
                                                                   bias=lnba[:, col0 + c:col0 + c + 1]),
                     R=[t2_b[j2], ln_c], W=[xs_b[c][t]])
                s.op("dve", lambda e, c=c, sl=sl, j2=j2: e.tensor_scalar(
                    out=xb[:, c, sl], in0=t2[j2][:], scalar1=lng[:, col0 + c:col0 + c + 1], scalar2=lnb[:, col0 + c:col0 + c + 1],
                    op0=ALU.mult, op1=ALU.add),
                    R=[t2_b[j2], ln_c], W=[xb_b[c][t]])
                fl["s2"] = None
            if fl["s1"] is not None:
                c, t, j = fl["s1"]
                p = t % 2
                j2 = cn["t2"] % NT
                cn["t2"] += 1
                s.op("pool", lambda e, j=j, j2=j2, p=p: e.tensor_tensor(out=t2[j2][:], in0=t1[j][:], in1=rstd_sb[p][:], op=ALU.mult),
                     R=[t1_b[j], rstd_b[p]], W=[t2_b[j2]])
                fl["s2"] = (c, t, j2)
                fl["s1"] = None
            if avail:
                c, t = avail.pop(0)
                sl = slice(t * TB, (t + 1) * TB)
                p = t % 2
                j = cn["t1"] % NT
                cn["t1"] += 1
                s.op("dve", lambda e, c=c, sl=sl, j=j, p=p: e.tensor_tensor(out=t1[j][:], in0=xs[:, c, sl], in1=mean_sb[p][:], op=ALU.subtract),
                     R=[xs_b[c][t], mean_b[p]], W=[t1_b[j]])
                fl["s1"] = (c, t, j)

        def emit(entry):
            dc, t, k = entry
            sl = slice(t * TB, (t + 1) * TB)
            p = t % 2
            bm, bq = sbanks[p]
            n = cn["seen"].get(t, 0)
            cn["seen"][t] = n + 1
            s.op("pe", lambda e, dc=dc, sl=sl, bm=bm, n=n: e.matmul(psum[bm][:], lhsT=ones, rhs=xs[:, dc, sl],
                                                                   start=(n == 0), stop=(n == NC_ - 1)),
                 R=[consts_b, xs_b[dc][t]], W=pb_(bm))
            s.op("pe", lambda e, k=k, bq=bq, n=n: e.matmul(psum[bq][:], lhsT=ones, rhs=sqr[k][:],
                                                           start=(n == 0), stop=(n == NC_ - 1)),
                 R=[consts_b, sqr_b[k]], W=pb_(bq))
            if n == NC_ - 1:
                s.op("act", lambda e, bm=bm, p=p: e.activation(out=m2_sb[p][:], in_=psum[bm][:], func=AF.Square), R=pb_(bm), W=[m2_b[p]])
                s.op("act", lambda e, bm=bm, p=p: e.copy(out=mean_sb[p][:], in_=psum[bm][:]), R=pb_(bm), W=[mean_b[p]])
                s.op("dve", lambda e, bq=bq, p=p: e.tensor_tensor(out=rstd_sb[p][:], in0=psum[bq][:], in1=m2_sb[p][:], op=ALU.subtract),
                     R=pb_(bq) + [m2_b[p]], W=[rstd_b[p]])
                s.op("act", lambda e, p=p: e.activation(out=m2_sb[p][:], in_=rstd_sb[p][:], func=AF.Ln, bias=LN_EPS, scale=1.0),
                     R=[rstd_b[p]], W=[m2_b[p]])
                s.op("act", lambda e, p=p: e.activation(out=rstd_sb[p][:], in_=m2_sb[p][:], func=AF.Exp, scale=-0.5),
                     R=[m2_b[p]], W=[rstd_b[p]])
                avail.extend((c, t) for c in range(NC_))

        def chunk_done(dc, t):
            sl = slice(t * TB, (t + 1) * TB)
            k = cn["sq"] % NSQ
            cn["sq"] += 1
            s.op("act", lambda e, dc=dc, sl=sl, k=k: e.activation(out=sqr[k][:], in_=xs[:, dc, sl], func=AF.Square),
                 R=[xs_b[dc][t]], W=[sqr_b[k]])
            pending.append((dc, t, k))
            if len(pending) > lag:
                emit(pending.pop(0))
            tick()

        def flush():
            while pending:
                emit(pending.pop(0))
                tick()
            while avail or fl["s1"] is not None or fl["s2"] is not None:
                tick()

        return chunk_done, flush

    def ffn(l, i):
        w1_d = din(f"f{i}w1_{l}", [NF, 128, D])
        w3_d = din(f"f{i}w3_{l}", [NF, 128, D])
        w2_d = din(f"f{i}w2_{l}", [DFF, D])
        s.fence()
        A.release(arena_base)
        W13_SLOTS = 3
        w13 = [A.alloc([128, 2, D], BF16, "w13") for _ in range(W13_SLOTS)]
        w13_b = [bufs(2) for _ in range(W13_SLOTS)]
        w13_sem = [[new_dsem("w13") for _ in range(2)] for _ in range(W13_SLOTS)]
        w2 = [A.alloc([128, GMAX, D], BF16, "w2") for _ in range(2)]
        w2_b = bufs(2)
        w2_sem = [new_dsem("w2") for _ in range(2)]
        gT = [A.alloc([128, GMAX, S], BF16, "gT") for _ in range(2)]
        gT_b = [[bufs(NTB) for _ in range(GMAX)] for _ in range(2)]
        silu_t = [A.alloc([128, TB], F32, "silu") for _ in range(2)]
        silu_b = bufs(2)
        ln_chunk, ln_flush = make_ln(l, 0 if i == 0 else 2, [A])
        cnt = {"w13": 0, "w2": 0, "psA": 0, "psY": 0, "silu": 0}
        m0 = 0
        for gi, G in enumerate(FFN_GROUPS):
            ms = list(range(m0, m0 + G))
            m0 += G
            gs = gi % 2
            ws = cnt["w2"] % 2
            cnt["w2"] += 1
            s.op("pool", lambda e, ws=ws, ms=ms, G=G: e.dma_start(
                out=w2[ws][:, 0:G, :],
                in_=w2_d[ms[0] * 128:(ms[0] + G) * 128, :].rearrange("(g p) n -> p g n", p=128)),
                W=[w2_b[ws]], dsem=w2_sem[ws])
            for ml, m in enumerate(ms):
                slot = cnt["w13"] % W13_SLOTS
                cnt["w13"] += 1
                s.op("pool", lambda e, slot=slot, m=m: e.dma_start(out=w13[slot][:, 0, :], in_=w1_d[m]),
                     W=[w13_b[slot][0]], dsem=w13_sem[slot][0])
                s.op("pool", lambda e, slot=slot, m=m: e.dma_start(out=w13[slot][:, 1, :], in_=w3_d[m]),
                     W=[w13_b[slot][1]], dsem=w13_sem[slot][1])
                for t in range(NTB):
                    sl = slice(t * TB, (t + 1) * TB)
                    pj = cnt["psA"] % 2
                    cnt["psA"] += 1
                    for which, bi in ((0, pj), (1, 2 + pj)):
                        for k in range(NC_):
                            s.op("pe", lambda e, slot=slot, which=which, k=k, sl=sl, bi=bi: e.matmul(
                                psum[bi][:], lhsT=w13[slot][:, which, k * 128:(k + 1) * 128], rhs=xb[:, k, sl],
                                start=(k == 0), stop=(k == NC_ - 1)),
                                R=[w13_b[slot][which], xb_b[k][t]], W=pb_(bi))
                    sj = cnt["silu"] % 2
                    cnt["silu"] += 1
                    s.op("act", lambda e, pj=pj, sj=sj: e.activation(out=silu_t[sj][:], in_=psum[pj][:], func=AF.Silu),
                         R=pb_(pj), W=[silu_b[sj]])
                    s.op("dve", lambda e, pj=pj, sj=sj, gs=gs, ml=ml, sl=sl: e.tensor_tensor(
                        out=gT[gs][:, ml, sl], in0=psum[2 + pj][:], in1=silu_t[sj][:], op=ALU.mult),
                        R=pb_(2 + pj) + [silu_b[sj]], W=[gT_b[gs][ml][t]])
            last = (gi == len(FFN_GROUPS) - 1)
            order = [(dc, t) for t in range(NTB) for dc in range(NC_)] if last else [(dc, t) for dc in range(NC_) for t in range(NTB)]
            for dc, t in order:
                if True:
                    sl = slice(t * TB, (t + 1) * TB)
                    bi = 4 + cnt["psY"] % 2
                    cnt["psY"] += 1
                    for ml in range(G):
                        s.op("pe", lambda e, ws=ws, ml=ml, dc=dc, gs=gs, sl=sl, bi=bi, G=G: e.matmul(
                            psum[bi][:], lhsT=w2[ws][:, ml, dc * 128:(dc + 1) * 128], rhs=gT[gs][:, ml, sl],
                            start=(ml == 0), stop=(ml == G - 1)),
                            R=[w2_b[ws], gT_b[gs][ml][t]], W=pb_(bi))
                    s.op("dve", lambda e, bi=bi, dc=dc, sl=sl: e.scalar_tensor_tensor(
                        out=xs[:, dc, sl], in0=psum[bi][:], scalar=0.5, in1=xs[:, dc, sl],
                        op0=ALU.mult, op1=ALU.add),
                        R=pb_(bi) + [xs_b[dc][t]], W=[xs_b[dc][t]])
                    if last:
                        ln_chunk(dc, t)
        ln_flush()

    def mixer(l, stop=None):
        amask_d = din("amask", [8, 128, S])
        wglr_d = din(f"wglr_{l}", [128, NC_ * 16])
        wgqk_d = din(f"wgqk_{l}", [128, NC_ * 512])
        wgkv_d = din(f"wgkv_{l}", [128, NC_ * 768])
        wgr_d = din(f"wgr_{l}", [128, NC_ * 512])
        waqkv_d = din(f"waqkv_{l}", [4, 128, NC_ * 384])
        wgab_d = din(f"wgab_{l}", [NC_, 128, NC_ * 256])
        wap_d = din(f"wap_{l}", [NC_, 128, 4 * 128])
        wgp_d = din(f"wgp_{l}", [NC_, 128, 4 * 128])
        wo_d = din(f"wo_{l}", [NC_, 128, NC_ * 128])
        blk = amask_blocks
        s.fence()
        A.release(arena_base)
        o_gnT = A.alloc([128, 4, S], BF16, "o_gnT")
        o_gn_b = [bufs(NTB) for _ in range(4)]
        o_a_b = [bufs(NTB) for _ in range(8)]
        mix_base = A.mark()

        qT = A.alloc([128, 2, S], BF16, "qT")
        kT = A.alloc([128, 2, S], BF16, "kT")
        qT_b = [bufs(16) for _ in range(2)]
        kT_b = [bufs(16) for _ in range(2)]
        khat = A.alloc([128, 16, 256], BF16, "khat")
        khat_b = bufs(16)
        gv = A.alloc([128, 16, 512], BF16, "gv")
        gv_b = bufs(16)
        dec = A.alloc([128, 4, 32], F32, "dec")
        dec_b = bufs(NTB)
        gla_base = A.mark()
        wglr = A.alloc([128, NC_, 16], BF16, "wglr")
        wgqk = A.alloc([128, NC_, 512], BF16, "wgqk")
        wgkv = A.alloc([128, NC_, 768], BF16, "wgkv")
        wB_b = bufs(3)
        s.op("pool", lambda e: e.dma_start(out=wglr[:], in_=wglr_d.rearrange("p (k n) -> p k n", k=NC_)),
             W=[wB_b[0]], dsem=new_dsem("wB"))
        s.op("pool", lambda e: e.dma_start(out=wgqk[:], in_=wgqk_d.rearrange("p (k n) -> p k n", k=NC_)),
             W=[wB_b[1]], dsem=new_dsem("wB"))
        s.op("pool", lambda e: e.dma_start(out=wgkv[:], in_=wgkv_d.rearrange("p (k n) -> p k n", k=NC_)),
             W=[wB_b[2]], dsem=new_dsem("wB"))
        glrT = [A.alloc([16, TB], F32, "glrT") for _ in range(2)]
        glrT_b = bufs(2)
        z_sb = [A.alloc([128, 256], F32, "z") for _ in range(2)]
        z_b = bufs(2)
        la_sb = [A.alloc([128, 256], F32, "la") for _ in range(2)]
        la_b = bufs(2)
        Eb = A.alloc([128, 2, TB], F32, "Eb")
        Einv = A.alloc([128, 2, TB], F32, "Einv")
        E_b, Einv_b = bufs(2), bufs(2)
        Ft = A.alloc([128, 4, 256], F32, "Ft")
        F_b = bufs(4)
        zc = 0
        pc = 0
        for tb in range(NTB):
            sl = slice(tb * TB, (tb + 1) * TB)
            gj = tb % 2
            for k in range(NC_):
                s.op("pe", lambda e, k=k, sl=sl: e.matmul(psum[0][0:16, :], lhsT=wglr[:, k, :], rhs=xb[:, k, sl],
                                                         start=(k == 0), stop=(k == NC_ - 1)),
                     R=[wB_b[0], xb_b[k][tb]], W=pb_(0))
            s.op("act", lambda e, gj=gj: e.copy(out=glrT[gj][:], in_=psum[0][0:16, :]), R=pb_(0), W=[glrT_b[gj]])
            for jj in range(4):
                j = tb * 4 + jj
                zi = zc % 2
                zc += 1
                bz = 1 + zi
                s.op("pe", lambda e, gj=gj, jj=jj, bz=bz: e.matmul(
                    psum[bz][:, 0:256], lhsT=glrT[gj][:, jj * 128:(jj + 1) * 128], rhs=wgu[:, l, :],
                    start=True, stop=True),
                    R=[glrT_b[gj], mixc_b], W=pb_(bz, 0, 256))
                s.op("dve", lambda e, zi=zi, bz=bz: e.tensor_tensor(out=z_sb[zi][:], in0=psum[bz][:, 0:256], in1=bgu[:, l, :], op=ALU.add),
                     R=pb_(bz, 0, 256) + [mixc_b], W=[z_b[zi]])
                tsl = slice(j * 128, (j + 1) * 128)
                bi2 = 6 + pc % 2
                pc += 1
                for k in range(NC_):
                    s.op("pe", lambda e, k=k, tsl=tsl, bi2=bi2: e.matmul(
                        psum[bi2][:], lhsT=xb[:, k, tsl], rhs=wgkv[:, k, 256:768],
                        start=(k == 0), stop=(k == NC_ - 1)),
                        R=[wB_b[2], xb_b[k][tb]], W=pb_(bi2))
                s.op("dve", lambda e, j=j, bi2=bi2: e.tensor_copy(out=gv[:, j, :], in_=psum[bi2][:]),
                     R=pb_(bi2), W=[gv_b[j]])
                s.op("act", lambda e, zi=zi: e.activation(out=z_sb[zi][:], in_=z_sb[zi][:], func=AF.Exp, scale=-1.0),
                     R=[z_b[zi]], W=[z_b[zi]])
                s.op("act", lambda e, zi=zi: e.activation(out=la_sb[zi][:], in_=z_sb[zi][:], func=AF.Ln, bias=1.0, scale=1.0),
                     R=[z_b[zi]], W=[la_b[zi]])
                for ch in range(2):
                    s.op("pe", lambda e, zi=zi, ch=ch, jj=jj: e.matmul(
                        psum[3 + ch][:, jj * 128:(jj + 1) * 128], lhsT=la_sb[zi][:, ch * 128:(ch + 1) * 128], rhs=Umat,
                        start=True, stop=True),
                        R=[la_b[zi], consts_b], W=[pq[3 + ch][jj]])
                s.op("pe", lambda e, zi=zi: e.matmul(psum[5][:, 0:256], lhsT=Lmat, rhs=la_sb[zi][:], start=True, stop=True),
                     R=[la_b[zi], consts_b], W=pb_(5, 0, 256))
                s.op("act", lambda e, jj=jj: e.activation(out=Ft[:, jj, :], in_=psum[5][:, 0:256], func=AF.Exp),
                     R=pb_(5, 0, 256), W=[F_b[jj]])
            for ch in range(2):
                s.op("act", lambda e, ch=ch: e.activation(out=Eb[:, ch, :], in_=psum[3 + ch][:], func=AF.Exp),
                     R=pb_(3 + ch), W=[E_b[ch]])
                s.op("act", lambda e, ch=ch: e.activation(out=Einv[:, ch, :], in_=psum[3 + ch][:], func=AF.Exp, scale=-1.0),
                     R=pb_(3 + ch), W=[Einv_b[ch]])
            for dup in range(2):
                s.op("dve", lambda e, tb=tb, dup=dup: e.tensor_copy(
                    out=dec[:].rearrange("p (c d) n -> p c d n", d=2)[:, :, dup, tb * 8:(tb + 1) * 8], in_=Eb[:, :, 63::64]),
                    R=E_b, W=[dec_b[tb]])
            for m in range(4):
                ch = m % 2
                bi = 6 + pc % 2
                pc += 1
                for k in range(NC_):
                    s.op("pe", lambda e, m=m, k=k, sl=sl, bi=bi: e.matmul(
                        psum[bi][:], lhsT=wgqk[:, k, m * 128:(m + 1) * 128], rhs=xb[:, k, sl],
                        start=(k == 0), stop=(k == NC_ - 1)),
                        R=[wB_b[1], xb_b[k][tb]], W=pb_(bi))
                if m < 2:
                    s.op("dve", lambda e, ch=ch, sl=sl, bi=bi: e.scalar_tensor_tensor(
                        out=qT[:, ch, sl], in0=psum[bi][:], scalar=0.125, in1=Eb[:, ch, :], op0=ALU.mult, op1=ALU.mult),
                        R=pb_(bi) + [E_b[ch]], W=qT_b[ch][tb * 4:(tb + 1) * 4])
                else:
                    s.op("dve", lambda e, ch=ch, sl=sl, bi=bi: e.tensor_tensor(
                        out=kT[:, ch, sl], in0=psum[bi][:], in1=Einv[:, ch, :], op=ALU.mult),
                        R=pb_(bi) + [Einv_b[ch]], W=kT_b[ch][tb * 4:(tb + 1) * 4])
            for jj in range(4):
                j = tb * 4 + jj
                tsl = slice(j * 128, (j + 1) * 128)
                bi = 1 + jj % 2
                for k in range(NC_):
                    s.op("pe", lambda e, k=k, tsl=tsl, bi=bi: e.matmul(
                        psum[bi][:, 0:256], lhsT=xb[:, k, tsl], rhs=wgkv[:, k, 0:256],
                        start=(k == 0), stop=(k == NC_ - 1)),
                        R=[wB_b[2], xb_b[k][tb]], W=pb_(bi, 0, 256))
                s.op("dve", lambda e, j=j, jj=jj, bi=bi: e.tensor_tensor(
                    out=khat[:, j, :], in0=psum[bi][:, 0:256], in1=Ft[:, jj, :], op=ALU.mult),
                    R=pb_(bi, 0, 256) + [F_b[jj]], W=[khat_b[j]])

        if stop == "B":
            return
        tap("qT", qT, [b for bb in qT_b for b in bb])
        tap("kT", kT, [b for bb in kT_b for b in bb])
        tap("khat", khat, khat_b)
        tap("gv", gv, gv_b)
        tap("dec", dec, dec_b)
        s.fence()
        A.release(gla_base)
        wgr = A.alloc([128, NC_, 512], BF16, "wgr")
        wgr_b = Buf()
        s.op("pool", lambda e: e.dma_start(out=wgr[:], in_=wgr_d.rearrange("p (k n) -> p k n", k=NC_)),
             W=[wgr_b], dsem=new_dsem("wgr"))
        ograw = [A.alloc([128, 4, TB], F32, "ograw") for _ in range(2)]
        ograw_b = [[bufs(4) for _ in range(4)] for _ in range(2)]
        st_f = A.alloc([128, 4, 128], F32, "st_f")
        st_b = A.alloc([128, 4, 128], BF16, "st_b")
        stf_b, stb_b = bufs(4), bufs(4)
        ATt = [A.alloc([128, 4, 128], BF16, "AT") for _ in range(2)]
        AT_b = [bufs(4) for _ in range(2)]
        sg = [A.alloc([128, TB], F32, "sg") for _ in range(4)]
        sg_b = bufs(4)
        gsq = A.alloc([128, TB], F32, "gsq")
        gsq_b = Buf()
        gm2 = A.alloc([128, TB], F32, "gm2")
        grs = A.alloc([128, TB], F32, "grs")
        gm2_b, grs_b = Buf(), Buf()
        s.op("dve", lambda e: e.memset(st_f[:], 0.0), W=stf_b)
        s.op("dve", lambda e: e.memset(st_b[:], 0.0), W=stb_b)
        bS0, bS1 = 4, 5
        HORD = (0, 2, 1, 3)

        def emit_AT_mm(j):
            bA = j % 2
            tok = slice(j * 128, (j + 1) * 128)
            prev = None
            for h in HORD:
                ch, pb = h // 2, (h % 2) * 64
                hs = slice(h * 128, (h + 1) * 128)
                prev = s.op("pe", lambda e, ch=ch, pb=pb, hs=hs, tok=tok, bA=bA: e.matmul(
                    psum[bA][:, hs], lhsT=kT[pb:pb + 64, ch, tok], rhs=qT[pb:pb + 64, ch, tok], start=True, stop=True),
                    R=[kT_b[ch][j], qT_b[ch][j]], W=[pq[bA][h]], after=[prev] if h == 1 else [])

        def emit_AT_mask(j):
            bA = j % 2
            aj = j % 2
            s.op("dve", lambda e, bA=bA, aj=aj: e.tensor_tensor(
                out=ATt[aj][:], in0=psum[bA][:].rearrange("p (h n) -> p h n", h=4),
                in1=Mblk.unsqueeze(1).broadcast_to([128, 4, 128]), op=ALU.mult),
                R=[pq[bA][0], consts_b], W=AT_b[aj])

        def emit_dS(j, half):
            bS = bS0 if half == 0 else bS1
            rows = slice(half * 64, half * 64 + 64)
            for h in range(4):
                ch = h // 2
                hs = slice(h * 128, (h + 1) * 128)
                s.op("pe", lambda e, ch=ch, hs=hs, rows=rows, bS=bS, j=j: e.matmul(
                    psum[bS][:, hs], lhsT=khat[rows, j, ch * 128:(ch + 1) * 128], rhs=gv[rows, j, hs],
                    start=True, stop=True),
                    R=[khat_b[j], gv_b[j]], W=[pq[bS][h]])

        def emit_decay(c):
            s.op("dve", lambda e, c=c: e.tensor_tensor(
                out=st_f[:], in0=st_f[:], in1=dec[:, :, c:c + 1].broadcast_to([128, 4, 128]), op=ALU.mult),
                R=stf_b + [dec_b[c // 8]], W=stf_b)

        def emit_update(j, half):
            bS = bS0 if half == 0 else bS1
            c = 2 * j + half
            s.op("dve", lambda e, bS=bS: e.tensor_tensor(
                out=st_f[:], in0=st_f[:], in1=psum[bS][:].rearrange("p (h n) -> p h n", h=4), op=ALU.add),
                R=stf_b + [pq[bS][0]], W=stf_b)
            s.op("act", lambda e: e.copy(out=st_b[:], in_=st_f[:]), R=stf_b, W=stb_b)
            if c + 1 < 32:
                emit_decay(c + 1)

        gtasks = []
        gstate = {"B": None}

        def gate_batch(tb):
            sl = slice(tb * TB, (tb + 1) * TB)
            for h in range(4):
                bi = 6 + h % 2
                for k in range(NC_):
                    s.op("pe", lambda e, k=k, h=h, sl=sl, bi=bi: e.matmul(
                        psum[bi][:], lhsT=wgr[:, k, h * 128:(h + 1) * 128], rhs=xb[:, k, sl],
                        start=(k == 0), stop=(k == NC_ - 1)),
                        R=[wgr_b, xb_b[k][tb]], W=pb_(bi))
                s.op("act", lambda e, h=h, bi=bi: e.activation(out=sg[h][:], in_=psum[bi][:], func=AF.Silu),
                     R=pb_(bi), W=[sg_b[h]])
            for h in range(4):
                gtasks.append((tb, h))

        def gn_A(tb, h):
            ob = tb % 2
            og = ograw[ob][:, h, :]
            ogb = ograw_b[ob][h]
            s.op("act", lambda e, og=og: e.activation(out=gsq[:], in_=og, func=AF.Square), R=ogb, W=[gsq_b])
            s.op("pe", lambda e, og=og: e.matmul(psum[6][:], lhsT=gones, rhs=og, start=True, stop=True),
                 R=ogb + [consts_b], W=pb_(6))
            s.op("pe", lambda e: e.matmul(psum[7][:], lhsT=gones, rhs=gsq[:], start=True, stop=True),
                 R=[gsq_b, consts_b], W=pb_(7))
            s.op("act", lambda e: e.activation(out=gm2[:], in_=psum[6][:], func=AF.Square), R=pb_(6), W=[gm2_b])
            s.op("dve", lambda e, og=og: e.tensor_tensor(out=og, in0=og, in1=psum[6][:], op=ALU.subtract),
                 R=ogb + pb_(6), W=ogb)
            s.op("dve", lambda e: e.tensor_tensor(out=grs[:], in0=psum[7][:], in1=gm2[:], op=ALU.subtract),
                 R=pb_(7) + [gm2_b], W=[grs_b])
            s.op("act", lambda e: e.activation(out=gm2[:], in_=grs[:], func=AF.Ln, bias=LN_EPS, scale=1.0),
                 R=[grs_b], W=[gm2_b])
            s.op("act", lambda e: e.activation(out=grs[:], in_=gm2[:], func=AF.Exp, scale=-0.5), R=[gm2_b], W=[grs_b])

        def gn_B(tb, h):
            ob = tb % 2
            og = ograw[ob][:, h, :]
            ogb = ograw_b[ob][h]
            col = l * 4 + h
            sl = slice(tb * TB, (tb + 1) * TB)
            s.op("pool", lambda e, og=og: e.tensor_tensor(out=og, in0=og, in1=grs[:], op=ALU.mult),
                 R=ogb + [grs_b], W=ogb)
            s.op("act", lambda e, og=og, col=col: e.activation(out=og, in_=og, func=AF.Identity,
                                                             scale=gng[:, col:col + 1], bias=gnb[:, col:col + 1]),
                 R=ogb + [mixc_b], W=ogb)
            s.op("dve", lambda e, og=og, h=h, sl=sl: e.tensor_tensor(out=o_gnT[:, h, sl], in0=og, in1=sg[h][:], op=ALU.mult),
                 R=ogb + [sg_b[h]], W=[o_gn_b[h][tb]])

        def gn_slot():
            if gstate["B"] is not None:
                gn_B(*gstate["B"])
                gstate["B"] = None
            if gtasks:
                t_ = gtasks.pop(0)
                gn_A(*t_)
                gstate["B"] = t_

        def gn_flush():
            while gtasks or gstate["B"] is not None:
                gn_slot()

        emit_AT_mm(0)
        emit_dS(0, 0)
        emit_dS(0, 1)
        emit_AT_mask(0)
        for j in range(16):
            tb, jj = j // 4, j % 4
            ob = tb % 2
            aj = j % 2
            bO = 2 + aj
            t0 = slice(j * 128, j * 128 + 64)
            t1_ = slice(j * 128 + 64, (j + 1) * 128)
            for h in range(4):
                hs = slice(h * 128, (h + 1) * 128)
                s.op("pe", lambda e, h=h, hs=hs, bO=bO, aj=aj, j=j: e.matmul(
                    psum[bO][:, hs], lhsT=gv[:, j, hs], rhs=ATt[aj][:, h, :], start=(h == 0), stop=False, skip_group_check=True),
                    R=[gv_b[j], AT_b[aj][h]], W=[pq[bO][h]])
            prev = None
            for h in HORD:
                ch, pb = h // 2, (h % 2) * 64
                prev = s.op("pe", lambda e, h=h, ch=ch, pb=pb, bO=bO, t0=t0: e.matmul(
                    psum[bO][:, h * 128:h * 128 + 64], lhsT=st_b[pb:pb + 64, h, :], rhs=qT[pb:pb + 64, ch, t0],
                    start=False, stop=False, skip_group_check=True),
                    R=[stb_b[h], qT_b[ch][j]], W=[pq[bO][h]], after=[prev] if h == 1 else [])
            if j + 1 < 16:
                emit_AT_mm(j + 1)
            emit_update(j, 0)
            prev = None
            for h in HORD:
                ch, pb = h // 2, (h % 2) * 64
                prev = s.op("pe", lambda e, h=h, ch=ch, pb=pb, bO=bO, t1_=t1_: e.matmul(
                    psum[bO][:, h * 128 + 64:(h + 1) * 128], lhsT=st_b[pb:pb + 64, h, :], rhs=qT[pb:pb + 64, ch, t1_],
                    start=False, stop=True, skip_group_check=True),
                    R=[stb_b[h], qT_b[ch][j]], W=[pq[bO][h]], after=[prev] if h == 1 else [])
            if j + 1 < 16:
                emit_AT_mask(j + 1)
                emit_dS(j + 1, 0)
            s.op("act", lambda e, bO=bO, ob=ob, jj=jj: e.copy(
                out=ograw[ob][:, :, jj * 128:(jj + 1) * 128], in_=psum[bO][:].rearrange("p (h n) -> p h n", h=4)),
                R=[pq[bO][0]], W=[ograw_b[ob][h][jj] for h in range(4)])
            emit_update(j, 1)
            if j + 1 < 16:
                emit_dS(j + 1, 1)
            if j == 0:
                tap("st1", st_f, stf_b)
            gn_slot()
            if jj == 3:
                gn_flush()
                if tb == 0:
                    tap("ograw0", ograw[0], [b for bb in ograw_b[0] for b in bb])
                gate_batch(tb)
        gn_flush()

        if stop == "D":
            return
        tap("o_gnT", o_gnT, [b for bb in o_gn_b for b in bb])
        s.fence()
        A.release(mix_base)
        o_aT = A.alloc([128, 4, S], BF16, "o_aT")
        mix_base = A.mark()
        waqkv = A.alloc([128, NC_, 384], BF16, "waqkv")
        waqkv_b = Buf()
        waqkv_sem = new_dsem("waqkv")
        aqT = [[A.alloc([128, S], BF16, "aqT") for _ in range(2)] for _ in range(2)]
        akT = [A.alloc([128, S], BF16, "akT") for _ in range(2)]
        aqT_b = [bufs(NTB) for _ in range(2)]
        aqz_b = [bufs(2) for _ in range(2)]
        for wi_ in range(2):
            for hh_ in range(2):
                oth = slice(64, 128) if hh_ == 0 else slice(0, 64)
                s.op("pool", lambda e, wi_=wi_, hh_=hh_, oth=oth: e.memset(aqT[wi_][hh_][oth, :], 0.0), W=[aqz_b[wi_][hh_]])
        akT_b = [bufs(16) for _ in range(2)]
        Vp = [A.alloc([128, 16, 128], BF16, "Vp") for _ in range(2)]
        Vp_b = [bufs(16) for _ in range(2)]
        mask_s = [A.alloc([128, S], F32, "mask") for _ in range(2)]
        mask_b = bufs(2)
        mask_sem = [new_dsem("mask") for _ in range(2)]
        NE = 5
        LOOK = 3
        Et = [A.alloc([128, TB], F32, "Et") for _ in range(NE)]
        Pt = [A.alloc([128, TB], BF16, "Pt") for _ in range(NE)]
        Et_b, Pt_b = bufs(NE), bufs(NE)
        dcp = [A.alloc([128, TB], F32, "dcp") for _ in range(1)] * 2
        dcp_b = bufs(1) * 2
        onesb = A.alloc([128, 128], BF16, "onesb")
        onesb_b = Buf()
        s.op("pool", lambda e: e.memset(onesb[:], 1.0), W=[onesb_b])
        SB = (0, 1, 2, 3)
        stepc = 0
        def load_waqkv(ch):
            s.op("pool", lambda e, ch=ch: e.dma_start(out=waqkv[:], in_=waqkv_d[ch].rearrange("p (k n) -> p k n", k=NC_)),
                 W=[waqkv_b], dsem=waqkv_sem)

        def load_mask(h):
            mi = h % 2
            s.op("sp", lambda e, mi=mi, h=h: e.dma_start(out=mask_s[mi][:], in_=amask_d[h]),
                 W=[mask_b[mi]], dsem=mask_sem[mi])

        load_waqkv(0)
        load_mask(0)
        load_mask(1)
        for ch in range(4):
            wi = ch % 2
            for tb in range(NTB):
                sl = slice(tb * TB, (tb + 1) * TB)
                for which in range(2):
                    bi = SB[(2 * tb + which) % 4]
                    for k in range(NC_):
                        s.op("pe", lambda e, k=k, which=which, sl=sl, bi=bi: e.matmul(
                            psum[bi][:], lhsT=waqkv[:, k, which * 128:(which + 1) * 128], rhs=xb[:, k, sl],
                            start=(k == 0), stop=(k == NC_ - 1)),
                            R=[waqkv_b, xb_b[k][tb]], W=pb_(bi))
                    if which == 0:
                        s.op("act", lambda e, wi=wi, sl=sl, bi=bi: e.mul(aqT[wi][0][0:64, sl], psum[bi][0:64, :], 0.125),
                             R=pb_(bi), W=[aqT_b[wi][tb]])
                        s.op("act", lambda e, wi=wi, sl=sl, bi=bi: e.mul(aqT[wi][1][64:128, sl], psum[bi][64:128, :], 0.125),
                             R=pb_(bi), W=[aqT_b[wi][tb]])
                    else:
                        s.op("dve", lambda e, wi=wi, sl=sl, bi=bi: e.tensor_copy(out=akT[wi][:, sl], in_=psum[bi][:]),
                             R=pb_(bi), W=akT_b[wi][tb * 4:(tb + 1) * 4])
            for j4 in range(4):
                bi = SB[j4 % 4]
                for jj in range(4):
                    j = j4 * 4 + jj
                    tsl = slice(j * 128, (j + 1) * 128)
                    for k in range(NC_):
                        s.op("pe", lambda e, k=k, tsl=tsl, bi=bi, jj=jj: e.matmul(
                            psum[bi][:, jj * 128:(jj + 1) * 128], lhsT=xb[:, k, tsl], rhs=waqkv[:, k, 256:384],
                            start=(k == 0 and jj == 0), stop=(k == NC_ - 1), skip_group_check=True),
                            R=[waqkv_b, xb_b[k][j4]], W=pb_(bi))
                s.op("act", lambda e, wi=wi, j4=j4, bi=bi: e.copy(
                    out=Vp[wi][:, j4 * 4:(j4 + 1) * 4, :], in_=psum[bi][:].rearrange("p (a b) -> p a b", a=4)),
                    R=pb_(bi), W=Vp_b[wi][j4 * 4:(j4 + 1) * 4])
            if ch + 1 < 4:
                load_waqkv(ch + 1)
            steps = []
            for hh in range(2):
                h = 2 * ch + hh
                for qp in range(NTB):
                    kbs = [kb for kb in range(4 * qp + 4) if blk[h][kb][qp]]
                    for ki, kb in enumerate(kbs):
                        steps.append((hh, h, qp, kb, ki, len(kbs)))
            info = {}
            for idx in range(len(steps) + LOOK):
                if idx < len(steps):
                    hh, h, qp, kb, ki, nk = steps[idx]
                    pb = hh * 64
                    mi = h % 2
                    q0 = qp * TB
                    n0 = max(q0, 128 * kb)
                    n = q0 + TB - n0
                    bS = SB[stepc % 4]
                    ei = stepc % NE
                    stepc += 1
                    info[idx] = (ei, n0, n)
                    s.op("pe", lambda e, wi=wi, hh=hh, kb=kb, n0=n0, n=n, bS=bS: e.matmul(
                        psum[bS][:, 0:n], lhsT=akT[wi][:, kb * 128:(kb + 1) * 128],
                        rhs=aqT[wi][hh][:, n0:n0 + n], start=True, stop=True),
                        R=[akT_b[wi][kb], aqT_b[wi][qp], aqz_b[wi][hh]], W=pb_(bS))
                    s.op("act", lambda e, ei=ei, bS=bS, n=n: e.activation(out=Et[ei][:, 0:n], in_=psum[bS][:, 0:n], func=AF.Exp),
                         R=pb_(bS), W=[Et_b[ei]])
                    mo = n0 - 128 * kb
                    s.op("dve", lambda e, ei=ei, mi=mi, mo=mo, n=n: e.tensor_tensor(
                        out=Pt[ei][:, 0:n], in0=Et[ei][:, 0:n], in1=mask_s[mi][:, mo:mo + n], op=ALU.mult),
                        R=[Et_b[ei], mask_b[mi]], W=[Pt_b[ei]])
                    if h + 2 < 8 and (idx + 1 == len(steps) or steps[idx + 1][1] != h):
                        load_mask(h + 2)
                pidx = idx - LOOK
                if pidx >= 0:
                    hh, h, qp, kb, ki, nk = steps[pidx]
                    ei, n0, n = info[pidx]
                    pb = hh * 64
                    q0 = qp * TB
                    par = (h * NTB + qp) % 2
                    bO, bD = 4 + par, 6 + par
                    cs = slice(n0 - q0, n0 - q0 + n)
                    s.op("pe", lambda e, wi=wi, kb=kb, ei=ei, n=n, cs=cs, bO=bO, ki=ki, nk=nk: e.matmul(
                        psum[bO][:, cs], lhsT=Vp[wi][:, kb, :], rhs=Pt[ei][:, 0:n],
                        start=(ki == 0), stop=(ki == nk - 1), skip_group_check=True),
                        R=[Vp_b[wi][kb], Pt_b[ei]], W=pb_(bO))
                    s.op("pe", lambda e, ei=ei, n=n, cs=cs, bD=bD, ki=ki, nk=nk: e.matmul(
                        psum[bD][:, cs], lhsT=onesb[:], rhs=Pt[ei][:, 0:n],
                        start=(ki == 0), stop=(ki == nk - 1), skip_group_check=True),
                        R=[onesb_b, Pt_b[ei]], W=pb_(bD))
                    if ki == nk - 1:
                        ps_ = slice(pb, pb + 64)
                        s.op("act", lambda e, par=par, bD=bD, ps_=ps_: e.activation(out=dcp[par][ps_, :], in_=psum[bD][ps_, :], func=AF.Ln),
                             R=pb_(bD), W=[dcp_b[par]])
                        s.op("act", lambda e, par=par, ps_=ps_: e.activation(out=dcp[par][ps_, :], in_=dcp[par][ps_, :], func=AF.Exp, scale=-1.0),
                             R=[dcp_b[par]], W=[dcp_b[par]])
                        s.op("dve", lambda e, ch=ch, ps_=ps_, q0=q0, bO=bO, par=par: e.tensor_tensor(
                            out=o_aT[ps_, ch, q0:q0 + TB], in0=psum[bO][ps_, :], in1=dcp[par][ps_, :], op=ALU.mult),
                            R=pb_(bO) + [dcp_b[par]], W=[o_a_b[h][qp]])

        if stop == "C":
            return
        tap("o_aT", o_aT, [b for bb in o_a_b for b in bb])
        s.fence()
        A.release(mix_base)
        e_low_top = A.mark()
        mT = A.alloc([128, NC_, S], BF16, "mT")
        mT_b = [bufs(NTB) for _ in range(NC_)]
        e_base = A.mark()
        wE = [A.alloc([128, NC_, 256], BF16, "wgab") for _ in range(2)]
        wap = [A.alloc([128, 4, 128], BF16, "wap") for _ in range(2)]
        wgp = [A.alloc([128, 4, 128], BF16, "wgp") for _ in range(2)]
        wE_b = [bufs(3) for _ in range(2)]
        wE_sem = [[new_dsem("wE") for _ in range(3)] for _ in range(2)]
        sa = [A.alloc([128, TB], F32, "sa") for _ in range(2)]
        sbt = [A.alloc([128, TB], F32, "sbt") for _ in range(2)]
        sa_b, sbt_b = bufs(2), bufs(2)
        wo = A.alloc([128, NC_, NC_, 128], BF16, "wo")
        wo_b = bufs(NC_)
        e_top = A.mark()
        ecn = 0

        def load_wE(dc):
            wi = dc % 2
            s.op("pool", lambda e, wi=wi, dc=dc: e.dma_start(out=wE[wi][:], in_=wgab_d[dc].rearrange("p (k n) -> p k n", k=NC_)),
                 W=[wE_b[wi][0]], dsem=wE_sem[wi][0])
            s.op("pool", lambda e, wi=wi, dc=dc: e.dma_start(out=wap[wi][:], in_=wap_d[dc].rearrange("p (k n) -> p k n", k=4)),
                 W=[wE_b[wi][1]], dsem=wE_sem[wi][1])
            s.op("pool", lambda e, wi=wi, dc=dc: e.dma_start(out=wgp[wi][:], in_=wgp_d[dc].rearrange("p (k n) -> p k n", k=4)),
                 W=[wE_b[wi][2]], dsem=wE_sem[wi][2])

        load_wE(0)
        for dc in range(NC_):
            wi = dc % 2
            if dc + 1 < NC_:
                load_wE(dc + 1)
            if dc >= 4:
                for dco in (2 * (dc - 4), 2 * (dc - 4) + 1):
                    s.op("pool", lambda e, dco=dco: e.dma_start(out=wo[:, dco, :, :], in_=wo_d[dco].rearrange("p (k n) -> p k n", k=NC_)),
                         W=[wo_b[dco]], dsem=new_dsem("wo"))
            for tb in range(NTB):
                sl = slice(tb * TB, (tb + 1) * TB)
                pj = ecn % 2
                ecn += 1
                bGA, bGB, bPA, bPG = 0 + pj, 2 + pj, 4 + pj, 6 + pj
                for which, bi in ((0, bGA), (1, bGB)):
                    for k in range(NC_):
                        s.op("pe", lambda e, wi=wi, which=which, k=k, sl=sl, bi=bi: e.matmul(
                            psum[bi][:], lhsT=wE[wi][:, k, which * 128:(which + 1) * 128], rhs=xb[:, k, sl],
                            start=(k == 0), stop=(k == NC_ - 1)),
                            R=[wE_b[wi][0], xb_b[k][tb]], W=pb_(bi))
                for c in range(4):
                    s.op("pe", lambda e, wi=wi, c=c, sl=sl, bPA=bPA: e.matmul(
                        psum[bPA][:], lhsT=wap[wi][:, c, :], rhs=o_aT[:, c, sl], start=(c == 0), stop=(c == 3)),
                        R=[wE_b[wi][1], o_a_b[2 * c][tb], o_a_b[2 * c + 1][tb]], W=pb_(bPA))
                for c in range(4):
                    s.op("pe", lambda e, wi=wi, c=c, sl=sl, bPG=bPG: e.matmul(
                        psum[bPG][:], lhsT=wgp[wi][:, c, :], rhs=o_gnT[:, c, sl], start=(c == 0), stop=(c == 3)),
                        R=[wE_b[wi][2], o_gn_b[c][tb]], W=pb_(bPG))
                s.op("act", lambda e, pj=pj, bGA=bGA: e.activation(out=sa[pj][:], in_=psum[bGA][:], func=AF.Sigmoid),
                     R=pb_(bGA), W=[sa_b[pj]])
                s.op("act", lambda e, pj=pj, bGB=bGB: e.activation(out=sbt[pj][:], in_=psum[bGB][:], func=AF.Sigmoid),
                     R=pb_(bGB), W=[sbt_b[pj]])
                s.op("dve", lambda e, pj=pj, bPA=bPA: e.tensor_tensor(out=sa[pj][:], in0=psum[bPA][:], in1=sa[pj][:], op=ALU.mult),
                     R=pb_(bPA) + [sa_b[pj]], W=[sa_b[pj]])
                s.op("dve", lambda e, pj=pj, bPG=bPG: e.tensor_tensor(out=sbt[pj][:], in0=psum[bPG][:], in1=sbt[pj][:], op=ALU.mult),
                     R=pb_(bPG) + [sbt_b[pj]], W=[sbt_b[pj]])
                s.op("pool", lambda e, pj=pj, dc=dc, sl=sl: e.tensor_tensor(out=mT[:, dc, sl], in0=sa[pj][:], in1=sbt[pj][:], op=ALU.add),
                     R=[sa_b[pj], sbt_b[pj]], W=[mT_b[dc][tb]])
        tap("mT", mT, [b for bb in mT_b for b in bb])
        s.fence()
        A.release(e_top)
        Alow = Arena(nc, arena_base, e_low_top)
        ln_chunk, ln_flush = make_ln(l, 1, [Alow, A])
        yc = 0
        for tb in range(NTB):
            sl = slice(tb * TB, (tb + 1) * TB)
            for dc in range(NC_):
                bi = 4 + yc % 2
                yc += 1
                for k in range(NC_):
                    s.op("pe", lambda e, dc=dc, k=k, sl=sl, bi=bi: e.matmul(
                        psum[bi][:], lhsT=wo[:, dc, k, :], rhs=mT[:, k, sl], start=(k == 0), stop=(k == NC_ - 1)),
                        R=[wo_b[dc], mT_b[k][tb]], W=pb_(bi))
                s.op("dve", lambda e, bi=bi, dc=dc, sl=sl: e.tensor_tensor(
                    out=xs[:, dc, sl], in0=psum[bi][:], in1=xs[:, dc, sl], op=ALU.add),
                    R=pb_(bi) + [xs_b[dc][tb]], W=[xs_b[dc][tb]])
                ln_chunk(dc, tb)
        ln_flush()

    amask_blocks = _mask_blocks()
    for ph in phases:
        if ph[0] == "ffn":
            ffn(ph[1], ph[2])
        elif ph[0] == "mix":
            mixer(ph[1], ph[2] if len(ph) > 2 else None)
        else:
            raise ValueError(ph)

    for c in range(NC_):
        for t in range(NTB):
            sl = slice(t * TB, (t + 1) * TB)
            s.op("dve", lambda e, c=c, sl=sl: e.tensor_scalar_mul(out=xs[:, c, sl], in0=xs[:, c, sl], scalar1=1.0 / ALPHA),
                 R=[xs_b[c][t]], W=[xs_b[c][t]])
    last = None
    for c in range(NC_):
        last = s.op("sp", lambda e, c=c: e.dma_start(out=yT_d[c * 128:(c + 1) * 128, :], in_=xs[:, c, :]),
                    R=xs_b[c], dsem=out_sem)
    fin = Buf()
    fin.w = last
    s.op("sp", lambda e: e.nop(), R=[fin])

    s.finalize()
    from contextlib import ExitStack
    with ExitStack() as ctx:
        esem = {}
        for en in Sched.ENGS:
            esem[en] = ctx.enter_context(nc.semaphore(f"sem_{en}"))
        dsems = {}
        for nm in dsem_names:
            dsems[nm] = ctx.enter_context(nc.semaphore(f"d_{nm}"))
        with nc.Block() as block:
            @block.tensor
            def _(e):
                s.replay("pe", e, esem, dsems)

            @block.scalar
            def _(e):
                s.replay("act", e, esem, dsems)

            @block.vector
            def _(e):
                s.replay("dve", e, esem, dsems)

            @block.gpsimd
            def _(e):
                s.replay("pool", e, esem, dsems)

            @block.sync
            def _(e):
                s.replay("sp", e, esem, dsems)
    return nc


_MASK = None


def _alibi_mask():
    global _MASK
    if _MASK is None:
        d = np.arange(S)[None, :] - np.arange(128)[:, None]
        mult = ((d <= 128).astype(np.float64) + ((d % 4 == 0) & (d <= 512)) + ((d % 16 == 0) & (d <= 2048)))
        mult = np.where(d >= 0, mult, 0.0)
        slopes = np.exp2(-8.0 * np.arange(1, 9) / 8.0)
        m = mult[None] * np.exp(-slopes[:, None, None] * np.maximum(d, 0)[None])
        m = np.where(m < 1e-37, 0.0, m)
        _MASK = np.ascontiguousarray(m.astype(np.float32))
    return _MASK


def _mask_blocks():
    m = _alibi_mask()
    blk = [[[False] * NTB for _ in range(16)] for _ in range(8)]
    for h in range(8):
        for kb in range(16):
            for qp in range(NTB):
                n0 = max(qp * TB, 128 * kb)
                n1 = qp * TB + TB
                if n1 <= n0:
                    continue
                blk[h][kb][qp] = bool(m[h][:, n0 - 128 * kb:n1 - 128 * kb].any())
    return blk


def _consts():
    s_ = np.arange(128)[:, None]
    t_ = np.arange(128)[None, :]
    same = (s_ // 64) == (t_ // 64)
    U = np.where(same & (s_ <= t_), -1.0 / 16.0, 0.0)
    L = np.where(same & (s_ > t_), -1.0 / 16.0, 0.0)
    M = np.where(same & (s_ <= t_), 1.0, 0.0)
    o1 = np.full((128, 128), 1.0 / D)
    o2 = np.full((128, 128), 1.0 / 128.0)
    return np.ascontiguousarray(np.concatenate([U, L, M, o1, o2], axis=1).astype(np.float32))


def _lay_w13(w):
    return np.ascontiguousarray(w.reshape(NC_, 128, NF, 128).transpose(2, 1, 0, 3).reshape(NF, 128, D))


def _lay_ln(v):
    return np.ascontiguousarray(v.reshape(DEPTH, 3, NC_, 128).transpose(3, 0, 1, 2).reshape(128, NL3))


def _lay_cols(w):
    n = w.shape[1]
    return np.ascontiguousarray(w.reshape(NC_, 128, n).transpose(1, 0, 2).reshape(128, NC_ * n))


def make_inputs(phases, inp):
    m = {"ln_g": _lay_ln(inp["ln_g"]), "ln_b": _lay_ln(inp["ln_b"]), "consts": _consts()}
    has_mix = any(p[0] == "mix" for p in phases)
    if has_mix:
        m["amask"] = _alibi_mask()
        m["wgu"] = np.ascontiguousarray(inp["w_gate_up"].transpose(1, 0, 2))
        m["bgu"] = np.ascontiguousarray(np.broadcast_to(inp["b_gate_up"][None], (128, DEPTH, 256)))
        m["gng"] = np.ascontiguousarray(inp["gla_norm_g"].reshape(DEPTH, 4, 128).transpose(2, 0, 1).reshape(128, DEPTH * 4))
        m["gnb"] = np.ascontiguousarray(inp["gla_norm_b"].reshape(DEPTH, 4, 128).transpose(2, 0, 1).reshape(128, DEPTH * 4))
    for ph in phases:
        if ph[0] == "ffn":
            l, i = ph[1], ph[2]
            pre = "ffn1" if i == 0 else "ffn2"
            m[f"f{i}w1_{l}"] = _lay_w13(inp[pre + "_w1"][l])
            m[f"f{i}w3_{l}"] = _lay_w13(inp[pre + "_w3"][l])
            m[f"f{i}w2_{l}"] = np.ascontiguousarray(inp[pre + "_w2"][l])
        else:
            l = ph[1]
            w = inp["w_in"][l]
            m[f"wglr_{l}"] = _lay_cols(w[:, O_GLR:O_GLR + 16])
            m[f"wgqk_{l}"] = _lay_cols(w[:, O_GQ:O_GQ + 512])
            m[f"wgkv_{l}"] = _lay_cols(w[:, O_GK:O_GK + 768])
            m[f"wgr_{l}"] = _lay_cols(w[:, O_GR:O_GR + 512])
            m[f"waqkv_{l}"] = np.stack([_lay_cols(np.concatenate(
                [w[:, O_AQ + c * 128:O_AQ + (c + 1) * 128], w[:, O_AK + c * 128:O_AK + (c + 1) * 128],
                 w[:, O_AV + c * 128:O_AV + (c + 1) * 128]], axis=1)) for c in range(4)], axis=0)
            m[f"wgab_{l}"] = np.stack([_lay_cols(np.concatenate(
                [w[:, O_GA + c * 128:O_GA + (c + 1) * 128], w[:, O_GB + c * 128:O_GB + (c + 1) * 128]], axis=1))
                for c in range(NC_)], axis=0)
            wa = inp["w_attn_proj"][l]
            m[f"wap_{l}"] = np.ascontiguousarray(wa.reshape(4, 128, NC_, 128).transpose(2, 1, 0, 3).reshape(NC_, 128, 4 * 128))
            wg = inp["w_gla_proj"][l]
            m[f"wgp_{l}"] = np.ascontiguousarray(wg.reshape(4, 128, NC_, 128).transpose(2, 1, 0, 3).reshape(NC_, 128, 4 * 128))
            wo_ = inp["w_out"][l]
            m[f"wo_{l}"] = np.ascontiguousarray(wo_.reshape(NC_, 128, NC_, 128).transpose(2, 1, 0, 3).reshape(NC_, 128, NC_ * 128))
    return m


def run_phases(phases, x, inp, n_cores=8, trace=False, debug=None):
    nc = build(phases, debug)
    shared = make_inputs(phases, inp)
    in_maps = []
    for b in range(n_cores):
        d = dict(shared)
        d["xT"] = np.ascontiguousarray(x[b].T)
        in_maps.append(d)
    res = run_bass_kernel_spmd(nc, in_maps, core_ids=list(range(n_cores)), trace=trace)
    out = np.stack([np.ascontiguousarray(r["yT"].T) for r in res.results], axis=0)
    return out, res


LAUNCHES = [[("ffn", 0, 0), ("mix", 0), ("ffn", 0, 1), ("ffn", 1, 0), ("mix", 1), ("ffn", 1, 1)]]


def kernel(**inputs):
    inp = {k: np.asarray(v) for k, v in inputs.items()}
    x = np.ascontiguousarray(inp["x"], dtype=np.float32)
    for phases in LAUNCHES:
        x, _ = run_phases(phases, x, inp)
    return np.ascontiguousarray(x, dtype=np.float32)
```

```python
import numpy as np
import concourse.bass as bass
import concourse.mybir as mybir
from concourse.bass_utils import run_bass_kernel_spmd

F32 = mybir.dt.float32
F32R = mybir.dt.float32r

BF16 = mybir.dt.bfloat16
AF = mybir.ActivationFunctionType
ALU = mybir.AluOpType

S = 2048
D = 1024
DFF = 2816
NC_ = 8
NTB = 4
TB = 512
NF = 22
DEPTH = 2
ALPHA = float((2 * DEPTH) ** 0.25)
LN_EPS = 1e-5
FFN_GROUPS = [4, 4, 4, 4, 3, 3]
GMAX = 4
NL3 = DEPTH * 3 * NC_
SB_BASE = 16512
SB_TOP = 229344

O_AQ, O_AK, O_AV = 0, 512, 1024
O_GQ, O_GK, O_GV, O_GLR, O_GR = 1536, 1792, 2048, 2560, 2576
O_GA, O_GB = 3088, 4112
N_IN = 5136


class Buf:
    __slots__ = ("name", "w", "r", "excl")

    def __init__(self, name="", excl=False):
        self.name = name
        self.w = None
        self.r = {}
        self.excl = excl


def bufs(n):
    return [Buf() for _ in range(n)]


class Op:
    __slots__ = ("eng", "fn", "deps", "needed", "semval", "dsem", "dval")

    def __init__(self, eng, fn, deps, dsem):
        self.eng = eng
        self.fn = fn
        self.deps = deps
        self.needed = False
        self.semval = None
        self.dsem = dsem
        self.dval = None


class Sched:
    ENGS = ("pe", "act", "dve", "pool", "sp")

    def __init__(self):
        self.q = {e: [] for e in self.ENGS}
        self.dma_count = {}
        self.last_dma = {}
        self.extra = {e: [] for e in self.ENGS}

    def op(self, eng, fn, R=(), W=(), dsem=None, after=()):
        deps = [(3, a) for a in after if a is not None]
        if any(b.excl for b in R):
            W = list(W) + [b for b in R if b.excl]
            R = [b for b in R if not b.excl]
        W = list(dict.fromkeys(W))
        for b in R:
            if b.w is not None:
                deps.append((0, b.w))
        for b in W:
            if b.w is not None:
                deps.append((1, b.w))
            for r in b.r.values():
                deps.append((2, r))
        if self.extra[eng]:
            deps.extend((0, d) for d in self.extra[eng])
            self.extra[eng] = []
        o = Op(eng, fn, deps, dsem)
        if dsem is not None:
            self.dma_count[dsem] = self.dma_count.get(dsem, 0) + 16
            o.dval = self.dma_count[dsem]
            self.last_dma[dsem] = o
        self.q[eng].append(o)
        key = dsem if dsem is not None else eng
        for b in R:
            b.r[key] = o
        for b in W:
            b.w = o
            b.r = {}
        return o

    def fence(self):
        snap = []
        for e in self.ENGS:
            for o in reversed(self.q[e]):
                if o.dsem is None:
                    snap.append(o)
                    break
        snap.extend(self.last_dma.values())
        for e in self.ENGS:
            self.extra[e] = list(snap)

    def finalize(self):
        for eng in self.ENGS:
            for o in self.q[eng]:
                keep = []
                for kind, d in o.deps:
                    if d is o:
                        continue
                    if d.dsem is not None:
                        keep.append(d)
                    elif d.eng == o.eng:
                        if o.eng == "pe" and kind != 3:
                            continue
                        keep.append(d)
                    else:
                        keep.append(d)
                for d in keep:
                    if d.dsem is None:
                        d.needed = True
                o.deps = keep
        for eng in self.ENGS:
            c = 0
            for o in self.q[eng]:
                if o.dsem is None and o.needed:
                    c += 1
                    o.semval = c

    def replay(self, eng, e, esem, dsems):
        seen = {}
        for o in self.q[eng]:
            for d in o.deps:
                if d.dsem is not None:
                    key, val, sem = ("d", d.dsem), d.dval, dsems[d.dsem]
                else:
                    key, val, sem = ("e", d.eng), d.semval, esem[d.eng]
                if seen.get(key, 0) >= val:
                    continue
                seen[key] = val
                e.wait_ge(sem, val)
            ins = o.fn(e)
            if o.dsem is not None:
                ins.then_inc(dsems[o.dsem], 16)
            elif o.needed:
                ins.then_inc(esem[eng], 1)


DT_SIZE = {F32: 4, BF16: 2, F32R: 4}


class Arena:
    UID = 0

    def __init__(self, nc, base, top):
        self.nc = nc
        self.base = base
        self.top = top
        self.off = base
        self.uid = 0
        self.peak = base

    def alloc(self, shape, dt, name="t"):
        n = 1
        for d in shape[1:]:
            n *= d
        nbytes = (n * DT_SIZE[dt] + 63) // 64 * 64
        if self.off + nbytes > self.top:
            raise RuntimeError(f"arena overflow allocating {name} {shape}: off={self.off - self.base} need {nbytes} cap {self.top - self.base}")
        Arena.UID += 1
        t = self.nc.alloc_sbuf_tensor_at(f"{name}_{Arena.UID}", list(shape), dt, offset=self.off)
        self.off += nbytes
        self.peak = max(self.peak, self.off)
        return t

    def mark(self):
        return self.off

    def release(self, m):
        self.off = m


def build(phases, debug=None):
    nc = bass.Bass("TRN2", target_bir_lowering=False)
    s = Sched()
    dram = {}
    dsem_names = []

    def new_dsem(name):
        nm = f"{name}_{len(dsem_names)}"
        dsem_names.append(nm)
        return nm

    def din(name, shape, dt=F32):
        if name not in dram:
            dram[name] = nc.dram_tensor(name, list(shape), dt, kind="ExternalInput").ap()
        return dram[name]

    debug = debug or ()

    def tap(name, t, bl):
        if name in debug:
            dd = nc.dram_tensor("dbg_" + name, list(t.shape), t.dtype, kind="ExternalOutput").ap()
            s.op("sp", lambda e: e.dma_start(out=dd, in_=t[:]), R=bl, dsem=new_dsem("dbg"))

    xT_d = din("xT", [D, S])
    yT_d = nc.dram_tensor("yT", [D, S], F32, kind="ExternalOutput").ap()
    lng_d = din("ln_g", [128, NL3])
    lnb_d = din("ln_b", [128, NL3])
    consts_d = din("consts", [128, 5 * 128])
    has_mix = any(p[0] == "mix" for p in phases)

    A = Arena(nc, SB_BASE, SB_TOP)
    xs = A.alloc([128, NC_, S], F32, "xs")
    xb = A.alloc([128, NC_, S], BF16, "xb")
    xs_b = [bufs(NTB) for _ in range(NC_)]
    xb_b = [bufs(NTB) for _ in range(NC_)]
    lng = A.alloc([128, NL3], F32, "lng")
    lnb = A.alloc([128, NL3], F32, "lnb")
    lnga = A.alloc([128, NL3], F32, "lnga")
    lnba = A.alloc([128, NL3], F32, "lnba")
    ln_c = Buf()
    consts = A.alloc([128, 5 * 128], F32, "consts")
    consts_b = Buf()
    Umat = consts[:, 0:128]
    Lmat = consts[:, 128:256]
    Mblk = consts[:, 256:384]
    ones = consts[:, 384:512]
    gones = consts[:, 512:640]
    ones1 = A.alloc([128, 64], F32, "ones1")
    ones1_b = Buf()
    ones_r = A.alloc([128, 128], F32R, "ones_r")
    gones_r = A.alloc([128, 128], F32R, "gones_r")
    onesr_b = Buf()
    mixc_b = Buf()
    if has_mix:
        wgu = A.alloc([16, DEPTH, 256], F32, "wgu")
        bgu = A.alloc([128, DEPTH, 256], F32, "bgu")
        gng = A.alloc([128, DEPTH * 4], F32, "gng")
        gnb = A.alloc([128, DEPTH * 4], F32, "gnb")
    arena_base = A.mark()

    psum = [nc.alloc_psum_tensor(f"bank{i}", [128, TB], F32) for i in range(8)]
    pq = [[Buf(f"bank{i}", excl=True)] * 4 for i in range(8)]

    def pb_(i, c0=0, c1=TB):
        return pq[i][c0 // 128:(c1 + 127) // 128]

    out_sem = new_dsem("out")

    lng_b0, lnb_b0 = Buf(), Buf()
    s.op("sp", lambda e: e.dma_start(out=lng[:], in_=lng_d), W=[lng_b0], dsem=new_dsem("io"))
    s.op("sp", lambda e: e.dma_start(out=lnb[:], in_=lnb_d), W=[lnb_b0], dsem=new_dsem("io"))
    s.op("sp", lambda e: e.dma_start(out=consts[:], in_=consts_d), W=[consts_b], dsem=new_dsem("io"))
    if has_mix:
        wgu_d = din("wgu", [16, DEPTH, 256])
        bgu_d = din("bgu", [128, DEPTH, 256])
        gng_d = din("gng", [128, DEPTH * 4])
        gnb_d = din("gnb", [128, DEPTH * 4])
        mb = bufs(4)
        s.op("sp", lambda e: e.dma_start(out=wgu[:], in_=wgu_d), W=[mb[0]], dsem=new_dsem("io"))
        s.op("sp", lambda e: e.dma_start(out=bgu[:], in_=bgu_d), W=[mb[1]], dsem=new_dsem("io"))
        s.op("sp", lambda e: e.dma_start(out=gng[:], in_=gng_d), W=[mb[2]], dsem=new_dsem("io"))
        s.op("sp", lambda e: e.dma_start(out=gnb[:], in_=gnb_d), W=[mb[3]], dsem=new_dsem("io"))
        s.op("dve", lambda e: e.memset(ones1[:], 1.0), R=mb, W=[ones1_b, mixc_b])
    for c in range(NC_):
        s.op("sp", lambda e, c=c: e.dma_start(out=xs[:, c, :], in_=xT_d[c * 128:(c + 1) * 128, :]),
             W=xs_b[c], dsem=new_dsem("iox"))
    s.op("act", lambda e: e.copy(out=ones_r[:], in_=ones), R=[consts_b], W=[onesr_b])
    s.op("act", lambda e: e.copy(out=gones_r[:], in_=gones), R=[consts_b], W=[onesr_b])
    s.op("act", lambda e: e.mul(lnga[:], lng[:], ALPHA), R=[lng_b0], W=[ln_c])
    s.op("act", lambda e: e.mul(lnba[:], lnb[:], ALPHA), R=[lnb_b0], W=[ln_c])
    for c in range(NC_):
        for t in range(NTB):
            sl = slice(t * TB, (t + 1) * TB)
            s.op("dve", lambda e, c=c, sl=sl: e.tensor_copy(out=xb[:, c, sl], in_=xs[:, c, sl]),
                 R=[xs_b[c][t]], W=[xb_b[c][t]])
            s.op("act", lambda e, c=c, sl=sl: e.mul(xs[:, c, sl], xs[:, c, sl], ALPHA),
                 R=[xs_b[c][t]], W=[xs_b[c][t]])

    def layer_norm(l, i):
        col0 = (l * 3 + i) * NC_
        s.fence()
        A.release(arena_base)
        sq = A.alloc([128, NC_, TB], F32, "sq")
        sq_b = bufs(NC_)
        mean_sb = [A.alloc([128, TB], F32, "mean") for _ in range(2)]
        m2_sb = [A.alloc([128, TB], F32, "m2") for _ in range(2)]
        rstd_sb = [A.alloc([128, TB], F32, "rstd") for _ in range(2)]
        mean_b, m2_b, rstd_b = bufs(2), bufs(2), bufs(2)
        t1 = [A.alloc([128, TB], F32, "t1") for _ in range(2)]
        t2 = [A.alloc([128, TB], F32, "t2") for _ in range(3)]
        t1_b, t2_b = bufs(2), bufs(3)
        cn = {"t1": 0, "t2": 0}

        def stats(t):
            sl = slice(t * TB, (t + 1) * TB)
            p = t % 2
            bm, bq = (6, 7) if p == 0 else (4, 5)
            pm, pq_ = psum[bm], psum[bq]
            for c in range(NC_):
                s.op("act", lambda e, c=c, sl=sl: e.activation(out=sq[:, c, :], in_=xs[:, c, sl], func=AF.Square),
                     R=[xs_b[c][t]], W=[sq_b[c]])
            for c in range(NC_):
                s.op("pe", lambda e, c=c, sl=sl, pm=pm: e.matmul(pm[:], lhsT=ones, rhs=xs[:, c, sl],
                                                                 start=(c == 0), stop=(c == NC_ - 1)),
                     R=[consts_b, xs_b[c][t]], W=pb_(bm))
            for c in range(NC_):
                s.op("pe", lambda e, c=c, pq_=pq_: e.matmul(pq_[:], lhsT=ones, rhs=sq[:, c, :],
                                                            start=(c == 0), stop=(c == NC_ - 1)),
                     R=[consts_b, sq_b[c]], W=pb_(bq))
            s.op("act", lambda e, pm=pm, p=p: e.activation(out=m2_sb[p][:], in_=pm[:], func=AF.Square), R=pb_(bm), W=[m2_b[p]])
            s.op("act", lambda e, pm=pm, p=p: e.copy(out=mean_sb[p][:], in_=pm[:]), R=pb_(bm), W=[mean_b[p]])
            s.op("dve", lambda e, pq_=pq_, p=p: e.tensor_tensor(out=rstd_sb[p][:], in0=pq_[:], in1=m2_sb[p][:], op=ALU.subtract),
                 R=pb_(bq) + [m2_b[p]], W=[rstd_b[p]])
            s.op("act", lambda e, p=p: e.activation(out=m2_sb[p][:], in_=rstd_sb[p][:], func=AF.Ln, bias=LN_EPS, scale=1.0),
                 R=[rstd_b[p]], W=[m2_b[p]])
            s.op("act", lambda e, p=p: e.activation(out=rstd_sb[p][:], in_=m2_sb[p][:], func=AF.Exp, scale=-0.5),
                 R=[m2_b[p]], W=[rstd_b[p]])

        def norm(t):
            sl = slice(t * TB, (t + 1) * TB)
            p = t % 2
            for c in range(NC_):
                j = cn["t1"] % 2
                cn["t1"] += 1
                j2 = cn["t2"] % 3
                cn["t2"] += 1
                s.op("dve", lambda e, c=c, sl=sl, j=j, p=p: e.tensor_tensor(out=t1[j][:], in0=xs[:, c, sl], in1=mean_sb[p][:], op=ALU.subtract),
                     R=[xs_b[c][t], mean_b[p]], W=[t1_b[j]])
                s.op("pool", lambda e, j=j, j2=j2, p=p: e.tensor_tensor(out=t2[j2][:], in0=t1[j][:], in1=rstd_sb[p][:], op=ALU.mult),
                     R=[t1_b[j], rstd_b[p]], W=[t2_b[j2]])
                s.op("act", lambda e, c=c, sl=sl, j2=j2: e.activation(out=xs[:, c, sl], in_=t2[j2][:], func=AF.Identity,
                                                                   scale=lnga[:, col0 + c:col0 + c + 1],
                                                                   bias=lnba[:, col0 + c:col0 + c + 1]),
                     R=[t2_b[j2], ln_c], W=[xs_b[c][t]])
                s.op("dve", lambda e, c=c, sl=sl, j2=j2: e.tensor_scalar(
                    out=xb[:, c, sl], in0=t2[j2][:], scalar1=lng[:, col0 + c:col0 + c + 1], scalar2=lnb[:, col0 + c:col0 + c + 1],
                    op0=ALU.mult, op1=ALU.add),
                    R=[t2_b[j2], ln_c], W=[xb_b[c][t]])

        stats(0)
        for t in range(NTB):
            if t + 1 < NTB:
                stats(t + 1)
            norm(t)

    def make_ln(l, i, arenas, sbanks=((0, 1), (2, 3)), lag=2):
        col0 = (l * 3 + i) * NC_

        def al(shape, dt, name):
            for a in arenas:
                n = 1
                for d in shape[1:]:
                    n *= d
                if a.off + (n * DT_SIZE[dt] + 63) // 64 * 64 <= a.top:
                    return a.alloc(shape, dt, name)
            raise RuntimeError("make_ln: no room for " + name)

        NSQ = 3
        sqr = [al([128, TB], F32R, "lsq") for _ in range(NSQ)]
        sqr_b = bufs(NSQ)
        mean_sb = [al([128, TB], F32, "lmean") for _ in range(2)]
        m2_sb = [al([128, TB], F32, "lm2") for _ in range(2)]
        rstd_sb = [al([128, TB], F32, "lrstd") for _ in range(2)]
        mean_b, m2_b, rstd_b = bufs(2), bufs(2), bufs(2)
        NT = 3
        t1 = [al([128, TB], F32, "lt1") for _ in range(NT)]
        t2 = [al([128, TB], F32, "lt2") for _ in range(NT)]
        t1_b, t2_b = bufs(NT), bufs(NT)
        cn = {"sq": 0, "t1": 0, "t2": 0, "seen": {}}
        pending = []
        avail = []
        fl = {"s1": None, "s2": None}

        def tick():
            if fl["s2"] is not None:
                c, t, j2 = fl["s2"]
                sl = slice(t * TB, (t + 1) * TB)
                s.op("act", lambda e, c=c, sl=sl, j2=j2: e.activation(out=xs[:, c, sl], in_=t2[j2][:], func=AF.Identity,
                                                                   scale=lnga[:, col0 + c:col0 + c + 1],
                                                                   bias=lnba[:, col0 + c:col0 + c + 1]),
                     R=[t2_b[j2], ln_c], W=[xs_b[c][t]])
                s.op("dve", lambda e, c=c, sl=sl, j2=j2: e.tensor_scalar(
                    out=xb[:, c, sl], in0=t2[j2][:], scalar1=lng[:, col0 + c:col0 + c + 1], scalar2=lnb[:, col0 + c:col0 + c + 1],
                    op0=ALU.mult, op1=ALU.add),
                    R=[t2_b[j2], ln_c], W=[xb_b[c][t]])
                fl["s2"] = None
            if fl["s1"] is not None:
                c, t, j = fl["s1"]
                p = t % 2
                j2 = cn["t2"] % NT
                cn["t2"] += 1
                s.op("pool", lambda e, j=j, j2=j2, p=p: e.tensor_tensor(out=t2[j2][:], in0=t1[j][:], in1=rstd_sb[p][:], op=ALU.mult),
                     R=[t1_b[j], rstd_b[p]], W=[t2_b[j2]])
                fl["s2"] = (c, t, j2)
                fl["s1"] = None
            if avail:
                c, t = avail.pop(0)
                sl = slice(t * TB, (t + 1) * TB)
                p = t % 2
                j = cn["t1"] % NT
                cn["t1"] += 1
                s.op("dve", lambda e, c=c, sl=sl, j=j, p=p: e.tensor_tensor(out=t1[j][:], in0=xs[:, c, sl], in1=mean_sb[p][:], op=ALU.subtract),
                     R=[xs_b[c][t], mean_b[p]], W=[t1_b[j]])
                fl["s1"] = (c, t, j)

        def emit(entry):
            dc, t, k = entry
            sl = slice(t * TB, (t + 1) * TB)
            p = t % 2
            bm, bq = sbanks[p]
            n = cn["seen"].get(t, 0)
            cn["seen"][t] = n + 1
            s.op("pe", lambda e, dc=dc, sl=sl, bm=bm, n=n: e.matmul(psum[bm][:], lhsT=ones, rhs=xs[:, dc, sl],
                                                                   start=(n == 0), stop=(n == NC_ - 1)),
                 R=[consts_b, xs_b[dc][t]], W=pb_(bm))
            s.op("pe", lambda e, k=k, bq=bq, n=n: e.matmul(psum[bq][:], lhsT=ones_r[:], rhs=sqr[k][:],
                                                           start=(n == 0), stop=(n == NC_ - 1)),
                 R=[onesr_b, sqr_b[k]], W=pb_(bq))
            if n == NC_ - 1:
                s.op("act", lambda e, bm=bm, p=p: e.activation(out=m2_sb[p][:], in_=psum[bm][:], func=AF.Square), R=pb_(bm), W=[m2_b[p]])
                s.op("act", lambda e, bm=bm, p=p: e.copy(out=mean_sb[p][:], in_=psum[bm][:]), R=pb_(bm), W=[mean_b[p]])
                s.op("dve", lambda e, bq=bq, p=p: e.tensor_tensor(out=rstd_sb[p][:], in0=psum[bq][:], in1=m2_sb[p][:], op=ALU.subtract),
                     R=pb_(bq) + [m2_b[p]], W=[rstd_b[p]])
                s.op("act", lambda e, p=p: e.activation(out=m2_sb[p][:], in_=rstd_sb[p][:], func=AF.Ln, bias=LN_EPS, scale=1.0),
                     R=[rstd_b[p]], W=[m2_b[p]])
                s.op("act", lambda e, p=p: e.activation(out=rstd_sb[p][:], in_=m2_sb[p][:], func=AF.Exp, scale=-0.5),
                     R=[m2_b[p]], W=[rstd_b[p]])
                avail.extend((c, t) for c in range(NC_))

        def chunk_done(dc, t):
            sl = slice(t * TB, (t + 1) * TB)
            k = cn["sq"] % NSQ
            cn["sq"] += 1
            s.op("act", lambda e, dc=dc, sl=sl, k=k: e.activation(out=sqr[k][:], in_=xs[:, dc, sl], func=AF.Square),
                 R=[xs_b[dc][t]], W=[sqr_b[k]])
            pending.append((dc, t, k))
            if len(pending) > lag:
                emit(pending.pop(0))
            tick()

        def flush():
            while pending:
                emit(pending.pop(0))
                tick()
            while avail or fl["s1"] is not None or fl["s2"] is not None:
                tick()

        return chunk_done, flush

    def ffn(l, i):
        w1_d = din(f"f{i}w1_{l}", [NF, 128, D])
        w3_d = din(f"f{i}w3_{l}", [NF, 128, D])
        w2_d = din(f"f{i}w2_{l}", [DFF, D])
        s.fence()
        A.release(arena_base)
        W13_SLOTS = 3
        w13 = [A.alloc([128, 2, D], BF16, "w13") for _ in range(W13_SLOTS)]
        w13_b = [bufs(2) for _ in range(W13_SLOTS)]
        w13_sem = [[new_dsem("w13") for _ in range(2)] for _ in range(W13_SLOTS)]
        w2 = [A.alloc([128, GMAX, D], BF16, "w2") for _ in range(2)]
        w2_b = bufs(2)
        w2_sem = [new_dsem("w2") for _ in range(2)]
        gT = [A.alloc([128, GMAX, S], BF16, "gT") for _ in range(2)]
        gT_b = [[bufs(NTB) for _ in range(GMAX)] for _ in range(2)]
        silu_t = [A.alloc([128, TB], F32, "silu") for _ in range(2)]
        silu_b = bufs(2)
        ln_chunk, ln_flush = make_ln(l, 0 if i == 0 else 2, [A])
        cnt = {"w13": 0, "w2": 0, "psA": 0, "psY": 0, "silu": 0}
        m0 = 0
        for gi, G in enumerate(FFN_GROUPS):
            ms = list(range(m0, m0 + G))
            m0 += G
            gs = gi % 2
            ws = cnt["w2"] % 2
            cnt["w2"] += 1
            s.op("pool", lambda e, ws=ws, ms=ms, G=G: e.dma_start(
                out=w2[ws][:, 0:G, :],
                in_=w2_d[ms[0] * 128:(ms[0] + G) * 128, :].rearrange("(g p) n -> p g n", p=128)),
                W=[w2_b[ws]], dsem=w2_sem[ws])
            for ml, m in enumerate(ms):
                slot = cnt["w13"] % W13_SLOTS
                cnt["w13"] += 1
                s.op("pool", lambda e, slot=slot, m=m: e.dma_start(out=w13[slot][:, 0, :], in_=w1_d[m]),
                     W=[w13_b[slot][0]], dsem=w13_sem[slot][0])
                s.op("pool", lambda e, slot=slot, m=m: e.dma_start(out=w13[slot][:, 1, :], in_=w3_d[m]),
                     W=[w13_b[slot][1]], dsem=w13_sem[slot][1])
                for t in range(NTB):
                    sl = slice(t * TB, (t + 1) * TB)
                    pj = cnt["psA"] % 2
                    cnt["psA"] += 1
                    for which, bi in ((0, pj), (1, 2 + pj)):
                        for k in range(NC_):
                            s.op("pe", lambda e, slot=slot, which=which, k=k, sl=sl, bi=bi: e.matmul(
                                psum[bi][:], lhsT=w13[slot][:, which, k * 128:(k + 1) * 128], rhs=xb[:, k, sl],
                                start=(k == 0), stop=(k == NC_ - 1)),
                                R=[w13_b[slot][which], xb_b[k][t]], W=pb_(bi))
                    sj = cnt["silu"] % 2
                    cnt["silu"] += 1
                    s.op("act", lambda e, pj=pj, sj=sj: e.activation(out=silu_t[sj][:], in_=psum[pj][:], func=AF.Silu),
                         R=pb_(pj), W=[silu_b[sj]])
                    s.op("dve", lambda e, pj=pj, sj=sj, gs=gs, ml=ml, sl=sl: e.tensor_tensor(
                        out=gT[gs][:, ml, sl], in0=psum[2 + pj][:], in1=silu_t[sj][:], op=ALU.mult),
                        R=pb_(2 + pj) + [silu_b[sj]], W=[gT_b[gs][ml][t]])
            last = (gi == len(FFN_GROUPS) - 1)
            order = [(dc, t) for t in range(NTB) for dc in range(NC_)] if last else [(dc, t) for dc in range(NC_) for t in range(NTB)]
            for dc, t in order:
                if True:
                    sl = slice(t * TB, (t + 1) * TB)
                    bi = 4 + cnt["psY"] % 2
                    cnt["psY"] += 1
                    for ml in range(G):
                        s.op("pe", lambda e, ws=ws, ml=ml, dc=dc, gs=gs, sl=sl, bi=bi, G=G: e.matmul(
                            psum[bi][:], lhsT=w2[ws][:, ml, dc * 128:(dc + 1) * 128], rhs=gT[gs][:, ml, sl],
                            start=(ml == 0), stop=(ml == G - 1)),
                            R=[w2_b[ws], gT_b[gs][ml][t]], W=pb_(bi))
                    s.op("dve", lambda e, bi=bi, dc=dc, sl=sl: e.scalar_tensor_tensor(
                        out=xs[:, dc, sl], in0=psum[bi][:], scalar=0.5, in1=xs[:, dc, sl],
                        op0=ALU.mult, op1=ALU.add),
                        R=pb_(bi) + [xs_b[dc][t]], W=[xs_b[dc][t]])
                    if last:
                        ln_chunk(dc, t)
        ln_flush()

    def mixer(l, stop=None):
        amask_d = din("amask", [8, 128, S])
        wglr_d = din(f"wglr_{l}", [128, NC_ * 16])
        wgqk_d = din(f"wgqk_{l}", [128, NC_ * 512])
        wgkv_d = din(f"wgkv_{l}", [128, NC_ * 768])
        wgr_d = din(f"wgr_{l}", [128, NC_ * 512])
        waqkv_d = din(f"waqkv_{l}", [4, 128, NC_ * 384])
        wgab_d = din(f"wgab_{l}", [NC_, 128, NC_ * 256])
        wap_d = din(f"wap_{l}", [NC_, 128, 4 * 128])
        wgp_d = din(f"wgp_{l}", [NC_, 128, 4 * 128])
        wo_d = din(f"wo_{l}", [NC_, 128, NC_ * 128])
        blk = amask_blocks
        s.fence()
        A.release(arena_base)
        o_gnT = A.alloc([128, 4, S], BF16, "o_gnT")
        o_gn_b = [bufs(NTB) for _ in range(4)]
        o_a_b = [bufs(NTB) for _ in range(8)]
        mix_base = A.mark()

        qT = A.alloc([128, 2, S], BF16, "qT")
        kT = A.alloc([128, 2, S], BF16, "kT")
        qT_b = [bufs(16) for _ in range(2)]
        kT_b = [bufs(16) for _ in range(2)]
        khat = A.alloc([128, 16, 256], BF16, "khat")
        khat_b = bufs(16)
        gv = A.alloc([128, 16, 512], BF16, "gv")
        gv_b = bufs(16)
        dec = A.alloc([128, 4, 32], F32, "dec")
        dec_b = bufs(NTB)
        gla_base = A.mark()
        wglr = A.alloc([128, NC_, 16], BF16, "wglr")
        wgqk = A.alloc([128, NC_, 512], BF16, "wgqk")
        wgkv = A.alloc([128, NC_, 768], BF16, "wgkv")
        wB_b = bufs(3)
        s.op("pool", lambda e: e.dma_start(out=wglr[:], in_=wglr_d.rearrange("p (k n) -> p k n", k=NC_)),
             W=[wB_b[0]], dsem=new_dsem("wB"))
        s.op("pool", lambda e: e.dma_start(out=wgqk[:], in_=wgqk_d.rearrange("p (k n) -> p k n", k=NC_)),
             W=[wB_b[1]], dsem=new_dsem("wB"))
        s.op("pool", lambda e: e.dma_start(out=wgkv[:], in_=wgkv_d.rearrange("p (k n) -> p k n", k=NC_)),
             W=[wB_b[2]], dsem=new_dsem("wB"))
        glrT = [A.alloc([16, TB], F32, "glrT") for _ in range(2)]
        glrT_b = bufs(2)
        z_sb = [A.alloc([128, 256], F32, "z") for _ in range(2)]
        z_b = bufs(2)
        la_sb = [A.alloc([128, 256], F32, "la") for _ in range(2)]
        la_b = bufs(2)
        Eb = A.alloc([128, 2, TB], F32, "Eb")
        Einv = A.alloc([128, 2, TB], F32, "Einv")
        E_b, Einv_b = bufs(2), bufs(2)
        Ft = A.alloc([128, 4, 256], F32, "Ft")
        F_b = bufs(4)
        zc = 0
        pc = 0
        for tb in range(NTB):
            sl = slice(tb * TB, (tb + 1) * TB)
            gj = tb % 2
            for k in range(NC_):
                s.op("pe", lambda e, k=k, sl=sl: e.matmul(psum[0][0:16, :], lhsT=wglr[:, k, :], rhs=xb[:, k, sl],
                                                         start=(k == 0), stop=(k == NC_ - 1)),
                     R=[wB_b[0], xb_b[k][tb]], W=pb_(0))
            s.op("act", lambda e, gj=gj: e.copy(out=glrT[gj][:], in_=psum[0][0:16, :]), R=pb_(0), W=[glrT_b[gj]])
            for jj in range(4):
                j = tb * 4 + jj
                zi = zc % 2
                zc += 1
                bz = 1 + zi
                s.op("pe", lambda e, gj=gj, jj=jj, bz=bz: e.matmul(
                    psum[bz][:, 0:256], lhsT=glrT[gj][:, jj * 128:(jj + 1) * 128], rhs=wgu[:, l, :],
                    start=True, stop=True),
                    R=[glrT_b[gj], mixc_b], W=pb_(bz, 0, 256))
                s.op("dve", lambda e, zi=zi, bz=bz: e.tensor_tensor(out=z_sb[zi][:], in0=psum[bz][:, 0:256], in1=bgu[:, l, :], op=ALU.add),
                     R=pb_(bz, 0, 256) + [mixc_b], W=[z_b[zi]])
                tsl = slice(j * 128, (j + 1) * 128)
                bi2 = 6 + pc % 2
                pc += 1
                for k in range(NC_):
                    s.op("pe", lambda e, k=k, tsl=tsl, bi2=bi2: e.matmul(
                        psum[bi2][:], lhsT=xb[:, k, tsl], rhs=wgkv[:, k, 256:768],
                        start=(k == 0), stop=(k == NC_ - 1)),
                        R=[wB_b[2], xb_b[k][tb]], W=pb_(bi2))
                s.op("dve", lambda e, j=j, bi2=bi2: e.tensor_copy(out=gv[:, j, :], in_=psum[bi2][:]),
                     R=pb_(bi2), W=[gv_b[j]])
                s.op("act", lambda e, zi=zi: e.activation(out=z_sb[zi][:], in_=z_sb[zi][:], func=AF.Exp, scale=-1.0),
                     R=[z_b[zi]], W=[z_b[zi]])
                s.op("act", lambda e, zi=zi: e.activation(out=la_sb[zi][:], in_=z_sb[zi][:], func=AF.Ln, bias=1.0, scale=1.0),
                     R=[z_b[zi]], W=[la_b[zi]])
                for ch in range(2):
                    s.op("pe", lambda e, zi=zi, ch=ch, jj=jj: e.matmul(
                        psum[3 + ch][:, jj * 128:(jj + 1) * 128], lhsT=la_sb[zi][:, ch * 128:(ch + 1) * 128], rhs=Umat,
                        start=True, stop=True),
                        R=[la_b[zi], consts_b], W=[pq[3 + ch][jj]])
                s.op("pe", lambda e, zi=zi: e.matmul(psum[5][:, 0:256], lhsT=Lmat, rhs=la_sb[zi][:], start=True, stop=True),
                     R=[la_b[zi], consts_b], W=pb_(5, 0, 256))
                s.op("act", lambda e, jj=jj: e.activation(out=Ft[:, jj, :], in_=psum[5][:, 0:256], func=AF.Exp),
                     R=pb_(5, 0, 256), W=[F_b[jj]])
            for ch in range(2):
                s.op("act", lambda e, ch=ch: e.activation(out=Eb[:, ch, :], in_=psum[3 + ch][:], func=AF.Exp),
                     R=pb_(3 + ch), W=[E_b[ch]])
                s.op("act", lambda e, ch=ch: e.activation(out=Einv[:, ch, :], in_=psum[3 + ch][:], func=AF.Exp, scale=-1.0),
                     R=pb_(3 + ch), W=[Einv_b[ch]])
            for dup in range(2):
                s.op("dve", lambda e, tb=tb, dup=dup: e.tensor_copy(
                    out=dec[:].rearrange("p (c d) n -> p c d n", d=2)[:, :, dup, tb * 8:(tb + 1) * 8], in_=Eb[:, :, 63::64]),
                    R=E_b, W=[dec_b[tb]])
            for m in range(4):
                ch = m % 2
                bi = 6 + pc % 2
                pc += 1
                for k in range(NC_):
                    s.op("pe", lambda e, m=m, k=k, sl=sl, bi=bi: e.matmul(
                        psum[bi][:], lhsT=wgqk[:, k, m * 128:(m + 1) * 128], rhs=xb[:, k, sl],
                        start=(k == 0), stop=(k == NC_ - 1)),
                        R=[wB_b[1], xb_b[k][tb]], W=pb_(bi))
                if m < 2:
                    s.op("dve", lambda e, ch=ch, sl=sl, bi=bi: e.scalar_tensor_tensor(
                        out=qT[:, ch, sl], in0=psum[bi][:], scalar=0.125, in1=Eb[:, ch, :], op0=ALU.mult, op1=ALU.mult),
                        R=pb_(bi) + [E_b[ch]], W=qT_b[ch][tb * 4:(tb + 1) * 4])
                else:
                    s.op("dve", lambda e, ch=ch, sl=sl, bi=bi: e.tensor_tensor(
                        out=kT[:, ch, sl], in0=psum[bi][:], in1=Einv[:, ch, :], op=ALU.mult),
                        R=pb_(bi) + [Einv_b[ch]], W=kT_b[ch][tb * 4:(tb + 1) * 4])
            for jj in range(4):
                j = tb * 4 + jj
                tsl = slice(j * 128, (j + 1) * 128)
                bi = 1 + jj % 2
                for k in range(NC_):
                    s.op("pe", lambda e, k=k, tsl=tsl, bi=bi: e.matmul(
                        psum[bi][:, 0:256], lhsT=xb[:, k, tsl], rhs=wgkv[:, k, 0:256],
                        start=(k == 0), stop=(k == NC_ - 1)),
                        R=[wB_b[2], xb_b[k][tb]], W=pb_(bi, 0, 256))
                s.op("dve", lambda e, j=j, jj=jj, bi=bi: e.tensor_tensor(
                    out=khat[:, j, :], in0=psum[bi][:, 0:256], in1=Ft[:, jj, :], op=ALU.mult),
                    R=pb_(bi, 0, 256) + [F_b[jj]], W=[khat_b[j]])

        if stop == "B":
            return
        tap("qT", qT, [b for bb in qT_b for b in bb])
        tap("kT", kT, [b for bb in kT_b for b in bb])
        tap("khat", khat, khat_b)
        tap("gv", gv, gv_b)
        tap("dec", dec, dec_b)
        s.fence()
        A.release(gla_base)
        wgr = A.alloc([128, NC_, 512], BF16, "wgr")
        wgr_b = Buf()
        s.op("pool", lambda e: e.dma_start(out=wgr[:], in_=wgr_d.rearrange("p (k n) -> p k n", k=NC_)),
             W=[wgr_b], dsem=new_dsem("wgr"))
        ograw = [A.alloc([128, 4, TB], F32, "ograw") for _ in range(2)]
        ograw_b = [[bufs(4) for _ in range(4)] for _ in range(2)]
        st_f = A.alloc([128, 4, 128], F32, "st_f")
        st_b = A.alloc([128, 4, 128], BF16, "st_b")
        stf_b, stb_b = bufs(4), bufs(4)
        ATt = [A.alloc([128, 4, 128], BF16, "AT") for _ in range(2)]
        AT_b = [bufs(4) for _ in range(2)]
        sg = [A.alloc([128, TB], F32, "sg") for _ in range(4)]
        sg_b = bufs(4)
        gsq = A.alloc([128, TB], F32R, "gsq")
        gsq_b = Buf()
        gm2 = A.alloc([128, TB], F32, "gm2")
        grs = A.alloc([128, TB], F32, "grs")
        gm2_b, grs_b = Buf(), Buf()
        s.op("dve", lambda e: e.memset(st_f[:], 0.0), W=stf_b)
        s.op("dve", lambda e: e.memset(st_b[:], 0.0), W=stb_b)
        bS0, bS1 = 4, 5
        HORD = (0, 2, 1, 3)

        def emit_AT_mm(j):
            bA = j % 2
            tok = slice(j * 128, (j + 1) * 128)
            prev = None
            for h in HORD:
                ch, pb = h // 2, (h % 2) * 64
                hs = slice(h * 128, (h + 1) * 128)
                prev = s.op("pe", lambda e, ch=ch, pb=pb, hs=hs, tok=tok, bA=bA: e.matmul(
                    psum[bA][:, hs], lhsT=kT[pb:pb + 64, ch, tok], rhs=qT[pb:pb + 64, ch, tok], start=True, stop=True),
                    R=[kT_b[ch][j], qT_b[ch][j]], W=[pq[bA][h]], after=[prev] if h == 1 else [])

        def emit_AT_mask(j):
            bA = j % 2
            aj = j % 2
            s.op("dve", lambda e, bA=bA, aj=aj: e.tensor_tensor(
                out=ATt[aj][:], in0=psum[bA][:].rearrange("p (h n) -> p h n", h=4),
                in1=Mblk.unsqueeze(1).broadcast_to([128, 4, 128]), op=ALU.mult),
                R=[pq[bA][0], consts_b], W=AT_b[aj])

        def emit_dS(j, half):
            bS = bS0 if half == 0 else bS1
            rows = slice(half * 64, half * 64 + 64)
            for h in range(4):
                ch = h // 2
                hs = slice(h * 128, (h + 1) * 128)
                s.op("pe", lambda e, ch=ch, hs=hs, rows=rows, bS=bS, j=j: e.matmul(
                    psum[bS][:, hs], lhsT=khat[rows, j, ch * 128:(ch + 1) * 128], rhs=gv[rows, j, hs],
                    start=True, stop=True),
                    R=[khat_b[j], gv_b[j]], W=[pq[bS][h]])

        def emit_decay(c):
            s.op("dve", lambda e, c=c: e.tensor_tensor(
                out=st_f[:], in0=st_f[:], in1=dec[:, :, c:c + 1].broadcast_to([128, 4, 128]), op=ALU.mult),
                R=stf_b + [dec_b[c // 8]], W=stf_b)

        def emit_update(j, half):
            bS = bS0 if half == 0 else bS1
            c = 2 * j + half
            s.op("dve", lambda e, bS=bS: e.tensor_tensor(
                out=st_f[:], in0=st_f[:], in1=psum[bS][:].rearrange("p (h n) -> p h n", h=4), op=ALU.add),
                R=stf_b + [pq[bS][0]], W=stf_b)
            s.op("act", lambda e: e.copy(out=st_b[:], in_=st_f[:]), R=stf_b, W=stb_b)
            if c + 1 < 32:
                emit_decay(c + 1)

        gtasks = []
        gstate = {"B": None}

        def gate_batch(tb):
            sl = slice(tb * TB, (tb + 1) * TB)
            for h in range(4):
                bi = 6 + h % 2
                for k in range(NC_):
                    s.op("pe", lambda e, k=k, h=h, sl=sl, bi=bi: e.matmul(
                        psum[bi][:], lhsT=wgr[:, k, h * 128:(h + 1) * 128], rhs=xb[:, k, sl],
                        start=(k == 0), stop=(k == NC_ - 1)),
                        R=[wgr_b, xb_b[k][tb]], W=pb_(bi))
                s.op("act", lambda e, h=h, bi=bi: e.activation(out=sg[h][:], in_=psum[bi][:], func=AF.Silu),
                     R=pb_(bi), W=[sg_b[h]])
            for h in range(4):
                gtasks.append((tb, h))

        def gn_A(tb, h):
            ob = tb % 2
            og = ograw[ob][:, h, :]
            ogb = ograw_b[ob][h]
            s.op("act", lambda e, og=og: e.activation(out=gsq[:], in_=og, func=AF.Square), R=ogb, W=[gsq_b])
            s.op("pe", lambda e, og=og: e.matmul(psum[6][:], lhsT=gones, rhs=og, start=True, stop=True),
                 R=ogb + [consts_b], W=pb_(6))
            s.op("pe", lambda e: e.matmul(psum[7][:], lhsT=gones_r[:], rhs=gsq[:], start=True, stop=True),
                 R=[gsq_b, onesr_b], W=pb_(7))
            s.op("act", lambda e: e.activation(out=gm2[:], in_=psum[6][:], func=AF.Square), R=pb_(6), W=[gm2_b])
            s.op("dve", lambda e, og=og: e.tensor_tensor(out=og, in0=og, in1=psum[6][:], op=ALU.subtract),
                 R=ogb + pb_(6), W=ogb)
            s.op("dve", lambda e: e.tensor_tensor(out=grs[:], in0=psum[7][:], in1=gm2[:], op=ALU.subtract),
                 R=pb_(7) + [gm2_b], W=[grs_b])
            s.op("act", lambda e: e.activation(out=gm2[:], in_=grs[:], func=AF.Ln, bias=LN_EPS, scale=1.0),
                 R=[grs_b], W=[gm2_b])
            s.op("act", lambda e: e.activation(out=grs[:], in_=gm2[:], func=AF.Exp, scale=-0.5), R=[gm2_b], W=[grs_b])

        def gn_B(tb, h):
            ob = tb % 2
            og = ograw[ob][:, h, :]
            ogb = ograw_b[ob][h]
            col = l * 4 + h
            sl = slice(tb * TB, (tb + 1) * TB)
            s.op("pool", lambda e, og=og: e.tensor_tensor(out=og, in0=og, in1=grs[:], op=ALU.mult),
                 R=ogb + [grs_b], W=ogb)
            s.op("act", lambda e, og=og, col=col: e.activation(out=og, in_=og, func=AF.Identity,
                                                             scale=gng[:, col:col + 1], bias=gnb[:, col:col + 1]),
                 R=ogb + [mixc_b], W=ogb)
            s.op("dve", lambda e, og=og, h=h, sl=sl: e.tensor_tensor(out=o_gnT[:, h, sl], in0=og, in1=sg[h][:], op=ALU.mult),
                 R=ogb + [sg_b[h]], W=[o_gn_b[h][tb]])

        def gn_slot():
            if gstate["B"] is not None:
                gn_B(*gstate["B"])
                gstate["B"] = None
            if gtasks:
                t_ = gtasks.pop(0)
                gn_A(*t_)
                gstate["B"] = t_

        def gn_flush():
            while gtasks or gstate["B"] is not None:
                gn_slot()

        emit_AT_mm(0)
        emit_dS(0, 0)
        emit_dS(0, 1)
        emit_AT_mask(0)
        for j in range(16):
            tb, jj = j // 4, j % 4
            ob = tb % 2
            aj = j % 2
            bO = 2 + aj
            t0 = slice(j * 128, j * 128 + 64)
            t1_ = slice(j * 128 + 64, (j + 1) * 128)
            for h in range(4):
                hs = slice(h * 128, (h + 1) * 128)
                s.op("pe", lambda e, h=h, hs=hs, bO=bO, aj=aj, j=j: e.matmul(
                    psum[bO][:, hs], lhsT=gv[:, j, hs], rhs=ATt[aj][:, h, :], start=(h == 0), stop=False, skip_group_check=True),
                    R=[gv_b[j], AT_b[aj][h]], W=[pq[bO][h]])
            prev = None
            for h in HORD:
                ch, pb = h // 2, (h % 2) * 64
                prev = s.op("pe", lambda e, h=h, ch=ch, pb=pb, bO=bO, t0=t0: e.matmul(
                    psum[bO][:, h * 128:h * 128 + 64], lhsT=st_b[pb:pb + 64, h, :], rhs=qT[pb:pb + 64, ch, t0],
                    start=False, stop=False, skip_group_check=True),
                    R=[stb_b[h], qT_b[ch][j]], W=[pq[bO][h]], after=[prev] if h == 1 else [])
            if j + 1 < 16:
                emit_AT_mm(j + 1)
            emit_update(j, 0)
            prev = None
            for h in HORD:
                ch, pb = h // 2, (h % 2) * 64
                prev = s.op("pe", lambda e, h=h, ch=ch, pb=pb, bO=bO, t1_=t1_: e.matmul(
                    psum[bO][:, h * 128 + 64:(h + 1) * 128], lhsT=st_b[pb:pb + 64, h, :], rhs=qT[pb:pb + 64, ch, t1_],
                    start=False, stop=True, skip_group_check=True),
                    R=[stb_b[h], qT_b[ch][j]], W=[pq[bO][h]], after=[prev] if h == 1 else [])
            if j + 1 < 16:
                emit_AT_mask(j + 1)
                emit_dS(j + 1, 0)
            s.op("act", lambda e, bO=bO, ob=ob, jj=jj: e.copy(
                out=ograw[ob][:, :, jj * 128:(jj + 1) * 128], in_=psum[bO][:].rearrange("p (h n) -> p h n", h=4)),
                R=[pq[bO][0]], W=[ograw_b[ob][h][jj] for h in range(4)])
            emit_update(j, 1)
            if j + 1 < 16:
                emit_dS(j + 1, 1)
            if j == 0:
                tap("st1", st_f, stf_b)
            gn_slot()
            if jj == 3:
                gn_flush()
                if tb == 0:
                    tap("ograw0", ograw[0], [b for bb in ograw_b[0] for b in bb])
                gate_batch(tb)
        gn_flush()

        if stop == "D":
            return
        tap("o_gnT", o_gnT, [b for bb in o_gn_b for b in bb])
        s.fence()
        A.release(mix_base)
        o_aT = A.alloc([128, 4, S], BF16, "o_aT")
        mix_base = A.mark()
        waqkv = A.alloc([128, NC_, 384], BF16, "waqkv")
        waqkv_b = Buf()
        waqkv_sem = new_dsem("waqkv")
        aqT = [[A.alloc([128, S], BF16, "aqT") for _ in range(2)] for _ in range(2)]
        akT = [A.alloc([128, S], BF16, "akT") for _ in range(2)]
        aqT_b = [bufs(NTB) for _ in range(2)]
        aqz_b = [bufs(2) for _ in range(2)]
        for wi_ in range(2):
            for hh_ in range(2):
                oth = slice(64, 128) if hh_ == 0 else slice(0, 64)
                s.op("pool", lambda e, wi_=wi_, hh_=hh_, oth=oth: e.memset(aqT[wi_][hh_][oth, :], 0.0), W=[aqz_b[wi_][hh_]])
        akT_b = [bufs(16) for _ in range(2)]
        Vp = [A.alloc([128, 16, 128], BF16, "Vp") for _ in range(2)]
        Vp_b = [bufs(16) for _ in range(2)]
        mask_s = [A.alloc([128, S], F32, "mask") for _ in range(2)]
        mask_b = bufs(2)
        mask_sem = [new_dsem("mask") for _ in range(2)]
        NE = 4
        LOOK = 3
        Et = [A.alloc([128, TB], F32, "Et") for _ in range(NE)]
        Pt = [A.alloc([128, TB], BF16, "Pt") for _ in range(NE)]
        Et_b, Pt_b = bufs(NE), bufs(NE)
        dcp = [A.alloc([128, TB], F32, "dcp") for _ in range(1)] * 2
        dcp_b = bufs(1) * 2
        onesb = A.alloc([128, 128], BF16, "onesb")
        onesb_b = Buf()
        s.op("pool", lambda e: e.memset(onesb[:], 1.0), W=[onesb_b])
        SB = (0, 1, 2, 3)
        stepc = 0
        def load_waqkv(ch):
            s.op("pool", lambda e, ch=ch: e.dma_start(out=waqkv[:], in_=waqkv_d[ch].rearrange("p (k n) -> p k n", k=NC_)),
                 W=[waqkv_b], dsem=waqkv_sem)

        def load_mask(h):
            mi = h % 2
            s.op("sp", lambda e, mi=mi, h=h: e.dma_start(out=mask_s[mi][:], in_=amask_d[h]),
                 W=[mask_b[mi]], dsem=mask_sem[mi])

        load_waqkv(0)
        load_mask(0)
        load_mask(1)
        for ch in range(4):
            wi = ch % 2
            for tb in range(NTB):
                sl = slice(tb * TB, (tb + 1) * TB)
                for which in range(2):
                    bi = SB[(2 * tb + which) % 4]
                    for k in range(NC_):
                        s.op("pe", lambda e, k=k, which=which, sl=sl, bi=bi: e.matmul(
                            psum[bi][:], lhsT=waqkv[:, k, which * 128:(which + 1) * 128], rhs=xb[:, k, sl],
                            start=(k == 0), stop=(k == NC_ - 1)),
                            R=[waqkv_b, xb_b[k][tb]], W=pb_(bi))
                    if which == 0:
                        s.op("act", lambda e, wi=wi, sl=sl, bi=bi: e.mul(aqT[wi][0][0:64, sl], psum[bi][0:64, :], 0.125),
                             R=pb_(bi), W=[aqT_b[wi][tb]])
                        s.op("act", lambda e, wi=wi, sl=sl, bi=bi: e.mul(aqT[wi][1][64:128, sl], psum[bi][64:128, :], 0.125),
                             R=pb_(bi), W=[aqT_b[wi][tb]])
                    else:
                        s.op("dve", lambda e, wi=wi, sl=sl, bi=bi: e.tensor_copy(out=akT[wi][:, sl], in_=psum[bi][:]),
                             R=pb_(bi), W=akT_b[wi][tb * 4:(tb + 1) * 4])
            for j4 in range(4):
                bi = SB[j4 % 4]
                for jj in range(4):
                    j = j4 * 4 + jj
                    tsl = slice(j * 128, (j + 1) * 128)
                    for k in range(NC_):
                        s.op("pe", lambda e, k=k, tsl=tsl, bi=bi, jj=jj: e.matmul(
                            psum[bi][:, jj * 128:(jj + 1) * 128], lhsT=xb[:, k, tsl], rhs=waqkv[:, k, 256:384],
                            start=(k == 0 and jj == 0), stop=(k == NC_ - 1), skip_group_check=True),
                            R=[waqkv_b, xb_b[k][j4]], W=pb_(bi))
                s.op("act", lambda e, wi=wi, j4=j4, bi=bi: e.copy(
                    out=Vp[wi][:, j4 * 4:(j4 + 1) * 4, :], in_=psum[bi][:].rearrange("p (a b) -> p a b", a=4)),
                    R=pb_(bi), W=Vp_b[wi][j4 * 4:(j4 + 1) * 4])
            if ch + 1 < 4:
                load_waqkv(ch + 1)
            steps = []
            for hh in range(2):
                h = 2 * ch + hh
                for qp in range(NTB):
                    kbs = [kb for kb in range(4 * qp + 4) if blk[h][kb][qp]]
                    for ki, kb in enumerate(kbs):
                        steps.append((hh, h, qp, kb, ki, len(kbs)))
            info = {}
            for idx in range(len(steps) + LOOK):
                if idx < len(steps):
                    hh, h, qp, kb, ki, nk = steps[idx]
                    pb = hh * 64
                    mi = h % 2
                    q0 = qp * TB
                    n0 = max(q0, 128 * kb)
                    n = q0 + TB - n0
                    bS = SB[stepc % 4]
                    ei = stepc % NE
                    stepc += 1
                    info[idx] = (ei, n0, n)
                    s.op("pe", lambda e, wi=wi, hh=hh, kb=kb, n0=n0, n=n, bS=bS: e.matmul(
                        psum[bS][:, 0:n], lhsT=akT[wi][:, kb * 128:(kb + 1) * 128],
                        rhs=aqT[wi][hh][:, n0:n0 + n], start=True, stop=True),
                        R=[akT_b[wi][kb], aqT_b[wi][qp], aqz_b[wi][hh]], W=pb_(bS))
                    s.op("act", lambda e, ei=ei, bS=bS, n=n: e.activation(out=Et[ei][:, 0:n], in_=psum[bS][:, 0:n], func=AF.Exp),
                         R=pb_(bS), W=[Et_b[ei]])
                    mo = n0 - 128 * kb
                    s.op("dve", lambda e, ei=ei, mi=mi, mo=mo, n=n: e.tensor_tensor(
                        out=Pt[ei][:, 0:n], in0=Et[ei][:, 0:n], in1=mask_s[mi][:, mo:mo + n], op=ALU.mult),
                        R=[Et_b[ei], mask_b[mi]], W=[Pt_b[ei]])
                    if h + 2 < 8 and (idx + 1 == len(steps) or steps[idx + 1][1] != h):
                        load_mask(h + 2)
                pidx = idx - LOOK
                if pidx >= 0:
                    hh, h, qp, kb, ki, nk = steps[pidx]
                    ei, n0, n = info[pidx]
                    pb = hh * 64
                    q0 = qp * TB
                    par = (h * NTB + qp) % 2
                    bO, bD = 4 + par, 6 + par
                    cs = slice(n0 - q0, n0 - q0 + n)
                    s.op("pe", lambda e, wi=wi, kb=kb, ei=ei, n=n, cs=cs, bO=bO, ki=ki, nk=nk: e.matmul(
                        psum[bO][:, cs], lhsT=Vp[wi][:, kb, :], rhs=Pt[ei][:, 0:n],
                        start=(ki == 0), stop=(ki == nk - 1), skip_group_check=True),
                        R=[Vp_b[wi][kb], Pt_b[ei]], W=pb_(bO))
                    s.op("pe", lambda e, ei=ei, n=n, cs=cs, bD=bD, ki=ki, nk=nk: e.matmul(
                        psum[bD][:, cs], lhsT=onesb[:], rhs=Pt[ei][:, 0:n],
                        start=(ki == 0), stop=(ki == nk - 1), skip_group_check=True),
                        R=[onesb_b, Pt_b[ei]], W=pb_(bD))
                    if ki == nk - 1:
                        ps_ = slice(pb, pb + 64)
                        s.op("act", lambda e, par=par, bD=bD, ps_=ps_: e.activation(out=dcp[par][ps_, :], in_=psum[bD][ps_, :], func=AF.Ln),
                             R=pb_(bD), W=[dcp_b[par]])
                        s.op("act", lambda e, par=par, ps_=ps_: e.activation(out=dcp[par][ps_, :], in_=dcp[par][ps_, :], func=AF.Exp, scale=-1.0),
                             R=[dcp_b[par]], W=[dcp_b[par]])
                        s.op("dve", lambda e, ch=ch, ps_=ps_, q0=q0, bO=bO, par=par: e.tensor_tensor(
                            out=o_aT[ps_, ch, q0:q0 + TB], in0=psum[bO][ps_, :], in1=dcp[par][ps_, :], op=ALU.mult),
                            R=pb_(bO) + [dcp_b[par]], W=[o_a_b[h][qp]])

        if stop == "C":
            return
        tap("o_aT", o_aT, [b for bb in o_a_b for b in bb])
        s.fence()
        A.release(mix_base)
        e_low_top = A.mark()
        mT = A.alloc([128, NC_, S], BF16, "mT")
        mT_b = [bufs(NTB) for _ in range(NC_)]
        e_base = A.mark()
        wE = [A.alloc([128, NC_, 256], BF16, "wgab") for _ in range(2)]
        wap = [A.alloc([128, 4, 128], BF16, "wap") for _ in range(2)]
        wgp = [A.alloc([128, 4, 128], BF16, "wgp") for _ in range(2)]
        wE_b = [bufs(3) for _ in range(2)]
        wE_sem = [[new_dsem("wE") for _ in range(3)] for _ in range(2)]
        sa = [A.alloc([128, TB], F32, "sa") for _ in range(2)]
        sbt = [A.alloc([128, TB], F32, "sbt") for _ in range(2)]
        sa_b, sbt_b = bufs(2), bufs(2)
        wo = A.alloc([128, NC_, NC_, 128], BF16, "wo")
        wo_b = bufs(NC_)
        e_top = A.mark()
        ecn = 0

        def load_wE(dc):
            wi = dc % 2
            s.op("pool", lambda e, wi=wi, dc=dc: e.dma_start(out=wE[wi][:], in_=wgab_d[dc].rearrange("p (k n) -> p k n", k=NC_)),
                 W=[wE_b[wi][0]], dsem=wE_sem[wi][0])
            s.op("pool", lambda e, wi=wi, dc=dc: e.dma_start(out=wap[wi][:], in_=wap_d[dc].rearrange("p (k n) -> p k n", k=4)),
                 W=[wE_b[wi][1]], dsem=wE_sem[wi][1])
            s.op("pool", lambda e, wi=wi, dc=dc: e.dma_start(out=wgp[wi][:], in_=wgp_d[dc].rearrange("p (k n) -> p k n", k=4)),
                 W=[wE_b[wi][2]], dsem=wE_sem[wi][2])

        load_wE(0)
        for dc in range(NC_):
            wi = dc % 2
            if dc + 1 < NC_:
                load_wE(dc + 1)
            if dc >= 4:
                for dco in (2 * (dc - 4), 2 * (dc - 4) + 1):
                    s.op("pool", lambda e, dco=dco: e.dma_start(out=wo[:, dco, :, :], in_=wo_d[dco].rearrange("p (k n) -> p k n", k=NC_)),
                         W=[wo_b[dco]], dsem=new_dsem("wo"))
            for tb in range(NTB):
                sl = slice(tb * TB, (tb + 1) * TB)
                pj = ecn % 2
                ecn += 1
                bGA, bGB, bPA, bPG = 0 + pj, 2 + pj, 4 + pj, 6 + pj
                for which, bi in ((0, bGA), (1, bGB)):
                    for k in range(NC_):
                        s.op("pe", lambda e, wi=wi, which=which, k=k, sl=sl, bi=bi: e.matmul(
                            psum[bi][:], lhsT=wE[wi][:, k, which * 128:(which + 1) * 128], rhs=xb[:, k, sl],
                            start=(k == 0), stop=(k == NC_ - 1)),
                            R=[wE_b[wi][0], xb_b[k][tb]], W=pb_(bi))
                for c in range(4):
                    s.op("pe", lambda e, wi=wi, c=c, sl=sl, bPA=bPA: e.matmul(
                        psum[bPA][:], lhsT=wap[wi][:, c, :], rhs=o_aT[:, c, sl], start=(c == 0), stop=(c == 3)),
                        R=[wE_b[wi][1], o_a_b[2 * c][tb], o_a_b[2 * c + 1][tb]], W=pb_(bPA))
                for c in range(4):
                    s.op("pe", lambda e, wi=wi, c=c, sl=sl, bPG=bPG: e.matmul(
                        psum[bPG][:], lhsT=wgp[wi][:, c, :], rhs=o_gnT[:, c, sl], start=(c == 0), stop=(c == 3)),
                        R=[wE_b[wi][2], o_gn_b[c][tb]], W=pb_(bPG))
                s.op("act", lambda e, pj=pj, bGA=bGA: e.activation(out=sa[pj][:], in_=psum[bGA][:], func=AF.Sigmoid),
                     R=pb_(bGA), W=[sa_b[pj]])
                s.op("act", lambda e, pj=pj, bGB=bGB: e.activation(out=sbt[pj][:], in_=psum[bGB][:], func=AF.Sigmoid),
                     R=pb_(bGB), W=[sbt_b[pj]])
                s.op("dve", lambda e, pj=pj, bPA=bPA: e.tensor_tensor(out=sa[pj][:], in0=psum[bPA][:], in1=sa[pj][:], op=ALU.mult),
                     R=pb_(bPA) + [sa_b[pj]], W=[sa_b[pj]])
                s.op("dve", lambda e, pj=pj, bPG=bPG: e.tensor_tensor(out=sbt[pj][:], in0=psum[bPG][:], in1=sbt[pj][:], op=ALU.mult),
                     R=pb_(bPG) + [sbt_b[pj]], W=[sbt_b[pj]])
                s.op("pool", lambda e, pj=pj, dc=dc, sl=sl: e.tensor_tensor(out=mT[:, dc, sl], in0=sa[pj][:], in1=sbt[pj][:], op=ALU.add),
                     R=[sa_b[pj], sbt_b[pj]], W=[mT_b[dc][tb]])
        tap("mT", mT, [b for bb in mT_b for b in bb])
        s.fence()
        A.release(e_top)
        Alow = Arena(nc, arena_base, e_low_top)
        ln_chunk, ln_flush = make_ln(l, 1, [Alow, A])
        yc = 0
        for tb in range(NTB):
            sl = slice(tb * TB, (tb + 1) * TB)
            for dc in range(NC_):
                bi = 4 + yc % 2
                yc += 1
                for k in range(NC_):
                    s.op("pe", lambda e, dc=dc, k=k, sl=sl, bi=bi: e.matmul(
                        psum[bi][:], lhsT=wo[:, dc, k, :], rhs=mT[:, k, sl], start=(k == 0), stop=(k == NC_ - 1)),
                        R=[wo_b[dc], mT_b[k][tb]], W=pb_(bi))
                s.op("dve", lambda e, bi=bi, dc=dc, sl=sl: e.tensor_tensor(
                    out=xs[:, dc, sl], in0=psum[bi][:], in1=xs[:, dc, sl], op=ALU.add),
                    R=pb_(bi) + [xs_b[dc][tb]], W=[xs_b[dc][tb]])
                ln_chunk(dc, tb)
        ln_flush()

    amask_blocks = _mask_blocks()
    for ph in phases:
        if ph[0] == "ffn":
            ffn(ph[1], ph[2])
        elif ph[0] == "mix":
            mixer(ph[1], ph[2] if len(ph) > 2 else None)
        else:
            raise ValueError(ph)

    for c in range(NC_):
        for t in range(NTB):
            sl = slice(t * TB, (t + 1) * TB)
            s.op("dve", lambda e, c=c, sl=sl: e.tensor_scalar_mul(out=xs[:, c, sl], in0=xs[:, c, sl], scalar1=1.0 / ALPHA),
                 R=[xs_b[c][t]], W=[xs_b[c][t]])
    last = None
    for c in range(NC_):
        last = s.op("sp", lambda e, c=c: e.dma_start(out=yT_d[c * 128:(c + 1) * 128, :], in_=xs[:, c, :]),
                    R=xs_b[c], dsem=out_sem)
    fin = Buf()
    fin.w = last
    s.op("sp", lambda e: e.nop(), R=[fin])

    s.finalize()
    from contextlib import ExitStack
    with ExitStack() as ctx:
        esem = {}
        for en in Sched.ENGS:
            esem[en] = ctx.enter_context(nc.semaphore(f"sem_{en}"))
        dsems = {}
        for nm in dsem_names:
            dsems[nm] = ctx.enter_context(nc.semaphore(f"d_{nm}"))
        with nc.Block() as block:
            @block.tensor
            def _(e):
                s.replay("pe", e, esem, dsems)

            @block.scalar
            def _(e):
                s.replay("act", e, esem, dsems)

            @block.vector
            def _(e):
                s.replay("dve", e, esem, dsems)

            @block.gpsimd
            def _(e):
                s.replay("pool", e, esem, dsems)

            @block.sync
            def _(e):
                s.replay("sp", e, esem, dsems)
    return nc


_MASK = None


def _alibi_mask():
    global _MASK
    if _MASK is None:
        d = np.arange(S)[None, :] - np.arange(128)[:, None]
        mult = ((d <= 128).astype(np.float64) + ((d % 4 == 0) & (d <= 512)) + ((d % 16 == 0) & (d <= 2048)))
        mult = np.where(d >= 0, mult, 0.0)
        slopes = np.exp2(-8.0 * np.arange(1, 9) / 8.0)
        m = mult[None] * np.exp(-slopes[:, None, None] * np.maximum(d, 0)[None])
        m = np.where(m < 1e-37, 0.0, m)
        _MASK = np.ascontiguousarray(m.astype(np.float32))
    return _MASK


def _mask_blocks():
    m = _alibi_mask()
    blk = [[[False] * NTB for _ in range(16)] for _ in range(8)]
    for h in range(8):
        for kb in range(16):
            for qp in range(NTB):
                n0 = max(qp * TB, 128 * kb)
                n1 = qp * TB + TB
                if n1 <= n0:
                    continue
                blk[h][kb][qp] = bool(m[h][:, n0 - 128 * kb:n1 - 128 * kb].any())
    return blk


def _consts():
    s_ = np.arange(128)[:, None]
    t_ = np.arange(128)[None, :]
    same = (s_ // 64) == (t_ // 64)
    U = np.where(same & (s_ <= t_), -1.0 / 16.0, 0.0)
    L = np.where(same & (s_ > t_), -1.0 / 16.0, 0.0)
    M = np.where(same & (s_ <= t_), 1.0, 0.0)
    o1 = np.full((128, 128), 1.0 / D)
    o2 = np.full((128, 128), 1.0 / 128.0)
    return np.ascontiguousarray(np.concatenate([U, L, M, o1, o2], axis=1).astype(np.float32))


def _lay_w13(w):
    return np.ascontiguousarray(w.reshape(NC_, 128, NF, 128).transpose(2, 1, 0, 3).reshape(NF, 128, D))


def _lay_ln(v):
    return np.ascontiguousarray(v.reshape(DEPTH, 3, NC_, 128).transpose(3, 0, 1, 2).reshape(128, NL3))


def _lay_cols(w):
    n = w.shape[1]
    return np.ascontiguousarray(w.reshape(NC_, 128, n).transpose(1, 0, 2).reshape(128, NC_ * n))


def make_inputs(phases, inp):
    m = {"ln_g": _lay_ln(inp["ln_g"]), "ln_b": _lay_ln(inp["ln_b"]), "consts": _consts()}
    has_mix = any(p[0] == "mix" for p in phases)
    if has_mix:
        m["amask"] = _alibi_mask()
        m["wgu"] = np.ascontiguousarray(inp["w_gate_up"].transpose(1, 0, 2))
        m["bgu"] = np.ascontiguousarray(np.broadcast_to(inp["b_gate_up"][None], (128, DEPTH, 256)))
        m["gng"] = np.ascontiguousarray(inp["gla_norm_g"].reshape(DEPTH, 4, 128).transpose(2, 0, 1).reshape(128, DEPTH * 4))
        m["gnb"] = np.ascontiguousarray(inp["gla_norm_b"].reshape(DEPTH, 4, 128).transpose(2, 0, 1).reshape(128, DEPTH * 4))
    for ph in phases:
        if ph[0] == "ffn":
            l, i = ph[1], ph[2]
            pre = "ffn1" if i == 0 else "ffn2"
            m[f"f{i}w1_{l}"] = _lay_w13(inp[pre + "_w1"][l])
            m[f"f{i}w3_{l}"] = _lay_w13(inp[pre + "_w3"][l])
            m[f"f{i}w2_{l}"] = np.ascontiguousarray(inp[pre + "_w2"][l])
        else:
            l = ph[1]
            w = inp["w_in"][l]
            m[f"wglr_{l}"] = _lay_cols(w[:, O_GLR:O_GLR + 16])
            m[f"wgqk_{l}"] = _lay_cols(w[:, O_GQ:O_GQ + 512])
            m[f"wgkv_{l}"] = _lay_cols(w[:, O_GK:O_GK + 768])
            m[f"wgr_{l}"] = _lay_cols(w[:, O_GR:O_GR + 512])
            m[f"waqkv_{l}"] = np.stack([_lay_cols(np.concatenate(
                [w[:, O_AQ + c * 128:O_AQ + (c + 1) * 128], w[:, O_AK + c * 128:O_AK + (c + 1) * 128],
                 w[:, O_AV + c * 128:O_AV + (c + 1) * 128]], axis=1)) for c in range(4)], axis=0)
            m[f"wgab_{l}"] = np.stack([_lay_cols(np.concatenate(
                [w[:, O_GA + c * 128:O_GA + (c + 1) * 128], w[:, O_GB + c * 128:O_GB + (c + 1) * 128]], axis=1))
                for c in range(NC_)], axis=0)
            wa = inp["w_attn_proj"][l]
            m[f"wap_{l}"] = np.ascontiguousarray(wa.reshape(4, 128, NC_, 128).transpose(2, 1, 0, 3).reshape(NC_, 128, 4 * 128))
            wg = inp["w_gla_proj"][l]
            m[f"wgp_{l}"] = np.ascontiguousarray(wg.reshape(4, 128, NC_, 128).transpose(2, 1, 0, 3).reshape(NC_, 128, 4 * 128))
            wo_ = inp["w_out"][l]
            m[f"wo_{l}"] = np.ascontiguousarray(wo_.reshape(NC_, 128, NC_, 128).transpose(2, 1, 0, 3).reshape(NC_, 128, NC_ * 128))
    return m


def run_phases(phases, x, inp, n_cores=8, trace=False, debug=None):
    nc = build(phases, debug)
    shared = make_inputs(phases, inp)
    in_maps = []
    for b in range(n_cores):
        d = dict(shared)
        d["xT"] = np.ascontiguousarray(x[b].T)
        in_maps.append(d)
    res = run_bass_kernel_spmd(nc, in_maps, core_ids=list(range(n_cores)), trace=trace)
    out = np.stack([np.ascontiguousarray(r["yT"].T) for r in res.results], axis=0)
    return out, res


LAUNCHES = [[("ffn", 0, 0), ("mix", 0), ("ffn", 0, 1), ("ffn", 1, 0), ("mix", 1), ("ffn", 1, 1)]]


def kernel(**inputs):
    inp = {k: np.asarray(v) for k, v in inputs.items()}
    x = np.ascontiguousarray(inp["x"], dtype=np.float32)
    for phases in LAUNCHES:
        x, _ = run_phases(phases, x, inp)
    return np.ascontiguousarray(x, dtype=np.float32)
```

```python
import numpy as np
import concourse.bass as bass
import concourse.mybir as mybir
from concourse.bass_utils import run_bass_kernel_spmd

F32 = mybir.dt.float32
F32R = mybir.dt.float32r

BF16 = mybir.dt.bfloat16
AF = mybir.ActivationFunctionType
ALU = mybir.AluOpType

S = 2048
D = 1024
DFF = 2816
NC_ = 8
NTB = 4
TB = 512
NF = 22
DEPTH = 2
ALPHA = float((2 * DEPTH) ** 0.25)
LN_EPS = 1e-5
FFN_GROUPS = [4, 4, 4, 4, 3, 3]
GMAX = 4
NL3 = DEPTH * 3 * NC_
SB_BASE = 16512
SB_TOP = 229344

O_AQ, O_AK, O_AV = 0, 512, 1024
O_GQ, O_GK, O_GV, O_GLR, O_GR = 1536, 1792, 2048, 2560, 2576
O_GA, O_GB = 3088, 4112
N_IN = 5136


class Buf:
    __slots__ = ("name", "w", "r", "excl")

    def __init__(self, name="", excl=False):
        self.name = name
        self.w = None
        self.r = {}
        self.excl = excl


def bufs(n):
    return [Buf() for _ in range(n)]


class Op:
    __slots__ = ("eng", "fn", "deps", "needed", "semval", "dsem", "dval")

    def __init__(self, eng, fn, deps, dsem):
        self.eng = eng
        self.fn = fn
        self.deps = deps
        self.needed = False
        self.semval = None
        self.dsem = dsem
        self.dval = None


class Sched:
    ENGS = ("pe", "act", "dve", "pool", "sp")

    def __init__(self):
        self.q = {e: [] for e in self.ENGS}
        self.dma_count = {}
        self.last_dma = {}
        self.extra = {e: [] for e in self.ENGS}

    def op(self, eng, fn, R=(), W=(), dsem=None, after=()):
        deps = [(3, a) for a in after if a is not None]
        if any(b.excl for b in R):
            W = list(W) + [b for b in R if b.excl]
            R = [b for b in R if not b.excl]
        W = list(dict.fromkeys(W))
        for b in R:
            if b.w is not None:
                deps.append((0, b.w))
        for b in W:
            if b.w is not None:
                deps.append((1, b.w))
            for r in b.r.values():
                deps.append((2, r))
        if self.extra[eng]:
            deps.extend((0, d) for d in self.extra[eng])
            self.extra[eng] = []
        o = Op(eng, fn, deps, dsem)
        if dsem is not None:
            self.dma_count[dsem] = self.dma_count.get(dsem, 0) + 16
            o.dval = self.dma_count[dsem]
            self.last_dma[dsem] = o
        self.q[eng].append(o)
        key = dsem if dsem is not None else eng
        for b in R:
            b.r[key] = o
        for b in W:
            b.w = o
            b.r = {}
        return o

    def fence(self):
        snap = []
        for e in self.ENGS:
            for o in reversed(self.q[e]):
                if o.dsem is None:
                    snap.append(o)
                    break
        snap.extend(self.last_dma.values())
        for e in self.ENGS:
            self.extra[e] = list(snap)

    def finalize(self):
        for eng in self.ENGS:
            for o in self.q[eng]:
                keep = []
                for kind, d in o.deps:
                    if d is o:
                        continue
                    if d.dsem is not None:
                        keep.append(d)
                    elif d.eng == o.eng:
                        if o.eng == "pe" and kind != 3:
                            continue
                        keep.append(d)
                    else:
                        keep.append(d)
                for d in keep:
                    if d.dsem is None:
                        d.needed = True
                o.deps = keep
        for eng in self.ENGS:
            c = 0
            for o in self.q[eng]:
                if o.dsem is None and o.needed:
                    c += 1
                    o.semval = c

    def replay(self, eng, e, esem, dsems):
        seen = {}
        for o in self.q[eng]:
            for d in o.deps:
                if d.dsem is not None:
                    key, val, sem = ("d", d.dsem), d.dval, dsems[d.dsem]
                else:
                    key, val, sem = ("e", d.eng), d.semval, esem[d.eng]
                if seen.get(key, 0) >= val:
                    continue
                seen[key] = val
                e.wait_ge(sem, val)
            ins = o.fn(e)
            if o.dsem is not None:
                ins.then_inc(dsems[o.dsem], 16)
            elif o.needed:
                ins.then_inc(esem[eng], 1)


DT_SIZE = {F32: 4, BF16: 2, F32R: 4}


class Arena:
    UID = 0

    def __init__(self, nc, base, top):
        self.nc = nc
        self.base = base
        self.top = top
        self.off = base
        self.uid = 0
        self.peak = base

    def alloc(self, shape, dt, name="t"):
        n = 1
        for d in shape[1:]:
            n *= d
        nbytes = (n * DT_SIZE[dt] + 63) // 64 * 64
        if self.off + nbytes > self.top:
            raise RuntimeError(f"arena overflow allocating {name} {shape}: off={self.off - self.base} need {nbytes} cap {self.top - self.base}")
        Arena.UID += 1
        t = self.nc.alloc_sbuf_tensor_at(f"{name}_{Arena.UID}", list(shape), dt, offset=self.off)
        self.off += nbytes
        self.peak = max(self.peak, self.off)
        return t

    def mark(self):
        return self.off

    def release(self, m):
        self.off = m


def build(phases, debug=None):
    nc = bass.Bass("TRN2", target_bir_lowering=False)
    s = Sched()
    dram = {}
    dsem_names = []

    def new_dsem(name):
        nm = f"{name}_{len(dsem_names)}"
        dsem_names.append(nm)
        return nm

    def din(name, shape, dt=F32):
        if name not in dram:
            dram[name] = nc.dram_tensor(name, list(shape), dt, kind="ExternalInput").ap()
        return dram[name]

    debug = debug or ()

    def tap(name, t, bl):
        if name in debug:
            dd = nc.dram_tensor("dbg_" + name, list(t.shape), t.dtype, kind="ExternalOutput").ap()
            s.op("sp", lambda e: e.dma_start(out=dd, in_=t[:]), R=bl, dsem=new_dsem("dbg"))

    xT_d = din("xT", [D, S])
    yT_d = nc.dram_tensor("yT", [D, S], F32, kind="ExternalOutput").ap()
    lng_d = din("ln_g", [128, NL3])
    lnb_d = din("ln_b", [128, NL3])
    consts_d = din("consts", [128, 5 * 128])
    has_mix = any(p[0] == "mix" for p in phases)

    A = Arena(nc, SB_BASE, SB_TOP)
    xs = A.alloc([128, NC_, S], F32, "xs")
    xb = A.alloc([128, NC_, S], BF16, "xb")
    xs_b = [bufs(NTB) for _ in range(NC_)]
    xb_b = [bufs(NTB) for _ in range(NC_)]
    lng = A.alloc([128, NL3], F32, "lng")
    lnb = A.alloc([128, NL3], F32, "lnb")
    lnga = A.alloc([128, NL3], F32, "lnga")
    lnba = A.alloc([128, NL3], F32, "lnba")
    ln_c = Buf()
    consts = A.alloc([128, 5 * 128], F32, "consts")
    consts_b = Buf()
    Umat = consts[:, 0:128]
    Lmat = consts[:, 128:256]
    Mblk = consts[:, 256:384]
    ones = consts[:, 384:512]
    gones = consts[:, 512:640]
    ones1 = A.alloc([128, 64], F32, "ones1")
    ones1_b = Buf()
    ones_r = A.alloc([128, 128], F32R, "ones_r")
    gones_r = A.alloc([128, 128], F32R, "gones_r")
    onesr_b = Buf()
    mixc_b = Buf()
    if has_mix:
        wgu = A.alloc([16, DEPTH, 256], F32, "wgu")
        bgu = A.alloc([128, DEPTH, 256], F32, "bgu")
        gng = A.alloc([128, DEPTH * 4], F32, "gng")
        gnb = A.alloc([128, DEPTH * 4], F32, "gnb")
    arena_base = A.mark()

    psum = [nc.alloc_psum_tensor(f"bank{i}", [128, TB], F32) for i in range(8)]
    pq = [[Buf(f"bank{i}", excl=True)] * 4 for i in range(8)]

    def pb_(i, c0=0, c1=TB):
        return pq[i][c0 // 128:(c1 + 127) // 128]

    out_sem = new_dsem("out")

    lng_b0, lnb_b0 = Buf(), Buf()
    s.op("sp", lambda e: e.dma_start(out=lng[:], in_=lng_d), W=[lng_b0], dsem=new_dsem("io"))
    s.op("sp", lambda e: e.dma_start(out=lnb[:], in_=lnb_d), W=[lnb_b0], dsem=new_dsem("io"))
    s.op("sp", lambda e: e.dma_start(out=consts[:], in_=consts_d), W=[consts_b], dsem=new_dsem("io"))
    if has_mix:
        wgu_d = din("wgu", [16, DEPTH, 256])
        bgu_d = din("bgu", [128, DEPTH, 256])
        gng_d = din("gng", [128, DEPTH * 4])
        gnb_d = din("gnb", [128, DEPTH * 4])
        mb = bufs(4)
        s.op("sp", lambda e: e.dma_start(out=wgu[:], in_=wgu_d), W=[mb[0]], dsem=new_dsem("io"))
        s.op("sp", lambda e: e.dma_start(out=bgu[:], in_=bgu_d), W=[mb[1]], dsem=new_dsem("io"))
        s.op("sp", lambda e: e.dma_start(out=gng[:], in_=gng_d), W=[mb[2]], dsem=new_dsem("io"))
        s.op("sp", lambda e: e.dma_start(out=gnb[:], in_=gnb_d), W=[mb[3]], dsem=new_dsem("io"))
        s.op("dve", lambda e: e.memset(ones1[:], 1.0), R=mb, W=[ones1_b, mixc_b])
    for c in range(NC_):
        s.op("sp", lambda e, c=c: e.dma_start(out=xs[:, c, :], in_=xT_d[c * 128:(c + 1) * 128, :]),
             W=xs_b[c], dsem=new_dsem("iox"))
    s.op("act", lambda e: e.copy(out=ones_r[:], in_=ones), R=[consts_b], W=[onesr_b])
    s.op("act", lambda e: e.copy(out=gones_r[:], in_=gones), R=[consts_b], W=[onesr_b])
    s.op("act", lambda e: e.mul(lnga[:], lng[:], ALPHA), R=[lng_b0], W=[ln_c])
    s.op("act", lambda e: e.mul(lnba[:], lnb[:], ALPHA), R=[lnb_b0], W=[ln_c])
    for c in range(NC_):
        for t in range(NTB):
            sl = slice(t * TB, (t + 1) * TB)
            s.op("dve", lambda e, c=c, sl=sl: e.tensor_copy(out=xb[:, c, sl], in_=xs[:, c, sl]),
                 R=[xs_b[c][t]], W=[xb_b[c][t]])
            s.op("act", lambda e, c=c, sl=sl: e.mul(xs[:, c, sl], xs[:, c, sl], ALPHA),
                 R=[xs_b[c][t]], W=[xs_b[c][t]])

    def layer_norm(l, i):
        col0 = (l * 3 + i) * NC_
        s.fence()
        A.release(arena_base)
        sq = A.alloc([128, NC_, TB], F32, "sq")
        sq_b = bufs(NC_)
        mean_sb = [A.alloc([128, TB], F32, "mean") for _ in range(2)]
        m2_sb = [A.alloc([128, TB], F32, "m2") for _ in range(2)]
        rstd_sb = [A.alloc([128, TB], F32, "rstd") for _ in range(2)]
        mean_b, m2_b, rstd_b = bufs(2), bufs(2), bufs(2)
        t1 = [A.alloc([128, TB], F32, "t1") for _ in range(2)]
        t2 = [A.alloc([128, TB], F32, "t2") for _ in range(3)]
        t1_b, t2_b = bufs(2), bufs(3)
        cn = {"t1": 0, "t2": 0}

        def stats(t):
            sl = slice(t * TB, (t + 1) * TB)
            p = t % 2
            bm, bq = (6, 7) if p == 0 else (4, 5)
            pm, pq_ = psum[bm], psum[bq]
            for c in range(NC_):
                s.op("act", lambda e, c=c, sl=sl: e.activation(out=sq[:, c, :], in_=xs[:, c, sl], func=AF.Square),
                     R=[xs_b[c][t]], W=[sq_b[c]])
            for c in range(NC_):
                s.op("pe", lambda e, c=c, sl=sl, pm=pm: e.matmul(pm[:], lhsT=ones, rhs=xs[:, c, sl],
                                                                 start=(c == 0), stop=(c == NC_ - 1)),
                     R=[consts_b, xs_b[c][t]], W=pb_(bm))
            for c in range(NC_):
                s.op("pe", lambda e, c=c, pq_=pq_: e.matmul(pq_[:], lhsT=ones, rhs=sq[:, c, :],
                                                            start=(c == 0), stop=(c == NC_ - 1)),
                     R=[consts_b, sq_b[c]], W=pb_(bq))
            s.op("act", lambda e, pm=pm, p=p: e.activation(out=m2_sb[p][:], in_=pm[:], func=AF.Square), R=pb_(bm), W=[m2_b[p]])
            s.op("act", lambda e, pm=pm, p=p: e.copy(out=mean_sb[p][:], in_=pm[:]), R=pb_(bm), W=[mean_b[p]])
            s.op("dve", lambda e, pq_=pq_, p=p: e.tensor_tensor(out=rstd_sb[p][:], in0=pq_[:], in1=m2_sb[p][:], op=ALU.subtract),
                 R=pb_(bq) + [m2_b[p]], W=[rstd_b[p]])
            s.op("act", lambda e, p=p: e.activation(out=m2_sb[p][:], in_=rstd_sb[p][:], func=AF.Ln, bias=LN_EPS, scale=1.0),
                 R=[rstd_b[p]], W=[m2_b[p]])
            s.op("act", lambda e, p=p: e.activation(out=rstd_sb[p][:], in_=m2_sb[p][:], func=AF.Exp, scale=-0.5),
                 R=[m2_b[p]], W=[rstd_b[p]])

        def norm(t):
            sl = slice(t * TB, (t + 1) * TB)
            p = t % 2
            for c in range(NC_):
                j = cn["t1"] % 2
                cn["t1"] += 1
                j2 = cn["t2"] % 3
                cn["t2"] += 1
                s.op("dve", lambda e, c=c, sl=sl, j=j, p=p: e.tensor_tensor(out=t1[j][:], in0=xs[:, c, sl], in1=mean_sb[p][:], op=ALU.subtract),
                     R=[xs_b[c][t], mean_b[p]], W=[t1_b[j]])
                s.op("pool", lambda e, j=j, j2=j2, p=p: e.tensor_tensor(out=t2[j2][:], in0=t1[j][:], in1=rstd_sb[p][:], op=ALU.mult),
                     R=[t1_b[j], rstd_b[p]], W=[t2_b[j2]])
                s.op("act", lambda e, c=c, sl=sl, j2=j2: e.activation(out=xs[:, c, sl], in_=t2[j2][:], func=AF.Identity,
                                                                   scale=lnga[:, col0 + c:col0 + c + 1],
                                                                   bias=lnba[:, col0 + c:col0 + c + 1]),
                     R=[t2_b[j2], ln_c], W=[xs_b[c][t]])
                s.op("dve", lambda e, c=c, sl=sl, j2=j2: e.tensor_scalar(
                    out=xb[:, c, sl], in0=t2[j2][:], scalar1=lng[:, col0 + c:col0 + c + 1], scalar2=lnb[:, col0 + c:col0 + c + 1],
                    op0=ALU.mult, op1=ALU.add),
                    R=[t2_b[j2], ln_c], W=[xb_b[c][t]])

        stats(0)
        for t in range(NTB):
            if t + 1 < NTB:
                stats(t + 1)
            norm(t)

    def make_ln(l, i, arenas, sbanks=((0, 1), (2, 3)), lag=2):
        col0 = (l * 3 + i) * NC_

        def al(shape, dt, name):
            for a in arenas:
                n = 1
                for d in shape[1:]:
                    n *= d
                if a.off + (n * DT_SIZE[dt] + 63) // 64 * 64 <= a.top:
                    return a.alloc(shape, dt, name)
            raise RuntimeError("make_ln: no room for " + name)

        NSQ = 3
        sqr = [al([128, TB], F32R, "lsq") for _ in range(NSQ)]
        sqr_b = bufs(NSQ)
        mean_sb = [al([128, TB], F32, "lmean") for _ in range(2)]
        m2_sb = [al([128, TB], F32, "lm2") for _ in range(2)]
        rstd_sb = [al([128, TB], F32, "lrstd") for _ in range(2)]
        mean_b, m2_b, rstd_b = bufs(2), bufs(2), bufs(2)
        NT = 3
        t1 = [al([128, TB], F32, "lt1") for _ in range(NT)]
        t2 = [al([128, TB], F32, "lt2") for _ in range(NT)]
        t1_b, t2_b = bufs(NT), bufs(NT)
        cn = {"sq": 0, "t1": 0, "t2": 0, "seen": {}}
        pending = []
        avail = []
        fl = {"s1": None, "s2": None}

        def tick():
            if fl["s2"] is not None:
                c, t, j2 = fl["s2"]
                sl = slice(t * TB, (t + 1) * TB)
                s.op("act", lambda e, c=c, sl=sl, j2=j2: e.activation(out=xs[:, c, sl], in_=t2[j2][:], func=AF.Identity,
                                                                   scale=lnga[:, col0 + c:col0 + c + 1],
                                                                   bias=lnba[:, col0 + c:col0 + c + 1]),
                     R=[t2_b[j2], ln_c], W=[xs_b[c][t]])
                s.op("dve", lambda e, c=c, sl=sl, j2=j2: e.tensor_scalar(
                    out=xb[:, c, sl], in0=t2[j2][:], scalar1=lng[:, col0 + c:col0 + c + 1], scalar2=lnb[:, col0 + c:col0 + c + 1],
                    op0=ALU.mult, op1=ALU.add),
                    R=[t2_b[j2], ln_c], W=[xb_b[c][t]])
                fl["s2"] = None
            if fl["s1"] is not None:
                c, t, j = fl["s1"]
                p = t % 2
                j2 = cn["t2"] % NT
                cn["t2"] += 1
                s.op("pool", lambda e, j=j, j2=j2, p=p: e.tensor_tensor(out=t2[j2][:], in0=t1[j][:], in1=rstd_sb[p][:], op=ALU.mult),
                     R=[t1_b[j], rstd_b[p]], W=[t2_b[j2]])
                fl["s2"] = (c, t, j2)
                fl["s1"] = None
            if avail:
                c, t = avail.pop(0)
                sl = slice(t * TB, (t + 1) * TB)
                p = t % 2
                j = cn["t1"] % NT
                cn["t1"] += 1
                s.op("dve", lambda e, c=c, sl=sl, j=j, p=p: e.tensor_tensor(out=t1[j][:], in0=xs[:, c, sl], in1=mean_sb[p][:], op=ALU.subtract),
                     R=[xs_b[c][t], mean_b[p]], W=[t1_b[j]])
                fl["s1"] = (c, t, j)

        def emit(entry):
            dc, t, k = entry
            sl = slice(t * TB, (t + 1) * TB)
            p = t % 2
            bm, bq = sbanks[p]
            n = cn["seen"].get(t, 0)
            cn["seen"][t] = n + 1
            s.op("pe", lambda e, dc=dc, sl=sl, bm=bm, n=n: e.matmul(psum[bm][:], lhsT=ones, rhs=xs[:, dc, sl],
                                                                   start=(n == 0), stop=(n == NC_ - 1)),
                 R=[consts_b, xs_b[dc][t]], W=pb_(bm))
            s.op("pe", lambda e, k=k, bq=bq, n=n: e.matmul(psum[bq][:], lhsT=ones_r[:], rhs=sqr[k][:],
                                                           start=(n == 0), stop=(n == NC_ - 1)),
                 R=[onesr_b, sqr_b[k]], W=pb_(bq))
            if n == NC_ - 1:
                s.op("act", lambda e, bm=bm, p=p: e.activation(out=m2_sb[p][:], in_=psum[bm][:], func=AF.Square), R=pb_(bm), W=[m2_b[p]])
                s.op("act", lambda e, bm=bm, p=p: e.copy(out=mean_sb[p][:], in_=psum[bm][:]), R=pb_(bm), W=[mean_b[p]])
                s.op("dve", lambda e, bq=bq, p=p: e.tensor_tensor(out=rstd_sb[p][:], in0=psum[bq][:], in1=m2_sb[p][:], op=ALU.subtract),
                     R=pb_(bq) + [m2_b[p]], W=[rstd_b[p]])
                s.op("act", lambda e, p=p: e.activation(out=m2_sb[p][:], in_=rstd_sb[p][:], func=AF.Ln, bias=LN_EPS, scale=1.0),
                     R=[rstd_b[p]], W=[m2_b[p]])
                s.op("act", lambda e, p=p: e.activation(out=rstd_sb[p][:], in_=m2_sb[p][:], func=AF.Exp, scale=-0.5),
                     R=[m2_b[p]], W=[rstd_b[p]])
                avail.extend((c, t) for c in range(NC_))

        def chunk_done(dc, t):
            sl = slice(t * TB, (t + 1) * TB)
            k = cn["sq"] % NSQ
            cn["sq"] += 1
            s.op("act", lambda e, dc=dc, sl=sl, k=k: e.activation(out=sqr[k][:], in_=xs[:, dc, sl], func=AF.Square),
                 R=[xs_b[dc][t]], W=[sqr_b[k]])
            pending.append((dc, t, k))
            if len(pending) > lag:
                emit(pending.pop(0))
            tick()

        def flush():
            while pending:
                emit(pending.pop(0))
                tick()
            while avail or fl["s1"] is not None or fl["s2"] is not None:
                tick()

        return chunk_done, flush

    def ffn(l, i):
        w1_d = din(f"f{i}w1_{l}", [NF, 128, D])
        w3_d = din(f"f{i}w3_{l}", [NF, 128, D])
        w2_d = din(f"f{i}w2_{l}", [DFF, D])
        s.fence()
        A.release(arena_base)
        W13_SLOTS = 3
        w13 = [A.alloc([128, 2, D], BF16, "w13") for _ in range(W13_SLOTS)]
        w13_b = [bufs(2) for _ in range(W13_SLOTS)]
        w13_sem = [[new_dsem("w13") for _ in range(2)] for _ in range(W13_SLOTS)]
        w2 = [A.alloc([128, GMAX, D], BF16, "w2") for _ in range(2)]
        w2_b = bufs(2)
        w2_sem = [new_dsem("w2") for _ in range(2)]
        gT = [A.alloc([128, GMAX, S], BF16, "gT") for _ in range(2)]
        gT_b = [[bufs(NTB) for _ in range(GMAX)] for _ in range(2)]
        silu_t = [A.alloc([128, TB], F32, "silu") for _ in range(2)]
        silu_b = bufs(2)
        ln_chunk, ln_flush = make_ln(l, 0 if i == 0 else 2, [A])
        cnt = {"w13": 0, "w2": 0, "psA": 0, "psY": 0, "silu": 0}
        m0 = 0
        for gi, G in enumerate(FFN_GROUPS):
            ms = list(range(m0, m0 + G))
            m0 += G
            gs = gi % 2
            ws = cnt["w2"] % 2
            cnt["w2"] += 1
            s.op("pool", lambda e, ws=ws, ms=ms, G=G: e.dma_start(
                out=w2[ws][:, 0:G, :],
                in_=w2_d[ms[0] * 128:(ms[0] + G) * 128, :].rearrange("(g p) n -> p g n", p=128)),
                W=[w2_b[ws]], dsem=w2_sem[ws])
            for ml, m in enumerate(ms):
                slot = cnt["w13"] % W13_SLOTS
                cnt["w13"] += 1
                s.op("pool", lambda e, slot=slot, m=m: e.dma_start(out=w13[slot][:, 0, :], in_=w1_d[m]),
                     W=[w13_b[slot][0]], dsem=w13_sem[slot][0])
                s.op("pool", lambda e, slot=slot, m=m: e.dma_start(out=w13[slot][:, 1, :], in_=w3_d[m]),
                     W=[w13_b[slot][1]], dsem=w13_sem[slot][1])
                for t in range(NTB):
                    sl = slice(t * TB, (t + 1) * TB)
                    pj = cnt["psA"] % 2
                    cnt["psA"] += 1
                    for which, bi in ((0, pj), (1, 2 + pj)):
                        for k in range(NC_):
                            s.op("pe", lambda e, slot=slot, which=which, k=k, sl=sl, bi=bi: e.matmul(
                                psum[bi][:], lhsT=w13[slot][:, which, k * 128:(k + 1) * 128], rhs=xb[:, k, sl],
                                start=(k == 0), stop=(k == NC_ - 1)),
                                R=[w13_b[slot][which], xb_b[k][t]], W=pb_(bi))
                    sj = cnt["silu"] % 2
                    cnt["silu"] += 1
                    s.op("act", lambda e, pj=pj, sj=sj: e.activation(out=silu_t[sj][:], in_=psum[pj][:], func=AF.Silu),
                         R=pb_(pj), W=[silu_b[sj]])
                    s.op("dve", lambda e, pj=pj, sj=sj, gs=gs, ml=ml, sl=sl: e.tensor_tensor(
                        out=gT[gs][:, ml, sl], in0=psum[2 + pj][:], in1=silu_t[sj][:], op=ALU.mult),
                        R=pb_(2 + pj) + [silu_b[sj]], W=[gT_b[gs][ml][t]])
            last = (gi == len(FFN_GROUPS) - 1)
            order = [(dc, t) for t in range(NTB) for dc in range(NC_)] if last else [(dc, t) for dc in range(NC_) for t in range(NTB)]
            for dc, t in order:
                if True:
                    sl = slice(t * TB, (t + 1) * TB)
                    bi = 4 + cnt["psY"] % 2
                    cnt["psY"] += 1
                    for ml in range(G):
                        s.op("pe", lambda e, ws=ws, ml=ml, dc=dc, gs=gs, sl=sl, bi=bi, G=G: e.matmul(
                            psum[bi][:], lhsT=w2[ws][:, ml, dc * 128:(dc + 1) * 128], rhs=gT[gs][:, ml, sl],
                            start=(ml == 0), stop=(ml == G - 1)),
                            R=[w2_b[ws], gT_b[gs][ml][t]], W=pb_(bi))
                    s.op("dve", lambda e, bi=bi, dc=dc, sl=sl: e.scalar_tensor_tensor(
                        out=xs[:, dc, sl], in0=psum[bi][:], scalar=0.5, in1=xs[:, dc, sl],
                        op0=ALU.mult, op1=ALU.add),
                        R=pb_(bi) + [xs_b[dc][t]], W=[xs_b[dc][t]])
                    if last:
                        ln_chunk(dc, t)
        ln_flush()

    def mixer(l, stop=None):
        amask_d = din("amask", [8, 128, S])
        wglr_d = din(f"wglr_{l}", [128, NC_ * 16])
        wgqk_d = din(f"wgqk_{l}", [128, NC_ * 512])
        wgkv_d = din(f"wgkv_{l}", [128, NC_ * 768])
        wgr_d = din(f"wgr_{l}", [128, NC_ * 512])
        waqkv_d = din(f"waqkv_{l}", [4, 128, NC_ * 384])
        wgab_d = din(f"wgab_{l}", [NC_, 128, NC_ * 256])
        wap_d = din(f"wap_{l}", [NC_, 128, 4 * 128])
        wgp_d = din(f"wgp_{l}", [NC_, 128, 4 * 128])
        wo_d = din(f"wo_{l}", [NC_, 128, NC_ * 128])
        blk = amask_blocks
        s.fence()
        A.release(arena_base)
        o_gnT = A.alloc([128, 4, S], BF16, "o_gnT")
        o_gn_b = [bufs(NTB) for _ in range(4)]
        o_a_b = [bufs(NTB) for _ in range(8)]
        mix_base = A.mark()

        qT = A.alloc([128, 2, S], BF16, "qT")
        kT = A.alloc([128, 2, S], BF16, "kT")
        qT_b = [bufs(16) for _ in range(2)]
        kT_b = [bufs(16) for _ in range(2)]
        khat = A.alloc([128, 16, 256], BF16, "khat")
        khat_b = bufs(16)
        gv = A.alloc([128, 16, 512], BF16, "gv")
        gv_b = bufs(16)
        dec = A.alloc([128, 4, 32], F32, "dec")
        dec_b = bufs(NTB)
        gla_base = A.mark()
        wglr = A.alloc([128, NC_, 16], BF16, "wglr")
        wgqk = A.alloc([128, NC_, 512], BF16, "wgqk")
        wgkv = A.alloc([128, NC_, 768], BF16, "wgkv")
        wB_b = bufs(3)
        s.op("pool", lambda e: e.dma_start(out=wglr[:], in_=wglr_d.rearrange("p (k n) -> p k n", k=NC_)),
             W=[wB_b[0]], dsem=new_dsem("wB"))
        s.op("pool", lambda e: e.dma_start(out=wgqk[:], in_=wgqk_d.rearrange("p (k n) -> p k n", k=NC_)),
             W=[wB_b[1]], dsem=new_dsem("wB"))
        s.op("pool", lambda e: e.dma_start(out=wgkv[:], in_=wgkv_d.rearrange("p (k n) -> p k n", k=NC_)),
             W=[wB_b[2]], dsem=new_dsem("wB"))
        glrT = [A.alloc([16, TB], F32, "glrT") for _ in range(2)]
        glrT_b = bufs(2)
        z_sb = [A.alloc([128, 256], F32, "z") for _ in range(2)]
        z_b = bufs(2)
        la_sb = [A.alloc([128, 256], F32, "la") for _ in range(2)]
        la_b = bufs(2)
        Eb = A.alloc([128, 2, TB], F32, "Eb")
        Einv = A.alloc([128, 2, TB], F32, "Einv")
        E_b, Einv_b = bufs(2), bufs(2)
        Ft = A.alloc([128, 4, 256], F32, "Ft")
        F_b = bufs(4)
        zc = 0
        pc = 0
        for tb in range(NTB):
            sl = slice(tb * TB, (tb + 1) * TB)
            gj = tb % 2
            for k in range(NC_):
                s.op("pe", lambda e, k=k, sl=sl: e.matmul(psum[0][0:16, :], lhsT=wglr[:, k, :], rhs=xb[:, k, sl],
                                                         start=(k == 0), stop=(k == NC_ - 1)),
                     R=[wB_b[0], xb_b[k][tb]], W=pb_(0))
            s.op("act", lambda e, gj=gj: e.copy(out=glrT[gj][:], in_=psum[0][0:16, :]), R=pb_(0), W=[glrT_b[gj]])
            for jj in range(4):
                j = tb * 4 + jj
                zi = zc % 2
                zc += 1
                bz = 1 + zi
                s.op("pe", lambda e, gj=gj, jj=jj, bz=bz: e.matmul(
                    psum[bz][:, 0:256], lhsT=glrT[gj][:, jj * 128:(jj + 1) * 128], rhs=wgu[:, l, :],
                    start=True, stop=True),
                    R=[glrT_b[gj], mixc_b], W=pb_(bz, 0, 256))
                s.op("dve", lambda e, zi=zi, bz=bz: e.tensor_tensor(out=z_sb[zi][:], in0=psum[bz][:, 0:256], in1=bgu[:, l, :], op=ALU.add),
                     R=pb_(bz, 0, 256) + [mixc_b], W=[z_b[zi]])
                tsl = slice(j * 128, (j + 1) * 128)
                bi2 = 6 + pc % 2
                pc += 1
                for k in range(NC_):
                    s.op("pe", lambda e, k=k, tsl=tsl, bi2=bi2: e.matmul(
                        psum[bi2][:], lhsT=xb[:, k, tsl], rhs=wgkv[:, k, 256:768],
                        start=(k == 0), stop=(k == NC_ - 1)),
                        R=[wB_b[2], xb_b[k][tb]], W=pb_(bi2))
                s.op("dve", lambda e, j=j, bi2=bi2: e.tensor_copy(out=gv[:, j, :], in_=psum[bi2][:]),
                     R=pb_(bi2), W=[gv_b[j]])
                s.op("act", lambda e, zi=zi: e.activation(out=z_sb[zi][:], in_=z_sb[zi][:], func=AF.Exp, scale=-1.0),
                     R=[z_b[zi]], W=[z_b[zi]])
                s.op("act", lambda e, zi=zi: e.activation(out=la_sb[zi][:], in_=z_sb[zi][:], func=AF.Ln, bias=1.0, scale=1.0),
                     R=[z_b[zi]], W=[la_b[zi]])
                for ch in range(2):
                    s.op("pe", lambda e, zi=zi, ch=ch, jj=jj: e.matmul(
                        psum[3 + ch][:, jj * 128:(jj + 1) * 128], lhsT=la_sb[zi][:, ch * 128:(ch + 1) * 128], rhs=Umat,
                        start=True, stop=True),
                        R=[la_b[zi], consts_b], W=[pq[3 + ch][jj]])
                s.op("pe", lambda e, zi=zi: e.matmul(psum[5][:, 0:256], lhsT=Lmat, rhs=la_sb[zi][:], start=True, stop=True),
                     R=[la_b[zi], consts_b], W=pb_(5, 0, 256))
                s.op("act", lambda e, jj=jj: e.activation(out=Ft[:, jj, :], in_=psum[5][:, 0:256], func=AF.Exp),
                     R=pb_(5, 0, 256), W=[F_b[jj]])
            for ch in range(2):
                s.op("act", lambda e, ch=ch: e.activation(out=Eb[:, ch, :], in_=psum[3 + ch][:], func=AF.Exp),
                     R=pb_(3 + ch), W=[E_b[ch]])
                s.op("act", lambda e, ch=ch: e.activation(out=Einv[:, ch, :], in_=psum[3 + ch][:], func=AF.Exp, scale=-1.0),
                     R=pb_(3 + ch), W=[Einv_b[ch]])
            for dup in range(2):
                s.op("dve", lambda e, tb=tb, dup=dup: e.tensor_copy(
                    out=dec[:].rearrange("p (c d) n -> p c d n", d=2)[:, :, dup, tb * 8:(tb + 1) * 8], in_=Eb[:, :, 63::64]),
                    R=E_b, W=[dec_b[tb]])
            for m in range(4):
                ch = m % 2
                bi = 6 + pc % 2
                pc += 1
                for k in range(NC_):
                    s.op("pe", lambda e, m=m, k=k, sl=sl, bi=bi: e.matmul(
                        psum[bi][:], lhsT=wgqk[:, k, m * 128:(m + 1) * 128], rhs=xb[:, k, sl],
                        start=(k == 0), stop=(k == NC_ - 1)),
                        R=[wB_b[1], xb_b[k][tb]], W=pb_(bi))
                if m < 2:
                    s.op("dve", lambda e, ch=ch, sl=sl, bi=bi: e.scalar_tensor_tensor(
                        out=qT[:, ch, sl], in0=psum[bi][:], scalar=0.125, in1=Eb[:, ch, :], op0=ALU.mult, op1=ALU.mult),
                        R=pb_(bi) + [E_b[ch]], W=qT_b[ch][tb * 4:(tb + 1) * 4])
                else:
                    s.op("dve", lambda e, ch=ch, sl=sl, bi=bi: e.tensor_tensor(
                        out=kT[:, ch, sl], in0=psum[bi][:], in1=Einv[:, ch, :], op=ALU.mult),
                        R=pb_(bi) + [Einv_b[ch]], W=kT_b[ch][tb * 4:(tb + 1) * 4])
            for jj in range(4):
                j = tb * 4 + jj
                tsl = slice(j * 128, (j + 1) * 128)
                bi = 1 + jj % 2
                for k in range(NC_):
                    s.op("pe", lambda e, k=k, tsl=tsl, bi=bi: e.matmul(
                        psum[bi][:, 0:256], lhsT=xb[:, k, tsl], rhs=wgkv[:, k, 0:256],
                        start=(k == 0), stop=(k == NC_ - 1)),
                        R=[wB_b[2], xb_b[k][tb]], W=pb_(bi, 0, 256))
                s.op("dve", lambda e, j=j, jj=jj, bi=bi: e.tensor_tensor(
                    out=khat[:, j, :], in0=psum[bi][:, 0:256], in1=Ft[:, jj, :], op=ALU.mult),
                    R=pb_(bi, 0, 256) + [F_b[jj]], W=[khat_b[j]])

        if stop == "B":
            return
        tap("qT", qT, [b for bb in qT_b for b in bb])
        tap("kT", kT, [b for bb in kT_b for b in bb])
        tap("khat", khat, khat_b)
        tap("gv", gv, gv_b)
        tap("dec", dec, dec_b)
        s.fence()
        A.release(gla_base)
        wgr = A.alloc([128, NC_, 512], BF16, "wgr")
        wgr_b = Buf()
        s.op("pool", lambda e: e.dma_start(out=wgr[:], in_=wgr_d.rearrange("p (k n) -> p k n", k=NC_)),
             W=[wgr_b], dsem=new_dsem("wgr"))
        ograw = [A.alloc([128, 4, TB], F32, "ograw") for _ in range(2)]
        ograw_b = [[bufs(4) for _ in range(4)] for _ in range(2)]
        st_f = A.alloc([128, 4, 128], F32, "st_f")
        st_b = A.alloc([128, 4, 128], BF16, "st_b")
        stf_b, stb_b = bufs(4), bufs(4)
        ATt = [A.alloc([128, 4, 128], BF16, "AT") for _ in range(2)]
        AT_b = [bufs(4) for _ in range(2)]
        sg = [A.alloc([128, TB], F32, "sg") for _ in range(4)]
        sg_b = bufs(4)
        gsq = A.alloc([128, TB], F32R, "gsq")
        gsq_b = Buf()
        gm2 = A.alloc([128, TB], F32, "gm2")
        grs = A.alloc([128, TB], F32, "grs")
        gm2_b, grs_b = Buf(), Buf()
        s.op("dve", lambda e: e.memset(st_f[:], 0.0), W=stf_b)
        s.op("dve", lambda e: e.memset(st_b[:], 0.0), W=stb_b)
        bS0, bS1 = 4, 5
        HORD = (0, 2, 1, 3)

        def emit_AT_mm(j):
            bA = j % 2
            tok = slice(j * 128, (j + 1) * 128)
            prev = None
            for h in HORD:
                ch, pb = h // 2, (h % 2) * 64
                hs = slice(h * 128, (h + 1) * 128)
                prev = s.op("pe", lambda e, ch=ch, pb=pb, hs=hs, tok=tok, bA=bA: e.matmul(
                    psum[bA][:, hs], lhsT=kT[pb:pb + 64, ch, tok], rhs=qT[pb:pb + 64, ch, tok], start=True, stop=True),
                    R=[kT_b[ch][j], qT_b[ch][j]], W=[pq[bA][h]], after=[prev] if h == 1 else [])

        def emit_AT_mask(j):
            bA = j % 2
            aj = j % 2
            s.op("dve", lambda e, bA=bA, aj=aj: e.tensor_tensor(
                out=ATt[aj][:], in0=psum[bA][:].rearrange("p (h n) -> p h n", h=4),
                in1=Mblk.unsqueeze(1).broadcast_to([128, 4, 128]), op=ALU.mult),
                R=[pq[bA][0], consts_b], W=AT_b[aj])

        def emit_dS(j, half):
            bS = bS0 if half == 0 else bS1
            rows = slice(half * 64, half * 64 + 64)
            for h in range(4):
                ch = h // 2
                hs = slice(h * 128, (h + 1) * 128)
                s.op("pe", lambda e, ch=ch, hs=hs, rows=rows, bS=bS, j=j: e.matmul(
                    psum[bS][:, hs], lhsT=khat[rows, j, ch * 128:(ch + 1) * 128], rhs=gv[rows, j, hs],
                    start=True, stop=True),
                    R=[khat_b[j], gv_b[j]], W=[pq[bS][h]])

        def emit_decay(c):
            s.op("dve", lambda e, c=c: e.tensor_tensor(
                out=st_f[:], in0=st_f[:], in1=dec[:, :, c:c + 1].broadcast_to([128, 4, 128]), op=ALU.mult),
                R=stf_b + [dec_b[c // 8]], W=stf_b)

        def emit_update(j, half):
            bS = bS0 if half == 0 else bS1
            c = 2 * j + half
            s.op("dve", lambda e, bS=bS: e.tensor_tensor(
                out=st_f[:], in0=st_f[:], in1=psum[bS][:].rearrange("p (h n) -> p h n", h=4), op=ALU.add),
                R=stf_b + [pq[bS][0]], W=stf_b)
            s.op("dve", lambda e: e.tensor_copy(out=st_b[:], in_=st_f[:]), R=stf_b, W=stb_b)
            if c + 1 < 32:
                emit_decay(c + 1)

        gtasks = []
        gstate = {"B": None}

        def gate_batch(tb):
            sl = slice(tb * TB, (tb + 1) * TB)
            for h in range(4):
                bi = 6 + h % 2
                for k in range(NC_):
                    s.op("pe", lambda e, k=k, h=h, sl=sl, bi=bi: e.matmul(
                        psum[bi][:], lhsT=wgr[:, k, h * 128:(h + 1) * 128], rhs=xb[:, k, sl],
                        start=(k == 0), stop=(k == NC_ - 1)),
                        R=[wgr_b, xb_b[k][tb]], W=pb_(bi))
                s.op("act", lambda e, h=h, bi=bi: e.activation(out=sg[h][:], in_=psum[bi][:], func=AF.Silu),
                     R=pb_(bi), W=[sg_b[h]])
            for h in range(4):
                gtasks.append((tb, h))

        def gn_A(tb, h):
            ob = tb % 2
            og = ograw[ob][:, h, :]
            ogb = ograw_b[ob][h]
            s.op("act", lambda e, og=og: e.activation(out=gsq[:], in_=og, func=AF.Square), R=ogb, W=[gsq_b])
            s.op("pe", lambda e, og=og: e.matmul(psum[6][:], lhsT=gones, rhs=og, start=True, stop=True),
                 R=ogb + [consts_b], W=pb_(6))
            s.op("pe", lambda e: e.matmul(psum[7][:], lhsT=gones_r[:], rhs=gsq[:], start=True, stop=True),
                 R=[gsq_b, onesr_b], W=pb_(7))
            s.op("act", lambda e: e.activation(out=gm2[:], in_=psum[6][:], func=AF.Square), R=pb_(6), W=[gm2_b])
            s.op("dve", lambda e, og=og: e.tensor_tensor(out=og, in0=og, in1=psum[6][:], op=ALU.subtract),
                 R=ogb + pb_(6), W=ogb)
            s.op("dve", lambda e: e.tensor_tensor(out=grs[:], in0=psum[7][:], in1=gm2[:], op=ALU.subtract),
                 R=pb_(7) + [gm2_b], W=[grs_b])
            s.op("act", lambda e: e.activation(out=gm2[:], in_=grs[:], func=AF.Ln, bias=LN_EPS, scale=1.0),
                 R=[grs_b], W=[gm2_b])
            s.op("act", lambda e: e.activation(out=grs[:], in_=gm2[:], func=AF.Exp, scale=-0.5), R=[gm2_b], W=[grs_b])

        def gn_B(tb, h):
            ob = tb % 2
            og = ograw[ob][:, h, :]
            ogb = ograw_b[ob][h]
            col = l * 4 + h
            sl = slice(tb * TB, (tb + 1) * TB)
            s.op("pool", lambda e, og=og: e.tensor_tensor(out=og, in0=og, in1=grs[:], op=ALU.mult),
                 R=ogb + [grs_b], W=ogb)
            s.op("act", lambda e, og=og, col=col: e.activation(out=og, in_=og, func=AF.Identity,
                                                             scale=gng[:, col:col + 1], bias=gnb[:, col:col + 1]),
                 R=ogb + [mixc_b], W=ogb)
            s.op("pool", lambda e, og=og, h=h, sl=sl: e.tensor_tensor(out=o_gnT[:, h, sl], in0=og, in1=sg[h][:], op=ALU.mult),
                 R=ogb + [sg_b[h]], W=[o_gn_b[h][tb]])

        def gn_slot():
            if gstate["B"] is not None:
                gn_B(*gstate["B"])
                gstate["B"] = None
            if gtasks:
                t_ = gtasks.pop(0)
                gn_A(*t_)
                gstate["B"] = t_

        def gn_flush():
            while gtasks or gstate["B"] is not None:
                gn_slot()

        emit_AT_mm(0)
        emit_dS(0, 0)
        emit_dS(0, 1)
        emit_AT_mask(0)
        for j in range(16):
            tb, jj = j // 4, j % 4
            ob = tb % 2
            aj = j % 2
            bO = 2 + aj
            t0 = slice(j * 128, j * 128 + 64)
            t1_ = slice(j * 128 + 64, (j + 1) * 128)
            for h in range(4):
                hs = slice(h * 128, (h + 1) * 128)
                s.op("pe", lambda e, h=h, hs=hs, bO=bO, aj=aj, j=j: e.matmul(
                    psum[bO][:, hs], lhsT=gv[:, j, hs], rhs=ATt[aj][:, h, :], start=(h == 0), stop=False, skip_group_check=True),
                    R=[gv_b[j], AT_b[aj][h]], W=[pq[bO][h]])
            prev = None
            for h in HORD:
                ch, pb = h // 2, (h % 2) * 64
                prev = s.op("pe", lambda e, h=h, ch=ch, pb=pb, bO=bO, t0=t0: e.matmul(
                    psum[bO][:, h * 128:h * 128 + 64], lhsT=st_b[pb:pb + 64, h, :], rhs=qT[pb:pb + 64, ch, t0],
                    start=False, stop=False, skip_group_check=True),
                    R=[stb_b[h], qT_b[ch][j]], W=[pq[bO][h]], after=[prev] if h == 1 else [])
            if j + 1 < 16:
                emit_AT_mm(j + 1)
            emit_update(j, 0)
            prev = None
            for h in HORD:
                ch, pb = h // 2, (h % 2) * 64
                prev = s.op("pe", lambda e, h=h, ch=ch, pb=pb, bO=bO, t1_=t1_: e.matmul(
                    psum[bO][:, h * 128 + 64:(h + 1) * 128], lhsT=st_b[pb:pb + 64, h, :], rhs=qT[pb:pb + 64, ch, t1_],
                    start=False, stop=True, skip_group_check=True),
                    R=[stb_b[h], qT_b[ch][j]], W=[pq[bO][h]], after=[prev] if h == 1 else [])
            if j + 1 < 16:
                emit_AT_mask(j + 1)
                emit_dS(j + 1, 0)
            s.op("act", lambda e, bO=bO, ob=ob, jj=jj: e.copy(
                out=ograw[ob][:, :, jj * 128:(jj + 1) * 128], in_=psum[bO][:].rearrange("p (h n) -> p h n", h=4)),
                R=[pq[bO][0]], W=[ograw_b[ob][h][jj] for h in range(4)])
            emit_update(j, 1)
            if j + 1 < 16:
                emit_dS(j + 1, 1)
            if j == 0:
                tap("st1", st_f, stf_b)
            gn_slot()
            if jj == 3:
                gn_flush()
                if tb == 0:
                    tap("ograw0", ograw[0], [b for bb in ograw_b[0] for b in bb])
                gate_batch(tb)
        gn_flush()

        if stop == "D":
            return
        tap("o_gnT", o_gnT, [b for bb in o_gn_b for b in bb])
        s.fence()
        A.release(mix_base)
        o_aT = A.alloc([128, 4, S], BF16, "o_aT")
        mix_base = A.mark()
        waqkv = A.alloc([128, NC_, 384], BF16, "waqkv")
        waqkv_b = Buf()
        waqkv_sem = new_dsem("waqkv")
        aqT = [[A.alloc([128, S], BF16, "aqT") for _ in range(2)] for _ in range(2)]
        akT = [A.alloc([128, S], BF16, "akT") for _ in range(2)]
        aqT_b = [bufs(NTB) for _ in range(2)]
        aqz_b = [bufs(2) for _ in range(2)]
        for wi_ in range(2):
            for hh_ in range(2):
                oth = slice(64, 128) if hh_ == 0 else slice(0, 64)
                s.op("pool", lambda e, wi_=wi_, hh_=hh_, oth=oth: e.memset(aqT[wi_][hh_][oth, :], 0.0), W=[aqz_b[wi_][hh_]])
        akT_b = [bufs(16) for _ in range(2)]
        Vp = [A.alloc([128, 16, 128], BF16, "Vp") for _ in range(2)]
        Vp_b = [bufs(16) for _ in range(2)]
        mask_s = [A.alloc([128, S], F32, "mask") for _ in range(2)]
        mask_b = bufs(2)
        mask_sem = [new_dsem("mask") for _ in range(2)]
        NE = 4
        LOOK = 3
        Et = [A.alloc([128, TB], F32, "Et") for _ in range(NE)]
        Pt = [A.alloc([128, TB], BF16, "Pt") for _ in range(NE)]
        Et_b, Pt_b = bufs(NE), bufs(NE)
        dcp = [A.alloc([128, TB], F32, "dcp") for _ in range(1)] * 2
        dcp_b = bufs(1) * 2
        onesb = A.alloc([128, 128], BF16, "onesb")
        onesb_b = Buf()
        s.op("pool", lambda e: e.memset(onesb[:], 1.0), W=[onesb_b])
        SB = (0, 1, 2, 3)
        stepc = 0
        def load_waqkv(ch):
            s.op("pool", lambda e, ch=ch: e.dma_start(out=waqkv[:], in_=waqkv_d[ch].rearrange("p (k n) -> p k n", k=NC_)),
                 W=[waqkv_b], dsem=waqkv_sem)

        def load_mask(h):
            mi = h % 2
            s.op("sp", lambda e, mi=mi, h=h: e.dma_start(out=mask_s[mi][:], in_=amask_d[h]),
                 W=[mask_b[mi]], dsem=mask_sem[mi])

        load_waqkv(0)
        load_mask(0)
        load_mask(1)
        for ch in range(4):
            wi = ch % 2
            for tb in range(NTB):
                sl = slice(tb * TB, (tb + 1) * TB)
                for which in range(2):
                    bi = SB[(2 * tb + which) % 4]
                    for k in range(NC_):
                        s.op("pe", lambda e, k=k, which=which, sl=sl, bi=bi: e.matmul(
                            psum[bi][:], lhsT=waqkv[:, k, which * 128:(which + 1) * 128], rhs=xb[:, k, sl],
                            start=(k == 0), stop=(k == NC_ - 1)),
                            R=[waqkv_b, xb_b[k][tb]], W=pb_(bi))
                    if which == 0:
                        s.op("act", lambda e, wi=wi, sl=sl, bi=bi: e.mul(aqT[wi][0][0:64, sl], psum[bi][0:64, :], 0.125),
                             R=pb_(bi), W=[aqT_b[wi][tb]])
                        s.op("act", lambda e, wi=wi, sl=sl, bi=bi: e.mul(aqT[wi][1][64:128, sl], psum[bi][64:128, :], 0.125),
                             R=pb_(bi), W=[aqT_b[wi][tb]])
                    else:
                        s.op("dve", lambda e, wi=wi, sl=sl, bi=bi: e.tensor_copy(out=akT[wi][:, sl], in_=psum[bi][:]),
                             R=pb_(bi), W=akT_b[wi][tb * 4:(tb + 1) * 4])
            for j4 in range(4):
                bi = SB[j4 % 4]
                for jj in range(4):
                    j = j4 * 4 + jj
                    tsl = slice(j * 128, (j + 1) * 128)
                    for k in range(NC_):
                        s.op("pe", lambda e, k=k, tsl=tsl, bi=bi, jj=jj: e.matmul(
                            psum[bi][:, jj * 128:(jj + 1) * 128], lhsT=xb[:, k, tsl], rhs=waqkv[:, k, 256:384],
                            start=(k == 0 and jj == 0), stop=(k == NC_ - 1), skip_group_check=True),
                            R=[waqkv_b, xb_b[k][j4]], W=pb_(bi))
                s.op("act", lambda e, wi=wi, j4=j4, bi=bi: e.copy(
                    out=Vp[wi][:, j4 * 4:(j4 + 1) * 4, :], in_=psum[bi][:].rearrange("p (a b) -> p a b", a=4)),
                    R=pb_(bi), W=Vp_b[wi][j4 * 4:(j4 + 1) * 4])
            if ch + 1 < 4:
                load_waqkv(ch + 1)
            steps = []
            for hh in range(2):
                h = 2 * ch + hh
                for qp in range(NTB):
                    kbs = [kb for kb in range(4 * qp + 4) if blk[h][kb][qp]]
                    for ki, kb in enumerate(kbs):
                        steps.append((hh, h, qp, kb, ki, len(kbs)))
            info = {}
            for idx in range(len(steps) + LOOK):
                if idx < len(steps):
                    hh, h, qp, kb, ki, nk = steps[idx]
                    pb = hh * 64
                    mi = h % 2
                    q0 = qp * TB
                    n0 = max(q0, 128 * kb)
                    n = q0 + TB - n0
                    bS = SB[stepc % 4]
                    ei = stepc % NE
                    stepc += 1
                    info[idx] = (ei, n0, n)
                    s.op("pe", lambda e, wi=wi, hh=hh, kb=kb, n0=n0, n=n, bS=bS: e.matmul(
                        psum[bS][:, 0:n], lhsT=akT[wi][:, kb * 128:(kb + 1) * 128],
                        rhs=aqT[wi][hh][:, n0:n0 + n], start=True, stop=True),
                        R=[akT_b[wi][kb], aqT_b[wi][qp], aqz_b[wi][hh]], W=pb_(bS))
                    s.op("act", lambda e, ei=ei, bS=bS, n=n: e.activation(out=Et[ei][:, 0:n], in_=psum[bS][:, 0:n], func=AF.Exp),
                         R=pb_(bS), W=[Et_b[ei]])
                    mo = n0 - 128 * kb
                    s.op("dve", lambda e, ei=ei, mi=mi, mo=mo, n=n: e.tensor_tensor(
                        out=Pt[ei][:, 0:n], in0=Et[ei][:, 0:n], in1=mask_s[mi][:, mo:mo + n], op=ALU.mult),
                        R=[Et_b[ei], mask_b[mi]], W=[Pt_b[ei]])
                    if h + 2 < 8 and (idx + 1 == len(steps) or steps[idx + 1][1] != h):
                        load_mask(h + 2)
                pidx = idx - LOOK
                if pidx >= 0:
                    hh, h, qp, kb, ki, nk = steps[pidx]
                    ei, n0, n = info[pidx]
                    pb = hh * 64
                    q0 = qp * TB
                    par = (h * NTB + qp) % 2
                    bO, bD = 4 + par, 6 + par
                    cs = slice(n0 - q0, n0 - q0 + n)
                    s.op("pe", lambda e, wi=wi, kb=kb, ei=ei, n=n, cs=cs, bO=bO, ki=ki, nk=nk: e.matmul(
                        psum[bO][:, cs], lhsT=Vp[wi][:, kb, :], rhs=Pt[ei][:, 0:n],
                        start=(ki == 0), stop=(ki == nk - 1), skip_group_check=True),
                        R=[Vp_b[wi][kb], Pt_b[ei]], W=pb_(bO))
                    s.op("pe", lambda e, ei=ei, n=n, cs=cs, bD=bD, ki=ki, nk=nk: e.matmul(
                        psum[bD][:, cs], lhsT=onesb[:], rhs=Pt[ei][:, 0:n],
                        start=(ki == 0), stop=(ki == nk - 1), skip_group_check=True),
                        R=[onesb_b, Pt_b[ei]], W=pb_(bD))
                    if ki == nk - 1:
                        ps_ = slice(pb, pb + 64)
                        s.op("act", lambda e, par=par, bD=bD, ps_=ps_: e.activation(out=dcp[par][ps_, :], in_=psum[bD][ps_, :], func=AF.Ln),
                             R=pb_(bD), W=[dcp_b[par]])
                        s.op("act", lambda e, par=par, ps_=ps_: e.activation(out=dcp[par][ps_, :], in_=dcp[par][ps_, :], func=AF.Exp, scale=-1.0),
                             R=[dcp_b[par]], W=[dcp_b[par]])
                        s.op("dve", lambda e, ch=ch, ps_=ps_, q0=q0, bO=bO, par=par: e.tensor_tensor(
                            out=o_aT[ps_, ch, q0:q0 + TB], in0=psum[bO][ps_, :], in1=dcp[par][ps_, :], op=ALU.mult),
                            R=pb_(bO) + [dcp_b[par]], W=[o_a_b[h][qp]])

        if stop == "C":
            return
        tap("o_aT", o_aT, [b for bb in o_a_b for b in bb])
        s.fence()
        A.release(mix_base)
        e_low_top = A.mark()
        mT = A.alloc([128, NC_, S], BF16, "mT")
        mT_b = [bufs(NTB) for _ in range(NC_)]
        e_base = A.mark()
        wE = [A.alloc([128, NC_, 256], BF16, "wgab") for _ in range(2)]
        wap = [A.alloc([128, 4, 128], BF16, "wap") for _ in range(2)]
        wgp = [A.alloc([128, 4, 128], BF16, "wgp") for _ in range(2)]
        wE_b = [bufs(3) for _ in range(2)]
        wE_sem = [[new_dsem("wE") for _ in range(3)] for _ in range(2)]
        sa = [A.alloc([128, TB], F32, "sa") for _ in range(2)]
        sbt = [A.alloc([128, TB], F32, "sbt") for _ in range(2)]
        sa_b, sbt_b = bufs(2), bufs(2)
        wo = A.alloc([128, NC_, NC_, 128], BF16, "wo")
        wo_b = bufs(NC_)
        e_top = A.mark()
        ecn = 0

        def load_wE(dc):
            wi = dc % 2
            s.op("pool", lambda e, wi=wi, dc=dc: e.dma_start(out=wE[wi][:], in_=wgab_d[dc].rearrange("p (k n) -> p k n", k=NC_)),
                 W=[wE_b[wi][0]], dsem=wE_sem[wi][0])
            s.op("pool", lambda e, wi=wi, dc=dc: e.dma_start(out=wap[wi][:], in_=wap_d[dc].rearrange("p (k n) -> p k n", k=4)),
                 W=[wE_b[wi][1]], dsem=wE_sem[wi][1])
            s.op("pool", lambda e, wi=wi, dc=dc: e.dma_start(out=wgp[wi][:], in_=wgp_d[dc].rearrange("p (k n) -> p k n", k=4)),
                 W=[wE_b[wi][2]], dsem=wE_sem[wi][2])

        load_wE(0)
        for dc in range(NC_):
            wi = dc % 2
            if dc + 1 < NC_:
                load_wE(dc + 1)
            if dc >= 4:
                for dco in (2 * (dc - 4), 2 * (dc - 4) + 1):
                    s.op("pool", lambda e, dco=dco: e.dma_start(out=wo[:, dco, :, :], in_=wo_d[dco].rearrange("p (k n) -> p k n", k=NC_)),
                         W=[wo_b[dco]], dsem=new_dsem("wo"))
            for tb in range(NTB):
                sl = slice(tb * TB, (tb + 1) * TB)
                pj = ecn % 2
                ecn += 1
                bGA, bGB, bPA, bPG = 0 + pj, 2 + pj, 4 + pj, 6 + pj
                for which, bi in ((0, bGA), (1, bGB)):
                    for k in range(NC_):
                        s.op("pe", lambda e, wi=wi, which=which, k=k, sl=sl, bi=bi: e.matmul(
                            psum[bi][:], lhsT=wE[wi][:, k, which * 128:(which + 1) * 128], rhs=xb[:, k, sl],
                            start=(k == 0), stop=(k == NC_ - 1)),
                            R=[wE_b[wi][0], xb_b[k][tb]], W=pb_(bi))
                for c in range(4):
                    s.op("pe", lambda e, wi=wi, c=c, sl=sl, bPA=bPA: e.matmul(
                        psum[bPA][:], lhsT=wap[wi][:, c, :], rhs=o_aT[:, c, sl], start=(c == 0), stop=(c == 3)),
                        R=[wE_b[wi][1], o_a_b[2 * c][tb], o_a_b[2 * c + 1][tb]], W=pb_(bPA))
                for c in range(4):
                    s.op("pe", lambda e, wi=wi, c=c, sl=sl, bPG=bPG: e.matmul(
                        psum[bPG][:], lhsT=wgp[wi][:, c, :], rhs=o_gnT[:, c, sl], start=(c == 0), stop=(c == 3)),
                        R=[wE_b[wi][2], o_gn_b[c][tb]], W=pb_(bPG))
                s.op("act", lambda e, pj=pj, bGA=bGA: e.activation(out=sa[pj][:], in_=psum[bGA][:], func=AF.Sigmoid),
                     R=pb_(bGA), W=[sa_b[pj]])
                s.op("act", lambda e, pj=pj, bGB=bGB: e.activation(out=sbt[pj][:], in_=psum[bGB][:], func=AF.Sigmoid),
                     R=pb_(bGB), W=[sbt_b[pj]])
                s.op("dve", lambda e, pj=pj, bPA=bPA: e.tensor_tensor(out=sa[pj][:], in0=psum[bPA][:], in1=sa[pj][:], op=ALU.mult),
                     R=pb_(bPA) + [sa_b[pj]], W=[sa_b[pj]])
                s.op("dve", lambda e, pj=pj, bPG=bPG: e.tensor_tensor(out=sbt[pj][:], in0=psum[bPG][:], in1=sbt[pj][:], op=ALU.mult),
                     R=pb_(bPG) + [sbt_b[pj]], W=[sbt_b[pj]])
                s.op("pool", lambda e, pj=pj, dc=dc, sl=sl: e.tensor_tensor(out=mT[:, dc, sl], in0=sa[pj][:], in1=sbt[pj][:], op=ALU.add),
                     R=[sa_b[pj], sbt_b[pj]], W=[mT_b[dc][tb]])
        tap("mT", mT, [b for bb in mT_b for b in bb])
        s.fence()
        A.release(e_top)
        Alow = Arena(nc, arena_base, e_low_top)
        ln_chunk, ln_flush = make_ln(l, 1, [Alow, A])
        yc = 0
        for tb in range(NTB):
            sl = slice(tb * TB, (tb + 1) * TB)
            for dc in range(NC_):
                bi = 4 + yc % 2
                yc += 1
                for k in range(NC_):
                    s.op("pe", lambda e, dc=dc, k=k, sl=sl, bi=bi: e.matmul(
                        psum[bi][:], lhsT=wo[:, dc, k, :], rhs=mT[:, k, sl], start=(k == 0), stop=(k == NC_ - 1)),
                        R=[wo_b[dc], mT_b[k][tb]], W=pb_(bi))
                s.op("dve", lambda e, bi=bi, dc=dc, sl=sl: e.tensor_tensor(
                    out=xs[:, dc, sl], in0=psum[bi][:], in1=xs[:, dc, sl], op=ALU.add),
                    R=pb_(bi) + [xs_b[dc][tb]], W=[xs_b[dc][tb]])
                ln_chunk(dc, tb)
        ln_flush()

    amask_blocks = _mask_blocks()
    for ph in phases:
        if ph[0] == "ffn":
            ffn(ph[1], ph[2])
        elif ph[0] == "mix":
            mixer(ph[1], ph[2] if len(ph) > 2 else None)
        else:
            raise ValueError(ph)

    for c in range(NC_):
        for t in range(NTB):
            sl = slice(t * TB, (t + 1) * TB)
            s.op("dve", lambda e, c=c, sl=sl: e.tensor_scalar_mul(out=xs[:, c, sl], in0=xs[:, c, sl], scalar1=1.0 / ALPHA),
                 R=[xs_b[c][t]], W=[xs_b[c][t]])
    last = None
    for c in range(NC_):
        last = s.op("sp", lambda e, c=c: e.dma_start(out=yT_d[c * 128:(c + 1) * 128, :], in_=xs[:, c, :]),
                    R=xs_b[c], dsem=out_sem)
    fin = Buf()
    fin.w = last
    s.op("sp", lambda e: e.nop(), R=[fin])

    s.finalize()
    from contextlib import ExitStack
    with ExitStack() as ctx:
        esem = {}
        for en in Sched.ENGS:
            esem[en] = ctx.enter_context(nc.semaphore(f"sem_{en}"))
        dsems = {}
        for nm in dsem_names:
            dsems[nm] = ctx.enter_context(nc.semaphore(f"d_{nm}"))
        with nc.Block() as block:
            @block.tensor
            def _(e):
                s.replay("pe", e, esem, dsems)

            @block.scalar
            def _(e):
                s.replay("act", e, esem, dsems)

            @block.vector
            def _(e):
                s.replay("dve", e, esem, dsems)

            @block.gpsimd
            def _(e):
                s.replay("pool", e, esem, dsems)

            @block.sync
            def _(e):
                s.replay("sp", e, esem, dsems)
    return nc


_MASK = None


def _alibi_mask():
    global _MASK
    if _MASK is None:
        d = np.arange(S)[None, :] - np.arange(128)[:, None]
        mult = ((d <= 128).astype(np.float64) + ((d % 4 == 0) & (d <= 512)) + ((d % 16 == 0) & (d <= 2048)))
        mult = np.where(d >= 0, mult, 0.0)
        slopes = np.exp2(-8.0 * np.arange(1, 9) / 8.0)
        m = mult[None] * np.exp(-slopes[:, None, None] * np.maximum(d, 0)[None])
        m = np.where(m < 1e-37, 0.0, m)
        _MASK = np.ascontiguousarray(m.astype(np.float32))
    return _MASK


def _mask_blocks():
    m = _alibi_mask()
    blk = [[[False] * NTB for _ in range(16)] for _ in range(8)]
    for h in range(8):
        for kb in range(16):
            for qp in range(NTB):
                n0 = max(qp * TB, 128 * kb)
                n1 = qp * TB + TB
                if n1 <= n0:
                    continue
                blk[h][kb][qp] = bool(m[h][:, n0 - 128 * kb:n1 - 128 * kb].any())
    return blk


def _consts():
    s_ = np.arange(128)[:, None]
    t_ = np.arange(128)[None, :]
    same = (s_ // 64) == (t_ // 64)
    U = np.where(same & (s_ <= t_), -1.0 / 16.0, 0.0)
    L = np.where(same & (s_ > t_), -1.0 / 16.0, 0.0)
    M = np.where(same & (s_ <= t_), 1.0, 0.0)
    o1 = np.full((128, 128), 1.0 / D)
    o2 = np.full((128, 128), 1.0 / 128.0)
    return np.ascontiguousarray(np.concatenate([U, L, M, o1, o2], axis=1).astype(np.float32))


def _lay_w13(w):
    return np.ascontiguousarray(w.reshape(NC_, 128, NF, 128).transpose(2, 1, 0, 3).reshape(NF, 128, D))


def _lay_ln(v):
    return np.ascontiguousarray(v.reshape(DEPTH, 3, NC_, 128).transpose(3, 0, 1, 2).reshape(128, NL3))


def _lay_cols(w):
    n = w.shape[1]
    return np.ascontiguousarray(w.reshape(NC_, 128, n).transpose(1, 0, 2).reshape(128, NC_ * n))


def make_inputs(phases, inp):
    m = {"ln_g": _lay_ln(inp["ln_g"]), "ln_b": _lay_ln(inp["ln_b"]), "consts": _consts()}
    has_mix = any(p[0] == "mix" for p in phases)
    if has_mix:
        m["amask"] = _alibi_mask()
        m["wgu"] = np.ascontiguousarray(inp["w_gate_up"].transpose(1, 0, 2))
        m["bgu"] = np.ascontiguousarray(np.broadcast_to(inp["b_gate_up"][None], (128, DEPTH, 256)))
        m["gng"] = np.ascontiguousarray(inp["gla_norm_g"].reshape(DEPTH, 4, 128).transpose(2, 0, 1).reshape(128, DEPTH * 4))
        m["gnb"] = np.ascontiguousarray(inp["gla_norm_b"].reshape(DEPTH, 4, 128).transpose(2, 0, 1).reshape(128, DEPTH * 4))
    for ph in phases:
        if ph[0] == "ffn":
            l, i = ph[1], ph[2]
            pre = "ffn1" if i == 0 else "ffn2"
            m[f"f{i}w1_{l}"] = _lay_w13(inp[pre + "_w1"][l])
            m[f"f{i}w3_{l}"] = _lay_w13(inp[pre + "_w3"][l])
            m[f"f{i}w2_{l}"] = np.ascontiguousarray(inp[pre + "_w2"][l])
        else:
            l = ph[1]
            w = inp["w_in"][l]
            m[f"wglr_{l}"] = _lay_cols(w[:, O_GLR:O_GLR + 16])
            m[f"wgqk_{l}"] = _lay_cols(w[:, O_GQ:O_GQ + 512])
            m[f"wgkv_{l}"] = _lay_cols(w[:, O_GK:O_GK + 768])
            m[f"wgr_{l}"] = _lay_cols(w[:, O_GR:O_GR + 512])
            m[f"waqkv_{l}"] = np.stack([_lay_cols(np.concatenate(
                [w[:, O_AQ + c * 128:O_AQ + (c + 1) * 128], w[:, O_AK + c * 128:O_AK + (c + 1) * 128],
                 w[:, O_AV + c * 128:O_AV + (c + 1) * 128]], axis=1)) for c in range(4)], axis=0)
            m[f"wgab_{l}"] = np.stack([_lay_cols(np.concatenate(
                [w[:, O_GA + c * 128:O_GA + (c + 1) * 128], w[:, O_GB + c * 128:O_GB + (c + 1) * 128]], axis=1))
                for c in range(NC_)], axis=0)
            wa = inp["w_attn_proj"][l]
            m[f"wap_{l}"] = np.ascontiguousarray(wa.reshape(4, 128, NC_, 128).transpose(2, 1, 0, 3).reshape(NC_, 128, 4 * 128))
            wg = inp["w_gla_proj"][l]
            m[f"wgp_{l}"] = np.ascontiguousarray(wg.reshape(4, 128, NC_, 128).transpose(2, 1, 0, 3).reshape(NC_, 128, 4 * 128))
            wo_ = inp["w_out"][l]
            m[f"wo_{l}"] = np.ascontiguousarray(wo_.reshape(NC_, 128, NC_, 128).transpose(2, 1, 0, 3).reshape(NC_, 128, NC_ * 128))
    return m


def run_phases(phases, x, inp, n_cores=8, trace=False, debug=None):
    nc = build(phases, debug)
    shared = make_inputs(phases, inp)
    in_maps = []
    for b in range(n_cores):
        d = dict(shared)
        d["xT"] = np.ascontiguousarray(x[b].T)
        in_maps.append(d)
    res = run_bass_kernel_spmd(nc, in_maps, core_ids=list(range(n_cores)), trace=trace)
    out = np.stack([np.ascontiguousarray(r["yT"].T) for r in res.results], axis=0)
    return out, res


LAUNCHES = [[("ffn", 0, 0), ("mix", 0), ("ffn", 0, 1), ("ffn", 1, 0), ("mix", 1), ("ffn", 1, 1)]]


def kernel(**inputs):
    inp = {k: np.asarray(v) for k, v in inputs.items()}
    x = np.ascontiguousarray(inp["x"], dtype=np.float32)
    for phases in LAUNCHES:
        x, _ = run_phases(phases, x, inp)
    return np.ascontiguousarray(x, dtype=np.float32)
```

```python
import numpy as np
import concourse.bass as bass
import concourse.mybir as mybir
from concourse.bass_utils import run_bass_kernel_spmd

F32 = mybir.dt.float32
F32R = mybir.dt.float32r

BF16 = mybir.dt.bfloat16
AF = mybir.ActivationFunctionType
ALU = mybir.AluOpType

S = 2048
D = 1024
DFF = 2816
NC_ = 8
NTB = 4
TB = 512
NF = 22
DEPTH = 2
ALPHA = float((2 * DEPTH) ** 0.25)
LN_EPS = 1e-5
FFN_GROUPS = [4, 4, 4, 4, 3, 3]
GMAX = 4
NL3 = DEPTH * 3 * NC_
SB_BASE = 16512
SB_TOP = 229344

O_AQ, O_AK, O_AV = 0, 512, 1024
O_GQ, O_GK, O_GV, O_GLR, O_GR = 1536, 1792, 2048, 2560, 2576
O_GA, O_GB = 3088, 4112
N_IN = 5136


class Buf:
    __slots__ = ("name", "w", "r", "excl")

    def __init__(self, name="", excl=False):
        self.name = name
        self.w = None
        self.r = {}
        self.excl = excl


def bufs(n):
    return [Buf() for _ in range(n)]


class Op:
    __slots__ = ("eng", "fn", "deps", "needed", "semval", "dsem", "dval")

    def __init__(self, eng, fn, deps, dsem):
        self.eng = eng
        self.fn = fn
        self.deps = deps
        self.needed = False
        self.semval = None
        self.dsem = dsem
        self.dval = None


class Sched:
    ENGS = ("pe", "act", "dve", "pool", "sp")

    def __init__(self):
        self.q = {e: [] for e in self.ENGS}
        self.dma_count = {}
        self.last_dma = {}
        self.extra = {e: [] for e in self.ENGS}

    def op(self, eng, fn, R=(), W=(), dsem=None, after=()):
        deps = [(3, a) for a in after if a is not None]
        if any(b.excl for b in R):
            W = list(W) + [b for b in R if b.excl]
            R = [b for b in R if not b.excl]
        W = list(dict.fromkeys(W))
        for b in R:
            if b.w is not None:
                deps.append((0, b.w))
        for b in W:
            if b.w is not None:
                deps.append((1, b.w))
            for r in b.r.values():
                deps.append((2, r))
        if self.extra[eng]:
            deps.extend((0, d) for d in self.extra[eng])
            self.extra[eng] = []
        o = Op(eng, fn, deps, dsem)
        if dsem is not None:
            self.dma_count[dsem] = self.dma_count.get(dsem, 0) + 16
            o.dval = self.dma_count[dsem]
            self.last_dma[dsem] = o
        self.q[eng].append(o)
        key = dsem if dsem is not None else eng
        for b in R:
            b.r[key] = o
        for b in W:
            b.w = o
            b.r = {}
        return o

    def fence(self):
        snap = []
        for e in self.ENGS:
            for o in reversed(self.q[e]):
                if o.dsem is None:
                    snap.append(o)
                    break
        snap.extend(self.last_dma.values())
        for e in self.ENGS:
            self.extra[e] = list(snap)

    def finalize(self):
        for eng in self.ENGS:
            for o in self.q[eng]:
                keep = []
                for kind, d in o.deps:
                    if d is o:
                        continue
                    if d.dsem is not None:
                        keep.append(d)
                    elif d.eng == o.eng:
                        if o.eng == "pe" and kind != 3:
                            continue
                        keep.append(d)
                    else:
                        keep.append(d)
                for d in keep:
                    if d.dsem is None:
                        d.needed = True
                o.deps = keep
        for eng in self.ENGS:
            c = 0
            for o in self.q[eng]:
                if o.dsem is None and o.needed:
                    c += 1
                    o.semval = c

    def replay(self, eng, e, esem, dsems):
        seen = {}
        for o in self.q[eng]:
            for d in o.deps:
                if d.dsem is not None:
                    key, val, sem = ("d", d.dsem), d.dval, dsems[d.dsem]
                else:
                    key, val, sem = ("e", d.eng), d.semval, esem[d.eng]
                if seen.get(key, 0) >= val:
                    continue
                seen[key] = val
                e.wait_ge(sem, val)
            ins = o.fn(e)
            if o.dsem is not None:
                ins.then_inc(dsems[o.dsem], 16)
            elif o.needed:
                ins.then_inc(esem[eng], 1)


DT_SIZE = {F32: 4, BF16: 2, F32R: 4}


class Arena:
    UID = 0

    def __init__(self, nc, base, top):
        self.nc = nc
        self.base = base
        self.top = top
        self.off = base
        self.uid = 0
        self.peak = base

    def alloc(self, shape, dt, name="t"):
        n = 1
        for d in shape[1:]:
            n *= d
        nbytes = (n * DT_SIZE[dt] + 63) // 64 * 64
        if self.off + nbytes > self.top:
            raise RuntimeError(f"arena overflow allocating {name} {shape}: off={self.off - self.base} need {nbytes} cap {self.top - self.base}")
        Arena.UID += 1
        t = self.nc.alloc_sbuf_tensor_at(f"{name}_{Arena.UID}", list(shape), dt, offset=self.off)
        self.off += nbytes
        self.peak = max(self.peak, self.off)
        return t

    def mark(self):
        return self.off

    def release(self, m):
        self.off = m


def build(phases, debug=None):
    nc = bass.Bass("TRN2", target_bir_lowering=False)
    s = Sched()
    dram = {}
    dsem_names = []

    def new_dsem(name):
        nm = f"{name}_{len(dsem_names)}"
        dsem_names.append(nm)
        return nm

    def din(name, shape, dt=F32):
        if name not in dram:
            dram[name] = nc.dram_tensor(name, list(shape), dt, kind="ExternalInput").ap()
        return dram[name]

    debug = debug or ()

    def tap(name, t, bl):
        if name in debug:
            dd = nc.dram_tensor("dbg_" + name, list(t.shape), t.dtype, kind="ExternalOutput").ap()
            s.op("sp", lambda e: e.dma_start(out=dd, in_=t[:]), R=bl, dsem=new_dsem("dbg"))

    xT_d = din("xT", [D, S])
    yT_d = nc.dram_tensor("yT", [D, S], F32, kind="ExternalOutput").ap()
    lng_d = din("ln_g", [128, NL3])
    lnb_d = din("ln_b", [128, NL3])
    consts_d = din("consts", [128, 5 * 128])
    has_mix = any(p[0] == "mix" for p in phases)

    A = Arena(nc, SB_BASE, SB_TOP)
    xs = A.alloc([128, NC_, S], F32, "xs")
    xb = A.alloc([128, NC_, S], BF16, "xb")
    xs_b = [bufs(NTB) for _ in range(NC_)]
    xb_b = [bufs(NTB) for _ in range(NC_)]
    lng = A.alloc([128, NL3], F32, "lng")
    lnb = A.alloc([128, NL3], F32, "lnb")
    lnga = A.alloc([128, NL3], F32, "lnga")
    lnba = A.alloc([128, NL3], F32, "lnba")
    ln_c = Buf()
    consts = A.alloc([128, 5 * 128], F32, "consts")
    consts_b = Buf()
    Umat = consts[:, 0:128]
    Lmat = consts[:, 128:256]
    Mblk = consts[:, 256:384]
    ones = consts[:, 384:512]
    gones = consts[:, 512:640]
    ones1 = A.alloc([128, 64], F32, "ones1")
    ones1_b = Buf()
    ones_r = A.alloc([128, 128], F32R, "ones_r")
    gones_r = A.alloc([128, 128], F32R, "gones_r")
    onesr_b = Buf()
    mixc_b = Buf()
    if has_mix:
        wgu = A.alloc([16, DEPTH, 256], F32, "wgu")
        bgu = A.alloc([128, DEPTH, 256], F32, "bgu")
        gng = A.alloc([128, DEPTH * 4], F32, "gng")
        gnb = A.alloc([128, DEPTH * 4], F32, "gnb")
    arena_base = A.mark()

    psum = [nc.alloc_psum_tensor(f"bank{i}", [128, TB], F32) for i in range(8)]
    pq = [[Buf(f"bank{i}", excl=True)] * 4 for i in range(8)]

    def pb_(i, c0=0, c1=TB):
        return pq[i][c0 // 128:(c1 + 127) // 128]

    out_sem = new_dsem("out")

    lng_b0, lnb_b0 = Buf(), Buf()
    s.op("sp", lambda e: e.dma_start(out=lng[:], in_=lng_d), W=[lng_b0], dsem=new_dsem("io"))
    s.op("sp", lambda e: e.dma_start(out=lnb[:], in_=lnb_d), W=[lnb_b0], dsem=new_dsem("io"))
    s.op("sp", lambda e: e.dma_start(out=consts[:], in_=consts_d), W=[consts_b], dsem=new_dsem("io"))
    if has_mix:
        wgu_d = din("wgu", [16, DEPTH, 256])
        bgu_d = din("bgu", [128, DEPTH, 256])
        gng_d = din("gng", [128, DEPTH * 4])
        gnb_d = din("gnb", [128, DEPTH * 4])
        mb = bufs(4)
        s.op("sp", lambda e: e.dma_start(out=wgu[:], in_=wgu_d), W=[mb[0]], dsem=new_dsem("io"))
        s.op("sp", lambda e: e.dma_start(out=bgu[:], in_=bgu_d), W=[mb[1]], dsem=new_dsem("io"))
        s.op("sp", lambda e: e.dma_start(out=gng[:], in_=gng_d), W=[mb[2]], dsem=new_dsem("io"))
        s.op("sp", lambda e: e.dma_start(out=gnb[:], in_=gnb_d), W=[mb[3]], dsem=new_dsem("io"))
        s.op("dve", lambda e: e.memset(ones1[:], 1.0), R=mb, W=[ones1_b, mixc_b])
    xT_v = xT_d.rearrange("(c p) t -> p c t", p=128)
    for t in range(NTB):
        s.op("sp", lambda e, t=t: e.dma_start(out=xs[:, :, t * TB:(t + 1) * TB], in_=xT_v[:, :, t * TB:(t + 1) * TB]),
             W=[xs_b[c][t] for c in range(NC_)], dsem=new_dsem("iox"))
    s.op("act", lambda e: e.copy(out=ones_r[:], in_=ones), R=[consts_b], W=[onesr_b])
    s.op("act", lambda e: e.copy(out=gones_r[:], in_=gones), R=[consts_b], W=[onesr_b])
    s.op("act", lambda e: e.mul(lnga[:], lng[:], ALPHA), R=[lng_b0], W=[ln_c])
    s.op("act", lambda e: e.mul(lnba[:], lnb[:], ALPHA), R=[lnb_b0], W=[ln_c])
    for t in range(NTB):
        for c in range(NC_):
            sl = slice(t * TB, (t + 1) * TB)
            s.op("dve", lambda e, c=c, sl=sl: e.tensor_copy(out=xb[:, c, sl], in_=xs[:, c, sl]),
                 R=[xs_b[c][t]], W=[xb_b[c][t]])
            s.op("act", lambda e, c=c, sl=sl: e.mul(xs[:, c, sl], xs[:, c, sl], ALPHA),
                 R=[xs_b[c][t]], W=[xs_b[c][t]])

    def layer_norm(l, i):
        col0 = (l * 3 + i) * NC_
        s.fence()
        A.release(arena_base)
        sq = A.alloc([128, NC_, TB], F32, "sq")
        sq_b = bufs(NC_)
        mean_sb = [A.alloc([128, TB], F32, "mean") for _ in range(2)]
        m2_sb = [A.alloc([128, TB], F32, "m2") for _ in range(2)]
        rstd_sb = [A.alloc([128, TB], F32, "rstd") for _ in range(2)]
        mean_b, m2_b, rstd_b = bufs(2), bufs(2), bufs(2)
        t1 = [A.alloc([128, TB], F32, "t1") for _ in range(2)]
        t2 = [A.alloc([128, TB], F32, "t2") for _ in range(3)]
        t1_b, t2_b = bufs(2), bufs(3)
        cn = {"t1": 0, "t2": 0}

        def stats(t):
            sl = slice(t * TB, (t + 1) * TB)
            p = t % 2
            bm, bq = (6, 7) if p == 0 else (4, 5)
            pm, pq_ = psum[bm], psum[bq]
            for c in range(NC_):
                s.op("act", lambda e, c=c, sl=sl: e.activation(out=sq[:, c, :], in_=xs[:, c, sl], func=AF.Square),
                     R=[xs_b[c][t]], W=[sq_b[c]])
            for c in range(NC_):
                s.op("pe", lambda e, c=c, sl=sl, pm=pm: e.matmul(pm[:], lhsT=ones, rhs=xs[:, c, sl],
                                                                 start=(c == 0), stop=(c == NC_ - 1)),
                     R=[consts_b, xs_b[c][t]], W=pb_(bm))
            for c in range(NC_):
                s.op("pe", lambda e, c=c, pq_=pq_: e.matmul(pq_[:], lhsT=ones, rhs=sq[:, c, :],
                                                            start=(c == 0), stop=(c == NC_ - 1)),
                     R=[consts_b, sq_b[c]], W=pb_(bq))
            s.op("act", lambda e, pm=pm, p=p: e.activation(out=m2_sb[p][:], in_=pm[:], func=AF.Square), R=pb_(bm), W=[m2_b[p]])
            s.op("act", lambda e, pm=pm, p=p: e.copy(out=mean_sb[p][:], in_=pm[:]), R=pb_(bm), W=[mean_b[p]])
            s.op("dve", lambda e, pq_=pq_, p=p: e.tensor_tensor(out=rstd_sb[p][:], in0=pq_[:], in1=m2_sb[p][:], op=ALU.subtract),
                 R=pb_(bq) + [m2_b[p]], W=[rstd_b[p]])
            s.op("act", lambda e, p=p: e.activation(out=m2_sb[p][:], in_=rstd_sb[p][:], func=AF.Ln, bias=LN_EPS, scale=1.0),
                 R=[rstd_b[p]], W=[m2_b[p]])
            s.op("act", lambda e, p=p: e.activation(out=rstd_sb[p][:], in_=m2_sb[p][:], func=AF.Exp, scale=-0.5),
                 R=[m2_b[p]], W=[rstd_b[p]])

        def norm(t):
            sl = slice(t * TB, (t + 1) * TB)
            p = t % 2
            for c in range(NC_):
                j = cn["t1"] % 2
                cn["t1"] += 1
                j2 = cn["t2"] % 3
                cn["t2"] += 1
                s.op("dve", lambda e, c=c, sl=sl, j=j, p=p: e.tensor_tensor(out=t1[j][:], in0=xs[:, c, sl], in1=mean_sb[p][:], op=ALU.subtract),
                     R=[xs_b[c][t], mean_b[p]], W=[t1_b[j]])
                s.op("pool", lambda e, j=j, j2=j2, p=p: e.tensor_tensor(out=t2[j2][:], in0=t1[j][:], in1=rstd_sb[p][:], op=ALU.mult),
                     R=[t1_b[j], rstd_b[p]], W=[t2_b[j2]])
                s.op("act", lambda e, c=c, sl=sl, j2=j2: e.activation(out=xs[:, c, sl], in_=t2[j2][:], func=AF.Identity,
                                                                   scale=lnga[:, col0 + c:col0 + c + 1],
                                                                   bias=lnba[:, col0 + c:col0 + c + 1]),
                     R=[t2_b[j2], ln_c], W=[xs_b[c][t]])
                s.op("dve", lambda e, c=c, sl=sl, j2=j2: e.tensor_scalar(
                    out=xb[:, c, sl], in0=t2[j2][:], scalar1=lng[:, col0 + c:col0 + c + 1], scalar2=lnb[:, col0 + c:col0 + c + 1],
                    op0=ALU.mult, op1=ALU.add),
                    R=[t2_b[j2], ln_c], W=[xb_b[c][t]])

        stats(0)
        for t in range(NTB):
            if t + 1 < NTB:
                stats(t + 1)
            norm(t)

    def make_ln(l, i, arenas, sbanks=((0, 1), (2, 3)), lag=2):
        col0 = (l * 3 + i) * NC_
        final = (l, i) == final_ln
        g_xs, b_xs = (lng, lnb) if final else (lnga, lnba)
        yT_v = yT_d.rearrange("(c p) t -> p c t", p=128)

        def al(shape, dt, name):
            for a in arenas:
                n = 1
                for d in shape[1:]:
                    n *= d
                if a.off + (n * DT_SIZE[dt] + 63) // 64 * 64 <= a.top:
                    return a.alloc(shape, dt, name)
            raise RuntimeError("make_ln: no room for " + name)

        NSQ = 3
        sqr = [al([128, TB], F32R, "lsq") for _ in range(NSQ)]
        sqr_b = bufs(NSQ)
        mean_sb = [al([128, TB], F32, "lmean") for _ in range(2)]
        m2_sb = [al([128, TB], F32, "lm2") for _ in range(2)]
        rstd_sb = [al([128, TB], F32, "lrstd") for _ in range(2)]
        mean_b, m2_b, rstd_b = bufs(2), bufs(2), bufs(2)
        NT = 3
        t1 = [al([128, TB], F32, "lt1") for _ in range(NT)]
        t2 = [al([128, TB], F32, "lt2") for _ in range(NT)]
        t1_b, t2_b = bufs(NT), bufs(NT)
        cn = {"sq": 0, "t1": 0, "t2": 0, "seen": {}}
        pending = []
        avail = []
        fl = {"s1": None, "s2": None}

        def tick():
            if fl["s2"] is not None:
                c, t, j2 = fl["s2"]
                sl = slice(t * TB, (t + 1) * TB)
                s.op("act", lambda e, c=c, sl=sl, j2=j2: e.activation(out=xs[:, c, sl], in_=t2[j2][:], func=AF.Identity,
                                                                   scale=g_xs[:, col0 + c:col0 + c + 1],
                                                                   bias=b_xs[:, col0 + c:col0 + c + 1]),
                     R=[t2_b[j2], ln_c], W=[xs_b[c][t]])
                if not final:
                    s.op("dve", lambda e, c=c, sl=sl, j2=j2: e.tensor_scalar(
                        out=xb[:, c, sl], in0=t2[j2][:], scalar1=lng[:, col0 + c:col0 + c + 1], scalar2=lnb[:, col0 + c:col0 + c + 1],
                        op0=ALU.mult, op1=ALU.add),
                        R=[t2_b[j2], ln_c], W=[xb_b[c][t]])
                else:
                    ndone = cn.get(("done", t), 0) + 1
                    cn[("done", t)] = ndone
                    if ndone == NC_:
                        s.op("sp", lambda e, sl=sl: e.dma_start(out=yT_v[:, :, sl], in_=xs[:, :, sl]),
                             R=[xs_b[cc][t] for cc in range(NC_)], dsem=out_sem)
                        out_ops.append(s.q["sp"][-1])
                fl["s2"] = None
            if fl["s1"] is not None:
                c, t, j = fl["s1"]
                p = t % 2
                j2 = cn["t2"] % NT
                cn["t2"] += 1
                s.op("pool", lambda e, j=j, j2=j2, p=p: e.tensor_tensor(out=t2[j2][:], in0=t1[j][:], in1=rstd_sb[p][:], op=ALU.mult),
                     R=[t1_b[j], rstd_b[p]], W=[t2_b[j2]])
                fl["s2"] = (c, t, j2)
                fl["s1"] = None
            if avail:
                c, t = avail.pop(0)
                sl = slice(t * TB, (t + 1) * TB)
                p = t % 2
                j = cn["t1"] % NT
                cn["t1"] += 1
                s.op("dve", lambda e, c=c, sl=sl, j=j, p=p: e.tensor_tensor(out=t1[j][:], in0=xs[:, c, sl], in1=mean_sb[p][:], op=ALU.subtract),
                     R=[xs_b[c][t], mean_b[p]], W=[t1_b[j]])
                fl["s1"] = (c, t, j)

        def emit(entry):
            dc, t, k = entry
            sl = slice(t * TB, (t + 1) * TB)
            p = t % 2
            bm, bq = sbanks[p]
            n = cn["seen"].get(t, 0)
            cn["seen"][t] = n + 1
            s.op("pe", lambda e, dc=dc, sl=sl, bm=bm, n=n: e.matmul(psum[bm][:], lhsT=ones, rhs=xs[:, dc, sl],
                                                                   start=(n == 0), stop=(n == NC_ - 1)),
                 R=[consts_b, xs_b[dc][t]], W=pb_(bm))
            s.op("pe", lambda e, k=k, bq=bq, n=n: e.matmul(psum[bq][:], lhsT=ones_r[:], rhs=sqr[k][:],
                                                           start=(n == 0), stop=(n == NC_ - 1)),
                 R=[onesr_b, sqr_b[k]], W=pb_(bq))
            if n == NC_ - 1:
                s.op("act", lambda e, bm=bm, p=p: e.activation(out=m2_sb[p][:], in_=psum[bm][:], func=AF.Square), R=pb_(bm), W=[m2_b[p]])
                s.op("act", lambda e, bm=bm, p=p: e.copy(out=mean_sb[p][:], in_=psum[bm][:]), R=pb_(bm), W=[mean_b[p]])
                s.op("dve", lambda e, bq=bq, p=p: e.tensor_tensor(out=rstd_sb[p][:], in0=psum[bq][:], in1=m2_sb[p][:], op=ALU.subtract),
                     R=pb_(bq) + [m2_b[p]], W=[rstd_b[p]])
                s.op("act", lambda e, p=p: e.activation(out=m2_sb[p][:], in_=rstd_sb[p][:], func=AF.Ln, bias=LN_EPS, scale=1.0),
                     R=[rstd_b[p]], W=[m2_b[p]])
                s.op("act", lambda e, p=p: e.activation(out=rstd_sb[p][:], in_=m2_sb[p][:], func=AF.Exp, scale=-0.5),
                     R=[m2_b[p]], W=[rstd_b[p]])
                avail.extend((c, t) for c in range(NC_))

        def chunk_done(dc, t):
            sl = slice(t * TB, (t + 1) * TB)
            k = cn["sq"] % NSQ
            cn["sq"] += 1
            s.op("act", lambda e, dc=dc, sl=sl, k=k: e.activation(out=sqr[k][:], in_=xs[:, dc, sl], func=AF.Square),
                 R=[xs_b[dc][t]], W=[sqr_b[k]])
            pending.append((dc, t, k))
            if len(pending) > lag:
                emit(pending.pop(0))
            tick()

        def flush():
            while pending:
                emit(pending.pop(0))
                tick()
            while avail or fl["s1"] is not None or fl["s2"] is not None:
                tick()

        return chunk_done, flush

    def ffn(l, i):
        w1_d = din(f"f{i}w1_{l}", [NF, 128, D])
        w3_d = din(f"f{i}w3_{l}", [NF, 128, D])
        w2_d = din(f"f{i}w2_{l}", [DFF, D])
        s.fence()
        A.release(arena_base)
        W13_SLOTS = 3
        w13 = [A.alloc([128, 2, D], BF16, "w13") for _ in range(W13_SLOTS)]
        w13_b = [bufs(2) for _ in range(W13_SLOTS)]
        w13_sem = [[new_dsem("w13") for _ in range(2)] for _ in range(W13_SLOTS)]
        w2 = [A.alloc([128, GMAX, D], BF16, "w2") for _ in range(2)]
        w2_b = bufs(2)
        w2_sem = [new_dsem("w2") for _ in range(2)]
        gT = [A.alloc([128, GMAX, S], BF16, "gT") for _ in range(2)]
        gT_b = [[bufs(NTB) for _ in range(GMAX)] for _ in range(2)]
        silu_t = [A.alloc([128, TB], F32, "silu") for _ in range(2)]
        silu_b = bufs(2)
        ln_chunk, ln_flush = make_ln(l, 0 if i == 0 else 2, [A])
        cnt = {"w13": 0, "w2": 0, "psA": 0, "psY": 0, "silu": 0}
        m0 = 0
        for gi, G in enumerate(FFN_GROUPS):
            ms = list(range(m0, m0 + G))
            m0 += G
            gs = gi % 2
            ws = cnt["w2"] % 2
            cnt["w2"] += 1
            s.op("pool", lambda e, ws=ws, ms=ms, G=G: e.dma_start(
                out=w2[ws][:, 0:G, :],
                in_=w2_d[ms[0] * 128:(ms[0] + G) * 128, :].rearrange("(g p) n -> p g n", p=128)),
                W=[w2_b[ws]], dsem=w2_sem[ws])
            for ml, m in enumerate(ms):
                slot = cnt["w13"] % W13_SLOTS
                cnt["w13"] += 1
                s.op("pool", lambda e, slot=slot, m=m: e.dma_start(out=w13[slot][:, 0, :], in_=w1_d[m]),
                     W=[w13_b[slot][0]], dsem=w13_sem[slot][0])
                s.op("pool", lambda e, slot=slot, m=m: e.dma_start(out=w13[slot][:, 1, :], in_=w3_d[m]),
                     W=[w13_b[slot][1]], dsem=w13_sem[slot][1])
                for t in range(NTB):
                    sl = slice(t * TB, (t + 1) * TB)
                    pj = cnt["psA"] % 2
                    cnt["psA"] += 1
                    for which, bi in ((0, pj), (1, 2 + pj)):
                        for k in range(NC_):
                            s.op("pe", lambda e, slot=slot, which=which, k=k, sl=sl, bi=bi: e.matmul(
                                psum[bi][:], lhsT=w13[slot][:, which, k * 128:(k + 1) * 128], rhs=xb[:, k, sl],
                                start=(k == 0), stop=(k == NC_ - 1)),
                                R=[w13_b[slot][which], xb_b[k][t]], W=pb_(bi))
                    sj = cnt["silu"] % 2
                    cnt["silu"] += 1
                    s.op("act", lambda e, pj=pj, sj=sj: e.activation(out=silu_t[sj][:], in_=psum[pj][:], func=AF.Silu),
                         R=pb_(pj), W=[silu_b[sj]])
                    s.op("dve", lambda e, pj=pj, sj=sj, gs=gs, ml=ml, sl=sl: e.tensor_tensor(
                        out=gT[gs][:, ml, sl], in0=psum[2 + pj][:], in1=silu_t[sj][:], op=ALU.mult),
                        R=pb_(2 + pj) + [silu_b[sj]], W=[gT_b[gs][ml][t]])
            last = (gi == len(FFN_GROUPS) - 1)
            order = [(dc, t) for t in range(NTB) for dc in range(NC_)] if last else [(dc, t) for dc in range(NC_) for t in range(NTB)]
            for dc, t in order:
                if True:
                    sl = slice(t * TB, (t + 1) * TB)
                    bi = 4 + cnt["psY"] % 2
                    cnt["psY"] += 1
                    for ml in range(G):
                        s.op("pe", lambda e, ws=ws, ml=ml, dc=dc, gs=gs, sl=sl, bi=bi, G=G: e.matmul(
                            psum[bi][:], lhsT=w2[ws][:, ml, dc * 128:(dc + 1) * 128], rhs=gT[gs][:, ml, sl],
                            start=(ml == 0), stop=(ml == G - 1)),
                            R=[w2_b[ws], gT_b[gs][ml][t]], W=pb_(bi))
                    s.op("dve", lambda e, bi=bi, dc=dc, sl=sl: e.scalar_tensor_tensor(
                        out=xs[:, dc, sl], in0=psum[bi][:], scalar=0.5, in1=xs[:, dc, sl],
                        op0=ALU.mult, op1=ALU.add),
                        R=pb_(bi) + [xs_b[dc][t]], W=[xs_b[dc][t]])
                    if last:
                        ln_chunk(dc, t)
        ln_flush()

    def mixer(l, stop=None):
        amask_d = din("amask", [8, 128, S])
        wglr_d = din(f"wglr_{l}", [128, NC_ * 16])
        wgqk_d = din(f"wgqk_{l}", [128, NC_ * 512])
        wgkv_d = din(f"wgkv_{l}", [128, NC_ * 768])
        wgr_d = din(f"wgr_{l}", [128, NC_ * 512])
        waqkv_d = din(f"waqkv_{l}", [4, 128, NC_ * 384])
        wgab_d = din(f"wgab_{l}", [NC_, 128, NC_ * 256])
        wap_d = din(f"wap_{l}", [NC_, 128, 4 * 128])
        wgp_d = din(f"wgp_{l}", [NC_, 128, 4 * 128])
        wo_d = din(f"wo_{l}", [NC_, 128, NC_ * 128])
        blk = amask_blocks
        s.fence()
        A.release(arena_base)
        o_gnT = A.alloc([128, 4, S], BF16, "o_gnT")
        o_gn_b = [bufs(NTB) for _ in range(4)]
        o_a_b = [bufs(NTB) for _ in range(8)]
        mix_base = A.mark()

        qT = A.alloc([128, 2, S], BF16, "qT")
        kT = A.alloc([128, 2, S], BF16, "kT")
        qT_b = [bufs(16) for _ in range(2)]
        kT_b = [bufs(16) for _ in range(2)]
        khat = A.alloc([128, 16, 256], BF16, "khat")
        khat_b = bufs(16)
        gv = A.alloc([128, 16, 512], BF16, "gv")
        gv_b = bufs(16)
        dec = A.alloc([128, 4, 32], F32, "dec")
        dec_b = bufs(NTB)
        gla_base = A.mark()
        wglr = A.alloc([128, NC_, 16], BF16, "wglr")
        wgqk = A.alloc([128, NC_, 512], BF16, "wgqk")
        wgkv = A.alloc([128, NC_, 768], BF16, "wgkv")
        wB_b = bufs(3)
        s.op("pool", lambda e: e.dma_start(out=wglr[:], in_=wglr_d.rearrange("p (k n) -> p k n", k=NC_)),
             W=[wB_b[0]], dsem=new_dsem("wB"))
        s.op("pool", lambda e: e.dma_start(out=wgqk[:], in_=wgqk_d.rearrange("p (k n) -> p k n", k=NC_)),
             W=[wB_b[1]], dsem=new_dsem("wB"))
        s.op("pool", lambda e: e.dma_start(out=wgkv[:], in_=wgkv_d.rearrange("p (k n) -> p k n", k=NC_)),
             W=[wB_b[2]], dsem=new_dsem("wB"))
        glrT = [A.alloc([16, TB], F32, "glrT") for _ in range(2)]
        glrT_b = bufs(2)
        z_sb = [A.alloc([128, 256], F32, "z") for _ in range(2)]
        z_b = bufs(2)
        la_sb = [A.alloc([128, 256], F32, "la") for _ in range(2)]
        la_b = bufs(2)
        Eb = A.alloc([128, 2, TB], F32, "Eb")
        Einv = A.alloc([128, 2, TB], F32, "Einv")
        E_b, Einv_b = bufs(2), bufs(2)
        Ft = A.alloc([128, 4, 256], F32, "Ft")
        F_b = bufs(4)
        zc = 0
        pc = 0
        for tb in range(NTB):
            sl = slice(tb * TB, (tb + 1) * TB)
            gj = tb % 2
            for k in range(NC_):
                s.op("pe", lambda e, k=k, sl=sl: e.matmul(psum[0][0:16, :], lhsT=wglr[:, k, :], rhs=xb[:, k, sl],
                                                         start=(k == 0), stop=(k == NC_ - 1)),
                     R=[wB_b[0], xb_b[k][tb]], W=pb_(0))
            s.op("act", lambda e, gj=gj: e.copy(out=glrT[gj][:], in_=psum[0][0:16, :]), R=pb_(0), W=[glrT_b[gj]])
            for jj in range(4):
                j = tb * 4 + jj
                zi = zc % 2
                zc += 1
                bz = 1 + zi
                s.op("pe", lambda e, gj=gj, jj=jj, bz=bz: e.matmul(
                    psum[bz][:, 0:256], lhsT=glrT[gj][:, jj * 128:(jj + 1) * 128], rhs=wgu[:, l, :],
                    start=True, stop=True),
                    R=[glrT_b[gj], mixc_b], W=pb_(bz, 0, 256))
                s.op("dve", lambda e, zi=zi, bz=bz: e.tensor_tensor(out=z_sb[zi][:], in0=psum[bz][:, 0:256], in1=bgu[:, l, :], op=ALU.add),
                     R=pb_(bz, 0, 256) + [mixc_b], W=[z_b[zi]])
                tsl = slice(j * 128, (j + 1) * 128)
                bi2 = 6 + pc % 2
                pc += 1
                for k in range(NC_):
                    s.op("pe", lambda e, k=k, tsl=tsl, bi2=bi2: e.matmul(
                        psum[bi2][:], lhsT=xb[:, k, tsl], rhs=wgkv[:, k, 256:768],
                        start=(k == 0), stop=(k == NC_ - 1)),
                        R=[wB_b[2], xb_b[k][tb]], W=pb_(bi2))
                s.op("dve", lambda e, j=j, bi2=bi2: e.tensor_copy(out=gv[:, j, :], in_=psum[bi2][:]),
                     R=pb_(bi2), W=[gv_b[j]])
                s.op("act", lambda e, zi=zi: e.activation(out=z_sb[zi][:], in_=z_sb[zi][:], func=AF.Exp, scale=-1.0),
                     R=[z_b[zi]], W=[z_b[zi]])
                s.op("act", lambda e, zi=zi: e.activation(out=la_sb[zi][:], in_=z_sb[zi][:], func=AF.Ln, bias=1.0, scale=1.0),
                     R=[z_b[zi]], W=[la_b[zi]])
                for ch in range(2):
                    s.op("pe", lambda e, zi=zi, ch=ch, jj=jj: e.matmul(
                        psum[3 + ch][:, jj * 128:(jj + 1) * 128], lhsT=la_sb[zi][:, ch * 128:(ch + 1) * 128], rhs=Umat,
                        start=True, stop=True),
                        R=[la_b[zi], consts_b], W=[pq[3 + ch][jj]])
                s.op("pe", lambda e, zi=zi: e.matmul(psum[5][:, 0:256], lhsT=Lmat, rhs=la_sb[zi][:], start=True, stop=True),
                     R=[la_b[zi], consts_b], W=pb_(5, 0, 256))
                s.op("act", lambda e, jj=jj: e.activation(out=Ft[:, jj, :], in_=psum[5][:, 0:256], func=AF.Exp),
                     R=pb_(5, 0, 256), W=[F_b[jj]])
            for ch in range(2):
                s.op("act", lambda e, ch=ch: e.activation(out=Eb[:, ch, :], in_=psum[3 + ch][:], func=AF.Exp),
                     R=pb_(3 + ch), W=[E_b[ch]])
                s.op("act", lambda e, ch=ch: e.activation(out=Einv[:, ch, :], in_=psum[3 + ch][:], func=AF.Exp, scale=-1.0),
                     R=pb_(3 + ch), W=[Einv_b[ch]])
            for dup in range(2):
                s.op("dve", lambda e, tb=tb, dup=dup: e.tensor_copy(
                    out=dec[:].rearrange("p (c d) n -> p c d n", d=2)[:, :, dup, tb * 8:(tb + 1) * 8], in_=Eb[:, :, 63::64]),
                    R=E_b, W=[dec_b[tb]])
            for m in range(4):
                ch = m % 2
                bi = 6 + pc % 2
                pc += 1
                for k in range(NC_):
                    s.op("pe", lambda e, m=m, k=k, sl=sl, bi=bi: e.matmul(
                        psum[bi][:], lhsT=wgqk[:, k, m * 128:(m + 1) * 128], rhs=xb[:, k, sl],
                        start=(k == 0), stop=(k == NC_ - 1)),
                        R=[wB_b[1], xb_b[k][tb]], W=pb_(bi))
                if m < 2:
                    s.op("dve", lambda e, ch=ch, sl=sl, bi=bi: e.scalar_tensor_tensor(
                        out=qT[:, ch, sl], in0=psum[bi][:], scalar=0.125, in1=Eb[:, ch, :], op0=ALU.mult, op1=ALU.mult),
                        R=pb_(bi) + [E_b[ch]], W=qT_b[ch][tb * 4:(tb + 1) * 4])
                else:
                    s.op("dve", lambda e, ch=ch, sl=sl, bi=bi: e.tensor_tensor(
                        out=kT[:, ch, sl], in0=psum[bi][:], in1=Einv[:, ch, :], op=ALU.mult),
                        R=pb_(bi) + [Einv_b[ch]], W=kT_b[ch][tb * 4:(tb + 1) * 4])
            for jj in range(4):
                j = tb * 4 + jj
                tsl = slice(j * 128, (j + 1) * 128)
                bi = 1 + jj % 2
                for k in range(NC_):
                    s.op("pe", lambda e, k=k, tsl=tsl, bi=bi: e.matmul(
                        psum[bi][:, 0:256], lhsT=xb[:, k, tsl], rhs=wgkv[:, k, 0:256],
                        start=(k == 0), stop=(k == NC_ - 1)),
                        R=[wB_b[2], xb_b[k][tb]], W=pb_(bi, 0, 256))
                s.op("dve", lambda e, j=j, jj=jj, bi=bi: e.tensor_tensor(
                    out=khat[:, j, :], in0=psum[bi][:, 0:256], in1=Ft[:, jj, :], op=ALU.mult),
                    R=pb_(bi, 0, 256) + [F_b[jj]], W=[khat_b[j]])

        if stop == "B":
            return
        tap("qT", qT, [b for bb in qT_b for b in bb])
        tap("kT", kT, [b for bb in kT_b for b in bb])
        tap("khat", khat, khat_b)
        tap("gv", gv, gv_b)
        tap("dec", dec, dec_b)
        s.fence()
        A.release(gla_base)
        wgr = A.alloc([128, NC_, 512], BF16, "wgr")
        wgr_b = Buf()
        s.op("pool", lambda e: e.dma_start(out=wgr[:], in_=wgr_d.rearrange("p (k n) -> p k n", k=NC_)),
             W=[wgr_b], dsem=new_dsem("wgr"))
        ograw = [A.alloc([128, 4, TB], F32, "ograw") for _ in range(2)]
        ograw_b = [[bufs(4) for _ in range(4)] for _ in range(2)]
        st_f = A.alloc([128, 4, 128], F32, "st_f")
        st_b = A.alloc([128, 4, 128], BF16, "st_b")
        stf_b, stb_b = bufs(4), bufs(4)
        ATt = [A.alloc([128, 4, 128], BF16, "AT") for _ in range(2)]
        AT_b = [bufs(4) for _ in range(2)]
        sg = [A.alloc([128, TB], F32, "sg") for _ in range(4)]
        sg_b = bufs(4)
        gsq = A.alloc([128, TB], F32R, "gsq")
        gsq_b = Buf()
        gm2 = A.alloc([128, TB], F32, "gm2")
        grs = A.alloc([128, TB], F32, "grs")
        gm2_b, grs_b = Buf(), Buf()
        s.op("dve", lambda e: e.memset(st_f[:], 0.0), W=stf_b)
        s.op("dve", lambda e: e.memset(st_b[:], 0.0), W=stb_b)
        bS0, bS1 = 4, 5
        HORD = (0, 2, 1, 3)

        def emit_AT_mm(j):
            bA = j % 2
            tok = slice(j * 128, (j + 1) * 128)
            prev = None
            for h in HORD:
                ch, pb = h // 2, (h % 2) * 64
                hs = slice(h * 128, (h + 1) * 128)
                prev = s.op("pe", lambda e, ch=ch, pb=pb, hs=hs, tok=tok, bA=bA: e.matmul(
                    psum[bA][:, hs], lhsT=kT[pb:pb + 64, ch, tok], rhs=qT[pb:pb + 64, ch, tok], start=True, stop=True),
                    R=[kT_b[ch][j], qT_b[ch][j]], W=[pq[bA][h]], after=[prev] if h == 1 else [])

        def emit_AT_mask(j):
            bA = j % 2
            aj = j % 2
            s.op("dve", lambda e, bA=bA, aj=aj: e.tensor_tensor(
                out=ATt[aj][:], in0=psum[bA][:].rearrange("p (h n) -> p h n", h=4),
                in1=Mblk.unsqueeze(1).broadcast_to([128, 4, 128]), op=ALU.mult),
                R=[pq[bA][0], consts_b], W=AT_b[aj])

        def emit_dS(j, half):
            bS = bS0 if half == 0 else bS1
            rows = slice(half * 64, half * 64 + 64)
            for h in range(4):
                ch = h // 2
                hs = slice(h * 128, (h + 1) * 128)
                s.op("pe", lambda e, ch=ch, hs=hs, rows=rows, bS=bS, j=j: e.matmul(
                    psum[bS][:, hs], lhsT=khat[rows, j, ch * 128:(ch + 1) * 128], rhs=gv[rows, j, hs],
                    start=True, stop=True),
                    R=[khat_b[j], gv_b[j]], W=[pq[bS][h]])

        def emit_decay(c):
            s.op("dve", lambda e, c=c: e.tensor_tensor(
                out=st_f[:], in0=st_f[:], in1=dec[:, :, c:c + 1].broadcast_to([128, 4, 128]), op=ALU.mult),
                R=stf_b + [dec_b[c // 8]], W=stf_b)

        def emit_update(j, half):
            bS = bS0 if half == 0 else bS1
            c = 2 * j + half
            s.op("dve", lambda e, bS=bS: e.tensor_tensor(
                out=st_f[:], in0=st_f[:], in1=psum[bS][:].rearrange("p (h n) -> p h n", h=4), op=ALU.add),
                R=stf_b + [pq[bS][0]], W=stf_b)
            s.op("dve", lambda e: e.tensor_copy(out=st_b[:], in_=st_f[:]), R=stf_b, W=stb_b)
            if c + 1 < 32:
                emit_decay(c + 1)

        gtasks = []
        gstate = {"B": None}

        def gate_batch(tb):
            sl = slice(tb * TB, (tb + 1) * TB)
            for h in range(4):
                bi = 6 + h % 2
                for k in range(NC_):
                    s.op("pe", lambda e, k=k, h=h, sl=sl, bi=bi: e.matmul(
                        psum[bi][:], lhsT=wgr[:, k, h * 128:(h + 1) * 128], rhs=xb[:, k, sl],
                        start=(k == 0), stop=(k == NC_ - 1)),
                        R=[wgr_b, xb_b[k][tb]], W=pb_(bi))
                s.op("act", lambda e, h=h, bi=bi: e.activation(out=sg[h][:], in_=psum[bi][:], func=AF.Silu),
                     R=pb_(bi), W=[sg_b[h]])
            for h in range(4):
                gtasks.append((tb, h))

        def gn_A(tb, h):
            ob = tb % 2
            og = ograw[ob][:, h, :]
            ogb = ograw_b[ob][h]
            s.op("act", lambda e, og=og: e.activation(out=gsq[:], in_=og, func=AF.Square), R=ogb, W=[gsq_b])
            s.op("pe", lambda e, og=og: e.matmul(psum[6][:], lhsT=gones, rhs=og, start=True, stop=True),
                 R=ogb + [consts_b], W=pb_(6))
            s.op("pe", lambda e: e.matmul(psum[7][:], lhsT=gones_r[:], rhs=gsq[:], start=True, stop=True),
                 R=[gsq_b, onesr_b], W=pb_(7))
            s.op("act", lambda e: e.activation(out=gm2[:], in_=psum[6][:], func=AF.Square), R=pb_(6), W=[gm2_b])
            s.op("dve", lambda e, og=og: e.tensor_tensor(out=og, in0=og, in1=psum[6][:], op=ALU.subtract),
                 R=ogb + pb_(6), W=ogb)
            s.op("dve", lambda e: e.tensor_tensor(out=grs[:], in0=psum[7][:], in1=gm2[:], op=ALU.subtract),
                 R=pb_(7) + [gm2_b], W=[grs_b])
            s.op("act", lambda e: e.activation(out=gm2[:], in_=grs[:], func=AF.Ln, bias=LN_EPS, scale=1.0),
                 R=[grs_b], W=[gm2_b])
            s.op("act", lambda e: e.activation(out=grs[:], in_=gm2[:], func=AF.Exp, scale=-0.5), R=[gm2_b], W=[grs_b])

        def gn_B(tb, h):
            ob = tb % 2
            og = ograw[ob][:, h, :]
            ogb = ograw_b[ob][h]
            col = l * 4 + h
            sl = slice(tb * TB, (tb + 1) * TB)
            s.op("pool", lambda e, og=og: e.tensor_tensor(out=og, in0=og, in1=grs[:], op=ALU.mult),
                 R=ogb + [grs_b], W=ogb)
            s.op("act", lambda e, og=og, col=col: e.activation(out=og, in_=og, func=AF.Identity,
                                                             scale=gng[:, col:col + 1], bias=gnb[:, col:col + 1]),
                 R=ogb + [mixc_b], W=ogb)
            s.op("pool", lambda e, og=og, h=h, sl=sl: e.tensor_tensor(out=o_gnT[:, h, sl], in0=og, in1=sg[h][:], op=ALU.mult),
                 R=ogb + [sg_b[h]], W=[o_gn_b[h][tb]])

        def gn_slot():
            if gstate["B"] is not None:
                gn_B(*gstate["B"])
                gstate["B"] = None
            if gtasks:
                t_ = gtasks.pop(0)
                gn_A(*t_)
                gstate["B"] = t_

        def gn_flush():
            while gtasks or gstate["B"] is not None:
                gn_slot()

        emit_AT_mm(0)
        emit_dS(0, 0)
        emit_dS(0, 1)
        emit_AT_mask(0)
        for j in range(16):
            tb, jj = j // 4, j % 4
            ob = tb % 2
            aj = j % 2
            bO = 2 + aj
            t0 = slice(j * 128, j * 128 + 64)
            t1_ = slice(j * 128 + 64, (j + 1) * 128)
            for h in range(4):
                hs = slice(h * 128, (h + 1) * 128)
                s.op("pe", lambda e, h=h, hs=hs, bO=bO, aj=aj, j=j: e.matmul(
                    psum[bO][:, hs], lhsT=gv[:, j, hs], rhs=ATt[aj][:, h, :], start=(h == 0), stop=False, skip_group_check=True),
                    R=[gv_b[j], AT_b[aj][h]], W=[pq[bO][h]])
            prev = None
            for h in HORD:
                ch, pb = h // 2, (h % 2) * 64
                prev = s.op("pe", lambda e, h=h, ch=ch, pb=pb, bO=bO, t0=t0: e.matmul(
                    psum[bO][:, h * 128:h * 128 + 64], lhsT=st_b[pb:pb + 64, h, :], rhs=qT[pb:pb + 64, ch, t0],
                    start=False, stop=False, skip_group_check=True),
                    R=[stb_b[h], qT_b[ch][j]], W=[pq[bO][h]], after=[prev] if h == 1 else [])
            if j + 1 < 16:
                emit_AT_mm(j + 1)
            emit_update(j, 0)
            prev = None
            for h in HORD:
                ch, pb = h // 2, (h % 2) * 64
                prev = s.op("pe", lambda e, h=h, ch=ch, pb=pb, bO=bO, t1_=t1_: e.matmul(
                    psum[bO][:, h * 128 + 64:(h + 1) * 128], lhsT=st_b[pb:pb + 64, h, :], rhs=qT[pb:pb + 64, ch, t1_],
                    start=False, stop=True, skip_group_check=True),
                    R=[stb_b[h], qT_b[ch][j]], W=[pq[bO][h]], after=[prev] if h == 1 else [])
            if j + 1 < 16:
                emit_AT_mask(j + 1)
                emit_dS(j + 1, 0)
            s.op("act", lambda e, bO=bO, ob=ob, jj=jj: e.copy(
                out=ograw[ob][:, :, jj * 128:(jj + 1) * 128], in_=psum[bO][:].rearrange("p (h n) -> p h n", h=4)),
                R=[pq[bO][0]], W=[ograw_b[ob][h][jj] for h in range(4)])
            emit_update(j, 1)
            if j + 1 < 16:
                emit_dS(j + 1, 1)
            if j == 0:
                tap("st1", st_f, stf_b)
            gn_slot()
            if jj == 3:
                gn_flush()
                if tb == 0:
                    tap("ograw0", ograw[0], [b for bb in ograw_b[0] for b in bb])
                gate_batch(tb)
        gn_flush()

        if stop == "D":
            return
        tap("o_gnT", o_gnT, [b for bb in o_gn_b for b in bb])
        s.fence()
        A.release(mix_base)
        o_aT = A.alloc([128, 4, S], BF16, "o_aT")
        mix_base = A.mark()
        waqkv = A.alloc([128, NC_, 384], BF16, "waqkv")
        waqkv_b = Buf()
        waqkv_sem = new_dsem("waqkv")
        aqT = [[A.alloc([128, S], BF16, "aqT") for _ in range(2)] for _ in range(2)]
        akT = [A.alloc([128, S], BF16, "akT") for _ in range(2)]
        aqT_b = [bufs(NTB) for _ in range(2)]
        aqz_b = [bufs(2) for _ in range(2)]
        for wi_ in range(2):
            for hh_ in range(2):
                oth = slice(64, 128) if hh_ == 0 else slice(0, 64)
                s.op("pool", lambda e, wi_=wi_, hh_=hh_, oth=oth: e.memset(aqT[wi_][hh_][oth, :], 0.0), W=[aqz_b[wi_][hh_]])
        akT_b = [bufs(16) for _ in range(2)]
        Vp = [A.alloc([128, 16, 128], BF16, "Vp") for _ in range(2)]
        Vp_b = [bufs(16) for _ in range(2)]
        mask_s = [A.alloc([128, S], F32, "mask") for _ in range(2)]
        mask_b = bufs(2)
        mask_sem = [new_dsem("mask") for _ in range(2)]
        NE = 4
        LOOK = 3
        Et = [A.alloc([128, TB], F32, "Et") for _ in range(NE)]
        Pt = [A.alloc([128, TB], BF16, "Pt") for _ in range(NE)]
        Et_b, Pt_b = bufs(NE), bufs(NE)
        dcp = [A.alloc([128, TB], F32, "dcp") for _ in range(1)] * 2
        dcp_b = bufs(1) * 2
        onesb = A.alloc([128, 128], BF16, "onesb")
        onesb_b = Buf()
        s.op("pool", lambda e: e.memset(onesb[:], 1.0), W=[onesb_b])
        SB = (0, 1, 2, 3)
        stepc = 0
        def load_waqkv(ch):
            s.op("pool", lambda e, ch=ch: e.dma_start(out=waqkv[:], in_=waqkv_d[ch].rearrange("p (k n) -> p k n", k=NC_)),
                 W=[waqkv_b], dsem=waqkv_sem)

        def load_mask(h):
            mi = h % 2
            s.op("sp", lambda e, mi=mi, h=h: e.dma_start(out=mask_s[mi][:], in_=amask_d[h]),
                 W=[mask_b[mi]], dsem=mask_sem[mi])

        load_waqkv(0)
        load_mask(0)
        load_mask(1)
        for ch in range(4):
            wi = ch % 2
            for tb in range(NTB):
                sl = slice(tb * TB, (tb + 1) * TB)
                for which in range(2):
                    bi = SB[(2 * tb + which) % 4]
                    for k in range(NC_):
                        s.op("pe", lambda e, k=k, which=which, sl=sl, bi=bi: e.matmul(
                            psum[bi][:], lhsT=waqkv[:, k, which * 128:(which + 1) * 128], rhs=xb[:, k, sl],
                            start=(k == 0), stop=(k == NC_ - 1)),
                            R=[waqkv_b, xb_b[k][tb]], W=pb_(bi))
                    if which == 0:
                        s.op("act", lambda e, wi=wi, sl=sl, bi=bi: e.mul(aqT[wi][0][0:64, sl], psum[bi][0:64, :], 0.125),
                             R=pb_(bi), W=[aqT_b[wi][tb]])
                        s.op("act", lambda e, wi=wi, sl=sl, bi=bi: e.mul(aqT[wi][1][64:128, sl], psum[bi][64:128, :], 0.125),
                             R=pb_(bi), W=[aqT_b[wi][tb]])
                    else:
                        s.op("dve", lambda e, wi=wi, sl=sl, bi=bi: e.tensor_copy(out=akT[wi][:, sl], in_=psum[bi][:]),
                             R=pb_(bi), W=akT_b[wi][tb * 4:(tb + 1) * 4])
            for j4 in range(4):
                bi = SB[j4 % 4]
                for jj in range(4):
                    j = j4 * 4 + jj
                    tsl = slice(j * 128, (j + 1) * 128)
                    for k in range(NC_):
                        s.op("pe", lambda e, k=k, tsl=tsl, bi=bi, jj=jj: e.matmul(
                            psum[bi][:, jj * 128:(jj + 1) * 128], lhsT=xb[:, k, tsl], rhs=waqkv[:, k, 256:384],
                            start=(k == 0 and jj == 0), stop=(k == NC_ - 1), skip_group_check=True),
                            R=[waqkv_b, xb_b[k][j4]], W=pb_(bi))
                s.op("act", lambda e, wi=wi, j4=j4, bi=bi: e.copy(
                    out=Vp[wi][:, j4 * 4:(j4 + 1) * 4, :], in_=psum[bi][:].rearrange("p (a b) -> p a b", a=4)),
                    R=pb_(bi), W=Vp_b[wi][j4 * 4:(j4 + 1) * 4])
            if ch + 1 < 4:
                load_waqkv(ch + 1)
            steps = []
            for hh in range(2):
                h = 2 * ch + hh
                for qp in range(NTB):
                    kbs = [kb for kb in range(4 * qp + 4) if blk[h][kb][qp]]
                    for ki, kb in enumerate(kbs):
                        steps.append((hh, h, qp, kb, ki, len(kbs)))
            info = {}
            for idx in range(len(steps) + LOOK):
                if idx < len(steps):
                    hh, h, qp, kb, ki, nk = steps[idx]
                    pb = hh * 64
                    mi = h % 2
                    q0 = qp * TB
                    n0 = max(q0, 128 * kb)
                    n = q0 + TB - n0
                    bS = SB[stepc % 4]
                    ei = stepc % NE
                    stepc += 1
                    info[idx] = (ei, n0, n)
                    s.op("pe", lambda e, wi=wi, hh=hh, kb=kb, n0=n0, n=n, bS=bS: e.matmul(
                        psum[bS][:, 0:n], lhsT=akT[wi][:, kb * 128:(kb + 1) * 128],
                        rhs=aqT[wi][hh][:, n0:n0 + n], start=True, stop=True),
                        R=[akT_b[wi][kb], aqT_b[wi][qp], aqz_b[wi][hh]], W=pb_(bS))
                    s.op("act", lambda e, ei=ei, bS=bS, n=n: e.activation(out=Et[ei][:, 0:n], in_=psum[bS][:, 0:n], func=AF.Exp),
                         R=pb_(bS), W=[Et_b[ei]])
                    mo = n0 - 128 * kb
                    s.op("dve", lambda e, ei=ei, mi=mi, mo=mo, n=n: e.tensor_tensor(
                        out=Pt[ei][:, 0:n], in0=Et[ei][:, 0:n], in1=mask_s[mi][:, mo:mo + n], op=ALU.mult),
                        R=[Et_b[ei], mask_b[mi]], W=[Pt_b[ei]])
                    if h + 2 < 8 and (idx + 1 == len(steps) or steps[idx + 1][1] != h):
                        load_mask(h + 2)
                pidx = idx - LOOK
                if pidx >= 0:
                    hh, h, qp, kb, ki, nk = steps[pidx]
                    ei, n0, n = info[pidx]
                    pb = hh * 64
                    q0 = qp * TB
                    par = (h * NTB + qp) % 2
                    bO, bD = 4 + par, 6 + par
                    cs = slice(n0 - q0, n0 - q0 + n)
                    s.op("pe", lambda e, wi=wi, kb=kb, ei=ei, n=n, cs=cs, bO=bO, ki=ki, nk=nk: e.matmul(
                        psum[bO][:, cs], lhsT=Vp[wi][:, kb, :], rhs=Pt[ei][:, 0:n],
                        start=(ki == 0), stop=(ki == nk - 1), skip_group_check=True),
                        R=[Vp_b[wi][kb], Pt_b[ei]], W=pb_(bO))
                    s.op("pe", lambda e, ei=ei, n=n, cs=cs, bD=bD, ki=ki, nk=nk: e.matmul(
                        psum[bD][:, cs], lhsT=onesb[:], rhs=Pt[ei][:, 0:n],
                        start=(ki == 0), stop=(ki == nk - 1), skip_group_check=True),
                        R=[onesb_b, Pt_b[ei]], W=pb_(bD))
                    if ki == nk - 1:
                        ps_ = slice(pb, pb + 64)
                        s.op("act", lambda e, par=par, bD=bD, ps_=ps_: e.activation(out=dcp[par][ps_, :], in_=psum[bD][ps_, :], func=AF.Ln),
                             R=pb_(bD), W=[dcp_b[par]])
                        s.op("act", lambda e, par=par, ps_=ps_: e.activation(out=dcp[par][ps_, :], in_=dcp[par][ps_, :], func=AF.Exp, scale=-1.0),
                             R=[dcp_b[par]], W=[dcp_b[par]])
                        s.op("dve", lambda e, ch=ch, ps_=ps_, q0=q0, bO=bO, par=par: e.tensor_tensor(
                            out=o_aT[ps_, ch, q0:q0 + TB], in0=psum[bO][ps_, :], in1=dcp[par][ps_, :], op=ALU.mult),
                            R=pb_(bO) + [dcp_b[par]], W=[o_a_b[h][qp]])

        if stop == "C":
            return
        tap("o_aT", o_aT, [b for bb in o_a_b for b in bb])
        s.fence()
        A.release(mix_base)
        e_low_top = A.mark()
        mT = A.alloc([128, NC_, S], BF16, "mT")
        mT_b = [bufs(NTB) for _ in range(NC_)]
        e_base = A.mark()
        wE = [A.alloc([128, NC_, 256], BF16, "wgab") for _ in range(2)]
        wap = [A.alloc([128, 4, 128], BF16, "wap") for _ in range(2)]
        wgp = [A.alloc([128, 4, 128], BF16, "wgp") for _ in range(2)]
        wE_b = [bufs(3) for _ in range(2)]
        wE_sem = [[new_dsem("wE") for _ in range(3)] for _ in range(2)]
        sa = [A.alloc([128, TB], F32, "sa") for _ in range(2)]
        sbt = [A.alloc([128, TB], F32, "sbt") for _ in range(2)]
        sa_b, sbt_b = bufs(2), bufs(2)
        wo = A.alloc([128, NC_, NC_, 128], BF16, "wo")
        wo_b = bufs(NC_)
        e_top = A.mark()
        ecn = 0

        def load_wE(dc):
            wi = dc % 2
            s.op("pool", lambda e, wi=wi, dc=dc: e.dma_start(out=wE[wi][:], in_=wgab_d[dc].rearrange("p (k n) -> p k n", k=NC_)),
                 W=[wE_b[wi][0]], dsem=wE_sem[wi][0])
            s.op("pool", lambda e, wi=wi, dc=dc: e.dma_start(out=wap[wi][:], in_=wap_d[dc].rearrange("p (k n) -> p k n", k=4)),
                 W=[wE_b[wi][1]], dsem=wE_sem[wi][1])
            s.op("pool", lambda e, wi=wi, dc=dc: e.dma_start(out=wgp[wi][:], in_=wgp_d[dc].rearrange("p (k n) -> p k n", k=4)),
                 W=[wE_b[wi][2]], dsem=wE_sem[wi][2])

        load_wE(0)
        for dc in range(NC_):
            wi = dc % 2
            if dc + 1 < NC_:
                load_wE(dc + 1)
            if dc >= 4:
                for dco in (2 * (dc - 4), 2 * (dc - 4) + 1):
                    s.op("pool", lambda e, dco=dco: e.dma_start(out=wo[:, dco, :, :], in_=wo_d[dco].rearrange("p (k n) -> p k n", k=NC_)),
                         W=[wo_b[dco]], dsem=new_dsem("wo"))
            for tb in range(NTB):
                sl = slice(tb * TB, (tb + 1) * TB)
                pj = ecn % 2
                ecn += 1
                bGA, bGB, bPA, bPG = 0 + pj, 2 + pj, 4 + pj, 6 + pj
                for which, bi in ((0, bGA), (1, bGB)):
                    for k in range(NC_):
                        s.op("pe", lambda e, wi=wi, which=which, k=k, sl=sl, bi=bi: e.matmul(
                            psum[bi][:], lhsT=wE[wi][:, k, which * 128:(which + 1) * 128], rhs=xb[:, k, sl],
                            start=(k == 0), stop=(k == NC_ - 1)),
                            R=[wE_b[wi][0], xb_b[k][tb]], W=pb_(bi))
                for c in range(4):
                    s.op("pe", lambda e, wi=wi, c=c, sl=sl, bPA=bPA: e.matmul(
                        psum[bPA][:], lhsT=wap[wi][:, c, :], rhs=o_aT[:, c, sl], start=(c == 0), stop=(c == 3)),
                        R=[wE_b[wi][1], o_a_b[2 * c][tb], o_a_b[2 * c + 1][tb]], W=pb_(bPA))
                for c in range(4):
                    s.op("pe", lambda e, wi=wi, c=c, sl=sl, bPG=bPG: e.matmul(
                        psum[bPG][:], lhsT=wgp[wi][:, c, :], rhs=o_gnT[:, c, sl], start=(c == 0), stop=(c == 3)),
                        R=[wE_b[wi][2], o_gn_b[c][tb]], W=pb_(bPG))
                s.op("act", lambda e, pj=pj, bGA=bGA: e.activation(out=sa[pj][:], in_=psum[bGA][:], func=AF.Sigmoid),
                     R=pb_(bGA), W=[sa_b[pj]])
                s.op("act", lambda e, pj=pj, bGB=bGB: e.activation(out=sbt[pj][:], in_=psum[bGB][:], func=AF.Sigmoid),
                     R=pb_(bGB), W=[sbt_b[pj]])
                s.op("dve", lambda e, pj=pj, bPA=bPA: e.tensor_tensor(out=sa[pj][:], in0=psum[bPA][:], in1=sa[pj][:], op=ALU.mult),
                     R=pb_(bPA) + [sa_b[pj]], W=[sa_b[pj]])
                s.op("dve", lambda e, pj=pj, bPG=bPG: e.tensor_tensor(out=sbt[pj][:], in0=psum[bPG][:], in1=sbt[pj][:], op=ALU.mult),
                     R=pb_(bPG) + [sbt_b[pj]], W=[sbt_b[pj]])
                s.op("pool", lambda e, pj=pj, dc=dc, sl=sl: e.tensor_tensor(out=mT[:, dc, sl], in0=sa[pj][:], in1=sbt[pj][:], op=ALU.add),
                     R=[sa_b[pj], sbt_b[pj]], W=[mT_b[dc][tb]])
        tap("mT", mT, [b for bb in mT_b for b in bb])
        s.fence()
        A.release(e_top)
        Alow = Arena(nc, arena_base, e_low_top)
        ln_chunk, ln_flush = make_ln(l, 1, [Alow, A])
        yc = 0
        for tb in range(NTB):
            sl = slice(tb * TB, (tb + 1) * TB)
            for dc in range(NC_):
                bi = 4 + yc % 2
                yc += 1
                for k in range(NC_):
                    s.op("pe", lambda e, dc=dc, k=k, sl=sl, bi=bi: e.matmul(
                        psum[bi][:], lhsT=wo[:, dc, k, :], rhs=mT[:, k, sl], start=(k == 0), stop=(k == NC_ - 1)),
                        R=[wo_b[dc], mT_b[k][tb]], W=pb_(bi))
                s.op("dve", lambda e, bi=bi, dc=dc, sl=sl: e.tensor_tensor(
                    out=xs[:, dc, sl], in0=psum[bi][:], in1=xs[:, dc, sl], op=ALU.add),
                    R=pb_(bi) + [xs_b[dc][tb]], W=[xs_b[dc][tb]])
                ln_chunk(dc, tb)
        ln_flush()

    amask_blocks = _mask_blocks()
    out_ops = []
    lastp = phases[-1]
    if lastp[0] == "ffn":
        final_ln = (lastp[1], 0 if lastp[2] == 0 else 2)
    elif lastp[0] == "mix" and len(lastp) == 2:
        final_ln = (lastp[1], 1)
    else:
        final_ln = None
    for ph in phases:
        if ph[0] == "ffn":
            ffn(ph[1], ph[2])
        elif ph[0] == "mix":
            mixer(ph[1], ph[2] if len(ph) > 2 else None)
        else:
            raise ValueError(ph)

    if not out_ops:
        for c in range(NC_):
            for t in range(NTB):
                sl = slice(t * TB, (t + 1) * TB)
                s.op("dve", lambda e, c=c, sl=sl: e.tensor_scalar_mul(out=xs[:, c, sl], in0=xs[:, c, sl], scalar1=1.0 / ALPHA),
                     R=[xs_b[c][t]], W=[xs_b[c][t]])
        for c in range(NC_):
            out_ops.append(s.op("sp", lambda e, c=c: e.dma_start(out=yT_d[c * 128:(c + 1) * 128, :], in_=xs[:, c, :]),
                                R=xs_b[c], dsem=out_sem))
    fin = Buf()
    fin.w = out_ops[-1]
    s.op("sp", lambda e: e.nop(), R=[fin])

    s.finalize()
    from contextlib import ExitStack
    with ExitStack() as ctx:
        esem = {}
        for en in Sched.ENGS:
            esem[en] = ctx.enter_context(nc.semaphore(f"sem_{en}"))
        dsems = {}
        for nm in dsem_names:
            dsems[nm] = ctx.enter_context(nc.semaphore(f"d_{nm}"))
        with nc.Block() as block:
            @block.tensor
            def _(e):
                s.replay("pe", e, esem, dsems)

            @block.scalar
            def _(e):
                s.replay("act", e, esem, dsems)

            @block.vector
            def _(e):
                s.replay("dve", e, esem, dsems)

            @block.gpsimd
            def _(e):
                s.replay("pool", e, esem, dsems)

            @block.sync
            def _(e):
                s.replay("sp", e, esem, dsems)
    return nc


_MASK = None


def _alibi_mask():
    global _MASK
    if _MASK is None:
        d = np.arange(S)[None, :] - np.arange(128)[:, None]
        mult = ((d <= 128).astype(np.float64) + ((d % 4 == 0) & (d <= 512)) + ((d % 16 == 0) & (d <= 2048)))
        mult = np.where(d >= 0, mult, 0.0)
        slopes = np.exp2(-8.0 * np.arange(1, 9) / 8.0)
        m = mult[None] * np.exp(-slopes[:, None, None] * np.maximum(d, 0)[None])
        m = np.where(m < 1e-37, 0.0, m)
        _MASK = np.ascontiguousarray(m.astype(np.float32))
    return _MASK


def _mask_blocks():
    m = _alibi_mask()
    blk = [[[False] * NTB for _ in range(16)] for _ in range(8)]
    for h in range(8):
        for kb in range(16):
            for qp in range(NTB):
                n0 = max(qp * TB, 128 * kb)
                n1 = qp * TB + TB
                if n1 <= n0:
                    continue
                blk[h][kb][qp] = bool(m[h][:, n0 - 128 * kb:n1 - 128 * kb].any())
    return blk


def _consts():
    s_ = np.arange(128)[:, None]
    t_ = np.arange(128)[None, :]
    same = (s_ // 64) == (t_ // 64)
    U = np.where(same & (s_ <= t_), -1.0 / 16.0, 0.0)
    L = np.where(same & (s_ > t_), -1.0 / 16.0, 0.0)
    M = np.where(same & (s_ <= t_), 1.0, 0.0)
    o1 = np.full((128, 128), 1.0 / D)
    o2 = np.full((128, 128), 1.0 / 128.0)
    return np.ascontiguousarray(np.concatenate([U, L, M, o1, o2], axis=1).astype(np.float32))


def _lay_w13(w):
    return np.ascontiguousarray(w.reshape(NC_, 128, NF, 128).transpose(2, 1, 0, 3).reshape(NF, 128, D))


def _lay_ln(v):
    return np.ascontiguousarray(v.reshape(DEPTH, 3, NC_, 128).transpose(3, 0, 1, 2).reshape(128, NL3))


def _lay_cols(w):
    n = w.shape[1]
    return np.ascontiguousarray(w.reshape(NC_, 128, n).transpose(1, 0, 2).reshape(128, NC_ * n))


def make_inputs(phases, inp):
    m = {"ln_g": _lay_ln(inp["ln_g"]), "ln_b": _lay_ln(inp["ln_b"]), "consts": _consts()}
    has_mix = any(p[0] == "mix" for p in phases)
    if has_mix:
        m["amask"] = _alibi_mask()
        m["wgu"] = np.ascontiguousarray(inp["w_gate_up"].transpose(1, 0, 2))
        m["bgu"] = np.ascontiguousarray(np.broadcast_to(inp["b_gate_up"][None], (128, DEPTH, 256)))
        m["gng"] = np.ascontiguousarray(inp["gla_norm_g"].reshape(DEPTH, 4, 128).transpose(2, 0, 1).reshape(128, DEPTH * 4))
        m["gnb"] = np.ascontiguousarray(inp["gla_norm_b"].reshape(DEPTH, 4, 128).transpose(2, 0, 1).reshape(128, DEPTH * 4))
    for ph in phases:
        if ph[0] == "ffn":
            l, i = ph[1], ph[2]
            pre = "ffn1" if i == 0 else "ffn2"
            m[f"f{i}w1_{l}"] = _lay_w13(inp[pre + "_w1"][l])
            m[f"f{i}w3_{l}"] = _lay_w13(inp[pre + "_w3"][l])
            m[f"f{i}w2_{l}"] = np.ascontiguousarray(inp[pre + "_w2"][l])
        else:
            l = ph[1]
            w = inp["w_in"][l]
            m[f"wglr_{l}"] = _lay_cols(w[:, O_GLR:O_GLR + 16])
            m[f"wgqk_{l}"] = _lay_cols(w[:, O_GQ:O_GQ + 512])
            m[f"wgkv_{l}"] = _lay_cols(w[:, O_GK:O_GK + 768])
            m[f"wgr_{l}"] = _lay_cols(w[:, O_GR:O_GR + 512])
            m[f"waqkv_{l}"] = np.stack([_lay_cols(np.concatenate(
                [w[:, O_AQ + c * 128:O_AQ + (c + 1) * 128], w[:, O_AK + c * 128:O_AK + (c + 1) * 128],
                 w[:, O_AV + c * 128:O_AV + (c + 1) * 128]], axis=1)) for c in range(4)], axis=0)
            m[f"wgab_{l}"] = np.stack([_lay_cols(np.concatenate(
                [w[:, O_GA + c * 128:O_GA + (c + 1) * 128], w[:, O_GB + c * 128:O_GB + (c + 1) * 128]], axis=1))
                for c in range(NC_)], axis=0)
            wa = inp["w_attn_proj"][l]
            m[f"wap_{l}"] = np.ascontiguousarray(wa.reshape(4, 128, NC_, 128).transpose(2, 1, 0, 3).reshape(NC_, 128, 4 * 128))
            wg = inp["w_gla_proj"][l]
            m[f"wgp_{l}"] = np.ascontiguousarray(wg.reshape(4, 128, NC_, 128).transpose(2, 1, 0, 3).reshape(NC_, 128, 4 * 128))
            wo_ = inp["w_out"][l]
            m[f"wo_{l}"] = np.ascontiguousarray(wo_.reshape(NC_, 128, NC_, 128).transpose(2, 1, 0, 3).reshape(NC_, 128, NC_ * 128))
    return m


def run_phases(phases, x, inp, n_cores=8, trace=False, debug=None):
    nc = build(phases, debug)
    shared = make_inputs(phases, inp)
    in_maps = []
    for b in range(n_cores):
        d = dict(shared)
        d["xT"] = np.ascontiguousarray(x[b].T)
        in_maps.append(d)
    res = run_bass_kernel_spmd(nc, in_maps, core_ids=list(range(n_cores)), trace=trace)
    out = np.stack([np.ascontiguousarray(r["yT"].T) for r in res.results], axis=0)
    return out, res


LAUNCHES = [[("ffn", 0, 0), ("mix", 0), ("ffn", 0, 1), ("ffn", 1, 0), ("mix", 1), ("ffn", 1, 1)]]


def kernel(**inputs):
    inp = {k: np.asarray(v) for k, v in inputs.items()}
    x = np.ascontiguousarray(inp["x"], dtype=np.float32)
    for phases in LAUNCHES:
        x, _ = run_phases(phases, x, inp)
    return np.ascontiguousarray(x, dtype=np.float32)
```

```python
import numpy as np
import concourse.bass as bass
import concourse.mybir as mybir
from concourse.bass_utils import run_bass_kernel_spmd

F32 = mybir.dt.float32
F32R = mybir.dt.float32r

BF16 = mybir.dt.bfloat16
AF = mybir.ActivationFunctionType
ALU = mybir.AluOpType

S = 2048
D = 1024
DFF = 2816
NC_ = 8
NTB = 4
TB = 512
NF = 22
DEPTH = 2
ALPHA = float((2 * DEPTH) ** 0.25)
LN_EPS = 1e-5
FFN_GROUPS = [3, 3, 4, 4, 4, 4]
GMAX = 4
NL3 = DEPTH * 3 * NC_
SB_BASE = 16512
SB_TOP = 229344

O_AQ, O_AK, O_AV = 0, 512, 1024
O_GQ, O_GK, O_GV, O_GLR, O_GR = 1536, 1792, 2048, 2560, 2576
O_GA, O_GB = 3088, 4112
N_IN = 5136


class Buf:
    __slots__ = ("name", "w", "r", "excl")

    def __init__(self, name="", excl=False):
        self.name = name
        self.w = None
        self.r = {}
        self.excl = excl


def bufs(n):
    return [Buf() for _ in range(n)]


class Op:
    __slots__ = ("eng", "fn", "deps", "needed", "semval", "dsem", "dval")

    def __init__(self, eng, fn, deps, dsem):
        self.eng = eng
        self.fn = fn
        self.deps = deps
        self.needed = False
        self.semval = None
        self.dsem = dsem
        self.dval = None


class Sched:
    ENGS = ("pe", "act", "dve", "pool", "sp")

    def __init__(self):
        self.q = {e: [] for e in self.ENGS}
        self.dma_count = {}
        self.last_dma = {}
        self.extra = {e: [] for e in self.ENGS}

    def op(self, eng, fn, R=(), W=(), dsem=None, after=()):
        deps = [(3, a) for a in after if a is not None]
        if any(b.excl for b in R):
            W = list(W) + [b for b in R if b.excl]
            R = [b for b in R if not b.excl]
        W = list(dict.fromkeys(W))
        for b in R:
            if b.w is not None:
                deps.append((0, b.w))
        for b in W:
            if b.w is not None:
                deps.append((1, b.w))
            for r in b.r.values():
                deps.append((2, r))
        if self.extra[eng]:
            deps.extend((0, d) for d in self.extra[eng])
            self.extra[eng] = []
        o = Op(eng, fn, deps, dsem)
        if dsem is not None:
            self.dma_count[dsem] = self.dma_count.get(dsem, 0) + 16
            o.dval = self.dma_count[dsem]
            self.last_dma[dsem] = o
        self.q[eng].append(o)
        key = dsem if dsem is not None else eng
        for b in R:
            b.r[key] = o
        for b in W:
            b.w = o
            b.r = {}
        return o

    def fence(self):
        snap = []
        for e in self.ENGS:
            for o in reversed(self.q[e]):
                if o.dsem is None:
                    snap.append(o)
                    break
        snap.extend(self.last_dma.values())
        for e in self.ENGS:
            self.extra[e] = list(snap)

    def finalize(self):
        for eng in self.ENGS:
            for o in self.q[eng]:
                keep = []
                for kind, d in o.deps:
                    if d is o:
                        continue
                    if d.dsem is not None:
                        keep.append(d)
                    elif d.eng == o.eng:
                        if o.eng == "pe" and kind != 3:
                            continue
                        keep.append(d)
                    else:
                        keep.append(d)
                for d in keep:
                    if d.dsem is None:
                        d.needed = True
                o.deps = keep
        for eng in self.ENGS:
            c = 0
            for o in self.q[eng]:
                if o.dsem is None and o.needed:
                    c += 1
                    o.semval = c

    def replay(self, eng, e, esem, dsems):
        seen = {}
        for o in self.q[eng]:
            for d in o.deps:
                if d.dsem is not None:
                    key, val, sem = ("d", d.dsem), d.dval, dsems[d.dsem]
                else:
                    key, val, sem = ("e", d.eng), d.semval, esem[d.eng]
                if seen.get(key, 0) >= val:
                    continue
                seen[key] = val
                e.wait_ge(sem, val)
            ins = o.fn(e)
            if o.dsem is not None:
                ins.then_inc(dsems[o.dsem], 16)
            elif o.needed:
                ins.then_inc(esem[eng], 1)


DT_SIZE = {F32: 4, BF16: 2, F32R: 4}


class Arena:
    UID = 0

    def __init__(self, nc, base, top):
        self.nc = nc
        self.base = base
        self.top = top
        self.off = base
        self.uid = 0
        self.peak = base

    def alloc(self, shape, dt, name="t"):
        n = 1
        for d in shape[1:]:
            n *= d
        nbytes = (n * DT_SIZE[dt] + 63) // 64 * 64
        if self.off + nbytes > self.top:
            raise RuntimeError(f"arena overflow allocating {name} {shape}: off={self.off - self.base} need {nbytes} cap {self.top - self.base}")
        Arena.UID += 1
        t = self.nc.alloc_sbuf_tensor_at(f"{name}_{Arena.UID}", list(shape), dt, offset=self.off)
        self.off += nbytes
        self.peak = max(self.peak, self.off)
        return t

    def mark(self):
        return self.off

    def release(self, m):
        self.off = m


def build(phases, debug=None):
    nc = bass.Bass("TRN2", target_bir_lowering=False)
    s = Sched()
    dram = {}
    dsem_names = []

    def new_dsem(name):
        nm = f"{name}_{len(dsem_names)}"
        dsem_names.append(nm)
        return nm

    def din(name, shape, dt=F32):
        if name not in dram:
            dram[name] = nc.dram_tensor(name, list(shape), dt, kind="ExternalInput").ap()
        return dram[name]

    debug = debug or ()

    def tap(name, t, bl):
        if name in debug:
            dd = nc.dram_tensor("dbg_" + name, list(t.shape), t.dtype, kind="ExternalOutput").ap()
            s.op("sp", lambda e: e.dma_start(out=dd, in_=t[:]), R=bl, dsem=new_dsem("dbg"))

    xT_d = din("xT", [D, S])
    yT_d = nc.dram_tensor("yT", [D, S], F32, kind="ExternalOutput").ap()
    lng_d = din("ln_g", [128, NL3])
    lnb_d = din("ln_b", [128, NL3])
    consts_d = din("consts", [128, 5 * 128])
    has_mix = any(p[0] == "mix" for p in phases)

    A = Arena(nc, SB_BASE, SB_TOP)
    xs = A.alloc([128, NC_, S], F32, "xs")
    xb = A.alloc([128, NC_, S], BF16, "xb")
    xs_b = [bufs(NTB) for _ in range(NC_)]
    xb_b = [bufs(NTB) for _ in range(NC_)]
    lng = A.alloc([128, NL3], F32, "lng")
    lnb = A.alloc([128, NL3], F32, "lnb")
    lnga = A.alloc([128, NL3], F32, "lnga")
    lnba = A.alloc([128, NL3], F32, "lnba")
    ln_c = Buf()
    consts = A.alloc([128, 5 * 128], F32, "consts")
    consts_b = Buf()
    Umat = consts[:, 0:128]
    Lmat = consts[:, 128:256]
    Mblk = consts[:, 256:384]
    ones = consts[:, 384:512]
    gones = consts[:, 512:640]
    ones1 = A.alloc([128, 64], F32, "ones1")
    ones1_b = Buf()
    ones_r = A.alloc([128, 128], F32R, "ones_r")
    gones_r = A.alloc([128, 128], F32R, "gones_r")
    onesr_b = Buf()
    mixc_b = Buf()
    if has_mix:
        wgu = A.alloc([16, DEPTH, 256], F32, "wgu")
        bgu = A.alloc([128, DEPTH, 256], F32, "bgu")
        gng = A.alloc([128, DEPTH * 4], F32, "gng")
        gnb = A.alloc([128, DEPTH * 4], F32, "gnb")
    arena_base = A.mark()

    psum = [nc.alloc_psum_tensor(f"bank{i}", [128, TB], F32) for i in range(8)]
    pq = [[Buf(f"bank{i}", excl=True)] * 4 for i in range(8)]

    def pb_(i, c0=0, c1=TB):
        return pq[i][c0 // 128:(c1 + 127) // 128]

    out_sem = new_dsem("out")

    lng_b0, lnb_b0 = Buf(), Buf()
    s.op("sp", lambda e: e.dma_start(out=lng[:], in_=lng_d), W=[lng_b0], dsem=new_dsem("io"))
    s.op("sp", lambda e: e.dma_start(out=lnb[:], in_=lnb_d), W=[lnb_b0], dsem=new_dsem("io"))
    s.op("sp", lambda e: e.dma_start(out=consts[:], in_=consts_d), W=[consts_b], dsem=new_dsem("io"))
    if has_mix:
        wgu_d = din("wgu", [16, DEPTH, 256])
        bgu_d = din("bgu", [128, DEPTH, 256])
        gng_d = din("gng", [128, DEPTH * 4])
        gnb_d = din("gnb", [128, DEPTH * 4])
        mb = bufs(4)
        s.op("sp", lambda e: e.dma_start(out=wgu[:], in_=wgu_d), W=[mb[0]], dsem=new_dsem("io"))
        s.op("sp", lambda e: e.dma_start(out=bgu[:], in_=bgu_d), W=[mb[1]], dsem=new_dsem("io"))
        s.op("sp", lambda e: e.dma_start(out=gng[:], in_=gng_d), W=[mb[2]], dsem=new_dsem("io"))
        s.op("sp", lambda e: e.dma_start(out=gnb[:], in_=gnb_d), W=[mb[3]], dsem=new_dsem("io"))
        s.op("dve", lambda e: e.memset(ones1[:], 1.0), R=mb, W=[ones1_b, mixc_b])
    xT_v = xT_d.rearrange("(c p) t -> p c t", p=128)
    for t in range(NTB):
        s.op("sp", lambda e, t=t: e.dma_start(out=xs[:, :, t * TB:(t + 1) * TB], in_=xT_v[:, :, t * TB:(t + 1) * TB]),
             W=[xs_b[c][t] for c in range(NC_)], dsem=new_dsem("iox"))
    s.op("act", lambda e: e.copy(out=ones_r[:], in_=ones), R=[consts_b], W=[onesr_b])
    s.op("act", lambda e: e.copy(out=gones_r[:], in_=gones), R=[consts_b], W=[onesr_b])
    s.op("act", lambda e: e.mul(lnga[:], lng[:], ALPHA), R=[lng_b0], W=[ln_c])
    s.op("act", lambda e: e.mul(lnba[:], lnb[:], ALPHA), R=[lnb_b0], W=[ln_c])
    for t in range(NTB):
        for c in range(NC_):
            sl = slice(t * TB, (t + 1) * TB)
            s.op("dve", lambda e, c=c, sl=sl: e.tensor_copy(out=xb[:, c, sl], in_=xs[:, c, sl]),
                 R=[xs_b[c][t]], W=[xb_b[c][t]])
            s.op("act", lambda e, c=c, sl=sl: e.mul(xs[:, c, sl], xs[:, c, sl], ALPHA),
                 R=[xs_b[c][t]], W=[xs_b[c][t]])

    def layer_norm(l, i):
        col0 = (l * 3 + i) * NC_
        s.fence()
        A.release(arena_base)
        sq = A.alloc([128, NC_, TB], F32, "sq")
        sq_b = bufs(NC_)
        mean_sb = [A.alloc([128, TB], F32, "mean") for _ in range(2)]
        m2_sb = [A.alloc([128, TB], F32, "m2") for _ in range(2)]
        rstd_sb = [A.alloc([128, TB], F32, "rstd") for _ in range(2)]
        mean_b, m2_b, rstd_b = bufs(2), bufs(2), bufs(2)
        t1 = [A.alloc([128, TB], F32, "t1") for _ in range(2)]
        t2 = [A.alloc([128, TB], F32, "t2") for _ in range(3)]
        t1_b, t2_b = bufs(2), bufs(3)
        cn = {"t1": 0, "t2": 0}

        def stats(t):
            sl = slice(t * TB, (t + 1) * TB)
            p = t % 2
            bm, bq = (6, 7) if p == 0 else (4, 5)
            pm, pq_ = psum[bm], psum[bq]
            for c in range(NC_):
                s.op("act", lambda e, c=c, sl=sl: e.activation(out=sq[:, c, :], in_=xs[:, c, sl], func=AF.Square),
                     R=[xs_b[c][t]], W=[sq_b[c]])
            for c in range(NC_):
                s.op("pe", lambda e, c=c, sl=sl, pm=pm: e.matmul(pm[:], lhsT=ones, rhs=xs[:, c, sl],
                                                                 start=(c == 0), stop=(c == NC_ - 1)),
                     R=[consts_b, xs_b[c][t]], W=pb_(bm))
            for c in range(NC_):
                s.op("pe", lambda e, c=c, pq_=pq_: e.matmul(pq_[:], lhsT=ones, rhs=sq[:, c, :],
                                                            start=(c == 0), stop=(c == NC_ - 1)),
                     R=[consts_b, sq_b[c]], W=pb_(bq))
            s.op("act", lambda e, pm=pm, p=p: e.activation(out=m2_sb[p][:], in_=pm[:], func=AF.Square), R=pb_(bm), W=[m2_b[p]])
            s.op("act", lambda e, pm=pm, p=p: e.copy(out=mean_sb[p][:], in_=pm[:]), R=pb_(bm), W=[mean_b[p]])
            s.op("dve", lambda e, pq_=pq_, p=p: e.tensor_tensor(out=rstd_sb[p][:], in0=pq_[:], in1=m2_sb[p][:], op=ALU.subtract),
                 R=pb_(bq) + [m2_b[p]], W=[rstd_b[p]])
            s.op("act", lambda e, p=p: e.activation(out=m2_sb[p][:], in_=rstd_sb[p][:], func=AF.Ln, bias=LN_EPS, scale=1.0),
                 R=[rstd_b[p]], W=[m2_b[p]])
            s.op("act", lambda e, p=p: e.activation(out=rstd_sb[p][:], in_=m2_sb[p][:], func=AF.Exp, scale=-0.5),
                 R=[m2_b[p]], W=[rstd_b[p]])

        def norm(t):
            sl = slice(t * TB, (t + 1) * TB)
            p = t % 2
            for c in range(NC_):
                j = cn["t1"] % 2
                cn["t1"] += 1
                j2 = cn["t2"] % 3
                cn["t2"] += 1
                s.op("dve", lambda e, c=c, sl=sl, j=j, p=p: e.tensor_tensor(out=t1[j][:], in0=xs[:, c, sl], in1=mean_sb[p][:], op=ALU.subtract),
                     R=[xs_b[c][t], mean_b[p]], W=[t1_b[j]])
                s.op("pool", lambda e, j=j, j2=j2, p=p: e.tensor_tensor(out=t2[j2][:], in0=t1[j][:], in1=rstd_sb[p][:], op=ALU.mult),
                     R=[t1_b[j], rstd_b[p]], W=[t2_b[j2]])
                s.op("act", lambda e, c=c, sl=sl, j2=j2: e.activation(out=xs[:, c, sl], in_=t2[j2][:], func=AF.Identity,
                                                                   scale=lnga[:, col0 + c:col0 + c + 1],
                                                                   bias=lnba[:, col0 + c:col0 + c + 1]),
                     R=[t2_b[j2], ln_c], W=[xs_b[c][t]])
                s.op("dve", lambda e, c=c, sl=sl, j2=j2: e.tensor_scalar(
                    out=xb[:, c, sl], in0=t2[j2][:], scalar1=lng[:, col0 + c:col0 + c + 1], scalar2=lnb[:, col0 + c:col0 + c + 1],
                    op0=ALU.mult, op1=ALU.add),
                    R=[t2_b[j2], ln_c], W=[xb_b[c][t]])

        stats(0)
        for t in range(NTB):
            if t + 1 < NTB:
                stats(t + 1)
            norm(t)

    def make_ln(l, i, arenas, sbanks=((0, 1), (2, 3)), lag=2):
        col0 = (l * 3 + i) * NC_
        final = (l, i) == final_ln
        g_xs, b_xs = (lng, lnb) if final else (lnga, lnba)
        yT_v = yT_d.rearrange("(c p) t -> p c t", p=128)

        def al(shape, dt, name):
            for a in arenas:
                n = 1
                for d in shape[1:]:
                    n *= d
                if a.off + (n * DT_SIZE[dt] + 63) // 64 * 64 <= a.top:
                    return a.alloc(shape, dt, name)
            raise RuntimeError("make_ln: no room for " + name)

        NSQ = 3
        sqr = [al([128, TB], F32R, "lsq") for _ in range(NSQ)]
        sqr_b = bufs(NSQ)
        mean_sb = [al([128, TB], F32, "lmean") for _ in range(2)]
        m2_sb = [al([128, TB], F32, "lm2") for _ in range(2)]
        rstd_sb = [al([128, TB], F32, "lrstd") for _ in range(2)]
        mean_b, m2_b, rstd_b = bufs(2), bufs(2), bufs(2)
        NT = 3
        t1 = [al([128, TB], F32, "lt1") for _ in range(NT)]
        t2 = [al([128, TB], F32, "lt2") for _ in range(NT)]
        t1_b, t2_b = bufs(NT), bufs(NT)
        cn = {"sq": 0, "t1": 0, "t2": 0, "seen": {}}
        pending = []
        avail = []
        fl = {"s1": None, "s2": None}

        def tick():
            if fl["s2"] is not None:
                c, t, j2 = fl["s2"]
                sl = slice(t * TB, (t + 1) * TB)
                s.op("act", lambda e, c=c, sl=sl, j2=j2: e.activation(out=xs[:, c, sl], in_=t2[j2][:], func=AF.Identity,
                                                                   scale=g_xs[:, col0 + c:col0 + c + 1],
                                                                   bias=b_xs[:, col0 + c:col0 + c + 1]),
                     R=[t2_b[j2], ln_c], W=[xs_b[c][t]])
                if not final:
                    s.op("dve", lambda e, c=c, sl=sl, j2=j2: e.tensor_scalar(
                        out=xb[:, c, sl], in0=t2[j2][:], scalar1=lng[:, col0 + c:col0 + c + 1], scalar2=lnb[:, col0 + c:col0 + c + 1],
                        op0=ALU.mult, op1=ALU.add),
                        R=[t2_b[j2], ln_c], W=[xb_b[c][t]])
                else:
                    ndone = cn.get(("done", t), 0) + 1
                    cn[("done", t)] = ndone
                    if ndone == NC_:
                        s.op("sp", lambda e, sl=sl: e.dma_start(out=yT_v[:, :, sl], in_=xs[:, :, sl]),
                             R=[xs_b[cc][t] for cc in range(NC_)], dsem=out_sem)
                        out_ops.append(s.q["sp"][-1])
                fl["s2"] = None
            if fl["s1"] is not None:
                c, t, j = fl["s1"]
                p = t % 2
                j2 = cn["t2"] % NT
                cn["t2"] += 1
                s.op("pool", lambda e, j=j, j2=j2, p=p: e.tensor_tensor(out=t2[j2][:], in0=t1[j][:], in1=rstd_sb[p][:], op=ALU.mult),
                     R=[t1_b[j], rstd_b[p]], W=[t2_b[j2]])
                fl["s2"] = (c, t, j2)
                fl["s1"] = None
            if avail:
                c, t = avail.pop(0)
                sl = slice(t * TB, (t + 1) * TB)
                p = t % 2
                j = cn["t1"] % NT
                cn["t1"] += 1
                s.op("dve", lambda e, c=c, sl=sl, j=j, p=p: e.tensor_tensor(out=t1[j][:], in0=xs[:, c, sl], in1=mean_sb[p][:], op=ALU.subtract),
                     R=[xs_b[c][t], mean_b[p]], W=[t1_b[j]])
                fl["s1"] = (c, t, j)

        def emit(entry):
            dc, t, k = entry
            sl = slice(t * TB, (t + 1) * TB)
            p = t % 2
            bm, bq = sbanks[p]
            n = cn["seen"].get(t, 0)
            cn["seen"][t] = n + 1
            s.op("pe", lambda e, dc=dc, sl=sl, bm=bm, n=n: e.matmul(psum[bm][:], lhsT=ones, rhs=xs[:, dc, sl],
                                                                   start=(n == 0), stop=(n == NC_ - 1)),
                 R=[consts_b, xs_b[dc][t]], W=pb_(bm))
            s.op("pe", lambda e, k=k, bq=bq, n=n: e.matmul(psum[bq][:], lhsT=ones_r[:], rhs=sqr[k][:],
                                                           start=(n == 0), stop=(n == NC_ - 1)),
                 R=[onesr_b, sqr_b[k]], W=pb_(bq))
            if n == NC_ - 1:
                s.op("act", lambda e, bm=bm, p=p: e.activation(out=m2_sb[p][:], in_=psum[bm][:], func=AF.Square), R=pb_(bm), W=[m2_b[p]])
                s.op("act", lambda e, bm=bm, p=p: e.copy(out=mean_sb[p][:], in_=psum[bm][:]), R=pb_(bm), W=[mean_b[p]])
                s.op("dve", lambda e, bq=bq, p=p: e.tensor_tensor(out=rstd_sb[p][:], in0=psum[bq][:], in1=m2_sb[p][:], op=ALU.subtract),
                     R=pb_(bq) + [m2_b[p]], W=[rstd_b[p]])
                s.op("act", lambda e, p=p: e.activation(out=m2_sb[p][:], in_=rstd_sb[p][:], func=AF.Ln, bias=LN_EPS, scale=1.0),
                     R=[rstd_b[p]], W=[m2_b[p]])
                s.op("act", lambda e, p=p: e.activation(out=rstd_sb[p][:], in_=m2_sb[p][:], func=AF.Exp, scale=-0.5),
                     R=[m2_b[p]], W=[rstd_b[p]])
                avail.extend((c, t) for c in range(NC_))

        def chunk_done(dc, t):
            sl = slice(t * TB, (t + 1) * TB)
            k = cn["sq"] % NSQ
            cn["sq"] += 1
            s.op("act", lambda e, dc=dc, sl=sl, k=k: e.activation(out=sqr[k][:], in_=xs[:, dc, sl], func=AF.Square),
                 R=[xs_b[dc][t]], W=[sqr_b[k]])
            pending.append((dc, t, k))
            if len(pending) > lag:
                emit(pending.pop(0))
            tick()

        def flush():
            while pending:
                emit(pending.pop(0))
                tick()
            while avail or fl["s1"] is not None or fl["s2"] is not None:
                tick()

        return chunk_done, flush

    def ffn(l, i):
        w1_d = din(f"f{i}w1_{l}", [NF, 128, D])
        w3_d = din(f"f{i}w3_{l}", [NF, 128, D])
        w2_d = din(f"f{i}w2_{l}", [DFF, D])
        s.fence()
        A.release(arena_base)
        W13_SLOTS = 3
        w13 = [A.alloc([128, 2, D], BF16, "w13") for _ in range(W13_SLOTS)]
        w13_b = [bufs(2) for _ in range(W13_SLOTS)]
        w13_sem = [[new_dsem("w13") for _ in range(2)] for _ in range(W13_SLOTS)]
        w2 = [A.alloc([128, GMAX, D], BF16, "w2") for _ in range(2)]
        w2_b = bufs(2)
        w2_sem = [new_dsem("w2") for _ in range(2)]
        gT = [A.alloc([128, GMAX, S], BF16, "gT") for _ in range(2)]
        gT_b = [[bufs(NTB) for _ in range(GMAX)] for _ in range(2)]
        silu_t = [A.alloc([128, TB], F32, "silu") for _ in range(2)]
        silu_b = bufs(2)
        ln_chunk, ln_flush = make_ln(l, 0 if i == 0 else 2, [A])
        cnt = {"w13": 0, "w2": 0, "psA": 0, "psY": 0, "silu": 0}
        m0 = 0
        for gi, G in enumerate(FFN_GROUPS):
            ms = list(range(m0, m0 + G))
            m0 += G
            gs = gi % 2
            ws = cnt["w2"] % 2
            cnt["w2"] += 1
            s.op("pool", lambda e, ws=ws, ms=ms, G=G: e.dma_start(
                out=w2[ws][:, 0:G, :],
                in_=w2_d[ms[0] * 128:(ms[0] + G) * 128, :].rearrange("(g p) n -> p g n", p=128)),
                W=[w2_b[ws]], dsem=w2_sem[ws])
            for ml, m in enumerate(ms):
                slot = cnt["w13"] % W13_SLOTS
                cnt["w13"] += 1
                s.op("pool", lambda e, slot=slot, m=m: e.dma_start(out=w13[slot][:, 0, :], in_=w1_d[m]),
                     W=[w13_b[slot][0]], dsem=w13_sem[slot][0])
                s.op("pool", lambda e, slot=slot, m=m: e.dma_start(out=w13[slot][:, 1, :], in_=w3_d[m]),
                     W=[w13_b[slot][1]], dsem=w13_sem[slot][1])
                for t in range(NTB):
                    sl = slice(t * TB, (t + 1) * TB)
                    pj = cnt["psA"] % 2
                    cnt["psA"] += 1
                    for which, bi in ((0, pj), (1, 2 + pj)):
                        for k in range(NC_):
                            s.op("pe", lambda e, slot=slot, which=which, k=k, sl=sl, bi=bi: e.matmul(
                                psum[bi][:], lhsT=w13[slot][:, which, k * 128:(k + 1) * 128], rhs=xb[:, k, sl],
                                start=(k == 0), stop=(k == NC_ - 1)),
                                R=[w13_b[slot][which], xb_b[k][t]], W=pb_(bi))
                    sj = cnt["silu"] % 2
                    cnt["silu"] += 1
                    s.op("act", lambda e, pj=pj, sj=sj: e.activation(out=silu_t[sj][:], in_=psum[pj][:], func=AF.Silu),
                         R=pb_(pj), W=[silu_b[sj]])
                    s.op("dve", lambda e, pj=pj, sj=sj, gs=gs, ml=ml, sl=sl: e.tensor_tensor(
                        out=gT[gs][:, ml, sl], in0=psum[2 + pj][:], in1=silu_t[sj][:], op=ALU.mult),
                        R=pb_(2 + pj) + [silu_b[sj]], W=[gT_b[gs][ml][t]])
            last = (gi == len(FFN_GROUPS) - 1)
            order = [(dc, t) for t in range(NTB) for dc in range(NC_)] if last else [(dc, t) for dc in range(NC_) for t in range(NTB)]
            for dc, t in order:
                if True:
                    sl = slice(t * TB, (t + 1) * TB)
                    bi = 4 + cnt["psY"] % 2
                    cnt["psY"] += 1
                    for ml in range(G):
                        s.op("pe", lambda e, ws=ws, ml=ml, dc=dc, gs=gs, sl=sl, bi=bi, G=G: e.matmul(
                            psum[bi][:], lhsT=w2[ws][:, ml, dc * 128:(dc + 1) * 128], rhs=gT[gs][:, ml, sl],
                            start=(ml == 0), stop=(ml == G - 1)),
                            R=[w2_b[ws], gT_b[gs][ml][t]], W=pb_(bi))
                    s.op("dve", lambda e, bi=bi, dc=dc, sl=sl: e.scalar_tensor_tensor(
                        out=xs[:, dc, sl], in0=psum[bi][:], scalar=0.5, in1=xs[:, dc, sl],
                        op0=ALU.mult, op1=ALU.add),
                        R=pb_(bi) + [xs_b[dc][t]], W=[xs_b[dc][t]])
                    if last:
                        ln_chunk(dc, t)
        ln_flush()

    def mixer(l, stop=None):
        amask_d = din("amask", [8, 128, S])
        wglr_d = din(f"wglr_{l}", [128, NC_ * 16])
        wgqk_d = din(f"wgqk_{l}", [128, NC_ * 512])
        wgkv_d = din(f"wgkv_{l}", [128, NC_ * 768])
        wgr_d = din(f"wgr_{l}", [128, NC_ * 512])
        waqkv_d = din(f"waqkv_{l}", [4, 128, NC_ * 384])
        wgab_d = din(f"wgab_{l}", [NC_, 128, NC_ * 256])
        wap_d = din(f"wap_{l}", [NC_, 128, 4 * 128])
        wgp_d = din(f"wgp_{l}", [NC_, 128, 4 * 128])
        wo_d = din(f"wo_{l}", [NC_, 128, NC_ * 128])
        blk = amask_blocks
        s.fence()
        A.release(arena_base)
        o_gnT = A.alloc([128, 4, S], BF16, "o_gnT")
        o_gn_b = [bufs(NTB) for _ in range(4)]
        o_a_b = [bufs(NTB) for _ in range(8)]
        mix_base = A.mark()

        qT = A.alloc([128, 2, S], BF16, "qT")
        kT = A.alloc([128, 2, S], BF16, "kT")
        qT_b = [bufs(16) for _ in range(2)]
        kT_b = [bufs(16) for _ in range(2)]
        khat = A.alloc([128, 16, 256], BF16, "khat")
        khat_b = bufs(16)
        gv = A.alloc([128, 16, 512], BF16, "gv")
        gv_b = bufs(16)
        dec = A.alloc([128, 4, 32], F32, "dec")
        dec_b = bufs(NTB)
        gla_base = A.mark()
        wglr = A.alloc([128, NC_, 16], BF16, "wglr")
        wgqk = A.alloc([128, NC_, 512], BF16, "wgqk")
        wgkv = A.alloc([128, NC_, 768], BF16, "wgkv")
        wB_b = bufs(3)
        s.op("pool", lambda e: e.dma_start(out=wglr[:], in_=wglr_d.rearrange("p (k n) -> p k n", k=NC_)),
             W=[wB_b[0]], dsem=new_dsem("wB"))
        s.op("pool", lambda e: e.dma_start(out=wgqk[:], in_=wgqk_d.rearrange("p (k n) -> p k n", k=NC_)),
             W=[wB_b[1]], dsem=new_dsem("wB"))
        s.op("pool", lambda e: e.dma_start(out=wgkv[:], in_=wgkv_d.rearrange("p (k n) -> p k n", k=NC_)),
             W=[wB_b[2]], dsem=new_dsem("wB"))
        glrT = [A.alloc([16, TB], F32, "glrT") for _ in range(2)]
        glrT_b = bufs(2)
        z_sb = [A.alloc([128, 256], F32, "z") for _ in range(2)]
        z_b = bufs(2)
        la_sb = [A.alloc([128, 256], F32, "la") for _ in range(2)]
        la_b = bufs(2)
        Eb = A.alloc([128, 2, TB], F32, "Eb")
        Einv = A.alloc([128, 2, TB], F32, "Einv")
        E_b, Einv_b = bufs(2), bufs(2)
        Ft = A.alloc([128, 4, 256], F32, "Ft")
        F_b = bufs(4)
        zc = 0
        pc = 0
        for tb in range(NTB):
            sl = slice(tb * TB, (tb + 1) * TB)
            gj = tb % 2
            for k in range(NC_):
                s.op("pe", lambda e, k=k, sl=sl: e.matmul(psum[0][0:16, :], lhsT=wglr[:, k, :], rhs=xb[:, k, sl],
                                                         start=(k == 0), stop=(k == NC_ - 1)),
                     R=[wB_b[0], xb_b[k][tb]], W=pb_(0))
            s.op("act", lambda e, gj=gj: e.copy(out=glrT[gj][:], in_=psum[0][0:16, :]), R=pb_(0), W=[glrT_b[gj]])
            for jj in range(4):
                j = tb * 4 + jj
                zi = zc % 2
                zc += 1
                bz = 1 + zi
                s.op("pe", lambda e, gj=gj, jj=jj, bz=bz: e.matmul(
                    psum[bz][:, 0:256], lhsT=glrT[gj][:, jj * 128:(jj + 1) * 128], rhs=wgu[:, l, :],
                    start=True, stop=True),
                    R=[glrT_b[gj], mixc_b], W=pb_(bz, 0, 256))
                s.op("dve", lambda e, zi=zi, bz=bz: e.tensor_tensor(out=z_sb[zi][:], in0=psum[bz][:, 0:256], in1=bgu[:, l, :], op=ALU.add),
                     R=pb_(bz, 0, 256) + [mixc_b], W=[z_b[zi]])
                tsl = slice(j * 128, (j + 1) * 128)
                bi2 = 6 + pc % 2
                pc += 1
                for k in range(NC_):
                    s.op("pe", lambda e, k=k, tsl=tsl, bi2=bi2: e.matmul(
                        psum[bi2][:], lhsT=xb[:, k, tsl], rhs=wgkv[:, k, 256:768],
                        start=(k == 0), stop=(k == NC_ - 1)),
                        R=[wB_b[2], xb_b[k][tb]], W=pb_(bi2))
                s.op("dve", lambda e, j=j, bi2=bi2: e.tensor_copy(out=gv[:, j, :], in_=psum[bi2][:]),
                     R=pb_(bi2), W=[gv_b[j]])
                s.op("act", lambda e, zi=zi: e.activation(out=z_sb[zi][:], in_=z_sb[zi][:], func=AF.Exp, scale=-1.0),
                     R=[z_b[zi]], W=[z_b[zi]])
                s.op("act", lambda e, zi=zi: e.activation(out=la_sb[zi][:], in_=z_sb[zi][:], func=AF.Ln, bias=1.0, scale=1.0),
                     R=[z_b[zi]], W=[la_b[zi]])
                for ch in range(2):
                    s.op("pe", lambda e, zi=zi, ch=ch, jj=jj: e.matmul(
                        psum[3 + ch][:, jj * 128:(jj + 1) * 128], lhsT=la_sb[zi][:, ch * 128:(ch + 1) * 128], rhs=Umat,
                        start=True, stop=True),
                        R=[la_b[zi], consts_b], W=[pq[3 + ch][jj]])
                s.op("pe", lambda e, zi=zi: e.matmul(psum[5][:, 0:256], lhsT=Lmat, rhs=la_sb[zi][:], start=True, stop=True),
                     R=[la_b[zi], consts_b], W=pb_(5, 0, 256))
                s.op("act", lambda e, jj=jj: e.activation(out=Ft[:, jj, :], in_=psum[5][:, 0:256], func=AF.Exp),
                     R=pb_(5, 0, 256), W=[F_b[jj]])
            for ch in range(2):
                s.op("act", lambda e, ch=ch: e.activation(out=Eb[:, ch, :], in_=psum[3 + ch][:], func=AF.Exp),
                     R=pb_(3 + ch), W=[E_b[ch]])
                s.op("act", lambda e, ch=ch: e.activation(out=Einv[:, ch, :], in_=psum[3 + ch][:], func=AF.Exp, scale=-1.0),
                     R=pb_(3 + ch), W=[Einv_b[ch]])
            for dup in range(2):
                s.op("dve", lambda e, tb=tb, dup=dup: e.tensor_copy(
                    out=dec[:].rearrange("p (c d) n -> p c d n", d=2)[:, :, dup, tb * 8:(tb + 1) * 8], in_=Eb[:, :, 63::64]),
                    R=E_b, W=[dec_b[tb]])
            for m in range(4):
                ch = m % 2
                bi = 6 + pc % 2
                pc += 1
                for k in range(NC_):
                    s.op("pe", lambda e, m=m, k=k, sl=sl, bi=bi: e.matmul(
                        psum[bi][:], lhsT=wgqk[:, k, m * 128:(m + 1) * 128], rhs=xb[:, k, sl],
                        start=(k == 0), stop=(k == NC_ - 1)),
                        R=[wB_b[1], xb_b[k][tb]], W=pb_(bi))
                if m < 2:
                    s.op("dve", lambda e, ch=ch, sl=sl, bi=bi: e.scalar_tensor_tensor(
                        out=qT[:, ch, sl], in0=psum[bi][:], scalar=0.125, in1=Eb[:, ch, :], op0=ALU.mult, op1=ALU.mult),
                        R=pb_(bi) + [E_b[ch]], W=qT_b[ch][tb * 4:(tb + 1) * 4])
                else:
                    s.op("dve", lambda e, ch=ch, sl=sl, bi=bi: e.tensor_tensor(
                        out=kT[:, ch, sl], in0=psum[bi][:], in1=Einv[:, ch, :], op=ALU.mult),
                        R=pb_(bi) + [Einv_b[ch]], W=kT_b[ch][tb * 4:(tb + 1) * 4])
            for jj in range(4):
                j = tb * 4 + jj
                tsl = slice(j * 128, (j + 1) * 128)
                bi = 1 + jj % 2
                for k in range(NC_):
                    s.op("pe", lambda e, k=k, tsl=tsl, bi=bi: e.matmul(
                        psum[bi][:, 0:256], lhsT=xb[:, k, tsl], rhs=wgkv[:, k, 0:256],
                        start=(k == 0), stop=(k == NC_ - 1)),
                        R=[wB_b[2], xb_b[k][tb]], W=pb_(bi, 0, 256))
                s.op("dve", lambda e, j=j, jj=jj, bi=bi: e.tensor_tensor(
                    out=khat[:, j, :], in0=psum[bi][:, 0:256], in1=Ft[:, jj, :], op=ALU.mult),
                    R=pb_(bi, 0, 256) + [F_b[jj]], W=[khat_b[j]])

        if stop == "B":
            return
        tap("qT", qT, [b for bb in qT_b for b in bb])
        tap("kT", kT, [b for bb in kT_b for b in bb])
        tap("khat", khat, khat_b)
        tap("gv", gv, gv_b)
        tap("dec", dec, dec_b)
        s.fence()
        A.release(gla_base)
        wgr = A.alloc([128, NC_, 512], BF16, "wgr")
        wgr_b = Buf()
        s.op("pool", lambda e: e.dma_start(out=wgr[:], in_=wgr_d.rearrange("p (k n) -> p k n", k=NC_)),
             W=[wgr_b], dsem=new_dsem("wgr"))
        ograw = [A.alloc([128, 4, TB], F32, "ograw") for _ in range(2)]
        ograw_b = [[bufs(4) for _ in range(4)] for _ in range(2)]
        st_f = A.alloc([128, 4, 128], F32, "st_f")
        st_b = A.alloc([128, 4, 128], BF16, "st_b")
        stf_b, stb_b = bufs(4), bufs(4)
        ATt = [A.alloc([128, 4, 128], BF16, "AT") for _ in range(2)]
        AT_b = [bufs(4) for _ in range(2)]
        sg = [A.alloc([128, TB], F32, "sg") for _ in range(4)]
        sg_b = bufs(4)
        gsq = A.alloc([128, TB], F32R, "gsq")
        gsq_b = Buf()
        gm2 = A.alloc([128, TB], F32, "gm2")
        grs = A.alloc([128, TB], F32, "grs")
        gm2_b, grs_b = Buf(), Buf()
        s.op("dve", lambda e: e.memset(st_f[:], 0.0), W=stf_b)
        s.op("dve", lambda e: e.memset(st_b[:], 0.0), W=stb_b)
        bS0, bS1 = 4, 5
        HORD = (0, 2, 1, 3)

        def emit_AT_mm(j):
            bA = j % 2
            tok = slice(j * 128, (j + 1) * 128)
            prev = None
            for h in HORD:
                ch, pb = h // 2, (h % 2) * 64
                hs = slice(h * 128, (h + 1) * 128)
                prev = s.op("pe", lambda e, ch=ch, pb=pb, hs=hs, tok=tok, bA=bA: e.matmul(
                    psum[bA][:, hs], lhsT=kT[pb:pb + 64, ch, tok], rhs=qT[pb:pb + 64, ch, tok], start=True, stop=True),
                    R=[kT_b[ch][j], qT_b[ch][j]], W=[pq[bA][h]], after=[prev] if h == 1 else [])

        def emit_AT_mask(j):
            bA = j % 2
            aj = j % 2
            s.op("dve", lambda e, bA=bA, aj=aj: e.tensor_tensor(
                out=ATt[aj][:], in0=psum[bA][:].rearrange("p (h n) -> p h n", h=4),
                in1=Mblk.unsqueeze(1).broadcast_to([128, 4, 128]), op=ALU.mult),
                R=[pq[bA][0], consts_b], W=AT_b[aj])

        def emit_dS(j, half):
            bS = bS0 if half == 0 else bS1
            rows = slice(half * 64, half * 64 + 64)
            for h in range(4):
                ch = h // 2
                hs = slice(h * 128, (h + 1) * 128)
                s.op("pe", lambda e, ch=ch, hs=hs, rows=rows, bS=bS, j=j: e.matmul(
                    psum[bS][:, hs], lhsT=khat[rows, j, ch * 128:(ch + 1) * 128], rhs=gv[rows, j, hs],
                    start=True, stop=True),
                    R=[khat_b[j], gv_b[j]], W=[pq[bS][h]])

        def emit_decay(c):
            s.op("dve", lambda e, c=c: e.tensor_tensor(
                out=st_f[:], in0=st_f[:], in1=dec[:, :, c:c + 1].broadcast_to([128, 4, 128]), op=ALU.mult),
                R=stf_b + [dec_b[c // 8]], W=stf_b)

        def emit_update(j, half):
            bS = bS0 if half == 0 else bS1
            c = 2 * j + half
            s.op("dve", lambda e, bS=bS: e.tensor_tensor(
                out=st_f[:], in0=st_f[:], in1=psum[bS][:].rearrange("p (h n) -> p h n", h=4), op=ALU.add),
                R=stf_b + [pq[bS][0]], W=stf_b)
            s.op("dve", lambda e: e.tensor_copy(out=st_b[:], in_=st_f[:]), R=stf_b, W=stb_b)
            if c + 1 < 32:
                emit_decay(c + 1)

        gtasks = []
        gstate = {"B": None}

        def gate_batch(tb):
            sl = slice(tb * TB, (tb + 1) * TB)
            for h in range(4):
                bi = 6 + h % 2
                for k in range(NC_):
                    s.op("pe", lambda e, k=k, h=h, sl=sl, bi=bi: e.matmul(
                        psum[bi][:], lhsT=wgr[:, k, h * 128:(h + 1) * 128], rhs=xb[:, k, sl],
                        start=(k == 0), stop=(k == NC_ - 1)),
                        R=[wgr_b, xb_b[k][tb]], W=pb_(bi))
                s.op("act", lambda e, h=h, bi=bi: e.activation(out=sg[h][:], in_=psum[bi][:], func=AF.Silu),
                     R=pb_(bi), W=[sg_b[h]])
            for h in range(4):
                gtasks.append((tb, h))

        def gn_A(tb, h):
            ob = tb % 2
            og = ograw[ob][:, h, :]
            ogb = ograw_b[ob][h]
            s.op("act", lambda e, og=og: e.activation(out=gsq[:], in_=og, func=AF.Square), R=ogb, W=[gsq_b])
            s.op("pe", lambda e, og=og: e.matmul(psum[6][:], lhsT=gones, rhs=og, start=True, stop=True),
                 R=ogb + [consts_b], W=pb_(6))
            s.op("pe", lambda e: e.matmul(psum[7][:], lhsT=gones_r[:], rhs=gsq[:], start=True, stop=True),
                 R=[gsq_b, onesr_b], W=pb_(7))
            s.op("act", lambda e: e.activation(out=gm2[:], in_=psum[6][:], func=AF.Square), R=pb_(6), W=[gm2_b])
            s.op("dve", lambda e, og=og: e.tensor_tensor(out=og, in0=og, in1=psum[6][:], op=ALU.subtract),
                 R=ogb + pb_(6), W=ogb)
            s.op("dve", lambda e: e.tensor_tensor(out=grs[:], in0=psum[7][:], in1=gm2[:], op=ALU.subtract),
                 R=pb_(7) + [gm2_b], W=[grs_b])
            s.op("act", lambda e: e.activation(out=gm2[:], in_=grs[:], func=AF.Ln, bias=LN_EPS, scale=1.0),
                 R=[grs_b], W=[gm2_b])
            s.op("act", lambda e: e.activation(out=grs[:], in_=gm2[:], func=AF.Exp, scale=-0.5), R=[gm2_b], W=[grs_b])

        def gn_B(tb, h):
            ob = tb % 2
            og = ograw[ob][:, h, :]
            ogb = ograw_b[ob][h]
            col = l * 4 + h
            sl = slice(tb * TB, (tb + 1) * TB)
            s.op("pool", lambda e, og=og: e.tensor_tensor(out=og, in0=og, in1=grs[:], op=ALU.mult),
                 R=ogb + [grs_b], W=ogb)
            s.op("act", lambda e, og=og, col=col: e.activation(out=og, in_=og, func=AF.Identity,
                                                             scale=gng[:, col:col + 1], bias=gnb[:, col:col + 1]),
                 R=ogb + [mixc_b], W=ogb)
            s.op("pool", lambda e, og=og, h=h, sl=sl: e.tensor_tensor(out=o_gnT[:, h, sl], in0=og, in1=sg[h][:], op=ALU.mult),
                 R=ogb + [sg_b[h]], W=[o_gn_b[h][tb]])

        def gn_slot():
            if gstate["B"] is not None:
                gn_B(*gstate["B"])
                gstate["B"] = None
            if gtasks:
                t_ = gtasks.pop(0)
                gn_A(*t_)
                gstate["B"] = t_

        def gn_flush():
            while gtasks or gstate["B"] is not None:
                gn_slot()

        emit_AT_mm(0)
        emit_dS(0, 0)
        emit_dS(0, 1)
        emit_AT_mask(0)
        for j in range(16):
            tb, jj = j // 4, j % 4
            ob = tb % 2
            aj = j % 2
            bO = 2 + aj
            t0 = slice(j * 128, j * 128 + 64)
            t1_ = slice(j * 128 + 64, (j + 1) * 128)
            for h in range(4):
                hs = slice(h * 128, (h + 1) * 128)
                s.op("pe", lambda e, h=h, hs=hs, bO=bO, aj=aj, j=j: e.matmul(
                    psum[bO][:, hs], lhsT=gv[:, j, hs], rhs=ATt[aj][:, h, :], start=(h == 0), stop=False, skip_group_check=True),
                    R=[gv_b[j], AT_b[aj][h]], W=[pq[bO][h]])
            prev = None
            for h in HORD:
                ch, pb = h // 2, (h % 2) * 64
                prev = s.op("pe", lambda e, h=h, ch=ch, pb=pb, bO=bO, t0=t0: e.matmul(
                    psum[bO][:, h * 128:h * 128 + 64], lhsT=st_b[pb:pb + 64, h, :], rhs=qT[pb:pb + 64, ch, t0],
                    start=False, stop=False, skip_group_check=True),
                    R=[stb_b[h], qT_b[ch][j]], W=[pq[bO][h]], after=[prev] if h == 1 else [])
            if j + 1 < 16:
                emit_AT_mm(j + 1)
            emit_update(j, 0)
            prev = None
            for h in HORD:
                ch, pb = h // 2, (h % 2) * 64
                prev = s.op("pe", lambda e, h=h, ch=ch, pb=pb, bO=bO, t1_=t1_: e.matmul(
                    psum[bO][:, h * 128 + 64:(h + 1) * 128], lhsT=st_b[pb:pb + 64, h, :], rhs=qT[pb:pb + 64, ch, t1_],
                    start=False, stop=True, skip_group_check=True),
                    R=[stb_b[h], qT_b[ch][j]], W=[pq[bO][h]], after=[prev] if h == 1 else [])
            if j + 1 < 16:
                emit_AT_mask(j + 1)
                emit_dS(j + 1, 0)
            s.op("act", lambda e, bO=bO, ob=ob, jj=jj: e.copy(
                out=ograw[ob][:, :, jj * 128:(jj + 1) * 128], in_=psum[bO][:].rearrange("p (h n) -> p h n", h=4)),
                R=[pq[bO][0]], W=[ograw_b[ob][h][jj] for h in range(4)])
            emit_update(j, 1)
            if j + 1 < 16:
                emit_dS(j + 1, 1)
            if j == 0:
                tap("st1", st_f, stf_b)
            gn_slot()
            if jj == 3:
                gn_flush()
                if tb == 0:
                    tap("ograw0", ograw[0], [b for bb in ograw_b[0] for b in bb])
                gate_batch(tb)
        gn_flush()

        if stop == "D":
            return
        tap("o_gnT", o_gnT, [b for bb in o_gn_b for b in bb])
        s.fence()
        A.release(mix_base)
        o_aT = A.alloc([128, 4, S], BF16, "o_aT")
        mix_base = A.mark()
        waqkv = A.alloc([128, NC_, 384], BF16, "waqkv")
        waqkv_b = Buf()
        waqkv_sem = new_dsem("waqkv")
        aqT = [[A.alloc([128, S], BF16, "aqT") for _ in range(2)] for _ in range(2)]
        akT = [A.alloc([128, S], BF16, "akT") for _ in range(2)]
        aqT_b = [bufs(NTB) for _ in range(2)]
        aqz_b = [bufs(2) for _ in range(2)]
        for wi_ in range(2):
            for hh_ in range(2):
                oth = slice(64, 128) if hh_ == 0 else slice(0, 64)
                s.op("pool", lambda e, wi_=wi_, hh_=hh_, oth=oth: e.memset(aqT[wi_][hh_][oth, :], 0.0), W=[aqz_b[wi_][hh_]])
        akT_b = [bufs(16) for _ in range(2)]
        Vp = [A.alloc([128, 16, 128], BF16, "Vp") for _ in range(2)]
        Vp_b = [bufs(16) for _ in range(2)]
        mask_s = [A.alloc([128, S], F32, "mask") for _ in range(2)]
        mask_b = bufs(2)
        mask_sem = [new_dsem("mask") for _ in range(2)]
        NE = 4
        LOOK = 3
        Et = [A.alloc([128, TB], F32, "Et") for _ in range(NE)]
        Pt = [A.alloc([128, TB], BF16, "Pt") for _ in range(NE)]
        Et_b, Pt_b = bufs(NE), bufs(NE)
        dcp = [A.alloc([128, TB], F32, "dcp") for _ in range(1)] * 2
        dcp_b = bufs(1) * 2
        onesb = A.alloc([128, 128], BF16, "onesb")
        onesb_b = Buf()
        s.op("pool", lambda e: e.memset(onesb[:], 1.0), W=[onesb_b])
        SB = (0, 1, 2, 3)
        stepc = 0
        def load_waqkv(ch):
            s.op("pool", lambda e, ch=ch: e.dma_start(out=waqkv[:], in_=waqkv_d[ch].rearrange("p (k n) -> p k n", k=NC_)),
                 W=[waqkv_b], dsem=waqkv_sem)

        def load_mask(h):
            mi = h % 2
            s.op("sp", lambda e, mi=mi, h=h: e.dma_start(out=mask_s[mi][:], in_=amask_d[h]),
                 W=[mask_b[mi]], dsem=mask_sem[mi])

        load_waqkv(0)
        load_mask(0)
        load_mask(1)
        for ch in range(4):
            wi = ch % 2
            for tb in range(NTB):
                sl = slice(tb * TB, (tb + 1) * TB)
                for which in range(2):
                    bi = SB[(2 * tb + which) % 4]
                    for k in range(NC_):
                        s.op("pe", lambda e, k=k, which=which, sl=sl, bi=bi: e.matmul(
                            psum[bi][:], lhsT=waqkv[:, k, which * 128:(which + 1) * 128], rhs=xb[:, k, sl],
                            start=(k == 0), stop=(k == NC_ - 1)),
                            R=[waqkv_b, xb_b[k][tb]], W=pb_(bi))
                    if which == 0:
                        s.op("act", lambda e, wi=wi, sl=sl, bi=bi: e.mul(aqT[wi][0][0:64, sl], psum[bi][0:64, :], 0.125),
                             R=pb_(bi), W=[aqT_b[wi][tb]])
                        s.op("act", lambda e, wi=wi, sl=sl, bi=bi: e.mul(aqT[wi][1][64:128, sl], psum[bi][64:128, :], 0.125),
                             R=pb_(bi), W=[aqT_b[wi][tb]])
                    else:
                        s.op("dve", lambda e, wi=wi, sl=sl, bi=bi: e.tensor_copy(out=akT[wi][:, sl], in_=psum[bi][:]),
                             R=pb_(bi), W=akT_b[wi][tb * 4:(tb + 1) * 4])
            for j4 in range(4):
                bi = SB[j4 % 4]
                for jj in range(4):
                    j = j4 * 4 + jj
                    tsl = slice(j * 128, (j + 1) * 128)
                    for k in range(NC_):
                        s.op("pe", lambda e, k=k, tsl=tsl, bi=bi, jj=jj: e.matmul(
                            psum[bi][:, jj * 128:(jj + 1) * 128], lhsT=xb[:, k, tsl], rhs=waqkv[:, k, 256:384],
                            start=(k == 0 and jj == 0), stop=(k == NC_ - 1), skip_group_check=True),
                            R=[waqkv_b, xb_b[k][j4]], W=pb_(bi))
                s.op("act", lambda e, wi=wi, j4=j4, bi=bi: e.copy(
                    out=Vp[wi][:, j4 * 4:(j4 + 1) * 4, :], in_=psum[bi][:].rearrange("p (a b) -> p a b", a=4)),
                    R=pb_(bi), W=Vp_b[wi][j4 * 4:(j4 + 1) * 4])
            if ch + 1 < 4:
                load_waqkv(ch + 1)
            steps = []
            for hh in range(2):
                h = 2 * ch + hh
                for qp in range(NTB):
                    kbs = [kb for kb in range(4 * qp + 4) if blk[h][kb][qp]]
                    for ki, kb in enumerate(kbs):
                        steps.append((hh, h, qp, kb, ki, len(kbs)))
            info = {}
            for idx in range(len(steps) + LOOK):
                if idx < len(steps):
                    hh, h, qp, kb, ki, nk = steps[idx]
                    pb = hh * 64
                    mi = h % 2
                    q0 = qp * TB
                    n0 = max(q0, 128 * kb)
                    n = q0 + TB - n0
                    bS = SB[stepc % 4]
                    ei = stepc % NE
                    stepc += 1
                    info[idx] = (ei, n0, n)
                    s.op("pe", lambda e, wi=wi, hh=hh, kb=kb, n0=n0, n=n, bS=bS: e.matmul(
                        psum[bS][:, 0:n], lhsT=akT[wi][:, kb * 128:(kb + 1) * 128],
                        rhs=aqT[wi][hh][:, n0:n0 + n], start=True, stop=True),
                        R=[akT_b[wi][kb], aqT_b[wi][qp], aqz_b[wi][hh]], W=pb_(bS))
                    s.op("act", lambda e, ei=ei, bS=bS, n=n: e.activation(out=Et[ei][:, 0:n], in_=psum[bS][:, 0:n], func=AF.Exp),
                         R=pb_(bS), W=[Et_b[ei]])
                    mo = n0 - 128 * kb
                    s.op("dve", lambda e, ei=ei, mi=mi, mo=mo, n=n: e.tensor_tensor(
                        out=Pt[ei][:, 0:n], in0=Et[ei][:, 0:n], in1=mask_s[mi][:, mo:mo + n], op=ALU.mult),
                        R=[Et_b[ei], mask_b[mi]], W=[Pt_b[ei]])
                    if h + 2 < 8 and (idx + 1 == len(steps) or steps[idx + 1][1] != h):
                        load_mask(h + 2)
                pidx = idx - LOOK
                if pidx >= 0:
                    hh, h, qp, kb, ki, nk = steps[pidx]
                    ei, n0, n = info[pidx]
                    pb = hh * 64
                    q0 = qp * TB
                    par = (h * NTB + qp) % 2
                    bO, bD = 4 + par, 6 + par
                    cs = slice(n0 - q0, n0 - q0 + n)
                    s.op("pe", lambda e, wi=wi, kb=kb, ei=ei, n=n, cs=cs, bO=bO, ki=ki, nk=nk: e.matmul(
                        psum[bO][:, cs], lhsT=Vp[wi][:, kb, :], rhs=Pt[ei][:, 0:n],
                        start=(ki == 0), stop=(ki == nk - 1), skip_group_check=True),
                        R=[Vp_b[wi][kb], Pt_b[ei]], W=pb_(bO))
                    s.op("pe", lambda e, ei=ei, n=n, cs=cs, bD=bD, ki=ki, nk=nk: e.matmul(
                        psum[bD][:, cs], lhsT=onesb[:], rhs=Pt[ei][:, 0:n],
                        start=(ki == 0), stop=(ki == nk - 1), skip_group_check=True),
                        R=[onesb_b, Pt_b[ei]], W=pb_(bD))
                    if ki == nk - 1:
                        ps_ = slice(pb, pb + 64)
                        s.op("act", lambda e, par=par, bD=bD, ps_=ps_: e.activation(out=dcp[par][ps_, :], in_=psum[bD][ps_, :], func=AF.Ln),
                             R=pb_(bD), W=[dcp_b[par]])
                        s.op("act", lambda e, par=par, ps_=ps_: e.activation(out=dcp[par][ps_, :], in_=dcp[par][ps_, :], func=AF.Exp, scale=-1.0),
                             R=[dcp_b[par]], W=[dcp_b[par]])
                        s.op("dve", lambda e, ch=ch, ps_=ps_, q0=q0, bO=bO, par=par: e.tensor_tensor(
                            out=o_aT[ps_, ch, q0:q0 + TB], in0=psum[bO][ps_, :], in1=dcp[par][ps_, :], op=ALU.mult),
                            R=pb_(bO) + [dcp_b[par]], W=[o_a_b[h][qp]])

        if stop == "C":
            return
        tap("o_aT", o_aT, [b for bb in o_a_b for b in bb])
        s.fence()
        A.release(mix_base)
        e_low_top = A.mark()
        mT = A.alloc([128, NC_, S], BF16, "mT")
        mT_b = [bufs(NTB) for _ in range(NC_)]
        e_base = A.mark()
        wE = [A.alloc([128, NC_, 256], BF16, "wgab") for _ in range(2)]
        wap = [A.alloc([128, 4, 128], BF16, "wap") for _ in range(2)]
        wgp = [A.alloc([128, 4, 128], BF16, "wgp") for _ in range(2)]
        wE_b = [bufs(3) for _ in range(2)]
        wE_sem = [[new_dsem("wE") for _ in range(3)] for _ in range(2)]
        sa = [A.alloc([128, TB], F32, "sa") for _ in range(2)]
        sbt = [A.alloc([128, TB], F32, "sbt") for _ in range(2)]
        sa_b, sbt_b = bufs(2), bufs(2)
        wo = A.alloc([128, NC_, NC_, 128], BF16, "wo")
        wo_b = bufs(NC_)
        e_top = A.mark()
        ecn = 0

        def load_wE(dc):
            wi = dc % 2
            s.op("pool", lambda e, wi=wi, dc=dc: e.dma_start(out=wE[wi][:], in_=wgab_d[dc].rearrange("p (k n) -> p k n", k=NC_)),
                 W=[wE_b[wi][0]], dsem=wE_sem[wi][0])
            s.op("pool", lambda e, wi=wi, dc=dc: e.dma_start(out=wap[wi][:], in_=wap_d[dc].rearrange("p (k n) -> p k n", k=4)),
                 W=[wE_b[wi][1]], dsem=wE_sem[wi][1])
            s.op("pool", lambda e, wi=wi, dc=dc: e.dma_start(out=wgp[wi][:], in_=wgp_d[dc].rearrange("p (k n) -> p k n", k=4)),
                 W=[wE_b[wi][2]], dsem=wE_sem[wi][2])

        load_wE(0)
        for dc in range(NC_):
            wi = dc % 2
            if dc + 1 < NC_:
                load_wE(dc + 1)
            if dc >= 4:
                for dco in (2 * (dc - 4), 2 * (dc - 4) + 1):
                    s.op("pool", lambda e, dco=dco: e.dma_start(out=wo[:, dco, :, :], in_=wo_d[dco].rearrange("p (k n) -> p k n", k=NC_)),
                         W=[wo_b[dco]], dsem=new_dsem("wo"))
            for tb in range(NTB):
                sl = slice(tb * TB, (tb + 1) * TB)
                pj = ecn % 2
                ecn += 1
                bGA, bGB, bPA, bPG = 0 + pj, 2 + pj, 4 + pj, 6 + pj
                for which, bi in ((0, bGA), (1, bGB)):
                    for k in range(NC_):
                        s.op("pe", lambda e, wi=wi, which=which, k=k, sl=sl, bi=bi: e.matmul(
                            psum[bi][:], lhsT=wE[wi][:, k, which * 128:(which + 1) * 128], rhs=xb[:, k, sl],
                            start=(k == 0), stop=(k == NC_ - 1)),
                            R=[wE_b[wi][0], xb_b[k][tb]], W=pb_(bi))
                for c in range(4):
                    s.op("pe", lambda e, wi=wi, c=c, sl=sl, bPA=bPA: e.matmul(
                        psum[bPA][:], lhsT=wap[wi][:, c, :], rhs=o_aT[:, c, sl], start=(c == 0), stop=(c == 3)),
                        R=[wE_b[wi][1], o_a_b[2 * c][tb], o_a_b[2 * c + 1][tb]], W=pb_(bPA))
                for c in range(4):
                    s.op("pe", lambda e, wi=wi, c=c, sl=sl, bPG=bPG: e.matmul(
                        psum[bPG][:], lhsT=wgp[wi][:, c, :], rhs=o_gnT[:, c, sl], start=(c == 0), stop=(c == 3)),
                        R=[wE_b[wi][2], o_gn_b[c][tb]], W=pb_(bPG))
                s.op("act", lambda e, pj=pj, bGA=bGA: e.activation(out=sa[pj][:], in_=psum[bGA][:], func=AF.Sigmoid),
                     R=pb_(bGA), W=[sa_b[pj]])
                s.op("act", lambda e, pj=pj, bGB=bGB: e.activation(out=sbt[pj][:], in_=psum[bGB][:], func=AF.Sigmoid),
                     R=pb_(bGB), W=[sbt_b[pj]])
                s.op("dve", lambda e, pj=pj, bPA=bPA: e.tensor_tensor(out=sa[pj][:], in0=psum[bPA][:], in1=sa[pj][:], op=ALU.mult),
                     R=pb_(bPA) + [sa_b[pj]], W=[sa_b[pj]])
                s.op("dve", lambda e, pj=pj, bPG=bPG: e.tensor_tensor(out=sbt[pj][:], in0=psum[bPG][:], in1=sbt[pj][:], op=ALU.mult),
                     R=pb_(bPG) + [sbt_b[pj]], W=[sbt_b[pj]])
                s.op("pool", lambda e, pj=pj, dc=dc, sl=sl: e.tensor_tensor(out=mT[:, dc, sl], in0=sa[pj][:], in1=sbt[pj][:], op=ALU.add),
                     R=[sa_b[pj], sbt_b[pj]], W=[mT_b[dc][tb]])
        tap("mT", mT, [b for bb in mT_b for b in bb])
        s.fence()
        A.release(e_top)
        Alow = Arena(nc, arena_base, e_low_top)
        ln_chunk, ln_flush = make_ln(l, 1, [Alow, A])
        yc = 0
        for tb in range(NTB):
            sl = slice(tb * TB, (tb + 1) * TB)
            for dc in range(NC_):
                bi = 4 + yc % 2
                yc += 1
                for k in range(NC_):
                    s.op("pe", lambda e, dc=dc, k=k, sl=sl, bi=bi: e.matmul(
                        psum[bi][:], lhsT=wo[:, dc, k, :], rhs=mT[:, k, sl], start=(k == 0), stop=(k == NC_ - 1)),
                        R=[wo_b[dc], mT_b[k][tb]], W=pb_(bi))
                s.op("dve", lambda e, bi=bi, dc=dc, sl=sl: e.tensor_tensor(
                    out=xs[:, dc, sl], in0=psum[bi][:], in1=xs[:, dc, sl], op=ALU.add),
                    R=pb_(bi) + [xs_b[dc][tb]], W=[xs_b[dc][tb]])
                ln_chunk(dc, tb)
        ln_flush()

    amask_blocks = _mask_blocks()
    out_ops = []
    lastp = phases[-1]
    if lastp[0] == "ffn":
        final_ln = (lastp[1], 0 if lastp[2] == 0 else 2)
    elif lastp[0] == "mix" and len(lastp) == 2:
        final_ln = (lastp[1], 1)
    else:
        final_ln = None
    for ph in phases:
        if ph[0] == "ffn":
            ffn(ph[1], ph[2])
        elif ph[0] == "mix":
            mixer(ph[1], ph[2] if len(ph) > 2 else None)
        else:
            raise ValueError(ph)

    if not out_ops:
        for c in range(NC_):
            for t in range(NTB):
                sl = slice(t * TB, (t + 1) * TB)
                s.op("dve", lambda e, c=c, sl=sl: e.tensor_scalar_mul(out=xs[:, c, sl], in0=xs[:, c, sl], scalar1=1.0 / ALPHA),
                     R=[xs_b[c][t]], W=[xs_b[c][t]])
        for c in range(NC_):
            out_ops.append(s.op("sp", lambda e, c=c: e.dma_start(out=yT_d[c * 128:(c + 1) * 128, :], in_=xs[:, c, :]),
                                R=xs_b[c], dsem=out_sem))
    fin = Buf()
    fin.w = out_ops[-1]
    s.op("sp", lambda e: e.nop(), R=[fin])

    s.finalize()
    from contextlib import ExitStack
    with ExitStack() as ctx:
        esem = {}
        for en in Sched.ENGS:
            esem[en] = ctx.enter_context(nc.semaphore(f"sem_{en}"))
        dsems = {}
        for nm in dsem_names:
            dsems[nm] = ctx.enter_context(nc.semaphore(f"d_{nm}"))
        with nc.Block() as block:
            @block.tensor
            def _(e):
                s.replay("pe", e, esem, dsems)

            @block.scalar
            def _(e):
                s.replay("act", e, esem, dsems)

            @block.vector
            def _(e):
                s.replay("dve", e, esem, dsems)

            @block.gpsimd
            def _(e):
                s.replay("pool", e, esem, dsems)

            @block.sync
            def _(e):
                s.replay("sp", e, esem, dsems)
    return nc


_MASK = None


def _alibi_mask():
    global _MASK
    if _MASK is None:
        d = np.arange(S)[None, :] - np.arange(128)[:, None]
        mult = ((d <= 128).astype(np.float64) + ((d % 4 == 0) & (d <= 512)) + ((d % 16 == 0) & (d <= 2048)))
        mult = np.where(d >= 0, mult, 0.0)
        slopes = np.exp2(-8.0 * np.arange(1, 9) / 8.0)
        m = mult[None] * np.exp(-slopes[:, None, None] * np.maximum(d, 0)[None])
        m = np.where(m < 1e-37, 0.0, m)
        _MASK = np.ascontiguousarray(m.astype(np.float32))
    return _MASK


def _mask_blocks():
    m = _alibi_mask()
    blk = [[[False] * NTB for _ in range(16)] for _ in range(8)]
    for h in range(8):
        for kb in range(16):
            for qp in range(NTB):
                n0 = max(qp * TB, 128 * kb)
                n1 = qp * TB + TB
                if n1 <= n0:
                    continue
                blk[h][kb][qp] = bool(m[h][:, n0 - 128 * kb:n1 - 128 * kb].any())
    return blk


def _consts():
    s_ = np.arange(128)[:, None]
    t_ = np.arange(128)[None, :]
    same = (s_ // 64) == (t_ // 64)
    U = np.where(same & (s_ <= t_), -1.0 / 16.0, 0.0)
    L = np.where(same & (s_ > t_), -1.0 / 16.0, 0.0)
    M = np.where(same & (s_ <= t_), 1.0, 0.0)
    o1 = np.full((128, 128), 1.0 / D)
    o2 = np.full((128, 128), 1.0 / 128.0)
    return np.ascontiguousarray(np.concatenate([U, L, M, o1, o2], axis=1).astype(np.float32))


def _lay_w13(w):
    return np.ascontiguousarray(w.reshape(NC_, 128, NF, 128).transpose(2, 1, 0, 3).reshape(NF, 128, D))


def _lay_ln(v):
    return np.ascontiguousarray(v.reshape(DEPTH, 3, NC_, 128).transpose(3, 0, 1, 2).reshape(128, NL3))


def _lay_cols(w):
    n = w.shape[1]
    return np.ascontiguousarray(w.reshape(NC_, 128, n).transpose(1, 0, 2).reshape(128, NC_ * n))


def make_inputs(phases, inp):
    m = {"ln_g": _lay_ln(inp["ln_g"]), "ln_b": _lay_ln(inp["ln_b"]), "consts": _consts()}
    has_mix = any(p[0] == "mix" for p in phases)
    if has_mix:
        m["amask"] = _alibi_mask()
        m["wgu"] = np.ascontiguousarray(inp["w_gate_up"].transpose(1, 0, 2))
        m["bgu"] = np.ascontiguousarray(np.broadcast_to(inp["b_gate_up"][None], (128, DEPTH, 256)))
        m["gng"] = np.ascontiguousarray(inp["gla_norm_g"].reshape(DEPTH, 4, 128).transpose(2, 0, 1).reshape(128, DEPTH * 4))
        m["gnb"] = np.ascontiguousarray(inp["gla_norm_b"].reshape(DEPTH, 4, 128).transpose(2, 0, 1).reshape(128, DEPTH * 4))
    for ph in phases:
        if ph[0] == "ffn":
            l, i = ph[1], ph[2]
            pre = "ffn1" if i == 0 else "ffn2"
            m[f"f{i}w1_{l}"] = _lay_w13(inp[pre + "_w1"][l])
            m[f"f{i}w3_{l}"] = _lay_w13(inp[pre + "_w3"][l])
            m[f"f{i}w2_{l}"] = np.ascontiguousarray(inp[pre + "_w2"][l])
        else:
            l = ph[1]
            w = inp["w_in"][l]
            m[f"wglr_{l}"] = _lay_cols(w[:, O_GLR:O_GLR + 16])
            m[f"wgqk_{l}"] = _lay_cols(w[:, O_GQ:O_GQ + 512])
            m[f"wgkv_{l}"] = _lay_cols(w[:, O_GK:O_GK + 768])
            m[f"wgr_{l}"] = _lay_cols(w[:, O_GR:O_GR + 512])
            m[f"waqkv_{l}"] = np.stack([_lay_cols(np.concatenate(
                [w[:, O_AQ + c * 128:O_AQ + (c + 1) * 128], w[:, O_AK + c * 128:O_AK + (c + 1) * 128],
                 w[:, O_AV + c * 128:O_AV + (c + 1) * 128]], axis=1)) for c in range(4)], axis=0)
            m[f"wgab_{l}"] = np.stack([_lay_cols(np.concatenate(
                [w[:, O_GA + c * 128:O_GA + (c + 1) * 128], w[:, O_GB + c * 128:O_GB + (c + 1) * 128]], axis=1))
                for c in range(NC_)], axis=0)
            wa = inp["w_attn_proj"][l]
            m[f"wap_{l}"] = np.ascontiguousarray(wa.reshape(4, 128, NC_, 128).transpose(2, 1, 0, 3).reshape(NC_, 128, 4 * 128))
            wg = inp["w_gla_proj"][l]
            m[f"wgp_{l}"] = np.ascontiguousarray(wg.reshape(4, 128, NC_, 128).transpose(2, 1, 0, 3).reshape(NC_, 128, 4 * 128))
            wo_ = inp["w_out"][l]
            m[f"wo_{l}"] = np.ascontiguousarray(wo_.reshape(NC_, 128, NC_, 128).transpose(2, 1, 0, 3).reshape(NC_, 128, NC_ * 128))
    return m


def run_phases(phases, x, inp, n_cores=8, trace=False, debug=None):
    nc = build(phases, debug)
    shared = make_inputs(phases, inp)
    in_maps = []
    for b in range(n_cores):
        d = dict(shared)
        d["xT"] = np.ascontiguousarray(x[b].T)
        in_maps.append(d)
    res = run_bass_kernel_spmd(nc, in_maps, core_ids=list(range(n_cores)), trace=trace)
    out = np.stack([np.ascontiguousarray(r["yT"].T) for r in res.results], axis=0)
    return out, res


LAUNCHES = [[("ffn", 0, 0), ("mix", 0), ("ffn", 0, 1), ("ffn", 1, 0), ("mix", 1), ("ffn", 1, 1)]]


def kernel(**inputs):
    inp = {k: np.asarray(v) for k, v in inputs.items()}
    x = np.ascontiguousarray(inp["x"], dtype=np.float32)
    for phases in LAUNCHES:
        x, _ = run_phases(phases, x, inp)
    return np.ascontiguousarray(x, dtype=np.float32)
```

```python
import numpy as np
import concourse.bass as bass
import concourse.mybir as mybir
from concourse.bass_utils import run_bass_kernel_spmd

F32 = mybir.dt.float32
F32R = mybir.dt.float32r

BF16 = mybir.dt.bfloat16
AF = mybir.ActivationFunctionType
ALU = mybir.AluOpType

S = 2048
D = 1024
DFF = 2816
NC_ = 8
NTB = 4
TB = 512
NF = 22
DEPTH = 2
ALPHA = float((2 * DEPTH) ** 0.25)
LN_EPS = 1e-5
FFN_GROUPS = [3, 3, 4, 4, 4, 4]
GMAX = 4
NL3 = DEPTH * 3 * NC_
SB_BASE = 16512
SB_TOP = 229344

O_AQ, O_AK, O_AV = 0, 512, 1024
O_GQ, O_GK, O_GV, O_GLR, O_GR = 1536, 1792, 2048, 2560, 2576
O_GA, O_GB = 3088, 4112
N_IN = 5136


class Buf:
    __slots__ = ("name", "w", "r", "excl")

    def __init__(self, name="", excl=False):
        self.name = name
        self.w = None
        self.r = {}
        self.excl = excl


def bufs(n):
    return [Buf() for _ in range(n)]


class Op:
    __slots__ = ("eng", "fn", "deps", "needed", "semval", "dsem", "dval")

    def __init__(self, eng, fn, deps, dsem):
        self.eng = eng
        self.fn = fn
        self.deps = deps
        self.needed = False
        self.semval = None
        self.dsem = dsem
        self.dval = None


class Sched:
    ENGS = ("pe", "act", "dve", "pool", "sp")

    def __init__(self):
        self.q = {e: [] for e in self.ENGS}
        self.dma_count = {}
        self.last_dma = {}
        self.extra = {e: [] for e in self.ENGS}

    def op(self, eng, fn, R=(), W=(), dsem=None, after=()):
        deps = [(3, a) for a in after if a is not None]
        if any(b.excl for b in R):
            W = list(W) + [b for b in R if b.excl]
            R = [b for b in R if not b.excl]
        W = list(dict.fromkeys(W))
        for b in R:
            if b.w is not None:
                deps.append((0, b.w))
        for b in W:
            if b.w is not None:
                deps.append((1, b.w))
            for r in b.r.values():
                deps.append((2, r))
        if self.extra[eng]:
            deps.extend((0, d) for d in self.extra[eng])
            self.extra[eng] = []
        o = Op(eng, fn, deps, dsem)
        if dsem is not None:
            self.dma_count[dsem] = self.dma_count.get(dsem, 0) + 16
            o.dval = self.dma_count[dsem]
            self.last_dma[dsem] = o
        self.q[eng].append(o)
        key = dsem if dsem is not None else eng
        for b in R:
            b.r[key] = o
        for b in W:
            b.w = o
            b.r = {}
        return o

    def fence(self):
        snap = []
        for e in self.ENGS:
            for o in reversed(self.q[e]):
                if o.dsem is None:
                    snap.append(o)
                    break
        snap.extend(self.last_dma.values())
        for e in self.ENGS:
            self.extra[e] = list(snap)

    def finalize(self):
        for eng in self.ENGS:
            for o in self.q[eng]:
                keep = []
                for kind, d in o.deps:
                    if d is o:
                        continue
                    if d.dsem is not None:
                        keep.append(d)
                    elif d.eng == o.eng:
                        if o.eng == "pe" and kind != 3:
                            continue
                        keep.append(d)
                    else:
                        keep.append(d)
                for d in keep:
                    if d.dsem is None:
                        d.needed = True
                o.deps = keep
        for eng in self.ENGS:
            c = 0
            for o in self.q[eng]:
                if o.dsem is None and o.needed:
                    c += 1
                    o.semval = c

    def replay(self, eng, e, esem, dsems):
        seen = {}
        for o in self.q[eng]:
            for d in o.deps:
                if d.dsem is not None:
                    key, val, sem = ("d", d.dsem), d.dval, dsems[d.dsem]
                else:
                    key, val, sem = ("e", d.eng), d.semval, esem[d.eng]
                if seen.get(key, 0) >= val:
                    continue
                seen[key] = val
                e.wait_ge(sem, val)
            ins = o.fn(e)
            if o.dsem is not None:
                ins.then_inc(dsems[o.dsem], 16)
            elif o.needed:
                ins.then_inc(esem[eng], 1)


DT_SIZE = {F32: 4, BF16: 2, F32R: 4}


class Arena:
    UID = 0

    def __init__(self, nc, base, top):
        self.nc = nc
        self.base = base
        self.top = top
        self.off = base
        self.uid = 0
        self.peak = base

    def alloc(self, shape, dt, name="t"):
        n = 1
        for d in shape[1:]:
            n *= d
        nbytes = (n * DT_SIZE[dt] + 63) // 64 * 64
        if self.off + nbytes > self.top:
            raise RuntimeError(f"arena overflow allocating {name} {shape}: off={self.off - self.base} need {nbytes} cap {self.top - self.base}")
        Arena.UID += 1
        t = self.nc.alloc_sbuf_tensor_at(f"{name}_{Arena.UID}", list(shape), dt, offset=self.off)
        self.off += nbytes
        self.peak = max(self.peak, self.off)
        return t

    def mark(self):
        return self.off

    def release(self, m):
        self.off = m


def build(phases, debug=None):
    nc = bass.Bass("TRN2", target_bir_lowering=False)
    s = Sched()
    dram = {}
    dsem_names = []

    def new_dsem(name):
        nm = f"{name}_{len(dsem_names)}"
        dsem_names.append(nm)
        return nm

    def din(name, shape, dt=F32):
        if name not in dram:
            dram[name] = nc.dram_tensor(name, list(shape), dt, kind="ExternalInput").ap()
        return dram[name]

    debug = debug or ()

    def tap(name, t, bl):
        if name in debug:
            dd = nc.dram_tensor("dbg_" + name, list(t.shape), t.dtype, kind="ExternalOutput").ap()
            s.op("sp", lambda e: e.dma_start(out=dd, in_=t[:]), R=bl, dsem=new_dsem("dbg"))

    xT_d = din("xT", [D, S])
    yT_d = nc.dram_tensor("yT", [D, S], F32, kind="ExternalOutput").ap()
    lng_d = din("ln_g", [128, NL3])
    lnb_d = din("ln_b", [128, NL3])
    consts_d = din("consts", [128, 5 * 128])
    has_mix = any(p[0] == "mix" for p in phases)

    A = Arena(nc, SB_BASE, SB_TOP)
    xs = A.alloc([128, NC_, S], F32, "xs")
    xb = A.alloc([128, NC_, S], BF16, "xb")
    xs_b = [bufs(NTB) for _ in range(NC_)]
    xb_b = [bufs(NTB) for _ in range(NC_)]
    lng = A.alloc([128, NL3], F32, "lng")
    lnb = A.alloc([128, NL3], F32, "lnb")
    lnga = A.alloc([128, NL3], F32, "lnga")
    lnba = A.alloc([128, NL3], F32, "lnba")
    ln_c = Buf()
    consts = A.alloc([128, 5 * 128], F32, "consts")
    consts_b = Buf()
    Umat = consts[:, 0:128]
    Lmat = consts[:, 128:256]
    Mblk = consts[:, 256:384]
    ones = consts[:, 384:512]
    gones = consts[:, 512:640]
    ones1 = A.alloc([128, 64], F32, "ones1")
    ones1_b = Buf()
    ones_r = A.alloc([128, 128], F32R, "ones_r")
    gones_r = A.alloc([128, 128], F32R, "gones_r")
    onesr_b = Buf()
    mixc_b = Buf()
    if has_mix:
        wgu = A.alloc([16, DEPTH, 256], F32, "wgu")
        bgu = A.alloc([128, DEPTH, 256], F32, "bgu")
        gng = A.alloc([128, DEPTH * 4], F32, "gng")
        gnb = A.alloc([128, DEPTH * 4], F32, "gnb")
    arena_base = A.mark()

    psum = [nc.alloc_psum_tensor(f"bank{i}", [128, TB], F32) for i in range(8)]
    pq = [[Buf(f"bank{i}", excl=True)] * 4 for i in range(8)]

    def pb_(i, c0=0, c1=TB):
        return pq[i][c0 // 128:(c1 + 127) // 128]

    out_sem = new_dsem("out")

    lng_b0, lnb_b0 = Buf(), Buf()
    s.op("sp", lambda e: e.dma_start(out=lng[:], in_=lng_d), W=[lng_b0], dsem=new_dsem("io"))
    s.op("sp", lambda e: e.dma_start(out=lnb[:], in_=lnb_d), W=[lnb_b0], dsem=new_dsem("io"))
    s.op("sp", lambda e: e.dma_start(out=consts[:], in_=consts_d), W=[consts_b], dsem=new_dsem("io"))
    if has_mix:
        wgu_d = din("wgu", [16, DEPTH, 256])
        bgu_d = din("bgu", [128, DEPTH, 256])
        gng_d = din("gng", [128, DEPTH * 4])
        gnb_d = din("gnb", [128, DEPTH * 4])
        mb = bufs(4)
        s.op("sp", lambda e: e.dma_start(out=wgu[:], in_=wgu_d), W=[mb[0]], dsem=new_dsem("io"))
        s.op("sp", lambda e: e.dma_start(out=bgu[:], in_=bgu_d), W=[mb[1]], dsem=new_dsem("io"))
        s.op("sp", lambda e: e.dma_start(out=gng[:], in_=gng_d), W=[mb[2]], dsem=new_dsem("io"))
        s.op("sp", lambda e: e.dma_start(out=gnb[:], in_=gnb_d), W=[mb[3]], dsem=new_dsem("io"))
        s.op("dve", lambda e: e.memset(ones1[:], 1.0), R=mb, W=[ones1_b, mixc_b])
    xT_v = xT_d.rearrange("(c p) t -> p c t", p=128)
    for t in range(NTB):
        s.op("sp", lambda e, t=t: e.dma_start(out=xs[:, :, t * TB:(t + 1) * TB], in_=xT_v[:, :, t * TB:(t + 1) * TB]),
             W=[xs_b[c][t] for c in range(NC_)], dsem=new_dsem("iox"))
    s.op("act", lambda e: e.copy(out=ones_r[:], in_=ones), R=[consts_b], W=[onesr_b])
    s.op("act", lambda e: e.copy(out=gones_r[:], in_=gones), R=[consts_b], W=[onesr_b])
    s.op("act", lambda e: e.mul(lnga[:], lng[:], ALPHA), R=[lng_b0], W=[ln_c])
    s.op("act", lambda e: e.mul(lnba[:], lnb[:], ALPHA), R=[lnb_b0], W=[ln_c])
    for t in range(NTB):
        for c in range(NC_):
            sl = slice(t * TB, (t + 1) * TB)
            s.op("dve", lambda e, c=c, sl=sl: e.tensor_copy(out=xb[:, c, sl], in_=xs[:, c, sl]),
                 R=[xs_b[c][t]], W=[xb_b[c][t]])
            s.op("act", lambda e, c=c, sl=sl: e.mul(xs[:, c, sl], xs[:, c, sl], ALPHA),
                 R=[xs_b[c][t]], W=[xs_b[c][t]])

    def layer_norm(l, i):
        col0 = (l * 3 + i) * NC_
        s.fence()
        A.release(arena_base)
        sq = A.alloc([128, NC_, TB], F32, "sq")
        sq_b = bufs(NC_)
        mean_sb = [A.alloc([128, TB], F32, "mean") for _ in range(2)]
        m2_sb = [A.alloc([128, TB], F32, "m2") for _ in range(2)]
        rstd_sb = [A.alloc([128, TB], F32, "rstd") for _ in range(2)]
        mean_b, m2_b, rstd_b = bufs(2), bufs(2), bufs(2)
        t1 = [A.alloc([128, TB], F32, "t1") for _ in range(2)]
        t2 = [A.alloc([128, TB], F32, "t2") for _ in range(3)]
        t1_b, t2_b = bufs(2), bufs(3)
        cn = {"t1": 0, "t2": 0}

        def stats(t):
            sl = slice(t * TB, (t + 1) * TB)
            p = t % 2
            bm, bq = (6, 7) if p == 0 else (4, 5)
            pm, pq_ = psum[bm], psum[bq]
            for c in range(NC_):
                s.op("act", lambda e, c=c, sl=sl: e.activation(out=sq[:, c, :], in_=xs[:, c, sl], func=AF.Square),
                     R=[xs_b[c][t]], W=[sq_b[c]])
            for c in range(NC_):
                s.op("pe", lambda e, c=c, sl=sl, pm=pm: e.matmul(pm[:], lhsT=ones, rhs=xs[:, c, sl],
                                                                 start=(c == 0), stop=(c == NC_ - 1)),
                     R=[consts_b, xs_b[c][t]], W=pb_(bm))
            for c in range(NC_):
                s.op("pe", lambda e, c=c, pq_=pq_: e.matmul(pq_[:], lhsT=ones, rhs=sq[:, c, :],
                                                            start=(c == 0), stop=(c == NC_ - 1)),
                     R=[consts_b, sq_b[c]], W=pb_(bq))
            s.op("act", lambda e, pm=pm, p=p: e.activation(out=m2_sb[p][:], in_=pm[:], func=AF.Square), R=pb_(bm), W=[m2_b[p]])
            s.op("act", lambda e, pm=pm, p=p: e.copy(out=mean_sb[p][:], in_=pm[:]), R=pb_(bm), W=[mean_b[p]])
            s.op("dve", lambda e, pq_=pq_, p=p: e.tensor_tensor(out=rstd_sb[p][:], in0=pq_[:], in1=m2_sb[p][:], op=ALU.subtract),
                 R=pb_(bq) + [m2_b[p]], W=[rstd_b[p]])
            s.op("act", lambda e, p=p: e.activation(out=m2_sb[p][:], in_=rstd_sb[p][:], func=AF.Ln, bias=LN_EPS, scale=1.0),
                 R=[rstd_b[p]], W=[m2_b[p]])
            s.op("act", lambda e, p=p: e.activation(out=rstd_sb[p][:], in_=m2_sb[p][:], func=AF.Exp, scale=-0.5),
                 R=[m2_b[p]], W=[rstd_b[p]])

        def norm(t):
            sl = slice(t * TB, (t + 1) * TB)
            p = t % 2
            for c in range(NC_):
                j = cn["t1"] % 2
                cn["t1"] += 1
                j2 = cn["t2"] % 3
                cn["t2"] += 1
                s.op("dve", lambda e, c=c, sl=sl, j=j, p=p: e.tensor_tensor(out=t1[j][:], in0=xs[:, c, sl], in1=mean_sb[p][:], op=ALU.subtract),
                     R=[xs_b[c][t], mean_b[p]], W=[t1_b[j]])
                s.op("pool", lambda e, j=j, j2=j2, p=p: e.tensor_tensor(out=t2[j2][:], in0=t1[j][:], in1=rstd_sb[p][:], op=ALU.mult),
                     R=[t1_b[j], rstd_b[p]], W=[t2_b[j2]])
                s.op("act", lambda e, c=c, sl=sl, j2=j2: e.activation(out=xs[:, c, sl], in_=t2[j2][:], func=AF.Identity,
                                                                   scale=lnga[:, col0 + c:col0 + c + 1],
                                                                   bias=lnba[:, col0 + c:col0 + c + 1]),
                     R=[t2_b[j2], ln_c], W=[xs_b[c][t]])
                s.op("dve", lambda e, c=c, sl=sl, j2=j2: e.tensor_scalar(
                    out=xb[:, c, sl], in0=t2[j2][:], scalar1=lng[:, col0 + c:col0 + c + 1], scalar2=lnb[:, col0 + c:col0 + c + 1],
                    op0=ALU.mult, op1=ALU.add),
                    R=[t2_b[j2], ln_c], W=[xb_b[c][t]])

        stats(0)
        for t in range(NTB):
            if t + 1 < NTB:
                stats(t + 1)
            norm(t)

    def make_ln(l, i, arenas, sbanks=((0, 1), (2, 3)), lag=2):
        col0 = (l * 3 + i) * NC_
        final = (l, i) == final_ln
        g_xs, b_xs = (lng, lnb) if final else (lnga, lnba)
        yT_v = yT_d.rearrange("(c p) t -> p c t", p=128)

        def al(shape, dt, name):
            for a in arenas:
                n = 1
                for d in shape[1:]:
                    n *= d
                if a.off + (n * DT_SIZE[dt] + 63) // 64 * 64 <= a.top:
                    return a.alloc(shape, dt, name)
            raise RuntimeError("make_ln: no room for " + name)

        NSQ = 3
        sqr = [al([128, TB], F32R, "lsq") for _ in range(NSQ)]
        sqr_b = bufs(NSQ)
        mean_sb = [al([128, TB], F32, "lmean") for _ in range(2)]
        m2_sb = [al([128, TB], F32, "lm2") for _ in range(2)]
        rstd_sb = [al([128, TB], F32, "lrstd") for _ in range(2)]
        mean_b, m2_b, rstd_b = bufs(2), bufs(2), bufs(2)
        NT = 3
        t1 = [al([128, TB], F32, "lt1") for _ in range(NT)]
        t2 = [al([128, TB], F32, "lt2") for _ in range(NT)]
        t1_b, t2_b = bufs(NT), bufs(NT)
        cn = {"sq": 0, "t1": 0, "t2": 0, "seen": {}}
        pending = []
        avail = []
        fl = {"s1": None, "s2": None}

        def tick():
            if fl["s2"] is not None:
                c, t, j2 = fl["s2"]
                sl = slice(t * TB, (t + 1) * TB)
                s.op("act", lambda e, c=c, sl=sl, j2=j2: e.activation(out=xs[:, c, sl], in_=t2[j2][:], func=AF.Identity,
                                                                   scale=g_xs[:, col0 + c:col0 + c + 1],
                                                                   bias=b_xs[:, col0 + c:col0 + c + 1]),
                     R=[t2_b[j2], ln_c], W=[xs_b[c][t]])
                if not final:
                    s.op("dve", lambda e, c=c, sl=sl, j2=j2: e.tensor_scalar(
                        out=xb[:, c, sl], in0=t2[j2][:], scalar1=lng[:, col0 + c:col0 + c + 1], scalar2=lnb[:, col0 + c:col0 + c + 1],
                        op0=ALU.mult, op1=ALU.add),
                        R=[t2_b[j2], ln_c], W=[xb_b[c][t]])
                else:
                    ndone = cn.get(("done", t), 0) + 1
                    cn[("done", t)] = ndone
                    if ndone == NC_:
                        s.op("sp", lambda e, sl=sl: e.dma_start(out=yT_v[:, :, sl], in_=xs[:, :, sl]),
                             R=[xs_b[cc][t] for cc in range(NC_)], dsem=out_sem)
                        out_ops.append(s.q["sp"][-1])
                fl["s2"] = None
            if fl["s1"] is not None:
                c, t, j = fl["s1"]
                p = t % 2
                j2 = cn["t2"] % NT
                cn["t2"] += 1
                s.op("pool", lambda e, j=j, j2=j2, p=p: e.tensor_tensor(out=t2[j2][:], in0=t1[j][:], in1=rstd_sb[p][:], op=ALU.mult),
                     R=[t1_b[j], rstd_b[p]], W=[t2_b[j2]])
                fl["s2"] = (c, t, j2)
                fl["s1"] = None
            if avail:
                c, t = avail.pop(0)
                sl = slice(t * TB, (t + 1) * TB)
                p = t % 2
                j = cn["t1"] % NT
                cn["t1"] += 1
                s.op("dve", lambda e, c=c, sl=sl, j=j, p=p: e.tensor_tensor(out=t1[j][:], in0=xs[:, c, sl], in1=mean_sb[p][:], op=ALU.subtract),
                     R=[xs_b[c][t], mean_b[p]], W=[t1_b[j]])
                fl["s1"] = (c, t, j)

        def emit(entry):
            dc, t, k = entry
            sl = slice(t * TB, (t + 1) * TB)
            p = t % 2
            bm, bq = sbanks[p]
            n = cn["seen"].get(t, 0)
            cn["seen"][t] = n + 1
            s.op("pe", lambda e, dc=dc, sl=sl, bm=bm, n=n: e.matmul(psum[bm][:], lhsT=ones, rhs=xs[:, dc, sl],
                                                                   start=(n == 0), stop=(n == NC_ - 1)),
                 R=[consts_b, xs_b[dc][t]], W=pb_(bm))
            s.op("pe", lambda e, k=k, bq=bq, n=n: e.matmul(psum[bq][:], lhsT=ones_r[:], rhs=sqr[k][:],
                                                           start=(n == 0), stop=(n == NC_ - 1)),
                 R=[onesr_b, sqr_b[k]], W=pb_(bq))
            if n == NC_ - 1:
                s.op("act", lambda e, bm=bm, p=p: e.activation(out=m2_sb[p][:], in_=psum[bm][:], func=AF.Square), R=pb_(bm), W=[m2_b[p]])
                s.op("act", lambda e, bm=bm, p=p: e.copy(out=mean_sb[p][:], in_=psum[bm][:]), R=pb_(bm), W=[mean_b[p]])
                s.op("dve", lambda e, bq=bq, p=p: e.tensor_tensor(out=rstd_sb[p][:], in0=psum[bq][:], in1=m2_sb[p][:], op=ALU.subtract),
                     R=pb_(bq) + [m2_b[p]], W=[rstd_b[p]])
                s.op("act", lambda e, p=p: e.activation(out=m2_sb[p][:], in_=rstd_sb[p][:], func=AF.Ln, bias=LN_EPS, scale=1.0),
                     R=[rstd_b[p]], W=[m2_b[p]])
                s.op("act", lambda e, p=p: e.activation(out=rstd_sb[p][:], in_=m2_sb[p][:], func=AF.Exp, scale=-0.5),
                     R=[m2_b[p]], W=[rstd_b[p]])
                avail.extend((c, t) for c in range(NC_))

        def chunk_done(dc, t):
            sl = slice(t * TB, (t + 1) * TB)
            k = cn["sq"] % NSQ
            cn["sq"] += 1
            s.op("act", lambda e, dc=dc, sl=sl, k=k: e.activation(out=sqr[k][:], in_=xs[:, dc, sl], func=AF.Square),
                 R=[xs_b[dc][t]], W=[sqr_b[k]])
            pending.append((dc, t, k))
            if len(pending) > lag:
                emit(pending.pop(0))
            tick()

        def flush():
            while pending:
                emit(pending.pop(0))
                tick()
            while avail or fl["s1"] is not None or fl["s2"] is not None:
                tick()

        return chunk_done, flush

    hook = {}
    nxt = {"ph": None}

    def at(name, shape, dt, off):
        Arena.UID += 1
        return nc.alloc_sbuf_tensor_at(f"{name}_{Arena.UID}", list(shape), dt, offset=off)

    def emit_prefetch(cur_kind, e_off=None):
        ph = nxt["ph"]
        if ph is None:
            return
        if ph[0] == "ffn":
            l2, i2 = ph[1], ph[2]
            w1n = din(f"f{i2}w1_{l2}", [NF, 128, D])
            w3n = din(f"f{i2}w3_{l2}", [NF, 128, D])
            off = (SB_TOP - 4096) if cur_kind == "ffn" else e_off
            tpre = at("w13pre", [128, 2, D], BF16, off)
            bb = bufs(2)
            s.op("pool", lambda e: e.dma_start(out=tpre[:, 0, :], in_=w1n[0]), W=[bb[0]], dsem=new_dsem("w13p"))
            s.op("pool", lambda e: e.dma_start(out=tpre[:, 1, :], in_=w3n[0]), W=[bb[1]], dsem=new_dsem("w13p"))
            hook["w13pre"] = (tpre, bb)
        elif ph[0] == "mix" and cur_kind == "ffn":
            l2 = ph[1]
            wglr_n = din(f"wglr_{l2}", [128, NC_ * 16])
            wgkv_n = din(f"wgkv_{l2}", [128, NC_ * 768])
            off = SB_TOP - 8704
            wglr_p = at("wglrp", [128, NC_, 16], BF16, off)
            wgv_p = at("wgvp", [128, NC_, 512], BF16, off + 256)
            bb = bufs(2)
            s.op("pool", lambda e: e.dma_start(out=wglr_p[:], in_=wglr_n.rearrange("p (k n) -> p k n", k=NC_)),
                 W=[bb[0]], dsem=new_dsem("wBp"))
            s.op("pool", lambda e: e.dma_start(out=wgv_p[:], in_=wgkv_n.rearrange("p (k n) -> p k n", k=NC_)[:, :, 256:768]),
                 W=[bb[1]], dsem=new_dsem("wBp"))
            hook["Bpre"] = (wglr_p, wgv_p, bb)

    def ffn(l, i):
        w1_d = din(f"f{i}w1_{l}", [NF, 128, D])
        w3_d = din(f"f{i}w3_{l}", [NF, 128, D])
        w2_d = din(f"f{i}w2_{l}", [DFF, D])
        s.fence()
        A.release(arena_base)
        W13_SLOTS = 3
        w13 = [A.alloc([128, 2, D], BF16, "w13") for _ in range(W13_SLOTS)]
        w13_b = [bufs(2) for _ in range(W13_SLOTS)]
        w13_sem = [[new_dsem("w13") for _ in range(2)] for _ in range(W13_SLOTS)]
        w2 = [A.alloc([128, GMAX, D], BF16, "w2") for _ in range(2)]
        w2_b = bufs(2)
        w2_sem = [new_dsem("w2") for _ in range(2)]
        gT = [A.alloc([128, GMAX, S], BF16, "gT") for _ in range(2)]
        gT_b = [[bufs(NTB) for _ in range(GMAX)] for _ in range(2)]
        silu_t = [A.alloc([128, TB], F32, "silu") for _ in range(2)]
        silu_b = bufs(2)
        ln_chunk, ln_flush = make_ln(l, 0 if i == 0 else 2, [A])
        pre13 = hook.pop("w13pre", None)
        cnt = {"w13": 0, "w2": 0, "psA": 0, "psY": 0, "silu": 0}
        m0 = 0
        for gi, G in enumerate(FFN_GROUPS):
            ms = list(range(m0, m0 + G))
            m0 += G
            gs = gi % 2
            ws = cnt["w2"] % 2
            cnt["w2"] += 1
            s.op("pool", lambda e, ws=ws, ms=ms, G=G: e.dma_start(
                out=w2[ws][:, 0:G, :],
                in_=w2_d[ms[0] * 128:(ms[0] + G) * 128, :].rearrange("(g p) n -> p g n", p=128)),
                W=[w2_b[ws]], dsem=w2_sem[ws])
            for ml, m in enumerate(ms):
                if m == 0 and pre13 is not None:
                    wt, wtb = pre13
                else:
                    slot = cnt["w13"] % W13_SLOTS
                    cnt["w13"] += 1
                    wt, wtb = w13[slot], w13_b[slot]
                    s.op("pool", lambda e, slot=slot, m=m: e.dma_start(out=w13[slot][:, 0, :], in_=w1_d[m]),
                         W=[w13_b[slot][0]], dsem=w13_sem[slot][0])
                    s.op("pool", lambda e, slot=slot, m=m: e.dma_start(out=w13[slot][:, 1, :], in_=w3_d[m]),
                         W=[w13_b[slot][1]], dsem=w13_sem[slot][1])
                for t in range(NTB):
                    sl = slice(t * TB, (t + 1) * TB)
                    pj = cnt["psA"] % 2
                    cnt["psA"] += 1
                    for which, bi in ((0, pj), (1, 2 + pj)):
                        for k in range(NC_):
                            s.op("pe", lambda e, wt=wt, which=which, k=k, sl=sl, bi=bi: e.matmul(
                                psum[bi][:], lhsT=wt[:, which, k * 128:(k + 1) * 128], rhs=xb[:, k, sl],
                                start=(k == 0), stop=(k == NC_ - 1)),
                                R=[wtb[which], xb_b[k][t]], W=pb_(bi))
                    sj = cnt["silu"] % 2
                    cnt["silu"] += 1
                    s.op("act", lambda e, pj=pj, sj=sj: e.activation(out=silu_t[sj][:], in_=psum[pj][:], func=AF.Silu),
                         R=pb_(pj), W=[silu_b[sj]])
                    s.op("dve", lambda e, pj=pj, sj=sj, gs=gs, ml=ml, sl=sl: e.tensor_tensor(
                        out=gT[gs][:, ml, sl], in0=psum[2 + pj][:], in1=silu_t[sj][:], op=ALU.mult),
                        R=pb_(2 + pj) + [silu_b[sj]], W=[gT_b[gs][ml][t]])
            last = (gi == len(FFN_GROUPS) - 1)
            order = [(dc, t) for t in range(NTB) for dc in range(NC_)] if last else [(dc, t) for dc in range(NC_) for t in range(NTB)]
            for dc, t in order:
                if True:
                    sl = slice(t * TB, (t + 1) * TB)
                    bi = 4 + cnt["psY"] % 2
                    cnt["psY"] += 1
                    for ml in range(G):
                        s.op("pe", lambda e, ws=ws, ml=ml, dc=dc, gs=gs, sl=sl, bi=bi, G=G: e.matmul(
                            psum[bi][:], lhsT=w2[ws][:, ml, dc * 128:(dc + 1) * 128], rhs=gT[gs][:, ml, sl],
                            start=(ml == 0), stop=(ml == G - 1)),
                            R=[w2_b[ws], gT_b[gs][ml][t]], W=pb_(bi))
                    s.op("dve", lambda e, bi=bi, dc=dc, sl=sl: e.scalar_tensor_tensor(
                        out=xs[:, dc, sl], in0=psum[bi][:], scalar=0.5, in1=xs[:, dc, sl],
                        op0=ALU.mult, op1=ALU.add),
                        R=pb_(bi) + [xs_b[dc][t]], W=[xs_b[dc][t]])
                    if last:
                        ln_chunk(dc, t)
        emit_prefetch("ffn")
        ln_flush()

    def mixer(l, stop=None):
        amask_d = din("amask", [8, 128, S])
        wglr_d = din(f"wglr_{l}", [128, NC_ * 16])
        wgqk_d = din(f"wgqk_{l}", [128, NC_ * 512])
        wgkv_d = din(f"wgkv_{l}", [128, NC_ * 768])
        wgr_d = din(f"wgr_{l}", [128, NC_ * 512])
        waqkv_d = din(f"waqkv_{l}", [4, 128, NC_ * 384])
        wgab_d = din(f"wgab_{l}", [NC_, 128, NC_ * 256])
        wap_d = din(f"wap_{l}", [NC_, 128, 4 * 128])
        wgp_d = din(f"wgp_{l}", [NC_, 128, 4 * 128])
        wo_d = din(f"wo_{l}", [NC_, 128, NC_ * 128])
        blk = amask_blocks
        preB = hook.pop("Bpre", None)
        s.fence()
        A.release(arena_base)
        o_gnT = A.alloc([128, 4, S], BF16, "o_gnT")
        o_gn_b = [bufs(NTB) for _ in range(4)]
        o_a_b = [bufs(NTB) for _ in range(8)]
        mix_base = A.mark()

        qT = A.alloc([128, 2, S], BF16, "qT")
        kT = A.alloc([128, 2, S], BF16, "kT")
        qT_b = [bufs(16) for _ in range(2)]
        kT_b = [bufs(16) for _ in range(2)]
        khat = A.alloc([128, 16, 256], BF16, "khat")
        khat_b = bufs(16)
        gv = A.alloc([128, 16, 512], BF16, "gv")
        gv_b = bufs(16)
        dec = A.alloc([128, 4, 32], F32, "dec")
        dec_b = bufs(NTB)
        gla_base = A.mark()
        wB_b = bufs(4)
        wgkv_v = wgkv_d.rearrange("p (k n) -> p k n", k=NC_)
        if preB is not None:
            wglr, wgv, pb2 = preB
            wB_b[0], wB_b[3] = pb2[0], pb2[1]
        else:
            wglr = A.alloc([128, NC_, 16], BF16, "wglr")
            wgv = A.alloc([128, NC_, 512], BF16, "wgv")
            s.op("pool", lambda e: e.dma_start(out=wglr[:], in_=wglr_d.rearrange("p (k n) -> p k n", k=NC_)),
                 W=[wB_b[0]], dsem=new_dsem("wB"))
            s.op("pool", lambda e: e.dma_start(out=wgv[:], in_=wgkv_v[:, :, 256:768]),
                 W=[wB_b[3]], dsem=new_dsem("wB"))
        wgqk = A.alloc([128, NC_, 512], BF16, "wgqk")
        wgk = A.alloc([128, NC_, 256], BF16, "wgk")
        s.op("pool", lambda e: e.dma_start(out=wgqk[:], in_=wgqk_d.rearrange("p (k n) -> p k n", k=NC_)),
             W=[wB_b[1]], dsem=new_dsem("wB"))
        s.op("pool", lambda e: e.dma_start(out=wgk[:], in_=wgkv_v[:, :, 0:256]),
             W=[wB_b[2]], dsem=new_dsem("wB"))
        glrT = [A.alloc([16, TB], F32, "glrT") for _ in range(2)]
        glrT_b = bufs(2)
        z_sb = [A.alloc([128, 256], F32, "z") for _ in range(2)]
        z_b = bufs(2)
        la_sb = [A.alloc([128, 256], F32, "la") for _ in range(2)]
        la_b = bufs(2)
        Eb = A.alloc([128, 2, TB], F32, "Eb")
        Einv = A.alloc([128, 2, TB], F32, "Einv")
        E_b, Einv_b = bufs(2), bufs(2)
        Ft = A.alloc([128, 4, 256], F32, "Ft")
        F_b = bufs(4)
        zc = 0
        pc = 0
        for tb in range(NTB):
            sl = slice(tb * TB, (tb + 1) * TB)
            gj = tb % 2
            for k in range(NC_):
                s.op("pe", lambda e, k=k, sl=sl: e.matmul(psum[0][0:16, :], lhsT=wglr[:, k, :], rhs=xb[:, k, sl],
                                                         start=(k == 0), stop=(k == NC_ - 1)),
                     R=[wB_b[0], xb_b[k][tb]], W=pb_(0))
            s.op("act", lambda e, gj=gj: e.copy(out=glrT[gj][:], in_=psum[0][0:16, :]), R=pb_(0), W=[glrT_b[gj]])
            for jj in range(4):
                j = tb * 4 + jj
                zi = zc % 2
                zc += 1
                bz = 1 + zi
                s.op("pe", lambda e, gj=gj, jj=jj, bz=bz: e.matmul(
                    psum[bz][:, 0:256], lhsT=glrT[gj][:, jj * 128:(jj + 1) * 128], rhs=wgu[:, l, :],
                    start=True, stop=True),
                    R=[glrT_b[gj], mixc_b], W=pb_(bz, 0, 256))
                s.op("dve", lambda e, zi=zi, bz=bz: e.tensor_tensor(out=z_sb[zi][:], in0=psum[bz][:, 0:256], in1=bgu[:, l, :], op=ALU.add),
                     R=pb_(bz, 0, 256) + [mixc_b], W=[z_b[zi]])
                tsl = slice(j * 128, (j + 1) * 128)
                bi2 = 6 + pc % 2
                pc += 1
                for k in range(NC_):
                    s.op("pe", lambda e, k=k, tsl=tsl, bi2=bi2: e.matmul(
                        psum[bi2][:], lhsT=xb[:, k, tsl], rhs=wgv[:, k, :],
                        start=(k == 0), stop=(k == NC_ - 1)),
                        R=[wB_b[3], xb_b[k][tb]], W=pb_(bi2))
                s.op("dve", lambda e, j=j, bi2=bi2: e.tensor_copy(out=gv[:, j, :], in_=psum[bi2][:]),
                     R=pb_(bi2), W=[gv_b[j]])
                s.op("act", lambda e, zi=zi: e.activation(out=z_sb[zi][:], in_=z_sb[zi][:], func=AF.Exp, scale=-1.0),
                     R=[z_b[zi]], W=[z_b[zi]])
                s.op("act", lambda e, zi=zi: e.activation(out=la_sb[zi][:], in_=z_sb[zi][:], func=AF.Ln, bias=1.0, scale=1.0),
                     R=[z_b[zi]], W=[la_b[zi]])
                for ch in range(2):
                    s.op("pe", lambda e, zi=zi, ch=ch, jj=jj: e.matmul(
                        psum[3 + ch][:, jj * 128:(jj + 1) * 128], lhsT=la_sb[zi][:, ch * 128:(ch + 1) * 128], rhs=Umat,
                        start=True, stop=True),
                        R=[la_b[zi], consts_b], W=[pq[3 + ch][jj]])
                s.op("pe", lambda e, zi=zi: e.matmul(psum[5][:, 0:256], lhsT=Lmat, rhs=la_sb[zi][:], start=True, stop=True),
                     R=[la_b[zi], consts_b], W=pb_(5, 0, 256))
                s.op("act", lambda e, jj=jj: e.activation(out=Ft[:, jj, :], in_=psum[5][:, 0:256], func=AF.Exp),
                     R=pb_(5, 0, 256), W=[F_b[jj]])
            for ch in range(2):
                s.op("act", lambda e, ch=ch: e.activation(out=Eb[:, ch, :], in_=psum[3 + ch][:], func=AF.Exp),
                     R=pb_(3 + ch), W=[E_b[ch]])
                s.op("act", lambda e, ch=ch: e.activation(out=Einv[:, ch, :], in_=psum[3 + ch][:], func=AF.Exp, scale=-1.0),
                     R=pb_(3 + ch), W=[Einv_b[ch]])
            for dup in range(2):
                s.op("dve", lambda e, tb=tb, dup=dup: e.tensor_copy(
                    out=dec[:].rearrange("p (c d) n -> p c d n", d=2)[:, :, dup, tb * 8:(tb + 1) * 8], in_=Eb[:, :, 63::64]),
                    R=E_b, W=[dec_b[tb]])
            for m in range(4):
                ch = m % 2
                bi = 6 + pc % 2
                pc += 1
                for k in range(NC_):
                    s.op("pe", lambda e, m=m, k=k, sl=sl, bi=bi: e.matmul(
                        psum[bi][:], lhsT=wgqk[:, k, m * 128:(m + 1) * 128], rhs=xb[:, k, sl],
                        start=(k == 0), stop=(k == NC_ - 1)),
                        R=[wB_b[1], xb_b[k][tb]], W=pb_(bi))
                if m < 2:
                    s.op("dve", lambda e, ch=ch, sl=sl, bi=bi: e.scalar_tensor_tensor(
                        out=qT[:, ch, sl], in0=psum[bi][:], scalar=0.125, in1=Eb[:, ch, :], op0=ALU.mult, op1=ALU.mult),
                        R=pb_(bi) + [E_b[ch]], W=qT_b[ch][tb * 4:(tb + 1) * 4])
                else:
                    s.op("dve", lambda e, ch=ch, sl=sl, bi=bi: e.tensor_tensor(
                        out=kT[:, ch, sl], in0=psum[bi][:], in1=Einv[:, ch, :], op=ALU.mult),
                        R=pb_(bi) + [Einv_b[ch]], W=kT_b[ch][tb * 4:(tb + 1) * 4])
            for jj in range(4):
                j = tb * 4 + jj
                tsl = slice(j * 128, (j + 1) * 128)
                bi = 1 + jj % 2
                for k in range(NC_):
                    s.op("pe", lambda e, k=k, tsl=tsl, bi=bi: e.matmul(
                        psum[bi][:, 0:256], lhsT=xb[:, k, tsl], rhs=wgk[:, k, :],
                        start=(k == 0), stop=(k == NC_ - 1)),
                        R=[wB_b[2], xb_b[k][tb]], W=pb_(bi, 0, 256))
                s.op("dve", lambda e, j=j, jj=jj, bi=bi: e.tensor_tensor(
                    out=khat[:, j, :], in0=psum[bi][:, 0:256], in1=Ft[:, jj, :], op=ALU.mult),
                    R=pb_(bi, 0, 256) + [F_b[jj]], W=[khat_b[j]])

        if stop == "B":
            return
        tap("qT", qT, [b for bb in qT_b for b in bb])
        tap("kT", kT, [b for bb in kT_b for b in bb])
        tap("khat", khat, khat_b)
        tap("gv", gv, gv_b)
        tap("dec", dec, dec_b)
        s.fence()
        A.release(gla_base)
        wgr = A.alloc([128, NC_, 512], BF16, "wgr")
        wgr_b = Buf()
        s.op("pool", lambda e: e.dma_start(out=wgr[:], in_=wgr_d.rearrange("p (k n) -> p k n", k=NC_)),
             W=[wgr_b], dsem=new_dsem("wgr"))
        ograw = [A.alloc([128, 4, TB], F32, "ograw") for _ in range(2)]
        ograw_b = [[bufs(4) for _ in range(4)] for _ in range(2)]
        st_f = A.alloc([128, 4, 128], F32, "st_f")
        st_b = A.alloc([128, 4, 128], BF16, "st_b")
        stf_b, stb_b = bufs(4), bufs(4)
        ATt = [A.alloc([128, 4, 128], BF16, "AT") for _ in range(2)]
        AT_b = [bufs(4) for _ in range(2)]
        sg = [A.alloc([128, TB], F32, "sg") for _ in range(4)]
        sg_b = bufs(4)
        gsq = A.alloc([128, TB], F32R, "gsq")
        gsq_b = Buf()
        gm2 = A.alloc([128, TB], F32, "gm2")
        grs = A.alloc([128, TB], F32, "grs")
        gm2_b, grs_b = Buf(), Buf()
        s.op("dve", lambda e: e.memset(st_f[:], 0.0), W=stf_b)
        s.op("dve", lambda e: e.memset(st_b[:], 0.0), W=stb_b)
        bS0, bS1 = 4, 5
        HORD = (0, 2, 1, 3)

        def emit_AT_mm(j):
            bA = j % 2
            tok = slice(j * 128, (j + 1) * 128)
            prev = None
            for h in HORD:
                ch, pb = h // 2, (h % 2) * 64
                hs = slice(h * 128, (h + 1) * 128)
                prev = s.op("pe", lambda e, ch=ch, pb=pb, hs=hs, tok=tok, bA=bA: e.matmul(
                    psum[bA][:, hs], lhsT=kT[pb:pb + 64, ch, tok], rhs=qT[pb:pb + 64, ch, tok], start=True, stop=True),
                    R=[kT_b[ch][j], qT_b[ch][j]], W=[pq[bA][h]], after=[prev] if h == 1 else [])

        def emit_AT_mask(j):
            bA = j % 2
            aj = j % 2
            s.op("dve", lambda e, bA=bA, aj=aj: e.tensor_tensor(
                out=ATt[aj][:], in0=psum[bA][:].rearrange("p (h n) -> p h n", h=4),
                in1=Mblk.unsqueeze(1).broadcast_to([128, 4, 128]), op=ALU.mult),
                R=[pq[bA][0], consts_b], W=AT_b[aj])

        def emit_dS(j, half):
            bS = bS0 if half == 0 else bS1
            rows = slice(half * 64, half * 64 + 64)
            for h in range(4):
                ch = h // 2
                hs = slice(h * 128, (h + 1) * 128)
                s.op("pe", lambda e, ch=ch, hs=hs, rows=rows, bS=bS, j=j: e.matmul(
                    psum[bS][:, hs], lhsT=khat[rows, j, ch * 128:(ch + 1) * 128], rhs=gv[rows, j, hs],
                    start=True, stop=True),
                    R=[khat_b[j], gv_b[j]], W=[pq[bS][h]])

        def emit_decay(c):
            s.op("dve", lambda e, c=c: e.tensor_tensor(
                out=st_f[:], in0=st_f[:], in1=dec[:, :, c:c + 1].broadcast_to([128, 4, 128]), op=ALU.mult),
                R=stf_b + [dec_b[c // 8]], W=stf_b)

        def emit_update(j, half):
            bS = bS0 if half == 0 else bS1
            c = 2 * j + half
            s.op("dve", lambda e, bS=bS: e.tensor_tensor(
                out=st_f[:], in0=st_f[:], in1=psum[bS][:].rearrange("p (h n) -> p h n", h=4), op=ALU.add),
                R=stf_b + [pq[bS][0]], W=stf_b)
            s.op("dve", lambda e: e.tensor_copy(out=st_b[:], in_=st_f[:]), R=stf_b, W=stb_b)
            if c + 1 < 32:
                emit_decay(c + 1)

        gtasks = []
        gstate = {"B": None}

        def gate_batch(tb):
            sl = slice(tb * TB, (tb + 1) * TB)
            for h in range(4):
                bi = 6 + h % 2
                for k in range(NC_):
                    s.op("pe", lambda e, k=k, h=h, sl=sl, bi=bi: e.matmul(
                        psum[bi][:], lhsT=wgr[:, k, h * 128:(h + 1) * 128], rhs=xb[:, k, sl],
                        start=(k == 0), stop=(k == NC_ - 1)),
                        R=[wgr_b, xb_b[k][tb]], W=pb_(bi))
                s.op("act", lambda e, h=h, bi=bi: e.activation(out=sg[h][:], in_=psum[bi][:], func=AF.Silu),
                     R=pb_(bi), W=[sg_b[h]])
            for h in range(4):
                gtasks.append((tb, h))

        def gn_A(tb, h):
            ob = tb % 2
            og = ograw[ob][:, h, :]
            ogb = ograw_b[ob][h]
            s.op("act", lambda e, og=og: e.activation(out=gsq[:], in_=og, func=AF.Square), R=ogb, W=[gsq_b])
            s.op("pe", lambda e, og=og: e.matmul(psum[6][:], lhsT=gones, rhs=og, start=True, stop=True),
                 R=ogb + [consts_b], W=pb_(6))
            s.op("pe", lambda e: e.matmul(psum[7][:], lhsT=gones_r[:], rhs=gsq[:], start=True, stop=True),
                 R=[gsq_b, onesr_b], W=pb_(7))
            s.op("act", lambda e: e.activation(out=gm2[:], in_=psum[6][:], func=AF.Square), R=pb_(6), W=[gm2_b])
            s.op("dve", lambda e, og=og: e.tensor_tensor(out=og, in0=og, in1=psum[6][:], op=ALU.subtract),
                 R=ogb + pb_(6), W=ogb)
            s.op("dve", lambda e: e.tensor_tensor(out=grs[:], in0=psum[7][:], in1=gm2[:], op=ALU.subtract),
                 R=pb_(7) + [gm2_b], W=[grs_b])
            s.op("act", lambda e: e.activation(out=gm2[:], in_=grs[:], func=AF.Ln, bias=LN_EPS, scale=1.0),
                 R=[grs_b], W=[gm2_b])
            s.op("act", lambda e: e.activation(out=grs[:], in_=gm2[:], func=AF.Exp, scale=-0.5), R=[gm2_b], W=[grs_b])

        def gn_B(tb, h):
            ob = tb % 2
            og = ograw[ob][:, h, :]
            ogb = ograw_b[ob][h]
            col = l * 4 + h
            sl = slice(tb * TB, (tb + 1) * TB)
            s.op("pool", lambda e, og=og: e.tensor_tensor(out=og, in0=og, in1=grs[:], op=ALU.mult),
                 R=ogb + [grs_b], W=ogb)
            s.op("act", lambda e, og=og, col=col: e.activation(out=og, in_=og, func=AF.Identity,
                                                             scale=gng[:, col:col + 1], bias=gnb[:, col:col + 1]),
                 R=ogb + [mixc_b], W=ogb)
            s.op("pool", lambda e, og=og, h=h, sl=sl: e.tensor_tensor(out=o_gnT[:, h, sl], in0=og, in1=sg[h][:], op=ALU.mult),
                 R=ogb + [sg_b[h]], W=[o_gn_b[h][tb]])

        def gn_slot():
            if gstate["B"] is not None:
                gn_B(*gstate["B"])
                gstate["B"] = None
            if gtasks:
                t_ = gtasks.pop(0)
                gn_A(*t_)
                gstate["B"] = t_

        def gn_flush():
            while gtasks or gstate["B"] is not None:
                gn_slot()

        emit_AT_mm(0)
        emit_dS(0, 0)
        emit_dS(0, 1)
        emit_AT_mask(0)
        for j in range(16):
            tb, jj = j // 4, j % 4
            ob = tb % 2
            aj = j % 2
            bO = 2 + aj
            t0 = slice(j * 128, j * 128 + 64)
            t1_ = slice(j * 128 + 64, (j + 1) * 128)
            for h in range(4):
                hs = slice(h * 128, (h + 1) * 128)
                s.op("pe", lambda e, h=h, hs=hs, bO=bO, aj=aj, j=j: e.matmul(
                    psum[bO][:, hs], lhsT=gv[:, j, hs], rhs=ATt[aj][:, h, :], start=(h == 0), stop=False, skip_group_check=True),
                    R=[gv_b[j], AT_b[aj][h]], W=[pq[bO][h]])
            prev = None
            for h in HORD:
                ch, pb = h // 2, (h % 2) * 64
                prev = s.op("pe", lambda e, h=h, ch=ch, pb=pb, bO=bO, t0=t0: e.matmul(
                    psum[bO][:, h * 128:h * 128 + 64], lhsT=st_b[pb:pb + 64, h, :], rhs=qT[pb:pb + 64, ch, t0],
                    start=False, stop=False, skip_group_check=True),
                    R=[stb_b[h], qT_b[ch][j]], W=[pq[bO][h]], after=[prev] if h == 1 else [])
            if j + 1 < 16:
                emit_AT_mm(j + 1)
            emit_update(j, 0)
            prev = None
            for h in HORD:
                ch, pb = h // 2, (h % 2) * 64
                prev = s.op("pe", lambda e, h=h, ch=ch, pb=pb, bO=bO, t1_=t1_: e.matmul(
                    psum[bO][:, h * 128 + 64:(h + 1) * 128], lhsT=st_b[pb:pb + 64, h, :], rhs=qT[pb:pb + 64, ch, t1_],
                    start=False, stop=True, skip_group_check=True),
                    R=[stb_b[h], qT_b[ch][j]], W=[pq[bO][h]], after=[prev] if h == 1 else [])
            if j + 1 < 16:
                emit_AT_mask(j + 1)
                emit_dS(j + 1, 0)
            s.op("act", lambda e, bO=bO, ob=ob, jj=jj: e.copy(
                out=ograw[ob][:, :, jj * 128:(jj + 1) * 128], in_=psum[bO][:].rearrange("p (h n) -> p h n", h=4)),
                R=[pq[bO][0]], W=[ograw_b[ob][h][jj] for h in range(4)])
            emit_update(j, 1)
            if j + 1 < 16:
                emit_dS(j + 1, 1)
            if j == 0:
                tap("st1", st_f, stf_b)
            gn_slot()
            if jj == 3:
                gn_flush()
                if tb == 0:
                    tap("ograw0", ograw[0], [b for bb in ograw_b[0] for b in bb])
                gate_batch(tb)
        gn_flush()

        if stop == "D":
            return
        tap("o_gnT", o_gnT, [b for bb in o_gn_b for b in bb])
        s.fence()
        A.release(mix_base)
        o_aT = A.alloc([128, 4, S], BF16, "o_aT")
        mix_base = A.mark()
        waqkv = A.alloc([128, NC_, 384], BF16, "waqkv")
        waqkv_b = Buf()
        waqkv_sem = new_dsem("waqkv")
        aqT = [[A.alloc([128, S], BF16, "aqT") for _ in range(2)] for _ in range(2)]
        akT = [A.alloc([128, S], BF16, "akT") for _ in range(2)]
        aqT_b = [bufs(NTB) for _ in range(2)]
        aqz_b = [bufs(2) for _ in range(2)]
        for wi_ in range(2):
            for hh_ in range(2):
                oth = slice(64, 128) if hh_ == 0 else slice(0, 64)
                s.op("pool", lambda e, wi_=wi_, hh_=hh_, oth=oth: e.memset(aqT[wi_][hh_][oth, :], 0.0), W=[aqz_b[wi_][hh_]])
        akT_b = [bufs(16) for _ in range(2)]
        Vp = [A.alloc([128, 16, 128], BF16, "Vp") for _ in range(2)]
        Vp_b = [bufs(16) for _ in range(2)]
        mask_s = [A.alloc([128, S], F32, "mask") for _ in range(2)]
        mask_b = bufs(2)
        mask_sem = [new_dsem("mask") for _ in range(2)]
        NE = 4
        LOOK = 3
        Et = [A.alloc([128, TB], F32, "Et") for _ in range(NE)]
        Pt = [A.alloc([128, TB], BF16, "Pt") for _ in range(NE)]
        Et_b, Pt_b = bufs(NE), bufs(NE)
        dcp = [A.alloc([128, TB], F32, "dcp") for _ in range(1)] * 2
        dcp_b = bufs(1) * 2
        onesb = A.alloc([128, 128], BF16, "onesb")
        onesb_b = Buf()
        s.op("pool", lambda e: e.memset(onesb[:], 1.0), W=[onesb_b])
        SB = (0, 1, 2, 3)
        stepc = 0
        def load_waqkv(ch):
            s.op("pool", lambda e, ch=ch: e.dma_start(out=waqkv[:], in_=waqkv_d[ch].rearrange("p (k n) -> p k n", k=NC_)),
                 W=[waqkv_b], dsem=waqkv_sem)

        def load_mask(h):
            mi = h % 2
            s.op("sp", lambda e, mi=mi, h=h: e.dma_start(out=mask_s[mi][:], in_=amask_d[h]),
                 W=[mask_b[mi]], dsem=mask_sem[mi])

        load_waqkv(0)
        load_mask(0)
        load_mask(1)
        for ch in range(4):
            wi = ch % 2
            for tb in range(NTB):
                sl = slice(tb * TB, (tb + 1) * TB)
                for which in range(2):
                    bi = SB[(2 * tb + which) % 4]
                    for k in range(NC_):
                        s.op("pe", lambda e, k=k, which=which, sl=sl, bi=bi: e.matmul(
                            psum[bi][:], lhsT=waqkv[:, k, which * 128:(which + 1) * 128], rhs=xb[:, k, sl],
                            start=(k == 0), stop=(k == NC_ - 1)),
                            R=[waqkv_b, xb_b[k][tb]], W=pb_(bi))
                    if which == 0:
                        s.op("act", lambda e, wi=wi, sl=sl, bi=bi: e.mul(aqT[wi][0][0:64, sl], psum[bi][0:64, :], 0.125),
                             R=pb_(bi), W=[aqT_b[wi][tb]])
                        s.op("act", lambda e, wi=wi, sl=sl, bi=bi: e.mul(aqT[wi][1][64:128, sl], psum[bi][64:128, :], 0.125),
                             R=pb_(bi), W=[aqT_b[wi][tb]])
                    else:
                        s.op("dve", lambda e, wi=wi, sl=sl, bi=bi: e.tensor_copy(out=akT[wi][:, sl], in_=psum[bi][:]),
                             R=pb_(bi), W=akT_b[wi][tb * 4:(tb + 1) * 4])
            for j4 in range(4):
                bi = SB[j4 % 4]
                for jj in range(4):
                    j = j4 * 4 + jj
                    tsl = slice(j * 128, (j + 1) * 128)
                    for k in range(NC_):
                        s.op("pe", lambda e, k=k, tsl=tsl, bi=bi, jj=jj: e.matmul(
                            psum[bi][:, jj * 128:(jj + 1) * 128], lhsT=xb[:, k, tsl], rhs=waqkv[:, k, 256:384],
                            start=(k == 0 and jj == 0), stop=(k == NC_ - 1), skip_group_check=True),
                            R=[waqkv_b, xb_b[k][j4]], W=pb_(bi))
                s.op("act", lambda e, wi=wi, j4=j4, bi=bi: e.copy(
                    out=Vp[wi][:, j4 * 4:(j4 + 1) * 4, :], in_=psum[bi][:].rearrange("p (a b) -> p a b", a=4)),
                    R=pb_(bi), W=Vp_b[wi][j4 * 4:(j4 + 1) * 4])
            if ch + 1 < 4:
                load_waqkv(ch + 1)
            steps = []
            for hh in range(2):
                h = 2 * ch + hh
                for qp in range(NTB):
                    kbs = [kb for kb in range(4 * qp + 4) if blk[h][kb][qp]]
                    for ki, kb in enumerate(kbs):
                        steps.append((hh, h, qp, kb, ki, len(kbs)))
            info = {}
            for idx in range(len(steps) + LOOK):
                if idx < len(steps):
                    hh, h, qp, kb, ki, nk = steps[idx]
                    pb = hh * 64
                    mi = h % 2
                    q0 = qp * TB
                    n0 = max(q0, 128 * kb)
                    n = q0 + TB - n0
                    bS = SB[stepc % 4]
                    ei = stepc % NE
                    stepc += 1
                    info[idx] = (ei, n0, n)
                    s.op("pe", lambda e, wi=wi, hh=hh, kb=kb, n0=n0, n=n, bS=bS: e.matmul(
                        psum[bS][:, 0:n], lhsT=akT[wi][:, kb * 128:(kb + 1) * 128],
                        rhs=aqT[wi][hh][:, n0:n0 + n], start=True, stop=True),
                        R=[akT_b[wi][kb], aqT_b[wi][qp], aqz_b[wi][hh]], W=pb_(bS))
                    s.op("act", lambda e, ei=ei, bS=bS, n=n: e.activation(out=Et[ei][:, 0:n], in_=psum[bS][:, 0:n], func=AF.Exp),
                         R=pb_(bS), W=[Et_b[ei]])
                    mo = n0 - 128 * kb
                    s.op("dve", lambda e, ei=ei, mi=mi, mo=mo, n=n: e.tensor_tensor(
                        out=Pt[ei][:, 0:n], in0=Et[ei][:, 0:n], in1=mask_s[mi][:, mo:mo + n], op=ALU.mult),
                        R=[Et_b[ei], mask_b[mi]], W=[Pt_b[ei]])
                    if h + 2 < 8 and (idx + 1 == len(steps) or steps[idx + 1][1] != h):
                        load_mask(h + 2)
                pidx = idx - LOOK
                if pidx >= 0:
                    hh, h, qp, kb, ki, nk = steps[pidx]
                    ei, n0, n = info[pidx]
                    pb = hh * 64
                    q0 = qp * TB
                    par = (h * NTB + qp) % 2
                    bO, bD = 4 + par, 6 + par
                    cs = slice(n0 - q0, n0 - q0 + n)
                    s.op("pe", lambda e, wi=wi, kb=kb, ei=ei, n=n, cs=cs, bO=bO, ki=ki, nk=nk: e.matmul(
                        psum[bO][:, cs], lhsT=Vp[wi][:, kb, :], rhs=Pt[ei][:, 0:n],
                        start=(ki == 0), stop=(ki == nk - 1), skip_group_check=True),
                        R=[Vp_b[wi][kb], Pt_b[ei]], W=pb_(bO))
                    s.op("pe", lambda e, ei=ei, n=n, cs=cs, bD=bD, ki=ki, nk=nk: e.matmul(
                        psum[bD][:, cs], lhsT=onesb[:], rhs=Pt[ei][:, 0:n],
                        start=(ki == 0), stop=(ki == nk - 1), skip_group_check=True),
                        R=[onesb_b, Pt_b[ei]], W=pb_(bD))
                    if ki == nk - 1:
                        ps_ = slice(pb, pb + 64)
                        s.op("act", lambda e, par=par, bD=bD, ps_=ps_: e.activation(out=dcp[par][ps_, :], in_=psum[bD][ps_, :], func=AF.Ln),
                             R=pb_(bD), W=[dcp_b[par]])
                        s.op("act", lambda e, par=par, ps_=ps_: e.activation(out=dcp[par][ps_, :], in_=dcp[par][ps_, :], func=AF.Exp, scale=-1.0),
                             R=[dcp_b[par]], W=[dcp_b[par]])
                        s.op("dve", lambda e, ch=ch, ps_=ps_, q0=q0, bO=bO, par=par: e.tensor_tensor(
                            out=o_aT[ps_, ch, q0:q0 + TB], in0=psum[bO][ps_, :], in1=dcp[par][ps_, :], op=ALU.mult),
                            R=pb_(bO) + [dcp_b[par]], W=[o_a_b[h][qp]])

        if stop == "C":
            return
        tap("o_aT", o_aT, [b for bb in o_a_b for b in bb])
        s.fence()
        A.release(mix_base)
        e_low_top = A.mark()
        mT = A.alloc([128, NC_, S], BF16, "mT")
        mT_b = [bufs(NTB) for _ in range(NC_)]
        e_base = A.mark()
        wE = [A.alloc([128, NC_, 256], BF16, "wgab") for _ in range(2)]
        wap = [A.alloc([128, 4, 128], BF16, "wap") for _ in range(2)]
        wgp = [A.alloc([128, 4, 128], BF16, "wgp") for _ in range(2)]
        wE_b = [bufs(3) for _ in range(2)]
        wE_sem = [[new_dsem("wE") for _ in range(3)] for _ in range(2)]
        sa = [A.alloc([128, TB], F32, "sa") for _ in range(2)]
        sbt = [A.alloc([128, TB], F32, "sbt") for _ in range(2)]
        sa_b, sbt_b = bufs(2), bufs(2)
        wo = A.alloc([128, NC_, NC_, 128], BF16, "wo")
        wo_b = bufs(NC_)
        e_top = A.mark()
        ecn = 0

        def load_wE(dc):
            wi = dc % 2
            s.op("pool", lambda e, wi=wi, dc=dc: e.dma_start(out=wE[wi][:], in_=wgab_d[dc].rearrange("p (k n) -> p k n", k=NC_)),
                 W=[wE_b[wi][0]], dsem=wE_sem[wi][0])
            s.op("pool", lambda e, wi=wi, dc=dc: e.dma_start(out=wap[wi][:], in_=wap_d[dc].rearrange("p (k n) -> p k n", k=4)),
                 W=[wE_b[wi][1]], dsem=wE_sem[wi][1])
            s.op("pool", lambda e, wi=wi, dc=dc: e.dma_start(out=wgp[wi][:], in_=wgp_d[dc].rearrange("p (k n) -> p k n", k=4)),
                 W=[wE_b[wi][2]], dsem=wE_sem[wi][2])

        load_wE(0)
        for dc in range(NC_):
            wi = dc % 2
            if dc + 1 < NC_:
                load_wE(dc + 1)
            if dc >= 4:
                for dco in (2 * (dc - 4), 2 * (dc - 4) + 1):
                    s.op("pool", lambda e, dco=dco: e.dma_start(out=wo[:, dco, :, :], in_=wo_d[dco].rearrange("p (k n) -> p k n", k=NC_)),
                         W=[wo_b[dco]], dsem=new_dsem("wo"))
            for tb in range(NTB):
                sl = slice(tb * TB, (tb + 1) * TB)
                pj = ecn % 2
                ecn += 1
                bGA, bGB, bPA, bPG = 0 + pj, 2 + pj, 4 + pj, 6 + pj
                for which, bi in ((0, bGA), (1, bGB)):
                    for k in range(NC_):
                        s.op("pe", lambda e, wi=wi, which=which, k=k, sl=sl, bi=bi: e.matmul(
                            psum[bi][:], lhsT=wE[wi][:, k, which * 128:(which + 1) * 128], rhs=xb[:, k, sl],
                            start=(k == 0), stop=(k == NC_ - 1)),
                            R=[wE_b[wi][0], xb_b[k][tb]], W=pb_(bi))
                for c in range(4):
                    s.op("pe", lambda e, wi=wi, c=c, sl=sl, bPA=bPA: e.matmul(
                        psum[bPA][:], lhsT=wap[wi][:, c, :], rhs=o_aT[:, c, sl], start=(c == 0), stop=(c == 3)),
                        R=[wE_b[wi][1], o_a_b[2 * c][tb], o_a_b[2 * c + 1][tb]], W=pb_(bPA))
                for c in range(4):
                    s.op("pe", lambda e, wi=wi, c=c, sl=sl, bPG=bPG: e.matmul(
                        psum[bPG][:], lhsT=wgp[wi][:, c, :], rhs=o_gnT[:, c, sl], start=(c == 0), stop=(c == 3)),
                        R=[wE_b[wi][2], o_gn_b[c][tb]], W=pb_(bPG))
                s.op("act", lambda e, pj=pj, bGA=bGA: e.activation(out=sa[pj][:], in_=psum[bGA][:], func=AF.Sigmoid),
                     R=pb_(bGA), W=[sa_b[pj]])
                s.op("act", lambda e, pj=pj, bGB=bGB: e.activation(out=sbt[pj][:], in_=psum[bGB][:], func=AF.Sigmoid),
                     R=pb_(bGB), W=[sbt_b[pj]])
                s.op("dve", lambda e, pj=pj, bPA=bPA: e.tensor_tensor(out=sa[pj][:], in0=psum[bPA][:], in1=sa[pj][:], op=ALU.mult),
                     R=pb_(bPA) + [sa_b[pj]], W=[sa_b[pj]])
                s.op("dve", lambda e, pj=pj, bPG=bPG: e.tensor_tensor(out=sbt[pj][:], in0=psum[bPG][:], in1=sbt[pj][:], op=ALU.mult),
                     R=pb_(bPG) + [sbt_b[pj]], W=[sbt_b[pj]])
                s.op("pool", lambda e, pj=pj, dc=dc, sl=sl: e.tensor_tensor(out=mT[:, dc, sl], in0=sa[pj][:], in1=sbt[pj][:], op=ALU.add),
                     R=[sa_b[pj], sbt_b[pj]], W=[mT_b[dc][tb]])
        tap("mT", mT, [b for bb in mT_b for b in bb])
        s.fence()
        A.release(e_top)
        Alow = Arena(nc, arena_base, e_low_top)
        ln_chunk, ln_flush = make_ln(l, 1, [Alow, A])
        yc = 0
        for tb in range(NTB):
            sl = slice(tb * TB, (tb + 1) * TB)
            for dc in range(NC_):
                bi = 4 + yc % 2
                yc += 1
                for k in range(NC_):
                    s.op("pe", lambda e, dc=dc, k=k, sl=sl, bi=bi: e.matmul(
                        psum[bi][:], lhsT=wo[:, dc, k, :], rhs=mT[:, k, sl], start=(k == 0), stop=(k == NC_ - 1)),
                        R=[wo_b[dc], mT_b[k][tb]], W=pb_(bi))
                s.op("dve", lambda e, bi=bi, dc=dc, sl=sl: e.tensor_tensor(
                    out=xs[:, dc, sl], in0=psum[bi][:], in1=xs[:, dc, sl], op=ALU.add),
                    R=pb_(bi) + [xs_b[dc][tb]], W=[xs_b[dc][tb]])
                ln_chunk(dc, tb)
        emit_prefetch("mix", e_base)
        ln_flush()

    amask_blocks = _mask_blocks()
    out_ops = []
    lastp = phases[-1]
    if lastp[0] == "ffn":
        final_ln = (lastp[1], 0 if lastp[2] == 0 else 2)
    elif lastp[0] == "mix" and len(lastp) == 2:
        final_ln = (lastp[1], 1)
    else:
        final_ln = None
    for pi, ph in enumerate(phases):
        nxt["ph"] = phases[pi + 1] if pi + 1 < len(phases) else None
        if ph[0] == "ffn":
            ffn(ph[1], ph[2])
        elif ph[0] == "mix":
            mixer(ph[1], ph[2] if len(ph) > 2 else None)
        else:
            raise ValueError(ph)

    if not out_ops:
        for c in range(NC_):
            for t in range(NTB):
                sl = slice(t * TB, (t + 1) * TB)
                s.op("dve", lambda e, c=c, sl=sl: e.tensor_scalar_mul(out=xs[:, c, sl], in0=xs[:, c, sl], scalar1=1.0 / ALPHA),
                     R=[xs_b[c][t]], W=[xs_b[c][t]])
        for c in range(NC_):
            out_ops.append(s.op("sp", lambda e, c=c: e.dma_start(out=yT_d[c * 128:(c + 1) * 128, :], in_=xs[:, c, :]),
                                R=xs_b[c], dsem=out_sem))
    fin = Buf()
    fin.w = out_ops[-1]
    s.op("sp", lambda e: e.nop(), R=[fin])

    s.finalize()
    from contextlib import ExitStack
    with ExitStack() as ctx:
        esem = {}
        for en in Sched.ENGS:
            esem[en] = ctx.enter_context(nc.semaphore(f"sem_{en}"))
        dsems = {}
        for nm in dsem_names:
            dsems[nm] = ctx.enter_context(nc.semaphore(f"d_{nm}"))
        with nc.Block() as block:
            @block.tensor
            def _(e):
                s.replay("pe", e, esem, dsems)

            @block.scalar
            def _(e):
                s.replay("act", e, esem, dsems)

            @block.vector
            def _(e):
                s.replay("dve", e, esem, dsems)

            @block.gpsimd
            def _(e):
                s.replay("pool", e, esem, dsems)

            @block.sync
            def _(e):
                s.replay("sp", e, esem, dsems)
    return nc


_MASK = None


def _alibi_mask():
    global _MASK
    if _MASK is None:
        d = np.arange(S)[None, :] - np.arange(128)[:, None]
        mult = ((d <= 128).astype(np.float64) + ((d % 4 == 0) & (d <= 512)) + ((d % 16 == 0) & (d <= 2048)))
        mult = np.where(d >= 0, mult, 0.0)
        slopes = np.exp2(-8.0 * np.arange(1, 9) / 8.0)
        m = mult[None] * np.exp(-slopes[:, None, None] * np.maximum(d, 0)[None])
        m = np.where(m < 1e-37, 0.0, m)
        _MASK = np.ascontiguousarray(m.astype(np.float32))
    return _MASK


def _mask_blocks():
    m = _alibi_mask()
    blk = [[[False] * NTB for _ in range(16)] for _ in range(8)]
    for h in range(8):
        for kb in range(16):
            for qp in range(NTB):
                n0 = max(qp * TB, 128 * kb)
                n1 = qp * TB + TB
                if n1 <= n0:
                    continue
                blk[h][kb][qp] = bool(m[h][:, n0 - 128 * kb:n1 - 128 * kb].any())
    return blk


def _consts():
    s_ = np.arange(128)[:, None]
    t_ = np.arange(128)[None, :]
    same = (s_ // 64) == (t_ // 64)
    U = np.where(same & (s_ <= t_), -1.0 / 16.0, 0.0)
    L = np.where(same & (s_ > t_), -1.0 / 16.0, 0.0)
    M = np.where(same & (s_ <= t_), 1.0, 0.0)
    o1 = np.full((128, 128), 1.0 / D)
    o2 = np.full((128, 128), 1.0 / 128.0)
    return np.ascontiguousarray(np.concatenate([U, L, M, o1, o2], axis=1).astype(np.float32))


def _lay_w13(w):
    return np.ascontiguousarray(w.reshape(NC_, 128, NF, 128).transpose(2, 1, 0, 3).reshape(NF, 128, D))


def _lay_ln(v):
    return np.ascontiguousarray(v.reshape(DEPTH, 3, NC_, 128).transpose(3, 0, 1, 2).reshape(128, NL3))


def _lay_cols(w):
    n = w.shape[1]
    return np.ascontiguousarray(w.reshape(NC_, 128, n).transpose(1, 0, 2).reshape(128, NC_ * n))


def make_inputs(phases, inp):
    m = {"ln_g": _lay_ln(inp["ln_g"]), "ln_b": _lay_ln(inp["ln_b"]), "consts": _consts()}
    has_mix = any(p[0] == "mix" for p in phases)
    if has_mix:
        m["amask"] = _alibi_mask()
        m["wgu"] = np.ascontiguousarray(inp["w_gate_up"].transpose(1, 0, 2))
        m["bgu"] = np.ascontiguousarray(np.broadcast_to(inp["b_gate_up"][None], (128, DEPTH, 256)))
        m["gng"] = np.ascontiguousarray(inp["gla_norm_g"].reshape(DEPTH, 4, 128).transpose(2, 0, 1).reshape(128, DEPTH * 4))
        m["gnb"] = np.ascontiguousarray(inp["gla_norm_b"].reshape(DEPTH, 4, 128).transpose(2, 0, 1).reshape(128, DEPTH * 4))
    for ph in phases:
        if ph[0] == "ffn":
            l, i = ph[1], ph[2]
            pre = "ffn1" if i == 0 else "ffn2"
            m[f"f{i}w1_{l}"] = _lay_w13(inp[pre + "_w1"][l])
            m[f"f{i}w3_{l}"] = _lay_w13(inp[pre + "_w3"][l])
            m[f"f{i}w2_{l}"] = np.ascontiguousarray(inp[pre + "_w2"][l])
        else:
            l = ph[1]
            w = inp["w_in"][l]
            m[f"wglr_{l}"] = _lay_cols(w[:, O_GLR:O_GLR + 16])
            m[f"wgqk_{l}"] = _lay_cols(w[:, O_GQ:O_GQ + 512])
            m[f"wgkv_{l}"] = _lay_cols(w[:, O_GK:O_GK + 768])
            m[f"wgr_{l}"] = _lay_cols(w[:, O_GR:O_GR + 512])
            m[f"waqkv_{l}"] = np.stack([_lay_cols(np.concatenate(
                [w[:, O_AQ + c * 128:O_AQ + (c + 1) * 128], w[:, O_AK + c * 128:O_AK + (c + 1) * 128],
                 w[:, O_AV + c * 128:O_AV + (c + 1) * 128]], axis=1)) for c in range(4)], axis=0)
            m[f"wgab_{l}"] = np.stack([_lay_cols(np.concatenate(
                [w[:, O_GA + c * 128:O_GA + (c + 1) * 128], w[:, O_GB + c * 128:O_GB + (c + 1) * 128]], axis=1))
                for c in range(NC_)], axis=0)
            wa = inp["w_attn_proj"][l]
            m[f"wap_{l}"] = np.ascontiguousarray(wa.reshape(4, 128, NC_, 128).transpose(2, 1, 0, 3).reshape(NC_, 128, 4 * 128))
            wg = inp["w_gla_proj"][l]
            m[f"wgp_{l}"] = np.ascontiguousarray(wg.reshape(4, 128, NC_, 128).transpose(2, 1, 0, 3).reshape(NC_, 128, 4 * 128))
            wo_ = inp["w_out"][l]
            m[f"wo_{l}"] = np.ascontiguousarray(wo_.reshape(NC_, 128, NC_, 128).transpose(2, 1, 0, 3).reshape(NC_, 128, NC_ * 128))
    return m


def run_phases(phases, x, inp, n_cores=8, trace=False, debug=None):
    nc = build(phases, debug)
    shared = make_inputs(phases, inp)
    in_maps = []
    for b in range(n_cores):
        d = dict(shared)
        d["xT"] = np.ascontiguousarray(x[b].T)
        in_maps.append(d)
    res = run_bass_kernel_spmd(nc, in_maps, core_ids=list(range(n_cores)), trace=trace)
    out = np.stack([np.ascontiguousarray(r["yT"].T) for r in res.results], axis=0)
    return out, res


LAUNCHES = [[("ffn", 0, 0), ("mix", 0), ("ffn", 0, 1), ("ffn", 1, 0), ("mix", 1), ("ffn", 1, 1)]]


def kernel(**inputs):
    inp = {k: np.asarray(v) for k, v in inputs.items()}
    x = np.ascontiguousarray(inp["x"], dtype=np.float32)
    for phases in LAUNCHES:
        x, _ = run_phases(phases, x, inp)
    return np.ascontiguousarray(x, dtype=np.float32)
```

```python
import numpy as np
import concourse.bass as bass
import concourse.mybir as mybir
from concourse.bass_utils import run_bass_kernel_spmd

F32 = mybir.dt.float32
F32R = mybir.dt.float32r

BF16 = mybir.dt.bfloat16
AF = mybir.ActivationFunctionType
ALU = mybir.AluOpType

S = 2048
D = 1024
DFF = 2816
NC_ = 8
NTB = 4
TB = 512
NF = 22
DEPTH = 2
ALPHA = float((2 * DEPTH) ** 0.25)
LN_EPS = 1e-5
FFN_GROUPS = [3, 3, 4, 4, 4, 4]
GMAX = 4
NL3 = DEPTH * 3 * NC_
SB_BASE = 16512
SB_TOP = 229344

O_AQ, O_AK, O_AV = 0, 512, 1024
O_GQ, O_GK, O_GV, O_GLR, O_GR = 1536, 1792, 2048, 2560, 2576
O_GA, O_GB = 3088, 4112
N_IN = 5136


class Buf:
    __slots__ = ("name", "w", "r", "excl")

    def __init__(self, name="", excl=False):
        self.name = name
        self.w = None
        self.r = {}
        self.excl = excl


def bufs(n):
    return [Buf() for _ in range(n)]


class Op:
    __slots__ = ("eng", "fn", "deps", "needed", "semval", "dsem", "dval")

    def __init__(self, eng, fn, deps, dsem):
        self.eng = eng
        self.fn = fn
        self.deps = deps
        self.needed = False
        self.semval = None
        self.dsem = dsem
        self.dval = None


class Sched:
    ENGS = ("pe", "act", "dve", "pool", "sp")

    def __init__(self):
        self.q = {e: [] for e in self.ENGS}
        self.dma_count = {}
        self.last_dma = {}
        self.extra = {e: [] for e in self.ENGS}

    def op(self, eng, fn, R=(), W=(), dsem=None, after=()):
        deps = [(3, a) for a in after if a is not None]
        if any(b.excl for b in R):
            W = list(W) + [b for b in R if b.excl]
            R = [b for b in R if not b.excl]
        W = list(dict.fromkeys(W))
        for b in R:
            if b.w is not None:
                deps.append((0, b.w))
        for b in W:
            if b.w is not None:
                deps.append((1, b.w))
            for r in b.r.values():
                deps.append((2, r))
        if self.extra[eng]:
            deps.extend((0, d) for d in self.extra[eng])
            self.extra[eng] = []
        o = Op(eng, fn, deps, dsem)
        if dsem is not None:
            self.dma_count[dsem] = self.dma_count.get(dsem, 0) + 16
            o.dval = self.dma_count[dsem]
            self.last_dma[dsem] = o
        self.q[eng].append(o)
        key = dsem if dsem is not None else eng
        for b in R:
            b.r[key] = o
        for b in W:
            b.w = o
            b.r = {}
        return o

    def fence(self):
        snap = []
        for e in self.ENGS:
            for o in reversed(self.q[e]):
                if o.dsem is None:
                    snap.append(o)
                    break
        snap.extend(self.last_dma.values())
        for e in self.ENGS:
            self.extra[e] = list(snap)

    def finalize(self):
        for eng in self.ENGS:
            for o in self.q[eng]:
                keep = []
                for kind, d in o.deps:
                    if d is o:
                        continue
                    if d.dsem is not None:
                        keep.append(d)
                    elif d.eng == o.eng:
                        if o.eng == "pe" and kind != 3:
                            continue
                        keep.append(d)
                    else:
                        keep.append(d)
                for d in keep:
                    if d.dsem is None:
                        d.needed = True
                o.deps = keep
        for eng in self.ENGS:
            c = 0
            for o in self.q[eng]:
                if o.dsem is None and o.needed:
                    c += 1
                    o.semval = c

    def replay(self, eng, e, esem, dsems):
        seen = {}
        for o in self.q[eng]:
            for d in o.deps:
                if d.dsem is not None:
                    key, val, sem = ("d", d.dsem), d.dval, dsems[d.dsem]
                else:
                    key, val, sem = ("e", d.eng), d.semval, esem[d.eng]
                if seen.get(key, 0) >= val:
                    continue
                seen[key] = val
                e.wait_ge(sem, val)
            ins = o.fn(e)
            if o.dsem is not None:
                ins.then_inc(dsems[o.dsem], 16)
            elif o.needed:
                ins.then_inc(esem[eng], 1)


DT_SIZE = {F32: 4, BF16: 2, F32R: 4}


class Arena:
    UID = 0

    def __init__(self, nc, base, top):
        self.nc = nc
        self.base = base
        self.top = top
        self.off = base
        self.uid = 0
        self.peak = base

    def alloc(self, shape, dt, name="t"):
        n = 1
        for d in shape[1:]:
            n *= d
        nbytes = (n * DT_SIZE[dt] + 63) // 64 * 64
        if self.off + nbytes > self.top:
            raise RuntimeError(f"arena overflow allocating {name} {shape}: off={self.off - self.base} need {nbytes} cap {self.top - self.base}")
        Arena.UID += 1
        t = self.nc.alloc_sbuf_tensor_at(f"{name}_{Arena.UID}", list(shape), dt, offset=self.off)
        self.off += nbytes
        self.peak = max(self.peak, self.off)
        return t

    def mark(self):
        return self.off

    def release(self, m):
        self.off = m


def build(phases, debug=None):
    nc = bass.Bass("TRN2", target_bir_lowering=False)
    s = Sched()
    dram = {}
    dsem_names = []

    def new_dsem(name):
        nm = f"{name}_{len(dsem_names)}"
        dsem_names.append(nm)
        return nm

    def din(name, shape, dt=F32):
        if name not in dram:
            dram[name] = nc.dram_tensor(name, list(shape), dt, kind="ExternalInput").ap()
        return dram[name]

    debug = debug or ()

    def tap(name, t, bl):
        if name in debug:
            dd = nc.dram_tensor("dbg_" + name, list(t.shape), t.dtype, kind="ExternalOutput").ap()
            s.op("sp", lambda e: e.dma_start(out=dd, in_=t[:]), R=bl, dsem=new_dsem("dbg"))

    xT_d = din("xT", [D, S])
    yT_d = nc.dram_tensor("yT", [D, S], F32, kind="ExternalOutput").ap()
    lng_d = din("ln_g", [128, NL3])
    lnb_d = din("ln_b", [128, NL3])
    consts_d = din("consts", [128, 5 * 128])
    has_mix = any(p[0] == "mix" for p in phases)

    A = Arena(nc, SB_BASE, SB_TOP)
    xs = A.alloc([128, NC_, S], F32, "xs")
    xb = A.alloc([128, NC_, S], BF16, "xb")
    xs_b = [bufs(NTB) for _ in range(NC_)]
    xb_b = [bufs(NTB) for _ in range(NC_)]
    lng = A.alloc([128, NL3], F32, "lng")
    lnb = A.alloc([128, NL3], F32, "lnb")
    lnga = A.alloc([128, NL3], F32, "lnga")
    lnba = A.alloc([128, NL3], F32, "lnba")
    ln_c = Buf()
    consts = A.alloc([128, 5 * 128], F32, "consts")
    consts_b = Buf()
    Umat = consts[:, 0:128]
    Lmat = consts[:, 128:256]
    Mblk = consts[:, 256:384]
    ones = consts[:, 384:512]
    gones = consts[:, 512:640]
    ones1 = A.alloc([128, 64], F32, "ones1")
    ones1_b = Buf()
    ones_r = A.alloc([128, 128], F32R, "ones_r")
    gones_r = A.alloc([128, 128], F32R, "gones_r")
    onesr_b = Buf()
    mixc_b = Buf()
    if has_mix:
        wgu = A.alloc([16, DEPTH, 256], F32, "wgu")
        bgu = A.alloc([128, DEPTH, 256], F32, "bgu")
        gng = A.alloc([128, DEPTH * 4], F32, "gng")
        gnb = A.alloc([128, DEPTH * 4], F32, "gnb")
    arena_base = A.mark()

    psum = [nc.alloc_psum_tensor(f"bank{i}", [128, TB], F32) for i in range(8)]
    pq = [[Buf(f"bank{i}", excl=True)] * 4 for i in range(8)]

    def pb_(i, c0=0, c1=TB):
        return pq[i][c0 // 128:(c1 + 127) // 128]

    out_sem = new_dsem("out")

    lng_b0, lnb_b0 = Buf(), Buf()
    s.op("sp", lambda e: e.dma_start(out=lng[:], in_=lng_d), W=[lng_b0], dsem=new_dsem("io"))
    s.op("sp", lambda e: e.dma_start(out=lnb[:], in_=lnb_d), W=[lnb_b0], dsem=new_dsem("io"))
    s.op("sp", lambda e: e.dma_start(out=consts[:], in_=consts_d), W=[consts_b], dsem=new_dsem("io"))
    if has_mix:
        wgu_d = din("wgu", [16, DEPTH, 256])
        bgu_d = din("bgu", [128, DEPTH, 256])
        gng_d = din("gng", [128, DEPTH * 4])
        gnb_d = din("gnb", [128, DEPTH * 4])
        mb = bufs(4)
        s.op("sp", lambda e: e.dma_start(out=wgu[:], in_=wgu_d), W=[mb[0]], dsem=new_dsem("io"))
        s.op("sp", lambda e: e.dma_start(out=bgu[:], in_=bgu_d), W=[mb[1]], dsem=new_dsem("io"))
        s.op("sp", lambda e: e.dma_start(out=gng[:], in_=gng_d), W=[mb[2]], dsem=new_dsem("io"))
        s.op("sp", lambda e: e.dma_start(out=gnb[:], in_=gnb_d), W=[mb[3]], dsem=new_dsem("io"))
        s.op("dve", lambda e: e.memset(ones1[:], 1.0), R=mb, W=[ones1_b, mixc_b])
    xT_v = xT_d.rearrange("(c p) t -> p c t", p=128)
    for t in range(NTB):
        s.op("sp", lambda e, t=t: e.dma_start(out=xs[:, :, t * TB:(t + 1) * TB], in_=xT_v[:, :, t * TB:(t + 1) * TB]),
             W=[xs_b[c][t] for c in range(NC_)], dsem=new_dsem("iox"))
    s.op("act", lambda e: e.copy(out=ones_r[:], in_=ones), R=[consts_b], W=[onesr_b])
    s.op("act", lambda e: e.copy(out=gones_r[:], in_=gones), R=[consts_b], W=[onesr_b])
    s.op("act", lambda e: e.mul(lnga[:], lng[:], ALPHA), R=[lng_b0], W=[ln_c])
    s.op("act", lambda e: e.mul(lnba[:], lnb[:], ALPHA), R=[lnb_b0], W=[ln_c])
    for t in range(NTB):
        for c in range(NC_):
            sl = slice(t * TB, (t + 1) * TB)
            s.op("dve", lambda e, c=c, sl=sl: e.tensor_copy(out=xb[:, c, sl], in_=xs[:, c, sl]),
                 R=[xs_b[c][t]], W=[xb_b[c][t]])
            s.op("act", lambda e, c=c, sl=sl: e.mul(xs[:, c, sl], xs[:, c, sl], ALPHA),
                 R=[xs_b[c][t]], W=[xs_b[c][t]])

    def layer_norm(l, i):
        col0 = (l * 3 + i) * NC_
        s.fence()
        A.release(arena_base)
        sq = A.alloc([128, NC_, TB], F32, "sq")
        sq_b = bufs(NC_)
        mean_sb = [A.alloc([128, TB], F32, "mean") for _ in range(2)]
        m2_sb = [A.alloc([128, TB], F32, "m2") for _ in range(2)]
        rstd_sb = [A.alloc([128, TB], F32, "rstd") for _ in range(2)]
        mean_b, m2_b, rstd_b = bufs(2), bufs(2), bufs(2)
        t1 = [A.alloc([128, TB], F32, "t1") for _ in range(2)]
        t2 = [A.alloc([128, TB], F32, "t2") for _ in range(3)]
        t1_b, t2_b = bufs(2), bufs(3)
        cn = {"t1": 0, "t2": 0}

        def stats(t):
            sl = slice(t * TB, (t + 1) * TB)
            p = t % 2
            bm, bq = (6, 7) if p == 0 else (4, 5)
            pm, pq_ = psum[bm], psum[bq]
            for c in range(NC_):
                s.op("act", lambda e, c=c, sl=sl: e.activation(out=sq[:, c, :], in_=xs[:, c, sl], func=AF.Square),
                     R=[xs_b[c][t]], W=[sq_b[c]])
            for c in range(NC_):
                s.op("pe", lambda e, c=c, sl=sl, pm=pm: e.matmul(pm[:], lhsT=ones, rhs=xs[:, c, sl],
                                                                 start=(c == 0), stop=(c == NC_ - 1)),
                     R=[consts_b, xs_b[c][t]], W=pb_(bm))
            for c in range(NC_):
                s.op("pe", lambda e, c=c, pq_=pq_: e.matmul(pq_[:], lhsT=ones, rhs=sq[:, c, :],
                                                            start=(c == 0), stop=(c == NC_ - 1)),
                     R=[consts_b, sq_b[c]], W=pb_(bq))
            s.op("act", lambda e, pm=pm, p=p: e.activation(out=m2_sb[p][:], in_=pm[:], func=AF.Square), R=pb_(bm), W=[m2_b[p]])
            s.op("act", lambda e, pm=pm, p=p: e.copy(out=mean_sb[p][:], in_=pm[:]), R=pb_(bm), W=[mean_b[p]])
            s.op("dve", lambda e, pq_=pq_, p=p: e.tensor_tensor(out=rstd_sb[p][:], in0=pq_[:], in1=m2_sb[p][:], op=ALU.subtract),
                 R=pb_(bq) + [m2_b[p]], W=[rstd_b[p]])
            s.op("act", lambda e, p=p: e.activation(out=m2_sb[p][:], in_=rstd_sb[p][:], func=AF.Ln, bias=LN_EPS, scale=1.0),
                 R=[rstd_b[p]], W=[m2_b[p]])
            s.op("act", lambda e, p=p: e.activation(out=rstd_sb[p][:], in_=m2_sb[p][:], func=AF.Exp, scale=-0.5),
                 R=[m2_b[p]], W=[rstd_b[p]])

        def norm(t):
            sl = slice(t * TB, (t + 1) * TB)
            p = t % 2
            for c in range(NC_):
                j = cn["t1"] % 2
                cn["t1"] += 1
                j2 = cn["t2"] % 3
                cn["t2"] += 1
                s.op("dve", lambda e, c=c, sl=sl, j=j, p=p: e.tensor_tensor(out=t1[j][:], in0=xs[:, c, sl], in1=mean_sb[p][:], op=ALU.subtract),
                     R=[xs_b[c][t], mean_b[p]], W=[t1_b[j]])
                s.op("pool", lambda e, j=j, j2=j2, p=p: e.tensor_tensor(out=t2[j2][:], in0=t1[j][:], in1=rstd_sb[p][:], op=ALU.mult),
                     R=[t1_b[j], rstd_b[p]], W=[t2_b[j2]])
                s.op("act", lambda e, c=c, sl=sl, j2=j2: e.activation(out=xs[:, c, sl], in_=t2[j2][:], func=AF.Identity,
                                                                   scale=lnga[:, col0 + c:col0 + c + 1],
                                                                   bias=lnba[:, col0 + c:col0 + c + 1]),
                     R=[t2_b[j2], ln_c], W=[xs_b[c][t]])
                s.op("dve", lambda e, c=c, sl=sl, j2=j2: e.tensor_scalar(
                    out=xb[:, c, sl], in0=t2[j2][:], scalar1=lng[:, col0 + c:col0 + c + 1], scalar2=lnb[:, col0 + c:col0 + c + 1],
                    op0=ALU.mult, op1=ALU.add),
                    R=[t2_b[j2], ln_c], W=[xb_b[c][t]])

        stats(0)
        for t in range(NTB):
            if t + 1 < NTB:
                stats(t + 1)
            norm(t)

    def make_ln(l, i, arenas, sbanks=((0, 1), (2, 3)), lag=2):
        col0 = (l * 3 + i) * NC_
        final = (l, i) == final_ln
        g_xs, b_xs = (lng, lnb) if final else (lnga, lnba)
        yT_v = yT_d.rearrange("(c p) t -> p c t", p=128)

        def al(shape, dt, name):
            for a in arenas:
                n = 1
                for d in shape[1:]:
                    n *= d
                if a.off + (n * DT_SIZE[dt] + 63) // 64 * 64 <= a.top:
                    return a.alloc(shape, dt, name)
            raise RuntimeError("make_ln: no room for " + name)

        NSQ = 3
        sqr = [al([128, TB], F32R, "lsq") for _ in range(NSQ)]
        sqr_b = bufs(NSQ)
        mean_sb = [al([128, TB], F32, "lmean") for _ in range(2)]
        m2_sb = [al([128, TB], F32, "lm2") for _ in range(2)]
        rstd_sb = [al([128, TB], F32, "lrstd") for _ in range(2)]
        mean_b, m2_b, rstd_b = bufs(2), bufs(2), bufs(2)
        NT = 3
        t1 = [al([128, TB], F32, "lt1") for _ in range(NT)]
        t2 = [al([128, TB], F32, "lt2") for _ in range(NT)]
        t1_b, t2_b = bufs(NT), bufs(NT)
        cn = {"sq": 0, "t1": 0, "t2": 0, "seen": {}}
        pending = []
        avail = []
        fl = {"s1": None, "s2": None}

        def tick():
            if fl["s2"] is not None:
                c, t, j2 = fl["s2"]
                sl = slice(t * TB, (t + 1) * TB)
                s.op("act", lambda e, c=c, sl=sl, j2=j2: e.activation(out=xs[:, c, sl], in_=t2[j2][:], func=AF.Identity,
                                                                   scale=g_xs[:, col0 + c:col0 + c + 1],
                                                                   bias=b_xs[:, col0 + c:col0 + c + 1]),
                     R=[t2_b[j2], ln_c], W=[xs_b[c][t]])
                if not final and c % 3 == 2:
                    s.op("act", lambda e, c=c, sl=sl, j2=j2: e.activation(out=xb[:, c, sl], in_=t2[j2][:], func=AF.Identity,
                                                                       scale=lng[:, col0 + c:col0 + c + 1],
                                                                       bias=lnb[:, col0 + c:col0 + c + 1]),
                         R=[t2_b[j2], ln_c], W=[xb_b[c][t]])
                elif not final:
                    s.op("dve", lambda e, c=c, sl=sl, j2=j2: e.tensor_scalar(
                        out=xb[:, c, sl], in0=t2[j2][:], scalar1=lng[:, col0 + c:col0 + c + 1], scalar2=lnb[:, col0 + c:col0 + c + 1],
                        op0=ALU.mult, op1=ALU.add),
                        R=[t2_b[j2], ln_c], W=[xb_b[c][t]])
                else:
                    ndone = cn.get(("done", t), 0) + 1
                    cn[("done", t)] = ndone
                    if ndone == NC_:
                        s.op("sp", lambda e, sl=sl: e.dma_start(out=yT_v[:, :, sl], in_=xs[:, :, sl]),
                             R=[xs_b[cc][t] for cc in range(NC_)], dsem=out_sem)
                        out_ops.append(s.q["sp"][-1])
                fl["s2"] = None
            if fl["s1"] is not None:
                c, t, j = fl["s1"]
                p = t % 2
                j2 = cn["t2"] % NT
                cn["t2"] += 1
                s.op("pool", lambda e, j=j, j2=j2, p=p: e.tensor_tensor(out=t2[j2][:], in0=t1[j][:], in1=rstd_sb[p][:], op=ALU.mult),
                     R=[t1_b[j], rstd_b[p]], W=[t2_b[j2]])
                fl["s2"] = (c, t, j2)
                fl["s1"] = None
            if avail:
                c, t = avail.pop(0)
                sl = slice(t * TB, (t + 1) * TB)
                p = t % 2
                j = cn["t1"] % NT
                cn["t1"] += 1
                s.op("dve", lambda e, c=c, sl=sl, j=j, p=p: e.tensor_tensor(out=t1[j][:], in0=xs[:, c, sl], in1=mean_sb[p][:], op=ALU.subtract),
                     R=[xs_b[c][t], mean_b[p]], W=[t1_b[j]])
                fl["s1"] = (c, t, j)

        def emit(entry):
            dc, t, k = entry
            sl = slice(t * TB, (t + 1) * TB)
            p = t % 2
            bm, bq = sbanks[p]
            n = cn["seen"].get(t, 0)
            cn["seen"][t] = n + 1
            s.op("pe", lambda e, dc=dc, sl=sl, bm=bm, n=n: e.matmul(psum[bm][:], lhsT=ones, rhs=xs[:, dc, sl],
                                                                   start=(n == 0), stop=(n == NC_ - 1)),
                 R=[consts_b, xs_b[dc][t]], W=pb_(bm))
            s.op("pe", lambda e, k=k, bq=bq, n=n: e.matmul(psum[bq][:], lhsT=ones_r[:], rhs=sqr[k][:],
                                                           start=(n == 0), stop=(n == NC_ - 1)),
                 R=[onesr_b, sqr_b[k]], W=pb_(bq))
            if n == NC_ - 1:
                s.op("act", lambda e, bm=bm, p=p: e.activation(out=m2_sb[p][:], in_=psum[bm][:], func=AF.Square), R=pb_(bm), W=[m2_b[p]])
                s.op("act", lambda e, bm=bm, p=p: e.copy(out=mean_sb[p][:], in_=psum[bm][:]), R=pb_(bm), W=[mean_b[p]])
                s.op("dve", lambda e, bq=bq, p=p: e.tensor_tensor(out=rstd_sb[p][:], in0=psum[bq][:], in1=m2_sb[p][:], op=ALU.subtract),
                     R=pb_(bq) + [m2_b[p]], W=[rstd_b[p]])
                s.op("act", lambda e, p=p: e.activation(out=m2_sb[p][:], in_=rstd_sb[p][:], func=AF.Ln, bias=LN_EPS, scale=1.0),
                     R=[rstd_b[p]], W=[m2_b[p]])
                s.op("act", lambda e, p=p: e.activation(out=rstd_sb[p][:], in_=m2_sb[p][:], func=AF.Exp, scale=-0.5),
                     R=[m2_b[p]], W=[rstd_b[p]])
                avail.extend((c, t) for c in range(NC_))

        def chunk_done(dc, t):
            sl = slice(t * TB, (t + 1) * TB)
            k = cn["sq"] % NSQ
            cn["sq"] += 1
            s.op("act", lambda e, dc=dc, sl=sl, k=k: e.activation(out=sqr[k][:], in_=xs[:, dc, sl], func=AF.Square),
                 R=[xs_b[dc][t]], W=[sqr_b[k]])
            pending.append((dc, t, k))
            if len(pending) > lag:
                emit(pending.pop(0))
            tick()

        def flush():
            while pending:
                emit(pending.pop(0))
                tick()
            while avail or fl["s1"] is not None or fl["s2"] is not None:
                tick()

        return chunk_done, flush

    hook = {}
    nxt = {"ph": None}

    def at(name, shape, dt, off):
        Arena.UID += 1
        return nc.alloc_sbuf_tensor_at(f"{name}_{Arena.UID}", list(shape), dt, offset=off)

    def emit_prefetch(cur_kind, e_off=None):
        ph = nxt["ph"]
        if ph is None:
            return
        if ph[0] == "ffn":
            l2, i2 = ph[1], ph[2]
            w1n = din(f"f{i2}w1_{l2}", [NF, 128, D])
            w3n = din(f"f{i2}w3_{l2}", [NF, 128, D])
            off = (SB_TOP - 4096) if cur_kind == "ffn" else e_off
            tpre = at("w13pre", [128, 2, D], BF16, off)
            bb = bufs(2)
            s.op("pool", lambda e: e.dma_start(out=tpre[:, 0, :], in_=w1n[0]), W=[bb[0]], dsem=new_dsem("w13p"))
            s.op("pool", lambda e: e.dma_start(out=tpre[:, 1, :], in_=w3n[0]), W=[bb[1]], dsem=new_dsem("w13p"))
            hook["w13pre"] = (tpre, bb)
        elif ph[0] == "mix" and cur_kind == "ffn":
            l2 = ph[1]
            wglr_n = din(f"wglr_{l2}", [128, NC_ * 16])
            wgkv_n = din(f"wgkv_{l2}", [128, NC_ * 768])
            off = SB_TOP - 8704
            wglr_p = at("wglrp", [128, NC_, 16], BF16, off)
            wgv_p = at("wgvp", [128, NC_, 512], BF16, off + 256)
            bb = bufs(2)
            s.op("pool", lambda e: e.dma_start(out=wglr_p[:], in_=wglr_n.rearrange("p (k n) -> p k n", k=NC_)),
                 W=[bb[0]], dsem=new_dsem("wBp"))
            s.op("pool", lambda e: e.dma_start(out=wgv_p[:], in_=wgkv_n.rearrange("p (k n) -> p k n", k=NC_)[:, :, 256:768]),
                 W=[bb[1]], dsem=new_dsem("wBp"))
            hook["Bpre"] = (wglr_p, wgv_p, bb)

    def ffn(l, i):
        w1_d = din(f"f{i}w1_{l}", [NF, 128, D])
        w3_d = din(f"f{i}w3_{l}", [NF, 128, D])
        w2_d = din(f"f{i}w2_{l}", [DFF, D])
        s.fence()
        A.release(arena_base)
        W13_SLOTS = 3
        w13 = [A.alloc([128, 2, D], BF16, "w13") for _ in range(W13_SLOTS)]
        w13_b = [bufs(2) for _ in range(W13_SLOTS)]
        w13_sem = [[new_dsem("w13") for _ in range(2)] for _ in range(W13_SLOTS)]
        w2 = [A.alloc([128, GMAX, D], BF16, "w2") for _ in range(2)]
        w2_b = bufs(2)
        w2_sem = [new_dsem("w2") for _ in range(2)]
        gT = [A.alloc([128, GMAX, S], BF16, "gT") for _ in range(2)]
        gT_b = [[bufs(NTB) for _ in range(GMAX)] for _ in range(2)]
        silu_t = [A.alloc([128, TB], F32, "silu") for _ in range(2)]
        silu_b = bufs(2)
        ln_chunk, ln_flush = make_ln(l, 0 if i == 0 else 2, [A])
        pre13 = hook.pop("w13pre", None)
        cnt = {"w13": 0, "w2": 0, "psA": 0, "psY": 0, "silu": 0}
        m0 = 0
        for gi, G in enumerate(FFN_GROUPS):
            ms = list(range(m0, m0 + G))
            m0 += G
            gs = gi % 2
            ws = cnt["w2"] % 2
            cnt["w2"] += 1
            s.op("pool", lambda e, ws=ws, ms=ms, G=G: e.dma_start(
                out=w2[ws][:, 0:G, :],
                in_=w2_d[ms[0] * 128:(ms[0] + G) * 128, :].rearrange("(g p) n -> p g n", p=128)),
                W=[w2_b[ws]], dsem=w2_sem[ws])
            for ml, m in enumerate(ms):
                if m == 0 and pre13 is not None:
                    wt, wtb = pre13
                else:
                    slot = cnt["w13"] % W13_SLOTS
                    cnt["w13"] += 1
                    wt, wtb = w13[slot], w13_b[slot]
                    s.op("pool", lambda e, slot=slot, m=m: e.dma_start(out=w13[slot][:, 0, :], in_=w1_d[m]),
                         W=[w13_b[slot][0]], dsem=w13_sem[slot][0])
                    s.op("pool", lambda e, slot=slot, m=m: e.dma_start(out=w13[slot][:, 1, :], in_=w3_d[m]),
                         W=[w13_b[slot][1]], dsem=w13_sem[slot][1])
                for t in range(NTB):
                    sl = slice(t * TB, (t + 1) * TB)
                    pj = cnt["psA"] % 2
                    cnt["psA"] += 1
                    for which, bi in ((0, pj), (1, 2 + pj)):
                        for k in range(NC_):
                            s.op("pe", lambda e, wt=wt, which=which, k=k, sl=sl, bi=bi: e.matmul(
                                psum[bi][:], lhsT=wt[:, which, k * 128:(k + 1) * 128], rhs=xb[:, k, sl],
                                start=(k == 0), stop=(k == NC_ - 1)),
                                R=[wtb[which], xb_b[k][t]], W=pb_(bi))
                    sj = cnt["silu"] % 2
                    cnt["silu"] += 1
                    s.op("act", lambda e, pj=pj, sj=sj: e.activation(out=silu_t[sj][:], in_=psum[pj][:], func=AF.Silu),
                         R=pb_(pj), W=[silu_b[sj]])
                    s.op("dve", lambda e, pj=pj, sj=sj, gs=gs, ml=ml, sl=sl: e.tensor_tensor(
                        out=gT[gs][:, ml, sl], in0=psum[2 + pj][:], in1=silu_t[sj][:], op=ALU.mult),
                        R=pb_(2 + pj) + [silu_b[sj]], W=[gT_b[gs][ml][t]])
            last = (gi == len(FFN_GROUPS) - 1)
            order = [(dc, t) for t in range(NTB) for dc in range(NC_)] if last else [(dc, t) for dc in range(NC_) for t in range(NTB)]
            for dc, t in order:
                if True:
                    sl = slice(t * TB, (t + 1) * TB)
                    bi = 4 + cnt["psY"] % 2
                    cnt["psY"] += 1
                    for ml in range(G):
                        s.op("pe", lambda e, ws=ws, ml=ml, dc=dc, gs=gs, sl=sl, bi=bi, G=G: e.matmul(
                            psum[bi][:], lhsT=w2[ws][:, ml, dc * 128:(dc + 1) * 128], rhs=gT[gs][:, ml, sl],
                            start=(ml == 0), stop=(ml == G - 1)),
                            R=[w2_b[ws], gT_b[gs][ml][t]], W=pb_(bi))
                    s.op("dve", lambda e, bi=bi, dc=dc, sl=sl: e.scalar_tensor_tensor(
                        out=xs[:, dc, sl], in0=psum[bi][:], scalar=0.5, in1=xs[:, dc, sl],
                        op0=ALU.mult, op1=ALU.add),
                        R=pb_(bi) + [xs_b[dc][t]], W=[xs_b[dc][t]])
                    if last:
                        ln_chunk(dc, t)
        emit_prefetch("ffn")
        ln_flush()

    def mixer(l, stop=None):
        amask_d = din("amask", [8, 128, S])
        wglr_d = din(f"wglr_{l}", [128, NC_ * 16])
        wgqk_d = din(f"wgqk_{l}", [128, NC_ * 512])
        wgkv_d = din(f"wgkv_{l}", [128, NC_ * 768])
        wgr_d = din(f"wgr_{l}", [128, NC_ * 512])
        waqkv_d = din(f"waqkv_{l}", [4, 128, NC_ * 384])
        wgab_d = din(f"wgab_{l}", [NC_, 128, NC_ * 256])
        wap_d = din(f"wap_{l}", [NC_, 128, 4 * 128])
        wgp_d = din(f"wgp_{l}", [NC_, 128, 4 * 128])
        wo_d = din(f"wo_{l}", [NC_, 128, NC_ * 128])
        blk = amask_blocks
        preB = hook.pop("Bpre", None)
        s.fence()
        A.release(arena_base)
        o_gnT = A.alloc([128, 4, S], BF16, "o_gnT")
        o_gn_b = [bufs(NTB) for _ in range(4)]
        o_a_b = [bufs(NTB) for _ in range(8)]
        mix_base = A.mark()

        qT = A.alloc([128, 2, S], BF16, "qT")
        kT = A.alloc([128, 2, S], BF16, "kT")
        qT_b = [bufs(16) for _ in range(2)]
        kT_b = [bufs(16) for _ in range(2)]
        khat = A.alloc([128, 16, 256], BF16, "khat")
        khat_b = bufs(16)
        gv = A.alloc([128, 16, 512], BF16, "gv")
        gv_b = bufs(16)
        dec = A.alloc([128, 4, 32], F32, "dec")
        dec_b = bufs(NTB)
        gla_base = A.mark()
        wB_b = bufs(4)
        wgkv_v = wgkv_d.rearrange("p (k n) -> p k n", k=NC_)
        if preB is not None:
            wglr, wgv, pb2 = preB
            wB_b[0], wB_b[3] = pb2[0], pb2[1]
        else:
            wglr = A.alloc([128, NC_, 16], BF16, "wglr")
            wgv = A.alloc([128, NC_, 512], BF16, "wgv")
            s.op("pool", lambda e: e.dma_start(out=wglr[:], in_=wglr_d.rearrange("p (k n) -> p k n", k=NC_)),
                 W=[wB_b[0]], dsem=new_dsem("wB"))
            s.op("pool", lambda e: e.dma_start(out=wgv[:], in_=wgkv_v[:, :, 256:768]),
                 W=[wB_b[3]], dsem=new_dsem("wB"))
        wgqk = A.alloc([128, NC_, 512], BF16, "wgqk")
        wgk = A.alloc([128, NC_, 256], BF16, "wgk")
        s.op("pool", lambda e: e.dma_start(out=wgqk[:], in_=wgqk_d.rearrange("p (k n) -> p k n", k=NC_)),
             W=[wB_b[1]], dsem=new_dsem("wB"))
        s.op("pool", lambda e: e.dma_start(out=wgk[:], in_=wgkv_v[:, :, 0:256]),
             W=[wB_b[2]], dsem=new_dsem("wB"))
        glrT = [A.alloc([16, TB], F32, "glrT") for _ in range(2)]
        glrT_b = bufs(2)
        z_sb = [A.alloc([128, 256], F32, "z") for _ in range(2)]
        z_b = bufs(2)
        la_sb = [A.alloc([128, 256], F32, "la") for _ in range(2)]
        la_b = bufs(2)
        Eb = A.alloc([128, 2, TB], F32, "Eb")
        Einv = A.alloc([128, 2, TB], F32, "Einv")
        E_b, Einv_b = bufs(2), bufs(2)
        Ft = A.alloc([128, 4, 256], F32, "Ft")
        F_b = bufs(4)
        zc = 0
        pc = 0
        for tb in range(NTB):
            sl = slice(tb * TB, (tb + 1) * TB)
            gj = tb % 2
            for k in range(NC_):
                s.op("pe", lambda e, k=k, sl=sl: e.matmul(psum[0][0:16, :], lhsT=wglr[:, k, :], rhs=xb[:, k, sl],
                                                         start=(k == 0), stop=(k == NC_ - 1)),
                     R=[wB_b[0], xb_b[k][tb]], W=pb_(0))
            s.op("act", lambda e, gj=gj: e.copy(out=glrT[gj][:], in_=psum[0][0:16, :]), R=pb_(0), W=[glrT_b[gj]])
            for jj in range(4):
                j = tb * 4 + jj
                zi = zc % 2
                zc += 1
                bz = 1 + zi
                s.op("pe", lambda e, gj=gj, jj=jj, bz=bz: e.matmul(
                    psum[bz][:, 0:256], lhsT=glrT[gj][:, jj * 128:(jj + 1) * 128], rhs=wgu[:, l, :],
                    start=True, stop=True),
                    R=[glrT_b[gj], mixc_b], W=pb_(bz, 0, 256))
                s.op("dve", lambda e, zi=zi, bz=bz: e.tensor_tensor(out=z_sb[zi][:], in0=psum[bz][:, 0:256], in1=bgu[:, l, :], op=ALU.add),
                     R=pb_(bz, 0, 256) + [mixc_b], W=[z_b[zi]])
                tsl = slice(j * 128, (j + 1) * 128)
                bi2 = 6 + pc % 2
                pc += 1
                for k in range(NC_):
                    s.op("pe", lambda e, k=k, tsl=tsl, bi2=bi2: e.matmul(
                        psum[bi2][:], lhsT=xb[:, k, tsl], rhs=wgv[:, k, :],
                        start=(k == 0), stop=(k == NC_ - 1)),
                        R=[wB_b[3], xb_b[k][tb]], W=pb_(bi2))
                s.op("dve", lambda e, j=j, bi2=bi2: e.tensor_copy(out=gv[:, j, :], in_=psum[bi2][:]),
                     R=pb_(bi2), W=[gv_b[j]])
                s.op("act", lambda e, zi=zi: e.activation(out=z_sb[zi][:], in_=z_sb[zi][:], func=AF.Exp, scale=-1.0),
                     R=[z_b[zi]], W=[z_b[zi]])
                s.op("act", lambda e, zi=zi: e.activation(out=la_sb[zi][:], in_=z_sb[zi][:], func=AF.Ln, bias=1.0, scale=1.0),
                     R=[z_b[zi]], W=[la_b[zi]])
                for ch in range(2):
                    s.op("pe", lambda e, zi=zi, ch=ch, jj=jj: e.matmul(
                        psum[3 + ch][:, jj * 128:(jj + 1) * 128], lhsT=la_sb[zi][:, ch * 128:(ch + 1) * 128], rhs=Umat,
                        start=True, stop=True),
                        R=[la_b[zi], consts_b], W=[pq[3 + ch][jj]])
                s.op("pe", lambda e, zi=zi: e.matmul(psum[5][:, 0:256], lhsT=Lmat, rhs=la_sb[zi][:], start=True, stop=True),
                     R=[la_b[zi], consts_b], W=pb_(5, 0, 256))
                s.op("act", lambda e, jj=jj: e.activation(out=Ft[:, jj, :], in_=psum[5][:, 0:256], func=AF.Exp),
                     R=pb_(5, 0, 256), W=[F_b[jj]])
            for ch in range(2):
                s.op("act", lambda e, ch=ch: e.activation(out=Eb[:, ch, :], in_=psum[3 + ch][:], func=AF.Exp),
                     R=pb_(3 + ch), W=[E_b[ch]])
                s.op("act", lambda e, ch=ch: e.activation(out=Einv[:, ch, :], in_=psum[3 + ch][:], func=AF.Exp, scale=-1.0),
                     R=pb_(3 + ch), W=[Einv_b[ch]])
            for dup in range(2):
                s.op("dve", lambda e, tb=tb, dup=dup: e.tensor_copy(
                    out=dec[:].rearrange("p (c d) n -> p c d n", d=2)[:, :, dup, tb * 8:(tb + 1) * 8], in_=Eb[:, :, 63::64]),
                    R=E_b, W=[dec_b[tb]])
            for m in range(4):
                ch = m % 2
                bi = 6 + pc % 2
                pc += 1
                for k in range(NC_):
                    s.op("pe", lambda e, m=m, k=k, sl=sl, bi=bi: e.matmul(
                        psum[bi][:], lhsT=wgqk[:, k, m * 128:(m + 1) * 128], rhs=xb[:, k, sl],
                        start=(k == 0), stop=(k == NC_ - 1)),
                        R=[wB_b[1], xb_b[k][tb]], W=pb_(bi))
                if m < 2:
                    s.op("dve", lambda e, ch=ch, sl=sl, bi=bi: e.scalar_tensor_tensor(
                        out=qT[:, ch, sl], in0=psum[bi][:], scalar=0.125, in1=Eb[:, ch, :], op0=ALU.mult, op1=ALU.mult),
                        R=pb_(bi) + [E_b[ch]], W=qT_b[ch][tb * 4:(tb + 1) * 4])
                else:
                    s.op("dve", lambda e, ch=ch, sl=sl, bi=bi: e.tensor_tensor(
                        out=kT[:, ch, sl], in0=psum[bi][:], in1=Einv[:, ch, :], op=ALU.mult),
                        R=pb_(bi) + [Einv_b[ch]], W=kT_b[ch][tb * 4:(tb + 1) * 4])
            for jj in range(4):
                j = tb * 4 + jj
                tsl = slice(j * 128, (j + 1) * 128)
                bi = 1 + jj % 2
                for k in range(NC_):
                    s.op("pe", lambda e, k=k, tsl=tsl, bi=bi: e.matmul(
                        psum[bi][:, 0:256], lhsT=xb[:, k, tsl], rhs=wgk[:, k, :],
                        start=(k == 0), stop=(k == NC_ - 1)),
                        R=[wB_b[2], xb_b[k][tb]], W=pb_(bi, 0, 256))
                s.op("dve", lambda e, j=j, jj=jj, bi=bi: e.tensor_tensor(
                    out=khat[:, j, :], in0=psum[bi][:, 0:256], in1=Ft[:, jj, :], op=ALU.mult),
                    R=pb_(bi, 0, 256) + [F_b[jj]], W=[khat_b[j]])

        if stop == "B":
            return
        tap("qT", qT, [b for bb in qT_b for b in bb])
        tap("kT", kT, [b for bb in kT_b for b in bb])
        tap("khat", khat, khat_b)
        tap("gv", gv, gv_b)
        tap("dec", dec, dec_b)
        s.fence()
        A.release(gla_base)
        wgr = A.alloc([128, NC_, 512], BF16, "wgr")
        wgr_b = Buf()
        s.op("pool", lambda e: e.dma_start(out=wgr[:], in_=wgr_d.rearrange("p (k n) -> p k n", k=NC_)),
             W=[wgr_b], dsem=new_dsem("wgr"))
        ograw = [A.alloc([128, 4, TB], F32, "ograw") for _ in range(2)]
        ograw_b = [[bufs(4) for _ in range(4)] for _ in range(2)]
        st_f = A.alloc([128, 4, 128], F32, "st_f")
        st_b = A.alloc([128, 4, 128], BF16, "st_b")
        stf_b, stb_b = bufs(4), bufs(4)
        ATt = [A.alloc([128, 4, 128], BF16, "AT") for _ in range(2)]
        AT_b = [bufs(4) for _ in range(2)]
        sg = [A.alloc([128, TB], F32, "sg") for _ in range(4)]
        sg_b = bufs(4)
        gsq = A.alloc([128, TB], F32R, "gsq")
        gsq_b = Buf()
        gm2 = A.alloc([128, TB], F32, "gm2")
        grs = A.alloc([128, TB], F32, "grs")
        gm2_b, grs_b = Buf(), Buf()
        s.op("dve", lambda e: e.memset(st_f[:], 0.0), W=stf_b)
        s.op("dve", lambda e: e.memset(st_b[:], 0.0), W=stb_b)
        bS0, bS1 = 4, 5
        HORD = (0, 2, 1, 3)

        def emit_AT_mm(j):
            bA = j % 2
            tok = slice(j * 128, (j + 1) * 128)
            prev = None
            for h in HORD:
                ch, pb = h // 2, (h % 2) * 64
                hs = slice(h * 128, (h + 1) * 128)
                prev = s.op("pe", lambda e, ch=ch, pb=pb, hs=hs, tok=tok, bA=bA: e.matmul(
                    psum[bA][:, hs], lhsT=kT[pb:pb + 64, ch, tok], rhs=qT[pb:pb + 64, ch, tok], start=True, stop=True),
                    R=[kT_b[ch][j], qT_b[ch][j]], W=[pq[bA][h]], after=[prev] if h == 1 else [])

        def emit_AT_mask(j):
            bA = j % 2
            aj = j % 2
            s.op("dve", lambda e, bA=bA, aj=aj: e.tensor_tensor(
                out=ATt[aj][:], in0=psum[bA][:].rearrange("p (h n) -> p h n", h=4),
                in1=Mblk.unsqueeze(1).broadcast_to([128, 4, 128]), op=ALU.mult),
                R=[pq[bA][0], consts_b], W=AT_b[aj])

        def emit_dS(j, half):
            bS = bS0 if half == 0 else bS1
            rows = slice(half * 64, half * 64 + 64)
            for h in range(4):
                ch = h // 2
                hs = slice(h * 128, (h + 1) * 128)
                s.op("pe", lambda e, ch=ch, hs=hs, rows=rows, bS=bS, j=j: e.matmul(
                    psum[bS][:, hs], lhsT=khat[rows, j, ch * 128:(ch + 1) * 128], rhs=gv[rows, j, hs],
                    start=True, stop=True),
                    R=[khat_b[j], gv_b[j]], W=[pq[bS][h]])

        def emit_decay(c):
            s.op("dve", lambda e, c=c: e.tensor_tensor(
                out=st_f[:], in0=st_f[:], in1=dec[:, :, c:c + 1].broadcast_to([128, 4, 128]), op=ALU.mult),
                R=stf_b + [dec_b[c // 8]], W=stf_b)

        def emit_update(j, half):
            bS = bS0 if half == 0 else bS1
            c = 2 * j + half
            s.op("dve", lambda e, bS=bS: e.tensor_tensor(
                out=st_f[:], in0=st_f[:], in1=psum[bS][:].rearrange("p (h n) -> p h n", h=4), op=ALU.add),
                R=stf_b + [pq[bS][0]], W=stf_b)
            s.op("dve", lambda e: e.tensor_copy(out=st_b[:], in_=st_f[:]), R=stf_b, W=stb_b)
            if c + 1 < 32:
                emit_decay(c + 1)

        gtasks = []
        gstate = {"B": None}

        def gate_batch(tb):
            sl = slice(tb * TB, (tb + 1) * TB)
            for h in range(4):
                bi = 6 + h % 2
                for k in range(NC_):
                    s.op("pe", lambda e, k=k, h=h, sl=sl, bi=bi: e.matmul(
                        psum[bi][:], lhsT=wgr[:, k, h * 128:(h + 1) * 128], rhs=xb[:, k, sl],
                        start=(k == 0), stop=(k == NC_ - 1)),
                        R=[wgr_b, xb_b[k][tb]], W=pb_(bi))
                s.op("act", lambda e, h=h, bi=bi: e.activation(out=sg[h][:], in_=psum[bi][:], func=AF.Silu),
                     R=pb_(bi), W=[sg_b[h]])
            for h in range(4):
                gtasks.append((tb, h))

        def gn_A(tb, h):
            ob = tb % 2
            og = ograw[ob][:, h, :]
            ogb = ograw_b[ob][h]
            s.op("act", lambda e, og=og: e.activation(out=gsq[:], in_=og, func=AF.Square), R=ogb, W=[gsq_b])
            s.op("pe", lambda e, og=og: e.matmul(psum[6][:], lhsT=gones, rhs=og, start=True, stop=True),
                 R=ogb + [consts_b], W=pb_(6))
            s.op("pe", lambda e: e.matmul(psum[7][:], lhsT=gones_r[:], rhs=gsq[:], start=True, stop=True),
                 R=[gsq_b, onesr_b], W=pb_(7))
            s.op("act", lambda e: e.activation(out=gm2[:], in_=psum[6][:], func=AF.Square), R=pb_(6), W=[gm2_b])
            s.op("dve", lambda e, og=og: e.tensor_tensor(out=og, in0=og, in1=psum[6][:], op=ALU.subtract),
                 R=ogb + pb_(6), W=ogb)
            s.op("dve", lambda e: e.tensor_tensor(out=grs[:], in0=psum[7][:], in1=gm2[:], op=ALU.subtract),
                 R=pb_(7) + [gm2_b], W=[grs_b])
            s.op("act", lambda e: e.activation(out=gm2[:], in_=grs[:], func=AF.Ln, bias=LN_EPS, scale=1.0),
                 R=[grs_b], W=[gm2_b])
            s.op("act", lambda e: e.activation(out=grs[:], in_=gm2[:], func=AF.Exp, scale=-0.5), R=[gm2_b], W=[grs_b])

        def gn_B(tb, h):
            ob = tb % 2
            og = ograw[ob][:, h, :]
            ogb = ograw_b[ob][h]
            col = l * 4 + h
            sl = slice(tb * TB, (tb + 1) * TB)
            s.op("pool", lambda e, og=og: e.tensor_tensor(out=og, in0=og, in1=grs[:], op=ALU.mult),
                 R=ogb + [grs_b], W=ogb)
            s.op("act", lambda e, og=og, col=col: e.activation(out=og, in_=og, func=AF.Identity,
                                                             scale=gng[:, col:col + 1], bias=gnb[:, col:col + 1]),
                 R=ogb + [mixc_b], W=ogb)
            s.op("pool", lambda e, og=og, h=h, sl=sl: e.tensor_tensor(out=o_gnT[:, h, sl], in0=og, in1=sg[h][:], op=ALU.mult),
                 R=ogb + [sg_b[h]], W=[o_gn_b[h][tb]])

        def gn_slot():
            if gstate["B"] is not None:
                gn_B(*gstate["B"])
                gstate["B"] = None
            if gtasks:
                t_ = gtasks.pop(0)
                gn_A(*t_)
                gstate["B"] = t_

        def gn_flush():
            while gtasks or gstate["B"] is not None:
                gn_slot()

        emit_AT_mm(0)
        emit_dS(0, 0)
        emit_dS(0, 1)
        emit_AT_mask(0)
        for j in range(16):
            tb, jj = j // 4, j % 4
            ob = tb % 2
            aj = j % 2
            bO = 2 + aj
            t0 = slice(j * 128, j * 128 + 64)
            t1_ = slice(j * 128 + 64, (j + 1) * 128)
            for h in range(4):
                hs = slice(h * 128, (h + 1) * 128)
                s.op("pe", lambda e, h=h, hs=hs, bO=bO, aj=aj, j=j: e.matmul(
                    psum[bO][:, hs], lhsT=gv[:, j, hs], rhs=ATt[aj][:, h, :], start=(h == 0), stop=False, skip_group_check=True),
                    R=[gv_b[j], AT_b[aj][h]], W=[pq[bO][h]])
            prev = None
            for h in HORD:
                ch, pb = h // 2, (h % 2) * 64
                prev = s.op("pe", lambda e, h=h, ch=ch, pb=pb, bO=bO, t0=t0: e.matmul(
                    psum[bO][:, h * 128:h * 128 + 64], lhsT=st_b[pb:pb + 64, h, :], rhs=qT[pb:pb + 64, ch, t0],
                    start=False, stop=False, skip_group_check=True),
                    R=[stb_b[h], qT_b[ch][j]], W=[pq[bO][h]], after=[prev] if h == 1 else [])
            if j + 1 < 16:
                emit_AT_mm(j + 1)
            emit_update(j, 0)
            prev = None
            for h in HORD:
                ch, pb = h // 2, (h % 2) * 64
                prev = s.op("pe", lambda e, h=h, ch=ch, pb=pb, bO=bO, t1_=t1_: e.matmul(
                    psum[bO][:, h * 128 + 64:(h + 1) * 128], lhsT=st_b[pb:pb + 64, h, :], rhs=qT[pb:pb + 64, ch, t1_],
                    start=False, stop=True, skip_group_check=True),
                    R=[stb_b[h], qT_b[ch][j]], W=[pq[bO][h]], after=[prev] if h == 1 else [])
            if j + 1 < 16:
                emit_AT_mask(j + 1)
                emit_dS(j + 1, 0)
            s.op("act", lambda e, bO=bO, ob=ob, jj=jj: e.copy(
                out=ograw[ob][:, :, jj * 128:(jj + 1) * 128], in_=psum[bO][:].rearrange("p (h n) -> p h n", h=4)),
                R=[pq[bO][0]], W=[ograw_b[ob][h][jj] for h in range(4)])
            emit_update(j, 1)
            if j + 1 < 16:
                emit_dS(j + 1, 1)
            if j == 0:
                tap("st1", st_f, stf_b)
            gn_slot()
            if jj == 3:
                gn_flush()
                if tb == 0:
                    tap("ograw0", ograw[0], [b for bb in ograw_b[0] for b in bb])
                gate_batch(tb)
        gn_flush()

        if stop == "D":
            return
        tap("o_gnT", o_gnT, [b for bb in o_gn_b for b in bb])
        s.fence()
        A.release(mix_base)
        o_aT = A.alloc([128, 4, S], BF16, "o_aT")
        mix_base = A.mark()
        waqkv = A.alloc([128, NC_, 384], BF16, "waqkv")
        waqkv_b = Buf()
        waqkv_sem = new_dsem("waqkv")
        aqT = [[A.alloc([128, S], BF16, "aqT") for _ in range(2)] for _ in range(2)]
        akT = [A.alloc([128, S], BF16, "akT") for _ in range(2)]
        aqT_b = [bufs(NTB) for _ in range(2)]
        aqz_b = [bufs(2) for _ in range(2)]
        for wi_ in range(2):
            for hh_ in range(2):
                oth = slice(64, 128) if hh_ == 0 else slice(0, 64)
                s.op("pool", lambda e, wi_=wi_, hh_=hh_, oth=oth: e.memset(aqT[wi_][hh_][oth, :], 0.0), W=[aqz_b[wi_][hh_]])
        akT_b = [bufs(16) for _ in range(2)]
        Vp = [A.alloc([128, 16, 128], BF16, "Vp") for _ in range(2)]
        Vp_b = [bufs(16) for _ in range(2)]
        mask_s = [A.alloc([128, S], F32, "mask") for _ in range(2)]
        mask_b = bufs(2)
        mask_sem = [new_dsem("mask") for _ in range(2)]
        NE = 4
        LOOK = 3
        Et = [A.alloc([128, TB], F32, "Et") for _ in range(NE)]
        Pt = [A.alloc([128, TB], BF16, "Pt") for _ in range(NE)]
        Et_b, Pt_b = bufs(NE), bufs(NE)
        dcp = [A.alloc([128, TB], F32, "dcp") for _ in range(1)] * 2
        dcp_b = bufs(1) * 2
        onesb = A.alloc([128, 128], BF16, "onesb")
        onesb_b = Buf()
        s.op("pool", lambda e: e.memset(onesb[:], 1.0), W=[onesb_b])
        SB = (0, 1, 2, 3)
        stepc = 0
        def load_waqkv(ch):
            s.op("pool", lambda e, ch=ch: e.dma_start(out=waqkv[:], in_=waqkv_d[ch].rearrange("p (k n) -> p k n", k=NC_)),
                 W=[waqkv_b], dsem=waqkv_sem)

        def load_mask(h):
            mi = h % 2
            s.op("sp", lambda e, mi=mi, h=h: e.dma_start(out=mask_s[mi][:], in_=amask_d[h]),
                 W=[mask_b[mi]], dsem=mask_sem[mi])

        load_waqkv(0)
        load_mask(0)
        load_mask(1)
        for ch in range(4):
            wi = ch % 2
            for tb in range(NTB):
                sl = slice(tb * TB, (tb + 1) * TB)
                for which in range(2):
                    bi = SB[(2 * tb + which) % 4]
                    for k in range(NC_):
                        s.op("pe", lambda e, k=k, which=which, sl=sl, bi=bi: e.matmul(
                            psum[bi][:], lhsT=waqkv[:, k, which * 128:(which + 1) * 128], rhs=xb[:, k, sl],
                            start=(k == 0), stop=(k == NC_ - 1)),
                            R=[waqkv_b, xb_b[k][tb]], W=pb_(bi))
                    if which == 0:
                        s.op("act", lambda e, wi=wi, sl=sl, bi=bi: e.mul(aqT[wi][0][0:64, sl], psum[bi][0:64, :], 0.125),
                             R=pb_(bi), W=[aqT_b[wi][tb]])
                        s.op("act", lambda e, wi=wi, sl=sl, bi=bi: e.mul(aqT[wi][1][64:128, sl], psum[bi][64:128, :], 0.125),
                             R=pb_(bi), W=[aqT_b[wi][tb]])
                    else:
                        s.op("dve", lambda e, wi=wi, sl=sl, bi=bi: e.tensor_copy(out=akT[wi][:, sl], in_=psum[bi][:]),
                             R=pb_(bi), W=akT_b[wi][tb * 4:(tb + 1) * 4])
            for j4 in range(4):
                bi = SB[j4 % 4]
                for jj in range(4):
                    j = j4 * 4 + jj
                    tsl = slice(j * 128, (j + 1) * 128)
                    for k in range(NC_):
                        s.op("pe", lambda e, k=k, tsl=tsl, bi=bi, jj=jj: e.matmul(
                            psum[bi][:, jj * 128:(jj + 1) * 128], lhsT=xb[:, k, tsl], rhs=waqkv[:, k, 256:384],
                            start=(k == 0 and jj == 0), stop=(k == NC_ - 1), skip_group_check=True),
                            R=[waqkv_b, xb_b[k][j4]], W=pb_(bi))
                s.op("act", lambda e, wi=wi, j4=j4, bi=bi: e.copy(
                    out=Vp[wi][:, j4 * 4:(j4 + 1) * 4, :], in_=psum[bi][:].rearrange("p (a b) -> p a b", a=4)),
                    R=pb_(bi), W=Vp_b[wi][j4 * 4:(j4 + 1) * 4])
            if ch + 1 < 4:
                load_waqkv(ch + 1)
            steps = []
            for hh in range(2):
                h = 2 * ch + hh
                for qp in range(NTB):
                    kbs = [kb for kb in range(4 * qp + 4) if blk[h][kb][qp]]
                    for ki, kb in enumerate(kbs):
                        steps.append((hh, h, qp, kb, ki, len(kbs)))
            info = {}
            for idx in range(len(steps) + LOOK):
                if idx < len(steps):
                    hh, h, qp, kb, ki, nk = steps[idx]
                    pb = hh * 64
                    mi = h % 2
                    q0 = qp * TB
                    n0 = max(q0, 128 * kb)
                    n = q0 + TB - n0
                    bS = SB[stepc % 4]
                    ei = stepc % NE
                    stepc += 1
                    info[idx] = (ei, n0, n)
                    s.op("pe", lambda e, wi=wi, hh=hh, kb=kb, n0=n0, n=n, bS=bS: e.matmul(
                        psum[bS][:, 0:n], lhsT=akT[wi][:, kb * 128:(kb + 1) * 128],
                        rhs=aqT[wi][hh][:, n0:n0 + n], start=True, stop=True),
                        R=[akT_b[wi][kb], aqT_b[wi][qp], aqz_b[wi][hh]], W=pb_(bS))
                    s.op("act", lambda e, ei=ei, bS=bS, n=n: e.activation(out=Et[ei][:, 0:n], in_=psum[bS][:, 0:n], func=AF.Exp),
                         R=pb_(bS), W=[Et_b[ei]])
                    mo = n0 - 128 * kb
                    s.op("dve", lambda e, ei=ei, mi=mi, mo=mo, n=n: e.tensor_tensor(
                        out=Pt[ei][:, 0:n], in0=Et[ei][:, 0:n], in1=mask_s[mi][:, mo:mo + n], op=ALU.mult),
                        R=[Et_b[ei], mask_b[mi]], W=[Pt_b[ei]])
                    if h + 2 < 8 and (idx + 1 == len(steps) or steps[idx + 1][1] != h):
                        load_mask(h + 2)
                pidx = idx - LOOK
                if pidx >= 0:
                    hh, h, qp, kb, ki, nk = steps[pidx]
                    ei, n0, n = info[pidx]
                    pb = hh * 64
                    q0 = qp * TB
                    par = (h * NTB + qp) % 2
                    bO, bD = 4 + par, 6 + par
                    cs = slice(n0 - q0, n0 - q0 + n)
                    s.op("pe", lambda e, wi=wi, kb=kb, ei=ei, n=n, cs=cs, bO=bO, ki=ki, nk=nk: e.matmul(
                        psum[bO][:, cs], lhsT=Vp[wi][:, kb, :], rhs=Pt[ei][:, 0:n],
                        start=(ki == 0), stop=(ki == nk - 1), skip_group_check=True),
                        R=[Vp_b[wi][kb], Pt_b[ei]], W=pb_(bO))
                    s.op("pe", lambda e, ei=ei, n=n, cs=cs, bD=bD, ki=ki, nk=nk: e.matmul(
                        psum[bD][:, cs], lhsT=onesb[:], rhs=Pt[ei][:, 0:n],
                        start=(ki == 0), stop=(ki == nk - 1), skip_group_check=True),
                        R=[onesb_b, Pt_b[ei]], W=pb_(bD))
                    if ki == nk - 1:
                        ps_ = slice(pb, pb + 64)
                        s.op("act", lambda e, par=par, bD=bD, ps_=ps_: e.activation(out=dcp[par][ps_, :], in_=psum[bD][ps_, :], func=AF.Ln),
                             R=pb_(bD), W=[dcp_b[par]])
                        s.op("act", lambda e, par=par, ps_=ps_: e.activation(out=dcp[par][ps_, :], in_=dcp[par][ps_, :], func=AF.Exp, scale=-1.0),
                             R=[dcp_b[par]], W=[dcp_b[par]])
                        s.op("dve", lambda e, ch=ch, ps_=ps_, q0=q0, bO=bO, par=par: e.tensor_tensor(
                            out=o_aT[ps_, ch, q0:q0 + TB], in0=psum[bO][ps_, :], in1=dcp[par][ps_, :], op=ALU.mult),
                            R=pb_(bO) + [dcp_b[par]], W=[o_a_b[h][qp]])

        if stop == "C":
            return
        tap("o_aT", o_aT, [b for bb in o_a_b for b in bb])
        s.fence()
        A.release(mix_base)
        e_low_top = A.mark()
        mT = A.alloc([128, NC_, S], BF16, "mT")
        mT_b = [bufs(NTB) for _ in range(NC_)]
        e_base = A.mark()
        wE = [A.alloc([128, NC_, 256], BF16, "wgab") for _ in range(2)]
        wap = [A.alloc([128, 4, 128], BF16, "wap") for _ in range(2)]
        wgp = [A.alloc([128, 4, 128], BF16, "wgp") for _ in range(2)]
        wE_b = [bufs(3) for _ in range(2)]
        wE_sem = [[new_dsem("wE") for _ in range(3)] for _ in range(2)]
        sa = [A.alloc([128, TB], F32, "sa") for _ in range(2)]
        sbt = [A.alloc([128, TB], F32, "sbt") for _ in range(2)]
        sa_b, sbt_b = bufs(2), bufs(2)
        wo = A.alloc([128, NC_, NC_, 128], BF16, "wo")
        wo_b = bufs(NC_)
        e_top = A.mark()
        ecn = 0

        def load_wE(dc):
            wi = dc % 2
            s.op("pool", lambda e, wi=wi, dc=dc: e.dma_start(out=wE[wi][:], in_=wgab_d[dc].rearrange("p (k n) -> p k n", k=NC_)),
                 W=[wE_b[wi][0]], dsem=wE_sem[wi][0])
            s.op("pool", lambda e, wi=wi, dc=dc: e.dma_start(out=wap[wi][:], in_=wap_d[dc].rearrange("p (k n) -> p k n", k=4)),
                 W=[wE_b[wi][1]], dsem=wE_sem[wi][1])
            s.op("pool", lambda e, wi=wi, dc=dc: e.dma_start(out=wgp[wi][:], in_=wgp_d[dc].rearrange("p (k n) -> p k n", k=4)),
                 W=[wE_b[wi][2]], dsem=wE_sem[wi][2])

        load_wE(0)
        for dc in range(NC_):
            wi = dc % 2
            if dc + 1 < NC_:
                load_wE(dc + 1)
            if dc >= 4:
                for dco in (2 * (dc - 4), 2 * (dc - 4) + 1):
                    s.op("pool", lambda e, dco=dco: e.dma_start(out=wo[:, dco, :, :], in_=wo_d[dco].rearrange("p (k n) -> p k n", k=NC_)),
                         W=[wo_b[dco]], dsem=new_dsem("wo"))
            for tb in range(NTB):
                sl = slice(tb * TB, (tb + 1) * TB)
                pj = ecn % 2
                ecn += 1
                bGA, bGB, bPA, bPG = 0 + pj, 2 + pj, 4 + pj, 6 + pj
                for which, bi in ((0, bGA), (1, bGB)):
                    for k in range(NC_):
                        s.op("pe", lambda e, wi=wi, which=which, k=k, sl=sl, bi=bi: e.matmul(
                            psum[bi][:], lhsT=wE[wi][:, k, which * 128:(which + 1) * 128], rhs=xb[:, k, sl],
                            start=(k == 0), stop=(k == NC_ - 1)),
                            R=[wE_b[wi][0], xb_b[k][tb]], W=pb_(bi))
                for c in range(4):
                    s.op("pe", lambda e, wi=wi, c=c, sl=sl, bPA=bPA: e.matmul(
                        psum[bPA][:], lhsT=wap[wi][:, c, :], rhs=o_aT[:, c, sl], start=(c == 0), stop=(c == 3)),
                        R=[wE_b[wi][1], o_a_b[2 * c][tb], o_a_b[2 * c + 1][tb]], W=pb_(bPA))
                for c in range(4):
                    s.op("pe", lambda e, wi=wi, c=c, sl=sl, bPG=bPG: e.matmul(
                        psum[bPG][:], lhsT=wgp[wi][:, c, :], rhs=o_gnT[:, c, sl], start=(c == 0), stop=(c == 3)),
                        R=[wE_b[wi][2], o_gn_b[c][tb]], W=pb_(bPG))
                s.op("act", lambda e, pj=pj, bGA=bGA: e.activation(out=sa[pj][:], in_=psum[bGA][:], func=AF.Sigmoid),
                     R=pb_(bGA), W=[sa_b[pj]])
                s.op("act", lambda e, pj=pj, bGB=bGB: e.activation(out=sbt[pj][:], in_=psum[bGB][:], func=AF.Sigmoid),
                     R=pb_(bGB), W=[sbt_b[pj]])
                s.op("dve", lambda e, pj=pj, bPA=bPA: e.tensor_tensor(out=sa[pj][:], in0=psum[bPA][:], in1=sa[pj][:], op=ALU.mult),
                     R=pb_(bPA) + [sa_b[pj]], W=[sa_b[pj]])
                s.op("dve", lambda e, pj=pj, bPG=bPG: e.tensor_tensor(out=sbt[pj][:], in0=psum[bPG][:], in1=sbt[pj][:], op=ALU.mult),
                     R=pb_(bPG) + [sbt_b[pj]], W=[sbt_b[pj]])
                s.op("pool", lambda e, pj=pj, dc=dc, sl=sl: e.tensor_tensor(out=mT[:, dc, sl], in0=sa[pj][:], in1=sbt[pj][:], op=ALU.add),
                     R=[sa_b[pj], sbt_b[pj]], W=[mT_b[dc][tb]])
        tap("mT", mT, [b for bb in mT_b for b in bb])
        s.fence()
        A.release(e_top)
        Alow = Arena(nc, arena_base, e_low_top)
        ln_chunk, ln_flush = make_ln(l, 1, [Alow, A])
        yc = 0
        for tb in range(NTB):
            sl = slice(tb * TB, (tb + 1) * TB)
            for dc in range(NC_):
                bi = 4 + yc % 2
                yc += 1
                for k in range(NC_):
                    s.op("pe", lambda e, dc=dc, k=k, sl=sl, bi=bi: e.matmul(
                        psum[bi][:], lhsT=wo[:, dc, k, :], rhs=mT[:, k, sl], start=(k == 0), stop=(k == NC_ - 1)),
                        R=[wo_b[dc], mT_b[k][tb]], W=pb_(bi))
                s.op("dve", lambda e, bi=bi, dc=dc, sl=sl: e.tensor_tensor(
                    out=xs[:, dc, sl], in0=psum[bi][:], in1=xs[:, dc, sl], op=ALU.add),
                    R=pb_(bi) + [xs_b[dc][tb]], W=[xs_b[dc][tb]])
                ln_chunk(dc, tb)
        emit_prefetch("mix", e_base)
        ln_flush()

    amask_blocks = _mask_blocks()
    out_ops = []
    lastp = phases[-1]
    if lastp[0] == "ffn":
        final_ln = (lastp[1], 0 if lastp[2] == 0 else 2)
    elif lastp[0] == "mix" and len(lastp) == 2:
        final_ln = (lastp[1], 1)
    else:
        final_ln = None
    for pi, ph in enumerate(phases):
        nxt["ph"] = phases[pi + 1] if pi + 1 < len(phases) else None
        if ph[0] == "ffn":
            ffn(ph[1], ph[2])
        elif ph[0] == "mix":
            mixer(ph[1], ph[2] if len(ph) > 2 else None)
        else:
            raise ValueError(ph)

    if not out_ops:
        for c in range(NC_):
            for t in range(NTB):
                sl = slice(t * TB, (t + 1) * TB)
                s.op("dve", lambda e, c=c, sl=sl: e.tensor_scalar_mul(out=xs[:, c, sl], in0=xs[:, c, sl], scalar1=1.0 / ALPHA),
                     R=[xs_b[c][t]], W=[xs_b[c][t]])
        for c in range(NC_):
            out_ops.append(s.op("sp", lambda e, c=c: e.dma_start(out=yT_d[c * 128:(c + 1) * 128, :], in_=xs[:, c, :]),
                                R=xs_b[c], dsem=out_sem))
    fin = Buf()
    fin.w = out_ops[-1]
    s.op("sp", lambda e: e.nop(), R=[fin])

    s.finalize()
    from contextlib import ExitStack
    with ExitStack() as ctx:
        esem = {}
        for en in Sched.ENGS:
            esem[en] = ctx.enter_context(nc.semaphore(f"sem_{en}"))
        dsems = {}
        for nm in dsem_names:
            dsems[nm] = ctx.enter_context(nc.semaphore(f"d_{nm}"))
        with nc.Block() as block:
            @block.tensor
            def _(e):
                s.replay("pe", e, esem, dsems)

            @block.scalar
            def _(e):
                s.replay("act", e, esem, dsems)

            @block.vector
            def _(e):
                s.replay("dve", e, esem, dsems)

            @block.gpsimd
            def _(e):
                s.replay("pool", e, esem, dsems)

            @block.sync
            def _(e):
                s.replay("sp", e, esem, dsems)
    return nc


_MASK = None


def _alibi_mask():
    global _MASK
    if _MASK is None:
        d = np.arange(S)[None, :] - np.arange(128)[:, None]
        mult = ((d <= 128).astype(np.float64) + ((d % 4 == 0) & (d <= 512)) + ((d % 16 == 0) & (d <= 2048)))
        mult = np.where(d >= 0, mult, 0.0)
        slopes = np.exp2(-8.0 * np.arange(1, 9) / 8.0)
        m = mult[None] * np.exp(-slopes[:, None, None] * np.maximum(d, 0)[None])
        m = np.where(m < 1e-37, 0.0, m)
        _MASK = np.ascontiguousarray(m.astype(np.float32))
    return _MASK


def _mask_blocks():
    m = _alibi_mask()
    blk = [[[False] * NTB for _ in range(16)] for _ in range(8)]
    for h in range(8):
        for kb in range(16):
            for qp in range(NTB):
                n0 = max(qp * TB, 128 * kb)
                n1 = qp * TB + TB
                if n1 <= n0:
                    continue
                blk[h][kb][qp] = bool(m[h][:, n0 - 128 * kb:n1 - 128 * kb].any())
    return blk


def _consts():
    s_ = np.arange(128)[:, None]
    t_ = np.arange(128)[None, :]
    same = (s_ // 64) == (t_ // 64)
    U = np.where(same & (s_ <= t_), -1.0 / 16.0, 0.0)
    L = np.where(same & (s_ > t_), -1.0 / 16.0, 0.0)
    M = np.where(same & (s_ <= t_), 1.0, 0.0)
    o1 = np.full((128, 128), 1.0 / D)
    o2 = np.full((128, 128), 1.0 / 128.0)
    return np.ascontiguousarray(np.concatenate([U, L, M, o1, o2], axis=1).astype(np.float32))


def _lay_w13(w):
    return np.ascontiguousarray(w.reshape(NC_, 128, NF, 128).transpose(2, 1, 0, 3).reshape(NF, 128, D))


def _lay_ln(v):
    return np.ascontiguousarray(v.reshape(DEPTH, 3, NC_, 128).transpose(3, 0, 1, 2).reshape(128, NL3))


def _lay_cols(w):
    n = w.shape[1]
    return np.ascontiguousarray(w.reshape(NC_, 128, n).transpose(1, 0, 2).reshape(128, NC_ * n))


def make_inputs(phases, inp):
    m = {"ln_g": _lay_ln(inp["ln_g"]), "ln_b": _lay_ln(inp["ln_b"]), "consts": _consts()}
    has_mix = any(p[0] == "mix" for p in phases)
    if has_mix:
        m["amask"] = _alibi_mask()
        m["wgu"] = np.ascontiguousarray(inp["w_gate_up"].transpose(1, 0, 2))
        m["bgu"] = np.ascontiguousarray(np.broadcast_to(inp["b_gate_up"][None], (128, DEPTH, 256)))
        m["gng"] = np.ascontiguousarray(inp["gla_norm_g"].reshape(DEPTH, 4, 128).transpose(2, 0, 1).reshape(128, DEPTH * 4))
        m["gnb"] = np.ascontiguousarray(inp["gla_norm_b"].reshape(DEPTH, 4, 128).transpose(2, 0, 1).reshape(128, DEPTH * 4))
    for ph in phases:
        if ph[0] == "ffn":
            l, i = ph[1], ph[2]
            pre = "ffn1" if i == 0 else "ffn2"
            m[f"f{i}w1_{l}"] = _lay_w13(inp[pre + "_w1"][l])
            m[f"f{i}w3_{l}"] = _lay_w13(inp[pre + "_w3"][l])
            m[f"f{i}w2_{l}"] = np.ascontiguousarray(inp[pre + "_w2"][l])
        else:
            l = ph[1]
            w = inp["w_in"][l]
            m[f"wglr_{l}"] = _lay_cols(w[:, O_GLR:O_GLR + 16])
            m[f"wgqk_{l}"] = _lay_cols(w[:, O_GQ:O_GQ + 512])
            m[f"wgkv_{l}"] = _lay_cols(w[:, O_GK:O_GK + 768])
            m[f"wgr_{l}"] = _lay_cols(w[:, O_GR:O_GR + 512])
            m[f"waqkv_{l}"] = np.stack([_lay_cols(np.concatenate(
                [w[:, O_AQ + c * 128:O_AQ + (c + 1) * 128], w[:, O_AK + c * 128:O_AK + (c + 1) * 128],
                 w[:, O_AV + c * 128:O_AV + (c + 1) * 128]], axis=1)) for c in range(4)], axis=0)
            m[f"wgab_{l}"] = np.stack([_lay_cols(np.concatenate(
                [w[:, O_GA + c * 128:O_GA + (c + 1) * 128], w[:, O_GB + c * 128:O_GB + (c + 1) * 128]], axis=1))
                for c in range(NC_)], axis=0)
            wa = inp["w_attn_proj"][l]
            m[f"wap_{l}"] = np.ascontiguousarray(wa.reshape(4, 128, NC_, 128).transpose(2, 1, 0, 3).reshape(NC_, 128, 4 * 128))
            wg = inp["w_gla_proj"][l]
            m[f"wgp_{l}"] = np.ascontiguousarray(wg.reshape(4, 128, NC_, 128).transpose(2, 1, 0, 3).reshape(NC_, 128, 4 * 128))
            wo_ = inp["w_out"][l]
            m[f"wo_{l}"] = np.ascontiguousarray(wo_.reshape(NC_, 128, NC_, 128).transpose(2, 1, 0, 3).reshape(NC_, 128, NC_ * 128))
    return m


def run_phases(phases, x, inp, n_cores=8, trace=False, debug=None):
    nc = build(phases, debug)
    shared = make_inputs(phases, inp)
    in_maps = []
    for b in range(n_cores):
        d = dict(shared)
        d["xT"] = np.ascontiguousarray(x[b].T)
        in_maps.append(d)
    res = run_bass_kernel_spmd(nc, in_maps, core_ids=list(range(n_cores)), trace=trace)
    out = np.stack([np.ascontiguousarray(r["yT"].T) for r in res.results], axis=0)
    return out, res


LAUNCHES = [[("ffn", 0, 0), ("mix", 0), ("ffn", 0, 1), ("ffn", 1, 0), ("mix", 1), ("ffn", 1, 1)]]


def kernel(**inputs):
    inp = {k: np.asarray(v) for k, v in inputs.items()}
    x = np.ascontiguousarray(inp["x"], dtype=np.float32)
    for phases in LAUNCHES:
        x, _ = run_phases(phases, x, inp)
    return np.ascontiguousarray(x, dtype=np.float32)
```

```python
import numpy as np
import concourse.bass as bass
import concourse.mybir as mybir
from concourse.bass_utils import run_bass_kernel_spmd

F32 = mybir.dt.float32
F32R = mybir.dt.float32r

BF16 = mybir.dt.bfloat16
AF = mybir.ActivationFunctionType
ALU = mybir.AluOpType

S = 2048
D = 1024
DFF = 2816
NC_ = 8
NTB = 4
TB = 512
NF = 22
DEPTH = 2
ALPHA = float((2 * DEPTH) ** 0.25)
LN_EPS = 1e-5
FFN_GROUPS = [3, 3, 4, 4, 4, 4]
GMAX = 4
NL3 = DEPTH * 3 * NC_
SB_BASE = 16512
SB_TOP = 229344

O_AQ, O_AK, O_AV = 0, 512, 1024
O_GQ, O_GK, O_GV, O_GLR, O_GR = 1536, 1792, 2048, 2560, 2576
O_GA, O_GB = 3088, 4112
N_IN = 5136


class Buf:
    __slots__ = ("name", "w", "r", "excl")

    def __init__(self, name="", excl=False):
        self.name = name
        self.w = None
        self.r = {}
        self.excl = excl


def bufs(n):
    return [Buf() for _ in range(n)]


class Op:
    __slots__ = ("eng", "fn", "deps", "needed", "semval", "dsem", "dval")

    def __init__(self, eng, fn, deps, dsem):
        self.eng = eng
        self.fn = fn
        self.deps = deps
        self.needed = False
        self.semval = None
        self.dsem = dsem
        self.dval = None


class Sched:
    ENGS = ("pe", "act", "dve", "pool", "sp")

    def __init__(self):
        self.q = {e: [] for e in self.ENGS}
        self.dma_count = {}
        self.last_dma = {}
        self.extra = {e: [] for e in self.ENGS}

    def op(self, eng, fn, R=(), W=(), dsem=None, after=()):
        deps = [(3, a) for a in after if a is not None]
        if any(b.excl for b in R):
            W = list(W) + [b for b in R if b.excl]
            R = [b for b in R if not b.excl]
        W = list(dict.fromkeys(W))
        for b in R:
            if b.w is not None:
                deps.append((0, b.w))
        for b in W:
            if b.w is not None:
                deps.append((1, b.w))
            for r in b.r.values():
                deps.append((2, r))
        if self.extra[eng]:
            deps.extend((0, d) for d in self.extra[eng])
            self.extra[eng] = []
        o = Op(eng, fn, deps, dsem)
        if dsem is not None:
            self.dma_count[dsem] = self.dma_count.get(dsem, 0) + 16
            o.dval = self.dma_count[dsem]
            self.last_dma[dsem] = o
        self.q[eng].append(o)
        key = dsem if dsem is not None else eng
        for b in R:
            b.r[key] = o
        for b in W:
            b.w = o
            b.r = {}
        return o

    def fence(self):
        snap = []
        for e in self.ENGS:
            for o in reversed(self.q[e]):
                if o.dsem is None:
                    snap.append(o)
                    break
        snap.extend(self.last_dma.values())
        for e in self.ENGS:
            self.extra[e] = list(snap)

    def finalize(self):
        for eng in self.ENGS:
            for o in self.q[eng]:
                keep = []
                for kind, d in o.deps:
                    if d is o:
                        continue
                    if d.dsem is not None:
                        keep.append(d)
                    elif d.eng == o.eng:
                        if o.eng == "pe" and kind != 3:
                            continue
                        keep.append(d)
                    else:
                        keep.append(d)
                for d in keep:
                    if d.dsem is None:
                        d.needed = True
                o.deps = keep
        for eng in self.ENGS:
            c = 0
            for o in self.q[eng]:
                if o.dsem is None and o.needed:
                    c += 1
                    o.semval = c

    def replay(self, eng, e, esem, dsems):
        seen = {}
        for o in self.q[eng]:
            for d in o.deps:
                if d.dsem is not None:
                    key, val, sem = ("d", d.dsem), d.dval, dsems[d.dsem]
                else:
                    key, val, sem = ("e", d.eng), d.semval, esem[d.eng]
                if seen.get(key, 0) >= val:
                    continue
                seen[key] = val
                e.wait_ge(sem, val)
            ins = o.fn(e)
            if o.dsem is not None:
                ins.then_inc(dsems[o.dsem], 16)
            elif o.needed:
                ins.then_inc(esem[eng], 1)


DT_SIZE = {F32: 4, BF16: 2, F32R: 4}


class Arena:
    UID = 0

    def __init__(self, nc, base, top):
        self.nc = nc
        self.base = base
        self.top = top
        self.off = base
        self.uid = 0
        self.peak = base

    def alloc(self, shape, dt, name="t"):
        n = 1
        for d in shape[1:]:
            n *= d
        nbytes = (n * DT_SIZE[dt] + 63) // 64 * 64
        if self.off + nbytes > self.top:
            raise RuntimeError(f"arena overflow allocating {name} {shape}: off={self.off - self.base} need {nbytes} cap {self.top - self.base}")
        Arena.UID += 1
        t = self.nc.alloc_sbuf_tensor_at(f"{name}_{Arena.UID}", list(shape), dt, offset=self.off)
        self.off += nbytes
        self.peak = max(self.peak, self.off)
        return t

    def mark(self):
        return self.off

    def release(self, m):
        self.off = m


def build(phases, debug=None):
    nc = bass.Bass("TRN2", target_bir_lowering=False)
    s = Sched()
    dram = {}
    dsem_names = []

    def new_dsem(name):
        nm = f"{name}_{len(dsem_names)}"
        dsem_names.append(nm)
        return nm

    def din(name, shape, dt=F32):
        if name not in dram:
            dram[name] = nc.dram_tensor(name, list(shape), dt, kind="ExternalInput").ap()
        return dram[name]

    debug = debug or ()

    def tap(name, t, bl):
        if name in debug:
            dd = nc.dram_tensor("dbg_" + name, list(t.shape), t.dtype, kind="ExternalOutput").ap()
            s.op("sp", lambda e: e.dma_start(out=dd, in_=t[:]), R=bl, dsem=new_dsem("dbg"))

    xT_d = din("xT", [D, S])
    yT_d = nc.dram_tensor("yT", [D, S], F32, kind="ExternalOutput").ap()
    lng_d = din("ln_g", [128, NL3])
    lnb_d = din("ln_b", [128, NL3])
    consts_d = din("consts", [128, 5 * 128])
    has_mix = any(p[0] == "mix" for p in phases)

    A = Arena(nc, SB_BASE, SB_TOP)
    xs = A.alloc([128, NC_, S], F32, "xs")
    xb = A.alloc([128, NC_, S], BF16, "xb")
    xs_b = [bufs(NTB) for _ in range(NC_)]
    xb_b = [bufs(NTB) for _ in range(NC_)]
    lng = A.alloc([128, NL3], F32, "lng")
    lnb = A.alloc([128, NL3], F32, "lnb")
    lnga = A.alloc([128, NL3], F32, "lnga")
    lnba = A.alloc([128, NL3], F32, "lnba")
    ln_c = Buf()
    consts = A.alloc([128, 5 * 128], F32, "consts")
    consts_b = Buf()
    Umat = consts[:, 0:128]
    Lmat = consts[:, 128:256]
    Mblk = consts[:, 256:384]
    ones = consts[:, 384:512]
    gones = consts[:, 512:640]
    ones1 = A.alloc([128, 64], F32, "ones1")
    ones1_b = Buf()
    ones_r = A.alloc([128, 128], F32R, "ones_r")
    gones_r = A.alloc([128, 128], F32R, "gones_r")
    onesr_b = Buf()
    mixc_b = Buf()
    if has_mix:
        wgu = A.alloc([16, DEPTH, 256], F32, "wgu")
        bgu = A.alloc([128, DEPTH, 256], F32, "bgu")
        gng = A.alloc([128, DEPTH * 4], F32, "gng")
        gnb = A.alloc([128, DEPTH * 4], F32, "gnb")
    arena_base = A.mark()

    psum = [nc.alloc_psum_tensor(f"bank{i}", [128, TB], F32) for i in range(8)]
    pq = [[Buf(f"bank{i}", excl=True)] * 4 for i in range(8)]

    def pb_(i, c0=0, c1=TB):
        return pq[i][c0 // 128:(c1 + 127) // 128]

    out_sem = new_dsem("out")

    lng_b0, lnb_b0 = Buf(), Buf()
    s.op("sp", lambda e: e.dma_start(out=lng[:], in_=lng_d), W=[lng_b0], dsem=new_dsem("io"))
    s.op("sp", lambda e: e.dma_start(out=lnb[:], in_=lnb_d), W=[lnb_b0], dsem=new_dsem("io"))
    s.op("sp", lambda e: e.dma_start(out=consts[:], in_=consts_d), W=[consts_b], dsem=new_dsem("io"))
    if has_mix:
        wgu_d = din("wgu", [16, DEPTH, 256])
        bgu_d = din("bgu", [128, DEPTH, 256])
        gng_d = din("gng", [128, DEPTH * 4])
        gnb_d = din("gnb", [128, DEPTH * 4])
        mb = bufs(4)
        s.op("sp", lambda e: e.dma_start(out=wgu[:], in_=wgu_d), W=[mb[0]], dsem=new_dsem("io"))
        s.op("sp", lambda e: e.dma_start(out=bgu[:], in_=bgu_d), W=[mb[1]], dsem=new_dsem("io"))
        s.op("sp", lambda e: e.dma_start(out=gng[:], in_=gng_d), W=[mb[2]], dsem=new_dsem("io"))
        s.op("sp", lambda e: e.dma_start(out=gnb[:], in_=gnb_d), W=[mb[3]], dsem=new_dsem("io"))
        s.op("dve", lambda e: e.memset(ones1[:], 1.0), R=mb, W=[ones1_b, mixc_b])
    xT_v = xT_d.rearrange("(c p) t -> p c t", p=128)
    for t in range(NTB):
        s.op("sp", lambda e, t=t: e.dma_start(out=xs[:, :, t * TB:(t + 1) * TB], in_=xT_v[:, :, t * TB:(t + 1) * TB]),
             W=[xs_b[c][t] for c in range(NC_)], dsem=new_dsem("iox"))
    s.op("act", lambda e: e.copy(out=ones_r[:], in_=ones), R=[consts_b], W=[onesr_b])
    s.op("act", lambda e: e.copy(out=gones_r[:], in_=gones), R=[consts_b], W=[onesr_b])
    s.op("act", lambda e: e.mul(lnga[:], lng[:], ALPHA), R=[lng_b0], W=[ln_c])
    s.op("act", lambda e: e.mul(lnba[:], lnb[:], ALPHA), R=[lnb_b0], W=[ln_c])
    for t in range(NTB):
        for c in range(NC_):
            sl = slice(t * TB, (t + 1) * TB)
            s.op("dve", lambda e, c=c, sl=sl: e.tensor_copy(out=xb[:, c, sl], in_=xs[:, c, sl]),
                 R=[xs_b[c][t]], W=[xb_b[c][t]])
            s.op("act", lambda e, c=c, sl=sl: e.mul(xs[:, c, sl], xs[:, c, sl], ALPHA),
                 R=[xs_b[c][t]], W=[xs_b[c][t]])

    def layer_norm(l, i):
        col0 = (l * 3 + i) * NC_
        s.fence()
        A.release(arena_base)
        sq = A.alloc([128, NC_, TB], F32, "sq")
        sq_b = bufs(NC_)
        mean_sb = [A.alloc([128, TB], F32, "mean") for _ in range(2)]
        m2_sb = [A.alloc([128, TB], F32, "m2") for _ in range(2)]
        rstd_sb = [A.alloc([128, TB], F32, "rstd") for _ in range(2)]
        mean_b, m2_b, rstd_b = bufs(2), bufs(2), bufs(2)
        t1 = [A.alloc([128, TB], F32, "t1") for _ in range(2)]
        t2 = [A.alloc([128, TB], F32, "t2") for _ in range(3)]
        t1_b, t2_b = bufs(2), bufs(3)
        cn = {"t1": 0, "t2": 0}

        def stats(t):
            sl = slice(t * TB, (t + 1) * TB)
            p = t % 2
            bm, bq = (6, 7) if p == 0 else (4, 5)
            pm, pq_ = psum[bm], psum[bq]
            for c in range(NC_):
                s.op("act", lambda e, c=c, sl=sl: e.activation(out=sq[:, c, :], in_=xs[:, c, sl], func=AF.Square),
                     R=[xs_b[c][t]], W=[sq_b[c]])
            for c in range(NC_):
                s.op("pe", lambda e, c=c, sl=sl, pm=pm: e.matmul(pm[:], lhsT=ones, rhs=xs[:, c, sl],
                                                                 start=(c == 0), stop=(c == NC_ - 1)),
                     R=[consts_b, xs_b[c][t]], W=pb_(bm))
            for c in range(NC_):
                s.op("pe", lambda e, c=c, pq_=pq_: e.matmul(pq_[:], lhsT=ones, rhs=sq[:, c, :],
                                                            start=(c == 0), stop=(c == NC_ - 1)),
                     R=[consts_b, sq_b[c]], W=pb_(bq))
            s.op("act", lambda e, pm=pm, p=p: e.activation(out=m2_sb[p][:], in_=pm[:], func=AF.Square), R=pb_(bm), W=[m2_b[p]])
            s.op("act", lambda e, pm=pm, p=p: e.copy(out=mean_sb[p][:], in_=pm[:]), R=pb_(bm), W=[mean_b[p]])
            s.op("dve", lambda e, pq_=pq_, p=p: e.tensor_tensor(out=rstd_sb[p][:], in0=pq_[:], in1=m2_sb[p][:], op=ALU.subtract),
                 R=pb_(bq) + [m2_b[p]], W=[rstd_b[p]])
            s.op("act", lambda e, p=p: e.activation(out=m2_sb[p][:], in_=rstd_sb[p][:], func=AF.Ln, bias=LN_EPS, scale=1.0),
                 R=[rstd_b[p]], W=[m2_b[p]])
            s.op("act", lambda e, p=p: e.activation(out=rstd_sb[p][:], in_=m2_sb[p][:], func=AF.Exp, scale=-0.5),
                 R=[m2_b[p]], W=[rstd_b[p]])

        def norm(t):
            sl = slice(t * TB, (t + 1) * TB)
            p = t % 2
            for c in range(NC_):
                j = cn["t1"] % 2
                cn["t1"] += 1
                j2 = cn["t2"] % 3
                cn["t2"] += 1
                s.op("dve", lambda e, c=c, sl=sl, j=j, p=p: e.tensor_tensor(out=t1[j][:], in0=xs[:, c, sl], in1=mean_sb[p][:], op=ALU.subtract),
                     R=[xs_b[c][t], mean_b[p]], W=[t1_b[j]])
                s.op("pool", lambda e, j=j, j2=j2, p=p: e.tensor_tensor(out=t2[j2][:], in0=t1[j][:], in1=rstd_sb[p][:], op=ALU.mult),
                     R=[t1_b[j], rstd_b[p]], W=[t2_b[j2]])
                s.op("act", lambda e, c=c, sl=sl, j2=j2: e.activation(out=xs[:, c, sl], in_=t2[j2][:], func=AF.Identity,
                                                                   scale=lnga[:, col0 + c:col0 + c + 1],
                                                                   bias=lnba[:, col0 + c:col0 + c + 1]),
                     R=[t2_b[j2], ln_c], W=[xs_b[c][t]])
                s.op("dve", lambda e, c=c, sl=sl, j2=j2: e.tensor_scalar(
                    out=xb[:, c, sl], in0=t2[j2][:], scalar1=lng[:, col0 + c:col0 + c + 1], scalar2=lnb[:, col0 + c:col0 + c + 1],
                    op0=ALU.mult, op1=ALU.add),
                    R=[t2_b[j2], ln_c], W=[xb_b[c][t]])

        stats(0)
        for t in range(NTB):
            if t + 1 < NTB:
                stats(t + 1)
            norm(t)

    def make_ln(l, i, arenas, sbanks=((0, 1), (2, 3)), lag=1):
        col0 = (l * 3 + i) * NC_
        final = (l, i) == final_ln
        g_xs, b_xs = (lng, lnb) if final else (lnga, lnba)
        yT_v = yT_d.rearrange("(c p) t -> p c t", p=128)

        def al(shape, dt, name):
            for a in arenas:
                n = 1
                for d in shape[1:]:
                    n *= d
                if a.off + (n * DT_SIZE[dt] + 63) // 64 * 64 <= a.top:
                    return a.alloc(shape, dt, name)
            raise RuntimeError("make_ln: no room for " + name)

        NSQ = 3
        sqr = [al([128, TB], F32R, "lsq") for _ in range(NSQ)]
        sqr_b = bufs(NSQ)
        mean_sb = [al([128, TB], F32, "lmean") for _ in range(2)]
        m2_sb = [al([128, TB], F32, "lm2") for _ in range(2)]
        rstd_sb = [al([128, TB], F32, "lrstd") for _ in range(2)]
        mean_b, m2_b, rstd_b = bufs(2), bufs(2), bufs(2)
        NT = 3
        t1 = [al([128, TB], F32, "lt1") for _ in range(NT)]
        t2 = [al([128, TB], F32, "lt2") for _ in range(NT)]
        t1_b, t2_b = bufs(NT), bufs(NT)
        cn = {"sq": 0, "t1": 0, "t2": 0, "seen": {}}
        pending = []
        avail = []
        fl = {"s1": None, "s2": None}

        def tick():
            if fl["s2"] is not None:
                c, t, j2 = fl["s2"]
                sl = slice(t * TB, (t + 1) * TB)
                s.op("act", lambda e, c=c, sl=sl, j2=j2: e.activation(out=xs[:, c, sl], in_=t2[j2][:], func=AF.Identity,
                                                                   scale=g_xs[:, col0 + c:col0 + c + 1],
                                                                   bias=b_xs[:, col0 + c:col0 + c + 1]),
                     R=[t2_b[j2], ln_c], W=[xs_b[c][t]])
                if not final and c % 3 == 2:
                    s.op("act", lambda e, c=c, sl=sl, j2=j2: e.activation(out=xb[:, c, sl], in_=t2[j2][:], func=AF.Identity,
                                                                       scale=lng[:, col0 + c:col0 + c + 1],
                                                                       bias=lnb[:, col0 + c:col0 + c + 1]),
                         R=[t2_b[j2], ln_c], W=[xb_b[c][t]])
                elif not final:
                    s.op("dve", lambda e, c=c, sl=sl, j2=j2: e.tensor_scalar(
                        out=xb[:, c, sl], in0=t2[j2][:], scalar1=lng[:, col0 + c:col0 + c + 1], scalar2=lnb[:, col0 + c:col0 + c + 1],
                        op0=ALU.mult, op1=ALU.add),
                        R=[t2_b[j2], ln_c], W=[xb_b[c][t]])
                else:
                    ndone = cn.get(("done", t), 0) + 1
                    cn[("done", t)] = ndone
                    if ndone == NC_:
                        s.op("sp", lambda e, sl=sl: e.dma_start(out=yT_v[:, :, sl], in_=xs[:, :, sl]),
                             R=[xs_b[cc][t] for cc in range(NC_)], dsem=out_sem)
                        out_ops.append(s.q["sp"][-1])
                fl["s2"] = None
            if fl["s1"] is not None:
                c, t, j = fl["s1"]
                p = t % 2
                j2 = cn["t2"] % NT
                cn["t2"] += 1
                s.op("pool", lambda e, j=j, j2=j2, p=p: e.tensor_tensor(out=t2[j2][:], in0=t1[j][:], in1=rstd_sb[p][:], op=ALU.mult),
                     R=[t1_b[j], rstd_b[p]], W=[t2_b[j2]])
                fl["s2"] = (c, t, j2)
                fl["s1"] = None
            if avail:
                c, t = avail.pop(0)
                sl = slice(t * TB, (t + 1) * TB)
                p = t % 2
                j = cn["t1"] % NT
                cn["t1"] += 1
                s.op("dve", lambda e, c=c, sl=sl, j=j, p=p: e.tensor_tensor(out=t1[j][:], in0=xs[:, c, sl], in1=mean_sb[p][:], op=ALU.subtract),
                     R=[xs_b[c][t], mean_b[p]], W=[t1_b[j]])
                fl["s1"] = (c, t, j)

        def emit(entry):
            dc, t, k = entry
            sl = slice(t * TB, (t + 1) * TB)
            p = t % 2
            bm, bq = sbanks[p]
            n = cn["seen"].get(t, 0)
            cn["seen"][t] = n + 1
            s.op("pe", lambda e, dc=dc, sl=sl, bm=bm, n=n: e.matmul(psum[bm][:], lhsT=ones, rhs=xs[:, dc, sl],
                                                                   start=(n == 0), stop=(n == NC_ - 1)),
                 R=[consts_b, xs_b[dc][t]], W=pb_(bm))
            s.op("pe", lambda e, k=k, bq=bq, n=n: e.matmul(psum[bq][:], lhsT=ones_r[:], rhs=sqr[k][:],
                                                           start=(n == 0), stop=(n == NC_ - 1)),
                 R=[onesr_b, sqr_b[k]], W=pb_(bq))
            if n == NC_ - 1:
                s.op("act", lambda e, bm=bm, p=p: e.activation(out=m2_sb[p][:], in_=psum[bm][:], func=AF.Square), R=pb_(bm), W=[m2_b[p]])
                s.op("act", lambda e, bm=bm, p=p: e.copy(out=mean_sb[p][:], in_=psum[bm][:]), R=pb_(bm), W=[mean_b[p]])
                s.op("dve", lambda e, bq=bq, p=p: e.tensor_tensor(out=rstd_sb[p][:], in0=psum[bq][:], in1=m2_sb[p][:], op=ALU.subtract),
                     R=pb_(bq) + [m2_b[p]], W=[rstd_b[p]])
                s.op("act", lambda e, p=p: e.activation(out=m2_sb[p][:], in_=rstd_sb[p][:], func=AF.Ln, bias=LN_EPS, scale=1.0),
                     R=[rstd_b[p]], W=[m2_b[p]])
                s.op("act", lambda e, p=p: e.activation(out=rstd_sb[p][:], in_=m2_sb[p][:], func=AF.Exp, scale=-0.5),
                     R=[m2_b[p]], W=[rstd_b[p]])
                avail.extend((c, t) for c in range(NC_))

        def chunk_done(dc, t):
            sl = slice(t * TB, (t + 1) * TB)
            k = cn["sq"] % NSQ
            cn["sq"] += 1
            s.op("act", lambda e, dc=dc, sl=sl, k=k: e.activation(out=sqr[k][:], in_=xs[:, dc, sl], func=AF.Square),
                 R=[xs_b[dc][t]], W=[sqr_b[k]])
            pending.append((dc, t, k))
            if len(pending) > lag:
                emit(pending.pop(0))
            tick()

        def flush():
            while pending:
                emit(pending.pop(0))
                tick()
            while avail or fl["s1"] is not None or fl["s2"] is not None:
                tick()

        return chunk_done, flush

    hook = {}
    nxt = {"ph": None}

    def at(name, shape, dt, off):
        Arena.UID += 1
        return nc.alloc_sbuf_tensor_at(f"{name}_{Arena.UID}", list(shape), dt, offset=off)

    def emit_prefetch(cur_kind, e_off=None):
        ph = nxt["ph"]
        if ph is None:
            return
        if ph[0] == "ffn":
            l2, i2 = ph[1], ph[2]
            w1n = din(f"f{i2}w1_{l2}", [NF, 128, D])
            w3n = din(f"f{i2}w3_{l2}", [NF, 128, D])
            off = (SB_TOP - 4096) if cur_kind == "ffn" else e_off
            tpre = at("w13pre", [128, 2, D], BF16, off)
            bb = bufs(2)
            s.op("pool", lambda e: e.dma_start(out=tpre[:, 0, :], in_=w1n[0]), W=[bb[0]], dsem=new_dsem("w13p"))
            s.op("pool", lambda e: e.dma_start(out=tpre[:, 1, :], in_=w3n[0]), W=[bb[1]], dsem=new_dsem("w13p"))
            hook["w13pre"] = (tpre, bb)
        elif ph[0] == "mix" and cur_kind == "ffn":
            l2 = ph[1]
            wglr_n = din(f"wglr_{l2}", [128, NC_ * 16])
            wgkv_n = din(f"wgkv_{l2}", [128, NC_ * 768])
            off = SB_TOP - 8704
            wglr_p = at("wglrp", [128, NC_, 16], BF16, off)
            wgv_p = at("wgvp", [128, NC_, 512], BF16, off + 256)
            bb = bufs(2)
            s.op("pool", lambda e: e.dma_start(out=wglr_p[:], in_=wglr_n.rearrange("p (k n) -> p k n", k=NC_)),
                 W=[bb[0]], dsem=new_dsem("wBp"))
            s.op("pool", lambda e: e.dma_start(out=wgv_p[:], in_=wgkv_n.rearrange("p (k n) -> p k n", k=NC_)[:, :, 256:768]),
                 W=[bb[1]], dsem=new_dsem("wBp"))
            hook["Bpre"] = (wglr_p, wgv_p, bb)

    def ffn(l, i):
        w1_d = din(f"f{i}w1_{l}", [NF, 128, D])
        w3_d = din(f"f{i}w3_{l}", [NF, 128, D])
        w2_d = din(f"f{i}w2_{l}", [DFF, D])
        s.fence()
        A.release(arena_base)
        W13_SLOTS = 3
        w13 = [A.alloc([128, 2, D], BF16, "w13") for _ in range(W13_SLOTS)]
        w13_b = [bufs(2) for _ in range(W13_SLOTS)]
        w13_sem = [[new_dsem("w13") for _ in range(2)] for _ in range(W13_SLOTS)]
        w2 = [A.alloc([128, GMAX, D], BF16, "w2") for _ in range(2)]
        w2_b = bufs(2)
        w2_sem = [new_dsem("w2") for _ in range(2)]
        gT = [A.alloc([128, GMAX, S], BF16, "gT") for _ in range(2)]
        gT_b = [[bufs(NTB) for _ in range(GMAX)] for _ in range(2)]
        silu_t = [A.alloc([128, TB], F32, "silu") for _ in range(2)]
        silu_b = bufs(2)
        ln_chunk, ln_flush = make_ln(l, 0 if i == 0 else 2, [A])
        pre13 = hook.pop("w13pre", None)
        cnt = {"w13": 0, "w2": 0, "psA": 0, "psY": 0, "silu": 0}
        m0 = 0
        for gi, G in enumerate(FFN_GROUPS):
            ms = list(range(m0, m0 + G))
            m0 += G
            gs = gi % 2
            ws = cnt["w2"] % 2
            cnt["w2"] += 1
            s.op("pool", lambda e, ws=ws, ms=ms, G=G: e.dma_start(
                out=w2[ws][:, 0:G, :],
                in_=w2_d[ms[0] * 128:(ms[0] + G) * 128, :].rearrange("(g p) n -> p g n", p=128)),
                W=[w2_b[ws]], dsem=w2_sem[ws])
            for ml, m in enumerate(ms):
                if m == 0 and pre13 is not None:
                    wt, wtb = pre13
                else:
                    slot = cnt["w13"] % W13_SLOTS
                    cnt["w13"] += 1
                    wt, wtb = w13[slot], w13_b[slot]
                    s.op("pool", lambda e, slot=slot, m=m: e.dma_start(out=w13[slot][:, 0, :], in_=w1_d[m]),
                         W=[w13_b[slot][0]], dsem=w13_sem[slot][0])
                    s.op("pool", lambda e, slot=slot, m=m: e.dma_start(out=w13[slot][:, 1, :], in_=w3_d[m]),
                         W=[w13_b[slot][1]], dsem=w13_sem[slot][1])
                for t in range(NTB):
                    sl = slice(t * TB, (t + 1) * TB)
                    pj = cnt["psA"] % 2
                    cnt["psA"] += 1
                    for which, bi in ((0, pj), (1, 2 + pj)):
                        for k in range(NC_):
                            s.op("pe", lambda e, wt=wt, which=which, k=k, sl=sl, bi=bi: e.matmul(
                                psum[bi][:], lhsT=wt[:, which, k * 128:(k + 1) * 128], rhs=xb[:, k, sl],
                                start=(k == 0), stop=(k == NC_ - 1)),
                                R=[wtb[which], xb_b[k][t]], W=pb_(bi))
                    sj = cnt["silu"] % 2
                    cnt["silu"] += 1
                    s.op("act", lambda e, pj=pj, sj=sj: e.activation(out=silu_t[sj][:], in_=psum[pj][:], func=AF.Silu),
                         R=pb_(pj), W=[silu_b[sj]])
                    s.op("dve", lambda e, pj=pj, sj=sj, gs=gs, ml=ml, sl=sl: e.tensor_tensor(
                        out=gT[gs][:, ml, sl], in0=psum[2 + pj][:], in1=silu_t[sj][:], op=ALU.mult),
                        R=pb_(2 + pj) + [silu_b[sj]], W=[gT_b[gs][ml][t]])
            last = (gi == len(FFN_GROUPS) - 1)
            order = [(dc, t) for t in range(NTB) for dc in range(NC_)] if last else [(dc, t) for dc in range(NC_) for t in range(NTB)]
            for dc, t in order:
                if True:
                    sl = slice(t * TB, (t + 1) * TB)
                    bi = 4 + cnt["psY"] % 2
                    cnt["psY"] += 1
                    for ml in range(G):
                        s.op("pe", lambda e, ws=ws, ml=ml, dc=dc, gs=gs, sl=sl, bi=bi, G=G: e.matmul(
                            psum[bi][:], lhsT=w2[ws][:, ml, dc * 128:(dc + 1) * 128], rhs=gT[gs][:, ml, sl],
                            start=(ml == 0), stop=(ml == G - 1)),
                            R=[w2_b[ws], gT_b[gs][ml][t]], W=pb_(bi))
                    s.op("dve", lambda e, bi=bi, dc=dc, sl=sl: e.scalar_tensor_tensor(
                        out=xs[:, dc, sl], in0=psum[bi][:], scalar=0.5, in1=xs[:, dc, sl],
                        op0=ALU.mult, op1=ALU.add),
                        R=pb_(bi) + [xs_b[dc][t]], W=[xs_b[dc][t]])
                    if last:
                        ln_chunk(dc, t)
        emit_prefetch("ffn")
        ln_flush()

    def mixer(l, stop=None):
        amask_d = din("amask", [8, 128, S])
        wglr_d = din(f"wglr_{l}", [128, NC_ * 16])
        wgqk_d = din(f"wgqk_{l}", [128, NC_ * 512])
        wgkv_d = din(f"wgkv_{l}", [128, NC_ * 768])
        wgr_d = din(f"wgr_{l}", [128, NC_ * 512])
        waqkv_d = din(f"waqkv_{l}", [4, 128, NC_ * 384])
        wgab_d = din(f"wgab_{l}", [NC_, 128, NC_ * 256])
        wap_d = din(f"wap_{l}", [NC_, 128, 4 * 128])
        wgp_d = din(f"wgp_{l}", [NC_, 128, 4 * 128])
        wo_d = din(f"wo_{l}", [NC_, 128, NC_ * 128])
        blk = amask_blocks
        preB = hook.pop("Bpre", None)
        s.fence()
        A.release(arena_base)
        o_gnT = A.alloc([128, 4, S], BF16, "o_gnT")
        o_gn_b = [bufs(NTB) for _ in range(4)]
        o_a_b = [bufs(NTB) for _ in range(8)]
        mix_base = A.mark()

        qT = A.alloc([128, 2, S], BF16, "qT")
        kT = A.alloc([128, 2, S], BF16, "kT")
        qT_b = [bufs(16) for _ in range(2)]
        kT_b = [bufs(16) for _ in range(2)]
        khat = A.alloc([128, 16, 256], BF16, "khat")
        khat_b = bufs(16)
        gv = A.alloc([128, 16, 512], BF16, "gv")
        gv_b = bufs(16)
        dec = A.alloc([128, 4, 32], F32, "dec")
        dec_b = bufs(NTB)
        gla_base = A.mark()
        wB_b = bufs(4)
        wgkv_v = wgkv_d.rearrange("p (k n) -> p k n", k=NC_)
        if preB is not None:
            wglr, wgv, pb2 = preB
            wB_b[0], wB_b[3] = pb2[0], pb2[1]
        else:
            wglr = A.alloc([128, NC_, 16], BF16, "wglr")
            wgv = A.alloc([128, NC_, 512], BF16, "wgv")
            s.op("pool", lambda e: e.dma_start(out=wglr[:], in_=wglr_d.rearrange("p (k n) -> p k n", k=NC_)),
                 W=[wB_b[0]], dsem=new_dsem("wB"))
            s.op("pool", lambda e: e.dma_start(out=wgv[:], in_=wgkv_v[:, :, 256:768]),
                 W=[wB_b[3]], dsem=new_dsem("wB"))
        wgqk = A.alloc([128, NC_, 512], BF16, "wgqk")
        wgk = A.alloc([128, NC_, 256], BF16, "wgk")
        s.op("pool", lambda e: e.dma_start(out=wgqk[:], in_=wgqk_d.rearrange("p (k n) -> p k n", k=NC_)),
             W=[wB_b[1]], dsem=new_dsem("wB"))
        s.op("pool", lambda e: e.dma_start(out=wgk[:], in_=wgkv_v[:, :, 0:256]),
             W=[wB_b[2]], dsem=new_dsem("wB"))
        glrT = [A.alloc([16, TB], F32, "glrT") for _ in range(2)]
        glrT_b = bufs(2)
        z_sb = [A.alloc([128, 256], F32, "z") for _ in range(2)]
        z_b = bufs(2)
        la_sb = [A.alloc([128, 256], F32, "la") for _ in range(2)]
        la_b = bufs(2)
        Eb = A.alloc([128, 2, TB], F32, "Eb")
        Einv = A.alloc([128, 2, TB], F32, "Einv")
        E_b, Einv_b = bufs(2), bufs(2)
        Ft = A.alloc([128, 4, 256], F32, "Ft")
        F_b = bufs(4)
        zc = 0
        pc = 0
        for tb in range(NTB):
            sl = slice(tb * TB, (tb + 1) * TB)
            gj = tb % 2
            for k in range(NC_):
                s.op("pe", lambda e, k=k, sl=sl: e.matmul(psum[0][0:16, :], lhsT=wglr[:, k, :], rhs=xb[:, k, sl],
                                                         start=(k == 0), stop=(k == NC_ - 1)),
                     R=[wB_b[0], xb_b[k][tb]], W=pb_(0))
            s.op("act", lambda e, gj=gj: e.copy(out=glrT[gj][:], in_=psum[0][0:16, :]), R=pb_(0), W=[glrT_b[gj]])
            for jj in range(4):
                j = tb * 4 + jj
                zi = zc % 2
                zc += 1
                bz = 1 + zi
                s.op("pe", lambda e, gj=gj, jj=jj, bz=bz: e.matmul(
                    psum[bz][:, 0:256], lhsT=glrT[gj][:, jj * 128:(jj + 1) * 128], rhs=wgu[:, l, :],
                    start=True, stop=True),
                    R=[glrT_b[gj], mixc_b], W=pb_(bz, 0, 256))
                s.op("dve", lambda e, zi=zi, bz=bz: e.tensor_tensor(out=z_sb[zi][:], in0=psum[bz][:, 0:256], in1=bgu[:, l, :], op=ALU.add),
                     R=pb_(bz, 0, 256) + [mixc_b], W=[z_b[zi]])
                tsl = slice(j * 128, (j + 1) * 128)
                bi2 = 6 + pc % 2
                pc += 1
                for k in range(NC_):
                    s.op("pe", lambda e, k=k, tsl=tsl, bi2=bi2: e.matmul(
                        psum[bi2][:], lhsT=xb[:, k, tsl], rhs=wgv[:, k, :],
                        start=(k == 0), stop=(k == NC_ - 1)),
                        R=[wB_b[3], xb_b[k][tb]], W=pb_(bi2))
                s.op("dve", lambda e, j=j, bi2=bi2: e.tensor_copy(out=gv[:, j, :], in_=psum[bi2][:]),
                     R=pb_(bi2), W=[gv_b[j]])
                s.op("act", lambda e, zi=zi: e.activation(out=z_sb[zi][:], in_=z_sb[zi][:], func=AF.Exp, scale=-1.0),
                     R=[z_b[zi]], W=[z_b[zi]])
                s.op("act", lambda e, zi=zi: e.activation(out=la_sb[zi][:], in_=z_sb[zi][:], func=AF.Ln, bias=1.0, scale=1.0),
                     R=[z_b[zi]], W=[la_b[zi]])
                for ch in range(2):
                    s.op("pe", lambda e, zi=zi, ch=ch, jj=jj: e.matmul(
                        psum[3 + ch][:, jj * 128:(jj + 1) * 128], lhsT=la_sb[zi][:, ch * 128:(ch + 1) * 128], rhs=Umat,
                        start=True, stop=True),
                        R=[la_b[zi], consts_b], W=[pq[3 + ch][jj]])
                s.op("pe", lambda e, zi=zi: e.matmul(psum[5][:, 0:256], lhsT=Lmat, rhs=la_sb[zi][:], start=True, stop=True),
                     R=[la_b[zi], consts_b], W=pb_(5, 0, 256))
                s.op("act", lambda e, jj=jj: e.activation(out=Ft[:, jj, :], in_=psum[5][:, 0:256], func=AF.Exp),
                     R=pb_(5, 0, 256), W=[F_b[jj]])
            for ch in range(2):
                s.op("act", lambda e, ch=ch: e.activation(out=Eb[:, ch, :], in_=psum[3 + ch][:], func=AF.Exp),
                     R=pb_(3 + ch), W=[E_b[ch]])
                s.op("act", lambda e, ch=ch: e.activation(out=Einv[:, ch, :], in_=psum[3 + ch][:], func=AF.Exp, scale=-1.0),
                     R=pb_(3 + ch), W=[Einv_b[ch]])
            for dup in range(2):
                s.op("dve", lambda e, tb=tb, dup=dup: e.tensor_copy(
                    out=dec[:].rearrange("p (c d) n -> p c d n", d=2)[:, :, dup, tb * 8:(tb + 1) * 8], in_=Eb[:, :, 63::64]),
                    R=E_b, W=[dec_b[tb]])
            for m in range(4):
                ch = m % 2
                bi = 6 + pc % 2
                pc += 1
                for k in range(NC_):
                    s.op("pe", lambda e, m=m, k=k, sl=sl, bi=bi: e.matmul(
                        psum[bi][:], lhsT=wgqk[:, k, m * 128:(m + 1) * 128], rhs=xb[:, k, sl],
                        start=(k == 0), stop=(k == NC_ - 1)),
                        R=[wB_b[1], xb_b[k][tb]], W=pb_(bi))
                if m < 2:
                    s.op("dve", lambda e, ch=ch, sl=sl, bi=bi: e.scalar_tensor_tensor(
                        out=qT[:, ch, sl], in0=psum[bi][:], scalar=0.125, in1=Eb[:, ch, :], op0=ALU.mult, op1=ALU.mult),
                        R=pb_(bi) + [E_b[ch]], W=qT_b[ch][tb * 4:(tb + 1) * 4])
                else:
                    s.op("dve", lambda e, ch=ch, sl=sl, bi=bi: e.tensor_tensor(
                        out=kT[:, ch, sl], in0=psum[bi][:], in1=Einv[:, ch, :], op=ALU.mult),
                        R=pb_(bi) + [Einv_b[ch]], W=kT_b[ch][tb * 4:(tb + 1) * 4])
            for jj in range(4):
                j = tb * 4 + jj
                tsl = slice(j * 128, (j + 1) * 128)
                bi = 1 + jj % 2
                for k in range(NC_):
                    s.op("pe", lambda e, k=k, tsl=tsl, bi=bi: e.matmul(
                        psum[bi][:, 0:256], lhsT=xb[:, k, tsl], rhs=wgk[:, k, :],
                        start=(k == 0), stop=(k == NC_ - 1)),
                        R=[wB_b[2], xb_b[k][tb]], W=pb_(bi, 0, 256))
                s.op("dve", lambda e, j=j, jj=jj, bi=bi: e.tensor_tensor(
                    out=khat[:, j, :], in0=psum[bi][:, 0:256], in1=Ft[:, jj, :], op=ALU.mult),
                    R=pb_(bi, 0, 256) + [F_b[jj]], W=[khat_b[j]])

        if stop == "B":
            return
        tap("qT", qT, [b for bb in qT_b for b in bb])
        tap("kT", kT, [b for bb in kT_b for b in bb])
        tap("khat", khat, khat_b)
        tap("gv", gv, gv_b)
        tap("dec", dec, dec_b)
        s.fence()
        A.release(gla_base)
        wgr = A.alloc([128, NC_, 512], BF16, "wgr")
        wgr_b = Buf()
        s.op("pool", lambda e: e.dma_start(out=wgr[:], in_=wgr_d.rearrange("p (k n) -> p k n", k=NC_)),
             W=[wgr_b], dsem=new_dsem("wgr"))
        ograw = [A.alloc([128, 4, TB], F32, "ograw") for _ in range(2)]
        ograw_b = [[bufs(4) for _ in range(4)] for _ in range(2)]
        st_f = A.alloc([128, 4, 128], F32, "st_f")
        st_b = A.alloc([128, 4, 128], BF16, "st_b")
        stf_b, stb_b = bufs(4), bufs(4)
        ATt = [A.alloc([128, 4, 128], BF16, "AT") for _ in range(2)]
        AT_b = [bufs(4) for _ in range(2)]
        sg = [A.alloc([128, TB], F32, "sg") for _ in range(4)]
        sg_b = bufs(4)
        gsq = A.alloc([128, TB], F32R, "gsq")
        gsq_b = Buf()
        gm2 = A.alloc([128, TB], F32, "gm2")
        grs = A.alloc([128, TB], F32, "grs")
        gm2_b, grs_b = Buf(), Buf()
        s.op("dve", lambda e: e.memset(st_f[:], 0.0), W=stf_b)
        s.op("dve", lambda e: e.memset(st_b[:], 0.0), W=stb_b)
        bS0, bS1 = 4, 5
        HORD = (0, 2, 1, 3)

        def emit_AT_mm(j):
            bA = j % 2
            tok = slice(j * 128, (j + 1) * 128)
            prev = None
            for h in HORD:
                ch, pb = h // 2, (h % 2) * 64
                hs = slice(h * 128, (h + 1) * 128)
                prev = s.op("pe", lambda e, ch=ch, pb=pb, hs=hs, tok=tok, bA=bA: e.matmul(
                    psum[bA][:, hs], lhsT=kT[pb:pb + 64, ch, tok], rhs=qT[pb:pb + 64, ch, tok], start=True, stop=True),
                    R=[kT_b[ch][j], qT_b[ch][j]], W=[pq[bA][h]], after=[prev] if h == 1 else [])

        def emit_AT_mask(j):
            bA = j % 2
            aj = j % 2
            s.op("dve", lambda e, bA=bA, aj=aj: e.tensor_tensor(
                out=ATt[aj][:], in0=psum[bA][:].rearrange("p (h n) -> p h n", h=4),
                in1=Mblk.unsqueeze(1).broadcast_to([128, 4, 128]), op=ALU.mult),
                R=[pq[bA][0], consts_b], W=AT_b[aj])

        def emit_dS(j, half):
            bS = bS0 if half == 0 else bS1
            rows = slice(half * 64, half * 64 + 64)
            for h in range(4):
                ch = h // 2
                hs = slice(h * 128, (h + 1) * 128)
                s.op("pe", lambda e, ch=ch, hs=hs, rows=rows, bS=bS, j=j: e.matmul(
                    psum[bS][:, hs], lhsT=khat[rows, j, ch * 128:(ch + 1) * 128], rhs=gv[rows, j, hs],
                    start=True, stop=True),
                    R=[khat_b[j], gv_b[j]], W=[pq[bS][h]])

        def emit_decay(c):
            s.op("dve", lambda e, c=c: e.tensor_tensor(
                out=st_f[:], in0=st_f[:], in1=dec[:, :, c:c + 1].broadcast_to([128, 4, 128]), op=ALU.mult),
                R=stf_b + [dec_b[c // 8]], W=stf_b)

        def emit_update(j, half):
            bS = bS0 if half == 0 else bS1
            c = 2 * j + half
            s.op("dve", lambda e, bS=bS: e.tensor_tensor(
                out=st_f[:], in0=st_f[:], in1=psum[bS][:].rearrange("p (h n) -> p h n", h=4), op=ALU.add),
                R=stf_b + [pq[bS][0]], W=stf_b)
            s.op("dve", lambda e: e.tensor_copy(out=st_b[:], in_=st_f[:]), R=stf_b, W=stb_b)
            if c + 1 < 32:
                emit_decay(c + 1)

        gtasks = []
        gstate = {"B": None}

        def gate_batch(tb):
            sl = slice(tb * TB, (tb + 1) * TB)
            for h in range(4):
                bi = 6 + h % 2
                for k in range(NC_):
                    s.op("pe", lambda e, k=k, h=h, sl=sl, bi=bi: e.matmul(
                        psum[bi][:], lhsT=wgr[:, k, h * 128:(h + 1) * 128], rhs=xb[:, k, sl],
                        start=(k == 0), stop=(k == NC_ - 1)),
                        R=[wgr_b, xb_b[k][tb]], W=pb_(bi))
                s.op("act", lambda e, h=h, bi=bi: e.activation(out=sg[h][:], in_=psum[bi][:], func=AF.Silu),
                     R=pb_(bi), W=[sg_b[h]])
            for h in range(4):
                gtasks.append((tb, h))

        def gn_A(tb, h):
            ob = tb % 2
            og = ograw[ob][:, h, :]
            ogb = ograw_b[ob][h]
            s.op("act", lambda e, og=og: e.activation(out=gsq[:], in_=og, func=AF.Square), R=ogb, W=[gsq_b])
            s.op("pe", lambda e, og=og: e.matmul(psum[6][:], lhsT=gones, rhs=og, start=True, stop=True),
                 R=ogb + [consts_b], W=pb_(6))
            s.op("pe", lambda e: e.matmul(psum[7][:], lhsT=gones_r[:], rhs=gsq[:], start=True, stop=True),
                 R=[gsq_b, onesr_b], W=pb_(7))
            s.op("act", lambda e: e.activation(out=gm2[:], in_=psum[6][:], func=AF.Square), R=pb_(6), W=[gm2_b])
            s.op("dve", lambda e, og=og: e.tensor_tensor(out=og, in0=og, in1=psum[6][:], op=ALU.subtract),
                 R=ogb + pb_(6), W=ogb)
            s.op("dve", lambda e: e.tensor_tensor(out=grs[:], in0=psum[7][:], in1=gm2[:], op=ALU.subtract),
                 R=pb_(7) + [gm2_b], W=[grs_b])
            s.op("act", lambda e: e.activation(out=gm2[:], in_=grs[:], func=AF.Ln, bias=LN_EPS, scale=1.0),
                 R=[grs_b], W=[gm2_b])
            s.op("act", lambda e: e.activation(out=grs[:], in_=gm2[:], func=AF.Exp, scale=-0.5), R=[gm2_b], W=[grs_b])

        def gn_B(tb, h):
            ob = tb % 2
            og = ograw[ob][:, h, :]
            ogb = ograw_b[ob][h]
            col = l * 4 + h
            sl = slice(tb * TB, (tb + 1) * TB)
            s.op("pool", lambda e, og=og: e.tensor_tensor(out=og, in0=og, in1=grs[:], op=ALU.mult),
                 R=ogb + [grs_b], W=ogb)
            s.op("act", lambda e, og=og, col=col: e.activation(out=og, in_=og, func=AF.Identity,
                                                             scale=gng[:, col:col + 1], bias=gnb[:, col:col + 1]),
                 R=ogb + [mixc_b], W=ogb)
            s.op("pool", lambda e, og=og, h=h, sl=sl: e.tensor_tensor(out=o_gnT[:, h, sl], in0=og, in1=sg[h][:], op=ALU.mult),
                 R=ogb + [sg_b[h]], W=[o_gn_b[h][tb]])

        def gn_slot():
            if gstate["B"] is not None:
                gn_B(*gstate["B"])
                gstate["B"] = None
            if gtasks:
                t_ = gtasks.pop(0)
                gn_A(*t_)
                gstate["B"] = t_

        def gn_flush():
            while gtasks or gstate["B"] is not None:
                gn_slot()

        emit_AT_mm(0)
        emit_dS(0, 0)
        emit_dS(0, 1)
        emit_AT_mask(0)
        for j in range(16):
            tb, jj = j // 4, j % 4
            ob = tb % 2
            aj = j % 2
            bO = 2 + aj
            t0 = slice(j * 128, j * 128 + 64)
            t1_ = slice(j * 128 + 64, (j + 1) * 128)
            for h in range(4):
                hs = slice(h * 128, (h + 1) * 128)
                s.op("pe", lambda e, h=h, hs=hs, bO=bO, aj=aj, j=j: e.matmul(
                    psum[bO][:, hs], lhsT=gv[:, j, hs], rhs=ATt[aj][:, h, :], start=(h == 0), stop=False, skip_group_check=True),
                    R=[gv_b[j], AT_b[aj][h]], W=[pq[bO][h]])
            prev = None
            for h in HORD:
                ch, pb = h // 2, (h % 2) * 64
                prev = s.op("pe", lambda e, h=h, ch=ch, pb=pb, bO=bO, t0=t0: e.matmul(
                    psum[bO][:, h * 128:h * 128 + 64], lhsT=st_b[pb:pb + 64, h, :], rhs=qT[pb:pb + 64, ch, t0],
                    start=False, stop=False, skip_group_check=True),
                    R=[stb_b[h], qT_b[ch][j]], W=[pq[bO][h]], after=[prev] if h == 1 else [])
            if j + 1 < 16:
                emit_AT_mm(j + 1)
            emit_update(j, 0)
            prev = None
            for h in HORD:
                ch, pb = h // 2, (h % 2) * 64
                prev = s.op("pe", lambda e, h=h, ch=ch, pb=pb, bO=bO, t1_=t1_: e.matmul(
                    psum[bO][:, h * 128 + 64:(h + 1) * 128], lhsT=st_b[pb:pb + 64, h, :], rhs=qT[pb:pb + 64, ch, t1_],
                    start=False, stop=True, skip_group_check=True),
                    R=[stb_b[h], qT_b[ch][j]], W=[pq[bO][h]], after=[prev] if h == 1 else [])
            if j + 1 < 16:
                emit_AT_mask(j + 1)
                emit_dS(j + 1, 0)
            s.op("act", lambda e, bO=bO, ob=ob, jj=jj: e.copy(
                out=ograw[ob][:, :, jj * 128:(jj + 1) * 128], in_=psum[bO][:].rearrange("p (h n) -> p h n", h=4)),
                R=[pq[bO][0]], W=[ograw_b[ob][h][jj] for h in range(4)])
            emit_update(j, 1)
            if j + 1 < 16:
                emit_dS(j + 1, 1)
            if j == 0:
                tap("st1", st_f, stf_b)
            gn_slot()
            if jj == 3:
                gn_flush()
                if tb == 0:
                    tap("ograw0", ograw[0], [b for bb in ograw_b[0] for b in bb])
                gate_batch(tb)
        gn_flush()

        if stop == "D":
            return
        tap("o_gnT", o_gnT, [b for bb in o_gn_b for b in bb])
        s.fence()
        A.release(mix_base)
        o_aT = A.alloc([128, 4, S], BF16, "o_aT")
        mix_base = A.mark()
        waqkv = A.alloc([128, NC_, 384], BF16, "waqkv")
        waqkv_b = Buf()
        waqkv_sem = new_dsem("waqkv")
        aqT = [[A.alloc([128, S], BF16, "aqT") for _ in range(2)] for _ in range(2)]
        akT = [A.alloc([128, S], BF16, "akT") for _ in range(2)]
        aqT_b = [bufs(NTB) for _ in range(2)]
        aqz_b = [bufs(2) for _ in range(2)]
        for wi_ in range(2):
            for hh_ in range(2):
                oth = slice(64, 128) if hh_ == 0 else slice(0, 64)
                s.op("pool", lambda e, wi_=wi_, hh_=hh_, oth=oth: e.memset(aqT[wi_][hh_][oth, :], 0.0), W=[aqz_b[wi_][hh_]])
        akT_b = [bufs(16) for _ in range(2)]
        Vp = [A.alloc([128, 16, 128], BF16, "Vp") for _ in range(2)]
        Vp_b = [bufs(16) for _ in range(2)]
        mask_s = [A.alloc([128, S], F32, "mask") for _ in range(2)]
        mask_b = bufs(2)
        mask_sem = [new_dsem("mask") for _ in range(2)]
        NE = 4
        LOOK = 3
        Et = [A.alloc([128, TB], F32, "Et") for _ in range(NE)]
        Pt = [A.alloc([128, TB], BF16, "Pt") for _ in range(NE)]
        Et_b, Pt_b = bufs(NE), bufs(NE)
        dcp = [A.alloc([128, TB], F32, "dcp") for _ in range(1)] * 2
        dcp_b = bufs(1) * 2
        onesb = A.alloc([128, 128], BF16, "onesb")
        onesb_b = Buf()
        s.op("pool", lambda e: e.memset(onesb[:], 1.0), W=[onesb_b])
        SB = (0, 1, 2, 3)
        stepc = 0
        def load_waqkv(ch):
            s.op("pool", lambda e, ch=ch: e.dma_start(out=waqkv[:], in_=waqkv_d[ch].rearrange("p (k n) -> p k n", k=NC_)),
                 W=[waqkv_b], dsem=waqkv_sem)

        def load_mask(h):
            mi = h % 2
            s.op("sp", lambda e, mi=mi, h=h: e.dma_start(out=mask_s[mi][:], in_=amask_d[h]),
                 W=[mask_b[mi]], dsem=mask_sem[mi])

        load_waqkv(0)
        load_mask(0)
        load_mask(1)
        for ch in range(4):
            wi = ch % 2
            for tb in range(NTB):
                sl = slice(tb * TB, (tb + 1) * TB)
                for which in range(2):
                    bi = SB[(2 * tb + which) % 4]
                    for k in range(NC_):
                        s.op("pe", lambda e, k=k, which=which, sl=sl, bi=bi: e.matmul(
                            psum[bi][:], lhsT=waqkv[:, k, which * 128:(which + 1) * 128], rhs=xb[:, k, sl],
                            start=(k == 0), stop=(k == NC_ - 1)),
                            R=[waqkv_b, xb_b[k][tb]], W=pb_(bi))
                    if which == 0:
                        s.op("act", lambda e, wi=wi, sl=sl, bi=bi: e.mul(aqT[wi][0][0:64, sl], psum[bi][0:64, :], 0.125),
                             R=pb_(bi), W=[aqT_b[wi][tb]])
                        s.op("act", lambda e, wi=wi, sl=sl, bi=bi: e.mul(aqT[wi][1][64:128, sl], psum[bi][64:128, :], 0.125),
                             R=pb_(bi), W=[aqT_b[wi][tb]])
                    else:
                        s.op("dve", lambda e, wi=wi, sl=sl, bi=bi: e.tensor_copy(out=akT[wi][:, sl], in_=psum[bi][:]),
                             R=pb_(bi), W=akT_b[wi][tb * 4:(tb + 1) * 4])
            for j4 in range(4):
                bi = SB[j4 % 4]
                for jj in range(4):
                    j = j4 * 4 + jj
                    tsl = slice(j * 128, (j + 1) * 128)
                    for k in range(NC_):
                        s.op("pe", lambda e, k=k, tsl=tsl, bi=bi, jj=jj: e.matmul(
                            psum[bi][:, jj * 128:(jj + 1) * 128], lhsT=xb[:, k, tsl], rhs=waqkv[:, k, 256:384],
                            start=(k == 0 and jj == 0), stop=(k == NC_ - 1), skip_group_check=True),
                            R=[waqkv_b, xb_b[k][j4]], W=pb_(bi))
                s.op("act", lambda e, wi=wi, j4=j4, bi=bi: e.copy(
                    out=Vp[wi][:, j4 * 4:(j4 + 1) * 4, :], in_=psum[bi][:].rearrange("p (a b) -> p a b", a=4)),
                    R=pb_(bi), W=Vp_b[wi][j4 * 4:(j4 + 1) * 4])
            if ch + 1 < 4:
                load_waqkv(ch + 1)
            steps = []
            for hh in range(2):
                h = 2 * ch + hh
                for qp in range(NTB):
                    kbs = [kb for kb in range(4 * qp + 4) if blk[h][kb][qp]]
                    for ki, kb in enumerate(kbs):
                        steps.append((hh, h, qp, kb, ki, len(kbs)))
            info = {}
            for idx in range(len(steps) + LOOK):
                if idx < len(steps):
                    hh, h, qp, kb, ki, nk = steps[idx]
                    pb = hh * 64
                    mi = h % 2
                    q0 = qp * TB
                    n0 = max(q0, 128 * kb)
                    n = q0 + TB - n0
                    bS = SB[stepc % 4]
                    ei = stepc % NE
                    stepc += 1
                    info[idx] = (ei, n0, n)
                    s.op("pe", lambda e, wi=wi, hh=hh, kb=kb, n0=n0, n=n, bS=bS: e.matmul(
                        psum[bS][:, 0:n], lhsT=akT[wi][:, kb * 128:(kb + 1) * 128],
                        rhs=aqT[wi][hh][:, n0:n0 + n], start=True, stop=True),
                        R=[akT_b[wi][kb], aqT_b[wi][qp], aqz_b[wi][hh]], W=pb_(bS))
                    s.op("act", lambda e, ei=ei, bS=bS, n=n: e.activation(out=Et[ei][:, 0:n], in_=psum[bS][:, 0:n], func=AF.Exp),
                         R=pb_(bS), W=[Et_b[ei]])
                    mo = n0 - 128 * kb
                    s.op("dve", lambda e, ei=ei, mi=mi, mo=mo, n=n: e.tensor_tensor(
                        out=Pt[ei][:, 0:n], in0=Et[ei][:, 0:n], in1=mask_s[mi][:, mo:mo + n], op=ALU.mult),
                        R=[Et_b[ei], mask_b[mi]], W=[Pt_b[ei]])
                    if h + 2 < 8 and (idx + 1 == len(steps) or steps[idx + 1][1] != h):
                        load_mask(h + 2)
                pidx = idx - LOOK
                if pidx >= 0:
                    hh, h, qp, kb, ki, nk = steps[pidx]
                    ei, n0, n = info[pidx]
                    pb = hh * 64
                    q0 = qp * TB
                    par = (h * NTB + qp) % 2
                    bO, bD = 4 + par, 6 + par
                    cs = slice(n0 - q0, n0 - q0 + n)
                    s.op("pe", lambda e, wi=wi, kb=kb, ei=ei, n=n, cs=cs, bO=bO, ki=ki, nk=nk: e.matmul(
                        psum[bO][:, cs], lhsT=Vp[wi][:, kb, :], rhs=Pt[ei][:, 0:n],
                        start=(ki == 0), stop=(ki == nk - 1), skip_group_check=True),
                        R=[Vp_b[wi][kb], Pt_b[ei]], W=pb_(bO))
                    s.op("pe", lambda e, ei=ei, n=n, cs=cs, bD=bD, ki=ki, nk=nk: e.matmul(
                        psum[bD][:, cs], lhsT=onesb[:], rhs=Pt[ei][:, 0:n],
                        start=(ki == 0), stop=(ki == nk - 1), skip_group_check=True),
                        R=[onesb_b, Pt_b[ei]], W=pb_(bD))
                    if ki == nk - 1:
                        ps_ = slice(pb, pb + 64)
                        s.op("act", lambda e, par=par, bD=bD, ps_=ps_: e.activation(out=dcp[par][ps_, :], in_=psum[bD][ps_, :], func=AF.Ln),
                             R=pb_(bD), W=[dcp_b[par]])
                        s.op("act", lambda e, par=par, ps_=ps_: e.activation(out=dcp[par][ps_, :], in_=dcp[par][ps_, :], func=AF.Exp, scale=-1.0),
                             R=[dcp_b[par]], W=[dcp_b[par]])
                        s.op("dve", lambda e, ch=ch, ps_=ps_, q0=q0, bO=bO, par=par: e.tensor_tensor(
                            out=o_aT[ps_, ch, q0:q0 + TB], in0=psum[bO][ps_, :], in1=dcp[par][ps_, :], op=ALU.mult),
                            R=pb_(bO) + [dcp_b[par]], W=[o_a_b[h][qp]])

        if stop == "C":
            return
        tap("o_aT", o_aT, [b for bb in o_a_b for b in bb])
        s.fence()
        A.release(mix_base)
        e_low_top = A.mark()
        mT = A.alloc([128, NC_, S], BF16, "mT")
        mT_b = [bufs(NTB) for _ in range(NC_)]
        e_base = A.mark()
        wE = [A.alloc([128, NC_, 256], BF16, "wgab") for _ in range(2)]
        wap = [A.alloc([128, 4, 128], BF16, "wap") for _ in range(2)]
        wgp = [A.alloc([128, 4, 128], BF16, "wgp") for _ in range(2)]
        wE_b = [bufs(3) for _ in range(2)]
        wE_sem = [[new_dsem("wE") for _ in range(3)] for _ in range(2)]
        sa = [A.alloc([128, TB], F32, "sa") for _ in range(2)]
        sbt = [A.alloc([128, TB], F32, "sbt") for _ in range(2)]
        sa_b, sbt_b = bufs(2), bufs(2)
        wo = A.alloc([128, NC_, NC_, 128], BF16, "wo")
        wo_b = bufs(NC_)
        e_top = A.mark()
        ecn = 0

        def load_wE(dc):
            wi = dc % 2
            s.op("pool", lambda e, wi=wi, dc=dc: e.dma_start(out=wE[wi][:], in_=wgab_d[dc].rearrange("p (k n) -> p k n", k=NC_)),
                 W=[wE_b[wi][0]], dsem=wE_sem[wi][0])
            s.op("pool", lambda e, wi=wi, dc=dc: e.dma_start(out=wap[wi][:], in_=wap_d[dc].rearrange("p (k n) -> p k n", k=4)),
                 W=[wE_b[wi][1]], dsem=wE_sem[wi][1])
            s.op("pool", lambda e, wi=wi, dc=dc: e.dma_start(out=wgp[wi][:], in_=wgp_d[dc].rearrange("p (k n) -> p k n", k=4)),
                 W=[wE_b[wi][2]], dsem=wE_sem[wi][2])

        load_wE(0)
        for dc in range(NC_):
            wi = dc % 2
            if dc + 1 < NC_:
                load_wE(dc + 1)
            if dc >= 4:
                for dco in (2 * (dc - 4), 2 * (dc - 4) + 1):
                    s.op("pool", lambda e, dco=dco: e.dma_start(out=wo[:, dco, :, :], in_=wo_d[dco].rearrange("p (k n) -> p k n", k=NC_)),
                         W=[wo_b[dco]], dsem=new_dsem("wo"))
            for tb in range(NTB):
                sl = slice(tb * TB, (tb + 1) * TB)
                pj = ecn % 2
                ecn += 1
                bGA, bGB, bPA, bPG = 0 + pj, 2 + pj, 4 + pj, 6 + pj
                for which, bi in ((0, bGA), (1, bGB)):
                    for k in range(NC_):
                        s.op("pe", lambda e, wi=wi, which=which, k=k, sl=sl, bi=bi: e.matmul(
                            psum[bi][:], lhsT=wE[wi][:, k, which * 128:(which + 1) * 128], rhs=xb[:, k, sl],
                            start=(k == 0), stop=(k == NC_ - 1)),
                            R=[wE_b[wi][0], xb_b[k][tb]], W=pb_(bi))
                for c in range(4):
                    s.op("pe", lambda e, wi=wi, c=c, sl=sl, bPA=bPA: e.matmul(
                        psum[bPA][:], lhsT=wap[wi][:, c, :], rhs=o_aT[:, c, sl], start=(c == 0), stop=(c == 3)),
                        R=[wE_b[wi][1], o_a_b[2 * c][tb], o_a_b[2 * c + 1][tb]], W=pb_(bPA))
                for c in range(4):
                    s.op("pe", lambda e, wi=wi, c=c, sl=sl, bPG=bPG: e.matmul(
                        psum[bPG][:], lhsT=wgp[wi][:, c, :], rhs=o_gnT[:, c, sl], start=(c == 0), stop=(c == 3)),
                        R=[wE_b[wi][2], o_gn_b[c][tb]], W=pb_(bPG))
                s.op("act", lambda e, pj=pj, bGA=bGA: e.activation(out=sa[pj][:], in_=psum[bGA][:], func=AF.Sigmoid),
                     R=pb_(bGA), W=[sa_b[pj]])
                s.op("act", lambda e, pj=pj, bGB=bGB: e.activation(out=sbt[pj][:], in_=psum[bGB][:], func=AF.Sigmoid),
                     R=pb_(bGB), W=[sbt_b[pj]])
                s.op("dve", lambda e, pj=pj, bPA=bPA: e.tensor_tensor(out=sa[pj][:], in0=psum[bPA][:], in1=sa[pj][:], op=ALU.mult),
                     R=pb_(bPA) + [sa_b[pj]], W=[sa_b[pj]])
                s.op("dve", lambda e, pj=pj, bPG=bPG: e.tensor_tensor(out=sbt[pj][:], in0=psum[bPG][:], in1=sbt[pj][:], op=ALU.mult),
                     R=pb_(bPG) + [sbt_b[pj]], W=[sbt_b[pj]])
                s.op("pool", lambda e, pj=pj, dc=dc, sl=sl: e.tensor_tensor(out=mT[:, dc, sl], in0=sa[pj][:], in1=sbt[pj][:], op=ALU.add),
                     R=[sa_b[pj], sbt_b[pj]], W=[mT_b[dc][tb]])
        tap("mT", mT, [b for bb in mT_b for b in bb])
        s.fence()
        A.release(e_top)
        Alow = Arena(nc, arena_base, e_low_top)
        ln_chunk, ln_flush = make_ln(l, 1, [Alow, A])
        yc = 0
        for tb in range(NTB):
            sl = slice(tb * TB, (tb + 1) * TB)
            for dc in range(NC_):
                bi = 4 + yc % 2
                yc += 1
                for k in range(NC_):
                    s.op("pe", lambda e, dc=dc, k=k, sl=sl, bi=bi: e.matmul(
                        psum[bi][:], lhsT=wo[:, dc, k, :], rhs=mT[:, k, sl], start=(k == 0), stop=(k == NC_ - 1)),
                        R=[wo_b[dc], mT_b[k][tb]], W=pb_(bi))
                s.op("dve", lambda e, bi=bi, dc=dc, sl=sl: e.tensor_tensor(
                    out=xs[:, dc, sl], in0=psum[bi][:], in1=xs[:, dc, sl], op=ALU.add),
                    R=pb_(bi) + [xs_b[dc][tb]], W=[xs_b[dc][tb]])
                ln_chunk(dc, tb)
        emit_prefetch("mix", e_base)
        ln_flush()

    amask_blocks = _mask_blocks()
    out_ops = []
    lastp = phases[-1]
    if lastp[0] == "ffn":
        final_ln = (lastp[1], 0 if lastp[2] == 0 else 2)
    elif lastp[0] == "mix" and len(lastp) == 2:
        final_ln = (lastp[1], 1)
    else:
        final_ln = None
    for pi, ph in enumerate(phases):
        nxt["ph"] = phases[pi + 1] if pi + 1 < len(phases) else None
        if ph[0] == "ffn":
            ffn(ph[1], ph[2])
        elif ph[0] == "mix":
            mixer(ph[1], ph[2] if len(ph) > 2 else None)
        else:
            raise ValueError(ph)

    if not out_ops:
        for c in range(NC_):
            for t in range(NTB):
                sl = slice(t * TB, (t + 1) * TB)
                s.op("dve", lambda e, c=c, sl=sl: e.tensor_scalar_mul(out=xs[:, c, sl], in0=xs[:, c, sl], scalar1=1.0 / ALPHA),
                     R=[xs_b[c][t]], W=[xs_b[c][t]])
        for c in range(NC_):
            out_ops.append(s.op("sp", lambda e, c=c: e.dma_start(out=yT_d[c * 128:(c + 1) * 128, :], in_=xs[:, c, :]),
                                R=xs_b[c], dsem=out_sem))
    fin = Buf()
    fin.w = out_ops[-1]
    s.op("sp", lambda e: e.nop(), R=[fin])

    s.finalize()
    from contextlib import ExitStack
    with ExitStack() as ctx:
        esem = {}
        for en in Sched.ENGS:
            esem[en] = ctx.enter_context(nc.semaphore(f"sem_{en}"))
        dsems = {}
        for nm in dsem_names:
            dsems[nm] = ctx.enter_context(nc.semaphore(f"d_{nm}"))
        with nc.Block() as block:
            @block.tensor
            def _(e):
                s.replay("pe", e, esem, dsems)

            @block.scalar
            def _(e):
                s.replay("act", e, esem, dsems)

            @block.vector
            def _(e):
                s.replay("dve", e, esem, dsems)

            @block.gpsimd
            def _(e):
                s.replay("pool", e, esem, dsems)

            @block.sync
            def _(e):
                s.replay("sp", e, esem, dsems)
    return nc


_MASK = None


def _alibi_mask():
    global _MASK
    if _MASK is None:
        d = np.arange(S)[None, :] - np.arange(128)[:, None]
        mult = ((d <= 128).astype(np.float64) + ((d % 4 == 0) & (d <= 512)) + ((d % 16 == 0) & (d <= 2048)))
        mult = np.where(d >= 0, mult, 0.0)
        slopes = np.exp2(-8.0 * np.arange(1, 9) / 8.0)
        m = mult[None] * np.exp(-slopes[:, None, None] * np.maximum(d, 0)[None])
        m = np.where(m < 1e-37, 0.0, m)
        _MASK = np.ascontiguousarray(m.astype(np.float32))
    return _MASK


def _mask_blocks():
    m = _alibi_mask()
    blk = [[[False] * NTB for _ in range(16)] for _ in range(8)]
    for h in range(8):
        for kb in range(16):
            for qp in range(NTB):
                n0 = max(qp * TB, 128 * kb)
                n1 = qp * TB + TB
                if n1 <= n0:
                    continue
                blk[h][kb][qp] = bool(m[h][:, n0 - 128 * kb:n1 - 128 * kb].any())
    return blk


def _consts():
    s_ = np.arange(128)[:, None]
    t_ = np.arange(128)[None, :]
    same = (s_ // 64) == (t_ // 64)
    U = np.where(same & (s_ <= t_), -1.0 / 16.0, 0.0)
    L = np.where(same & (s_ > t_), -1.0 / 16.0, 0.0)
    M = np.where(same & (s_ <= t_), 1.0, 0.0)
    o1 = np.full((128, 128), 1.0 / D)
    o2 = np.full((128, 128), 1.0 / 128.0)
    return np.ascontiguousarray(np.concatenate([U, L, M, o1, o2], axis=1).astype(np.float32))


def _lay_w13(w):
    return np.ascontiguousarray(w.reshape(NC_, 128, NF, 128).transpose(2, 1, 0, 3).reshape(NF, 128, D))


def _lay_ln(v):
    return np.ascontiguousarray(v.reshape(DEPTH, 3, NC_, 128).transpose(3, 0, 1, 2).reshape(128, NL3))


def _lay_cols(w):
    n = w.shape[1]
    return np.ascontiguousarray(w.reshape(NC_, 128, n).transpose(1, 0, 2).reshape(128, NC_ * n))


def make_inputs(phases, inp):
    m = {"ln_g": _lay_ln(inp["ln_g"]), "ln_b": _lay_ln(inp["ln_b"]), "consts": _consts()}
    has_mix = any(p[0] == "mix" for p in phases)
    if has_mix:
        m["amask"] = _alibi_mask()
        m["wgu"] = np.ascontiguousarray(inp["w_gate_up"].transpose(1, 0, 2))
        m["bgu"] = np.ascontiguousarray(np.broadcast_to(inp["b_gate_up"][None], (128, DEPTH, 256)))
        m["gng"] = np.ascontiguousarray(inp["gla_norm_g"].reshape(DEPTH, 4, 128).transpose(2, 0, 1).reshape(128, DEPTH * 4))
        m["gnb"] = np.ascontiguousarray(inp["gla_norm_b"].reshape(DEPTH, 4, 128).transpose(2, 0, 1).reshape(128, DEPTH * 4))
    for ph in phases:
        if ph[0] == "ffn":
            l, i = ph[1], ph[2]
            pre = "ffn1" if i == 0 else "ffn2"
            m[f"f{i}w1_{l}"] = _lay_w13(inp[pre + "_w1"][l])
            m[f"f{i}w3_{l}"] = _lay_w13(inp[pre + "_w3"][l])
            m[f"f{i}w2_{l}"] = np.ascontiguousarray(inp[pre + "_w2"][l])
        else:
            l = ph[1]
            w = inp["w_in"][l]
            m[f"wglr_{l}"] = _lay_cols(w[:, O_GLR:O_GLR + 16])
            m[f"wgqk_{l}"] = _lay_cols(w[:, O_GQ:O_GQ + 512])
            m[f"wgkv_{l}"] = _lay_cols(w[:, O_GK:O_GK + 768])
            m[f"wgr_{l}"] = _lay_cols(w[:, O_GR:O_GR + 512])
            m[f"waqkv_{l}"] = np.stack([_lay_cols(np.concatenate(
                [w[:, O_AQ + c * 128:O_AQ + (c + 1) * 128], w[:, O_AK + c * 128:O_AK + (c + 1) * 128],
                 w[:, O_AV + c * 128:O_AV + (c + 1) * 128]], axis=1)) for c in range(4)], axis=0)
            m[f"wgab_{l}"] = np.stack([_lay_cols(np.concatenate(
                [w[:, O_GA + c * 128:O_GA + (c + 1) * 128], w[:, O_GB + c * 128:O_GB + (c + 1) * 128]], axis=1))
                for c in range(NC_)], axis=0)
            wa = inp["w_attn_proj"][l]
            m[f"wap_{l}"] = np.ascontiguousarray(wa.reshape(4, 128, NC_, 128).transpose(2, 1, 0, 3).reshape(NC_, 128, 4 * 128))
            wg = inp["w_gla_proj"][l]
            m[f"wgp_{l}"] = np.ascontiguousarray(wg.reshape(4, 128, NC_, 128).transpose(2, 1, 0, 3).reshape(NC_, 128, 4 * 128))
            wo_ = inp["w_out"][l]
            m[f"wo_{l}"] = np.ascontiguousarray(wo_.reshape(NC_, 128, NC_, 128).transpose(2, 1, 0, 3).reshape(NC_, 128, NC_ * 128))
    return m


def run_phases(phases, x, inp, n_cores=8, trace=False, debug=None):
    nc = build(phases, debug)
    shared = make_inputs(phases, inp)
    in_maps = []
    for b in range(n_cores):
        d = dict(shared)
        d["xT"] = np.ascontiguousarray(x[b].T)
        in_maps.append(d)
    res = run_bass_kernel_spmd(nc, in_maps, core_ids=list(range(n_cores)), trace=trace)
    out = np.stack([np.ascontiguousarray(r["yT"].T) for r in res.results], axis=0)
    return out, res


LAUNCHES = [[("ffn", 0, 0), ("mix", 0), ("ffn", 0, 1), ("ffn", 1, 0), ("mix", 1), ("ffn", 1, 1)]]


def kernel(**inputs):
    inp = {k: np.asarray(v) for k, v in inputs.items()}
    x = np.ascontiguousarray(inp["x"], dtype=np.float32)
    for phases in LAUNCHES:
        x, _ = run_phases(phases, x, inp)
    return np.ascontiguousarray(x, dtype=np.float32)
```
